# Optimizing a Trainium2 kernel written in Bass

```python
import jax, jax.numpy as jnp
from jax import lax
import numpy as np

D_MODEL = 1024
BATCH = 4
SEQ = 4096
DEPTH = 2
DEC_BATCH = 16
DEC_SEQ = 16
PAST_LEN = 2048

CHUNK = 64
HEAD_DIM = 64
H_A = 8
H_B = 8
H_C = 8
H_D = 8
D_A = H_A * HEAD_DIM
D_B = H_B * HEAD_DIM
D_C = H_C * HEAD_DIM
D_D = H_D * HEAD_DIM
RWKV_W_LORA = 64
RWKV_A_LORA = 64
RWKV_G_LORA = 128
A_COLS = 3 * D_A + RWKV_W_LORA + RWKV_A_LORA + RWKV_G_LORA
B_COLS = 3 * D_B + H_B
AB_COLS = A_COLS + B_COLS
C_COLS = 3 * D_C
D_COLS = 4 * D_D
CD_COLS = C_COLS + D_COLS
FOX_Q_BLOCK = 128
BAND_CHUNKS = 8
BAND = BAND_CHUNKS * CHUNK
REL_CLIP = 128
D_FF = 2816
CONV_W = 3
N_AB = (DEPTH + 1) // 2
N_CD = DEPTH // 2
RMS_EPS = 1e-6
GN_EPS = 64e-5
ATTN_SCALE = HEAD_DIM ** -0.5

kernel_name = 'hybrid_streaming_encoder_step'

STATE_NAMES = ('fox_k', 'fox_v', 'fox_logf', 'rwkv', 'rwkv_shift', 'chunk_k', 'chunk_v', 'hgrn', 'ffn_conv')


def rmsnorm(x, g):
    xf = x.astype(jnp.float32)
    y = xf * lax.rsqrt(jnp.mean(xf * xf, axis=-1, keepdims=True) + RMS_EPS)
    return (y * g.astype(jnp.float32)).astype(x.dtype)


def adaln(cs, w, b):
    shift, scale, gate = jnp.split(cs @ w + b, 3, axis=-1)
    return shift[:, None], scale[:, None], gate[:, None]


def wkv7_scan(r, w, k, v, a, b, s0):
    seq = tuple(t.swapaxes(0, 1) for t in (r, w, k, v, a, b))
    def step(s, inp):
        r_t, w_t, k_t, v_t, a_t, b_t = inp
        sa = jnp.einsum('bhvk,bhk->bhv', s, a_t)
        s = s * w_t[:, :, None, :] + sa[..., None] * b_t[:, :, None, :] + v_t[..., None] * k_t[:, :, None, :]
        return s, jnp.einsum('bhvk,bhk->bhv', s, r_t)
    s_t, y = lax.scan(step, s0.astype(jnp.float32), seq)
    return y.swapaxes(0, 1), s_t


def rwkv7_mix(z, shift0, s0, mu, w0, w2, a0, a2, g2, k_k, k_a, r_k, lnx_g, lnx_b):
    B, T, _ = z.shape
    dt = z.dtype
    z_prev = jnp.concatenate([shift0[:, None].astype(dt), z[:, :-1]], axis=1)
    zs = z + mu * (z_prev - z)
    r, k, v = (zs[..., n * D_A:(n + 1) * D_A] for n in range(3))
    o = 3 * D_A
    w_lr = zs[..., o:o + RWKV_W_LORA]
    a_lr = zs[..., o + RWKV_W_LORA:o + RWKV_W_LORA + RWKV_A_LORA]
    g_lr = zs[..., o + RWKV_W_LORA + RWKV_A_LORA:]
    w = -jax.nn.softplus(-(w0 + jnp.tanh(w_lr) @ w2).astype(jnp.float32)) - 0.5
    decay = jnp.exp(-jnp.exp(w))
    a = jax.nn.sigmoid((a0 + a_lr @ a2).astype(jnp.float32))
    g = jax.nn.sigmoid(g_lr) @ g2
    heads = lambda t: t.reshape(B, T, H_A, HEAD_DIM).astype(jnp.float32)
    r, k, v, decay, a = heads(r), heads(k), heads(v), heads(decay), heads(a)
    kk = k * k_k.reshape(H_A, HEAD_DIM).astype(jnp.float32)
    kk = kk / jnp.maximum(jnp.sqrt(jnp.sum(kk * kk, axis=-1, keepdims=True)), 1e-12)
    k = k * (1.0 + (a - 1.0) * k_a.reshape(H_A, HEAD_DIM).astype(jnp.float32))
    y, s_t = wkv7_scan(r, decay, k, v, -kk, kk * a, s0)
    mean = jnp.mean(y, axis=-1, keepdims=True)
    var = jnp.mean(jnp.square(y - mean), axis=-1, keepdims=True)
    y = (y - mean) * lax.rsqrt(var + GN_EPS) * lnx_g.reshape(H_A, HEAD_DIM).astype(jnp.float32) \
        + lnx_b.reshape(H_A, HEAD_DIM).astype(jnp.float32)
    y = y + jnp.sum(r * k * r_k.astype(jnp.float32), axis=-1, keepdims=True) * v
    return y.reshape(B, T, D_A).astype(dt) * g, s_t, z[:, -1]


def fox_prompt(q, k, v, logf):
    B, S, H, dh = q.shape
    nb = S // FOX_Q_BLOCK
    cum = jnp.cumsum(logf.astype(jnp.float32), axis=1)
    pos = jnp.arange(S)
    q_blocks = q.reshape(B, nb, FOX_Q_BLOCK, H, dh).swapaxes(0, 1)
    f_blocks = cum.reshape(B, nb, FOX_Q_BLOCK, H).swapaxes(0, 1)
    p_blocks = pos.reshape(nb, FOX_Q_BLOCK)
    f_keys = cum.transpose(0, 2, 1)[:, :, None, :]
    def block(args):
        qi, fi, pi = args
        s = jnp.einsum('bqhd,bkhd->bhqk', qi, k).astype(jnp.float32) * ATTN_SCALE \
            + fi.transpose(0, 2, 1)[..., None] - f_keys
        s = jnp.where(pi[:, None] >= pos[None, :], s, -jnp.inf)
        p = jax.nn.softmax(s, axis=-1).astype(v.dtype)
        return jnp.einsum('bhqk,bkhd->bqhd', p, v)
    out = lax.map(block, (q_blocks, f_blocks, p_blocks))
    return out.swapaxes(0, 1).reshape(B, S, H, dh)


def fox_sample(q, k, v, logf, ck, cv, clogf):
    T = q.shape[1]
    P = ck.shape[1]
    keys = jnp.concatenate([ck.astype(k.dtype), k], axis=1)
    vals = jnp.concatenate([cv.astype(v.dtype), v], axis=1)
    cum = jnp.cumsum(jnp.concatenate([clogf.astype(jnp.float32), logf], axis=1), axis=1)
    s = jnp.einsum('bqhd,bkhd->bhqk', q, keys).astype(jnp.float32) * ATTN_SCALE \
        + cum[:, P:].transpose(0, 2, 1)[..., None] - cum.transpose(0, 2, 1)[:, :, None, :]
    mask = jnp.arange(P + T)[None, :] <= (P + jnp.arange(T))[:, None]
    s = jnp.where(mask, s, -jnp.inf)
    p = jax.nn.softmax(s, axis=-1).astype(vals.dtype)
    return jnp.einsum('bhqk,bkhd->bqhd', p, vals)


def chunk_attn_prompt(q, k, v, rel_bias):
    B, S, H, dh = q.shape
    nc = S // CHUNK
    nk = BAND + CHUNK
    pad = ((0, 0), (BAND, 0), (0, 0), (0, 0))
    kp, vp = jnp.pad(k, pad), jnp.pad(v, pad)
    idx = (jnp.arange(nc) * CHUNK)[:, None] + jnp.arange(nk)[None, :]
    kb, vb = kp[:, idx], vp[:, idx]
    qc = q.reshape(B, nc, CHUNK, H, dh)
    rel = jnp.arange(CHUNK)[:, None] + BAND - jnp.arange(nk)[None, :]
    bias = rel_bias[:, jnp.clip(rel, -REL_CLIP, REL_CLIP) + REL_CLIP].astype(jnp.float32)
    s = jnp.einsum('bcqhd,bckhd->bchqk', qc, kb).astype(jnp.float32) * ATTN_SCALE + bias[None, None]
    valid = idx >= BAND
    s = jnp.where(valid[None, :, None, None, :], s, -jnp.inf)
    p = jax.nn.softmax(s, axis=-1).astype(vb.dtype)
    return jnp.einsum('bchqk,bckhd->bcqhd', p, vb).reshape(B, S, H, dh)


def chunk_attn_sample(q, k, v, ck, cv, rel_bias):
    T = q.shape[1]
    P = ck.shape[1]
    keys = jnp.concatenate([ck.astype(k.dtype), k], axis=1)
    vals = jnp.concatenate([cv.astype(v.dtype), v], axis=1)
    rel = jnp.arange(T)[:, None] + P - jnp.arange(P + T)[None, :]
    bias = rel_bias[:, jnp.clip(rel, -REL_CLIP, REL_CLIP) + REL_CLIP].astype(jnp.float32)
    s = jnp.einsum('bqhd,bkhd->bhqk', q, keys).astype(jnp.float32) * ATTN_SCALE + bias[None]
    p = jax.nn.softmax(s, axis=-1).astype(vals.dtype)
    return jnp.einsum('bhqk,bkhd->bqhd', p, vals)


def gla_chunkwise(q, g, k, v, s0, L):
    B, T, H, K = q.shape
    n = T // L
    to_chunks = lambda t: t.reshape(B, n, L, H, t.shape[-1]).swapaxes(0, 1)
    causal = jnp.tril(jnp.ones((L, L), bool))
    def body(s, inp):
        qc, gc, kc, vc = inp
        b = jnp.cumsum(gc, axis=1)
        diff = jnp.where(causal[None, :, :, None, None], b[:, :, None] - b[:, None, :], -jnp.inf)
        att = jnp.einsum('bthk,bshk,btshk->bhts', qc, kc, jnp.exp(diff))
        o = jnp.einsum('bhts,bshv->bthv', att, vc) + jnp.einsum('bthk,bhkv->bthv', qc * jnp.exp(b), s)
        b_last = b[:, -1]
        s = jnp.exp(b_last)[..., None] * s + jnp.einsum('bshk,bshv->bhkv', kc * jnp.exp(b_last[:, None] - b), vc)
        return s, o
    s_t, o = lax.scan(body, s0, tuple(to_chunks(t) for t in (q, g, k, v)))
    return o.swapaxes(0, 1).reshape(B, T, H, v.shape[-1]), s_t


def hgrn2_mix(zq, zf, zi, zg, s0, lb, norm_g):
    B, T, _ = zq.shape
    dt = zq.dtype
    heads = lambda t: t.reshape(B, T, H_D, HEAD_DIM).astype(jnp.float32)
    q = jax.nn.silu(heads(zq))
    xf = heads(zf)
    v = heads(zi)
    lb = lb.reshape(H_D, HEAD_DIM)
    logf = jnp.logaddexp(jnp.log(lb), jnp.log1p(-lb) + jax.nn.log_sigmoid(xf))
    k_in = (1.0 - lb) * jax.nn.sigmoid(-xf)
    o, s_t = gla_chunkwise(q, logf, k_in, v, s0.astype(jnp.float32), min(CHUNK, T))
    o = o * lax.rsqrt(jnp.mean(o * o, axis=-1, keepdims=True) + RMS_EPS) * norm_g.reshape(H_D, HEAD_DIM).astype(jnp.float32)
    return o.reshape(B, T, D_D).astype(dt) * jax.nn.silu(zg), s_t


def ab_mixer(h, i, P, cache):
    B, T, _ = h.shape
    dt = h.dtype
    z = h @ P['ab_w_in'][i]
    za, zb = z[..., :A_COLS], z[..., A_COLS:]
    if cache is None:
        shift0 = jnp.zeros((B, A_COLS), dt)
        s0 = jnp.zeros((B, H_A, HEAD_DIM, HEAD_DIM), jnp.float32)
    else:
        shift0 = cache['rwkv_shift'][i]
        s0 = cache['rwkv'][i]
    ya, s_t, shift_t = rwkv7_mix(za, shift0, s0, P['rwkv_mu'][i], P['rwkv_w0'][i], P['rwkv_w2'][i],
                                 P['rwkv_a0'][i], P['rwkv_a2'][i], P['rwkv_g2'][i], P['rwkv_k_k'][i],
                                 P['rwkv_k_a'][i], P['rwkv_r_k'][i], P['rwkv_lnx_g'][i], P['rwkv_lnx_b'][i])
    q, k, v = (zb[..., n * D_B:(n + 1) * D_B].reshape(B, T, H_B, HEAD_DIM) for n in range(3))
    logf = jax.nn.log_sigmoid((zb[..., 3 * D_B:] + P['fox_b_f'][i]).astype(jnp.float32))
    if cache is None:
        yb = fox_prompt(q, k, v, logf)
    else:
        yb = fox_sample(q, k, v, logf, cache['fox_k'][i], cache['fox_v'][i], cache['fox_logf'][i])
    y = jnp.concatenate([ya, yb.reshape(B, T, D_B).astype(dt)], axis=-1) @ P['ab_w_out'][i]
    st = {'fox_k': k, 'fox_v': v, 'fox_logf': logf.astype(dt), 'rwkv': s_t.astype(dt), 'rwkv_shift': shift_t}
    return y, st


def cd_mixer(h, i, layer, P, cache):
    B, T, _ = h.shape
    dt = h.dtype
    z = h @ P['cd_w_in'][i]
    q, k, v = (z[..., n * D_C:(n + 1) * D_C].reshape(B, T, H_C, HEAD_DIM) for n in range(3))
    rel_bias = P['chunk_rel_bias'][i]
    if cache is None:
        yc = chunk_attn_prompt(q, k, v, rel_bias)
        keep = min(BAND, T)
        k_new, v_new = k[:, T - keep:], v[:, T - keep:]
        s0 = jnp.zeros((B, H_D, HEAD_DIM, HEAD_DIM), jnp.float32)
    else:
        yc = chunk_attn_sample(q, k, v, cache['chunk_k'][i], cache['chunk_v'][i], rel_bias)
        k_new, v_new = k, v
        s0 = cache['hgrn'][i]
    zq, zf, zi, zg = (z[..., C_COLS + n * D_D:C_COLS + (n + 1) * D_D] for n in range(4))
    sm = jax.nn.softmax(P['hgrn_lb_table'].astype(jnp.float32), axis=0)
    lb = (jnp.cumsum(sm, axis=0) - sm[0])[layer]
    yd, s_t = hgrn2_mix(zq, zf, zi, zg, s0, lb, P['hgrn_norm_g'][i])
    y = jnp.concatenate([yc.reshape(B, T, D_C).astype(dt), yd], axis=-1) @ P['cd_w_out'][i]
    st = {'chunk_k': k_new, 'chunk_v': v_new, 'hgrn': s_t.astype(dt)}
    return y, st


def conv_ffn(h, buf, w_up, conv_w, conv_b, w_down):
    T = h.shape[1]
    u = h @ w_up
    ext = jnp.concatenate([buf.astype(u.dtype), u], axis=1)
    acc = conv_b
    for j in range(CONV_W):
        acc = acc + conv_w[j] * ext[:, j:j + T]
    a, b = jnp.split(acc, 2, axis=-1)
    return (jax.nn.silu(a) * b) @ w_down, ext[:, T:]


def trunk(x, c, P, cache):
    B = x.shape[0]
    dt = x.dtype
    cs = jax.nn.silu(c)
    new = {name: [] for name in STATE_NAMES}
    for layer in range(DEPTH):
        shift, scale, gate = adaln(cs, P['ada_w'][layer, 0], P['ada_b'][layer, 0])
        h = rmsnorm(x, P['norm_mix_g'][layer]) * (1.0 + scale) + shift
        if layer % 2 == 0:
            y, st = ab_mixer(h, layer // 2, P, cache)
        else:
            y, st = cd_mixer(h, layer // 2, layer, P, cache)
        for name, val in st.items():
            new[name].append(val)
        x = x + gate * y
        shift, scale, gate = adaln(cs, P['ada_w'][layer, 1], P['ada_b'][layer, 1])
        h = rmsnorm(x, P['norm_ffn_g'][layer]) * (1.0 + scale) + shift
        buf = jnp.zeros((B, CONV_W - 1, 2 * D_FF), dt) if cache is None else cache['ffn_conv'][layer]
        y, buf_new = conv_ffn(h, buf, P['ffn_w_up'][layer], P['ffn_conv_w'][layer], P['ffn_conv_b'][layer], P['ffn_w_down'][layer])
        new['ffn_conv'].append(buf_new)
        x = x + gate * y
    return rmsnorm(x, P['final_norm_g']), {name: jnp.stack(vals) for name, vals in new.items()}


def setup_inputs(seed: int = 0) -> dict:
    key = jax.random.key(seed)
    ks = iter(jax.random.split(key, 64))
    nrm = lambda shape, s=1.0: s * jax.random.normal(next(ks), shape, jnp.float32)
    uni = lambda shape, lo, hi: jax.random.uniform(next(ks), shape, jnp.float32, lo, hi)
    c_cache = min(BAND, PAST_LEN)
    return {
        'x_prompt': nrm((BATCH, SEQ, D_MODEL)),
        'x_sample': nrm((DEC_BATCH, DEC_SEQ, D_MODEL)),
        'c_prompt': nrm((BATCH, D_MODEL)),
        'c_sample': nrm((DEC_BATCH, D_MODEL)),
        'cache_fox_k': nrm((N_AB, DEC_BATCH, PAST_LEN, H_B, HEAD_DIM)),
        'cache_fox_v': nrm((N_AB, DEC_BATCH, PAST_LEN, H_B, HEAD_DIM)),
        'cache_fox_logf': jax.nn.log_sigmoid(4.0 + nrm((N_AB, DEC_BATCH, PAST_LEN, H_B))),
        'state_rwkv': nrm((N_AB, DEC_BATCH, H_A, HEAD_DIM, HEAD_DIM), 0.5),
        'state_rwkv_shift': nrm((N_AB, DEC_BATCH, A_COLS)),
        'cache_chunk_k': nrm((N_CD, DEC_BATCH, c_cache, H_C, HEAD_DIM)),
        'cache_chunk_v': nrm((N_CD, DEC_BATCH, c_cache, H_C, HEAD_DIM)),
        'state_hgrn': nrm((N_CD, DEC_BATCH, H_D, HEAD_DIM, HEAD_DIM), 0.5),
        'state_ffn_conv': nrm((DEPTH, DEC_BATCH, CONV_W - 1, 2 * D_FF)),
        'ada_w': nrm((DEPTH, 2, D_MODEL, 3 * D_MODEL), 0.5 * D_MODEL ** -0.5),
        'ada_b': nrm((DEPTH, 2, 3 * D_MODEL), 0.02),
        'norm_mix_g': 1.0 + nrm((DEPTH, D_MODEL), 0.02),
        'norm_ffn_g': 1.0 + nrm((DEPTH, D_MODEL), 0.02),
        'ab_w_in': nrm((N_AB, D_MODEL, AB_COLS), D_MODEL ** -0.5),
        'rwkv_mu': uni((N_AB, A_COLS), 0.0, 1.0),
        'rwkv_w0': uni((N_AB, D_A), -6.0, -1.0),
        'rwkv_w2': nrm((N_AB, RWKV_W_LORA, D_A), 0.5 * RWKV_W_LORA ** -0.5),
        'rwkv_a0': nrm((N_AB, D_A), 0.1),
        'rwkv_a2': nrm((N_AB, RWKV_A_LORA, D_A), RWKV_A_LORA ** -0.5),
        'rwkv_g2': nrm((N_AB, RWKV_G_LORA, D_A), RWKV_G_LORA ** -0.5),
        'rwkv_k_k': 0.85 + nrm((N_AB, D_A), 0.05),
        'rwkv_k_a': 1.0 + nrm((N_AB, D_A), 0.05),
        'rwkv_r_k': nrm((N_AB, H_A, HEAD_DIM), 0.1),
        'rwkv_lnx_g': 1.0 + nrm((N_AB, D_A), 0.02),
        'rwkv_lnx_b': nrm((N_AB, D_A), 0.02),
        'fox_b_f': 4.0 + nrm((N_AB, H_B), 0.5),
        'ab_w_out': nrm((N_AB, D_A + D_B, D_MODEL), (D_A + D_B) ** -0.5),
        'cd_w_in': nrm((N_CD, D_MODEL, CD_COLS), D_MODEL ** -0.5),
        'chunk_rel_bias': nrm((N_CD, H_C, 2 * REL_CLIP + 1), 0.5),
        'hgrn_lb_table': nrm((DEPTH, D_D)),
        'hgrn_norm_g': 1.0 + nrm((N_CD, D_D), 0.02),
        'cd_w_out': nrm((N_CD, D_C + D_D, D_MODEL), (D_C + D_D) ** -0.5),
        'ffn_w_up': nrm((DEPTH, D_MODEL, 2 * D_FF), D_MODEL ** -0.5),
        'ffn_conv_w': nrm((DEPTH, CONV_W, 2 * D_FF), CONV_W ** -0.5),
        'ffn_conv_b': nrm((DEPTH, 2 * D_FF), 0.02),
        'ffn_w_down': nrm((DEPTH, D_FF, D_MODEL), D_FF ** -0.5),
        'final_norm_g': 1.0 + nrm((D_MODEL,), 0.02),
    }


def reference(x_prompt, x_sample, c_prompt, c_sample,
              cache_fox_k, cache_fox_v, cache_fox_logf, state_rwkv, state_rwkv_shift,
              cache_chunk_k, cache_chunk_v, state_hgrn, state_ffn_conv,
              ada_w, ada_b, norm_mix_g, norm_ffn_g,
              ab_w_in, rwkv_mu, rwkv_w0, rwkv_w2, rwkv_a0, rwkv_a2, rwkv_g2, rwkv_k_k, rwkv_k_a,
              rwkv_r_k, rwkv_lnx_g, rwkv_lnx_b, fox_b_f, ab_w_out,
              cd_w_in, chunk_rel_bias, hgrn_lb_table, hgrn_norm_g, cd_w_out,
              ffn_w_up, ffn_conv_w, ffn_conv_b, ffn_w_down, final_norm_g):
    P = {'ada_w': ada_w, 'ada_b': ada_b, 'norm_mix_g': norm_mix_g, 'norm_ffn_g': norm_ffn_g,
         'ab_w_in': ab_w_in, 'rwkv_mu': rwkv_mu, 'rwkv_w0': rwkv_w0, 'rwkv_w2': rwkv_w2,
         'rwkv_a0': rwkv_a0, 'rwkv_a2': rwkv_a2, 'rwkv_g2': rwkv_g2, 'rwkv_k_k': rwkv_k_k,
         'rwkv_k_a': rwkv_k_a, 'rwkv_r_k': rwkv_r_k, 'rwkv_lnx_g': rwkv_lnx_g, 'rwkv_lnx_b': rwkv_lnx_b,
         'fox_b_f': fox_b_f, 'ab_w_out': ab_w_out, 'cd_w_in': cd_w_in, 'chunk_rel_bias': chunk_rel_bias,
         'hgrn_lb_table': hgrn_lb_table, 'hgrn_norm_g': hgrn_norm_g, 'cd_w_out': cd_w_out,
         'ffn_w_up': ffn_w_up, 'ffn_conv_w': ffn_conv_w, 'ffn_conv_b': ffn_conv_b,
         'ffn_w_down': ffn_w_down, 'final_norm_g': final_norm_g}
    cache = {'fox_k': cache_fox_k, 'fox_v': cache_fox_v, 'fox_logf': cache_fox_logf,
             'rwkv': state_rwkv, 'rwkv_shift': state_rwkv_shift, 'chunk_k': cache_chunk_k,
             'chunk_v': cache_chunk_v, 'hgrn': state_hgrn, 'ffn_conv': state_ffn_conv}
    y_prompt, sp = trunk(x_prompt, c_prompt, P, None)
    y_sample, ss = trunk(x_sample, c_sample, P, cache)
    return (y_prompt, y_sample,
            sp['fox_k'], sp['fox_v'], sp['fox_logf'], sp['rwkv'], sp['rwkv_shift'],
            sp['chunk_k'], sp['chunk_v'], sp['hgrn'], sp['ffn_conv'],
            ss['fox_k'], ss['fox_v'], ss['fox_logf'], ss['rwkv'], ss['rwkv_shift'],
            ss['chunk_k'], ss['chunk_v'], ss['hgrn'], ss['ffn_conv'])
```

```python
import os
import numpy as np
import concourse.bass as bass
import concourse.mybir as mybir
from concourse.bass_utils import run_bass_kernel_spmd

F32 = mybir.dt.float32
BF16 = mybir.dt.bfloat16
AF = mybir.ActivationFunctionType
ALU = mybir.AluOpType
AX = mybir.AxisListType

N_CORES = 8


class Prog:
    ENGS = ("pe", "act", "dve", "pool", "sp")
    DMA_SLOTS = {"sp": 20, "pool": 12, "act": 8}

    def __init__(self, nc, stack):
        self.nc = nc
        self.ops = []
        self.same_engine_sync = True
        self.esem = {e: stack.enter_context(nc.semaphore("s_" + e)) for e in ("pe", "act", "dve", "pool")}
        self.dsem = {q: [stack.enter_context(nc.semaphore("d_%s%d" % (q, i))) for i in range(k)]
                     for q, k in self.DMA_SLOTS.items()}
        self.ecount = {e: 0 for e in self.esem}
        self.dcount = {q: 0 for q in self.dsem}
        self.slot_last = {}
        self.last_w = {}
        self.readers = {}
        self.known = {e: {} for e in self.ENGS}
        self.final_dma = {}
        self.n_total = 0

    def op(self, eng, fn, reads=(), writes=(), dma=False):
        self.n_rec = getattr(self, "n_rec", 0) + 1
        if self.n_rec == int(os.environ.get("K_SHOW", "-1")):
            import traceback
            traceback.print_stack(limit=4)
            print("SHOW op", eng, reads, writes)
        if self.n_rec > int(os.environ.get("K_MAXOPS", "100000000")):
            return
        self.ops.append(dict(eng=eng, fn=fn, reads=tuple(reads), writes=tuple(writes), dma=dma, serial=getattr(self, "pe_serial", False)))

    def mm(self, out, lhsT, rhs, start=True, stop=True, reads=(), writes=()):
        self.op("pe", lambda e: e.matmul(out, lhsT, rhs, start=start, stop=stop), reads, writes)

    def tr(self, out, in_, ident, reads=(), writes=()):
        self.op("pe", lambda e: e.transpose(out, in_, ident), reads, writes)

    def act(self, out, in_, func, bias=0.0, scale=1.0, reads=(), writes=(), accum_out=None):
        if accum_out is None:
            self.op("act", lambda e: e.activation(out, in_, func, bias=bias, scale=scale), reads, writes)
        else:
            self.op("act", lambda e: e.activation(out, in_, func, bias=bias, scale=scale, accum_out=accum_out), reads, writes)

    def tt(self, eng, out, in0, in1, op, reads=(), writes=()):
        self.op(eng, lambda e: e.tensor_tensor(out, in0, in1, op), reads, writes)

    def ts(self, eng, out, in0, s1, s2, op0, op1=None, reads=(), writes=()):
        if op1 is None:
            self.op(eng, lambda e: e.tensor_scalar(out, in0, s1, None, op0), reads, writes)
        else:
            self.op(eng, lambda e: e.tensor_scalar(out, in0, s1, s2, op0, op1), reads, writes)

    def stt(self, out, in0, scalar, in1, op0, op1, reads=(), writes=()):
        self.op("dve", lambda e: e.scalar_tensor_tensor(out, in0, scalar, in1, op0, op1), reads, writes)

    def copy(self, eng, out, in_, reads=(), writes=()):
        if eng == "act":
            self.op("act", lambda e: e.copy(out, in_), reads, writes)
        else:
            self.op(eng, lambda e: e.tensor_copy(out, in_), reads, writes)

    def memset(self, eng, ap, val, writes=()):
        self.op(eng, lambda e: e.memset(ap, val), (), writes)

    def dma(self, out, in_, reads=(), writes=(), q="sp", **kw):
        self.op(q, lambda e: e.dma_start(out=out, in_=in_, **kw), reads, writes, dma=True)

    def semof(self, key):
        return self.esem[key[1]] if key[0] == "e" else self.dsem[key[1]][key[2]]

    def flush(self, final=False):
        import inspect
        fr = inspect.stack()[1]
        if not hasattr(self, "flush_log"):
            self.flush_log = []
        _cnt = {}
        for _o in self.ops:
            _cnt[_o["eng"]] = _cnt.get(_o["eng"], 0) + 1
        self.flush_log.append(("%s:%d" % (fr.function, fr.lineno), len(self.ops), _cnt))
        nc = self.nc
        ops = self.ops
        self.ops = []
        n = len(ops)
        self.n_total += n
        plan = {e: [] for e in self.ENGS}
        for o in ops:
            e = o["eng"]
            deps = []
            if o["dma"]:
                j = self.dcount[e]
                self.dcount[e] += 1
                k = len(self.dsem[e])
                key = ("d", e, j % k)
                val = 16 * (j // k + 1)
                if key in self.slot_last:
                    deps.append(self.slot_last[key])
                mysig = (key, val, e, True)
                self.slot_last[key] = mysig
                self.final_dma[key] = val
            else:
                self.ecount[e] += 1
                mysig = (("e", e), self.ecount[e], e, False)
            for r in o["reads"]:
                if r in self.last_w:
                    deps.append(self.last_w[r])
            for w in o["writes"]:
                if w in self.last_w:
                    deps.append(self.last_w[w])
                deps.extend(self.readers.get(w, ()))
            for r in o["reads"]:
                self.readers.setdefault(r, []).append(mysig)
            for w in o["writes"]:
                self.last_w[w] = mysig
                self.readers[w] = []
            need = {}
            for (dkey, dval, deng, ddma) in deps:
                if (not ddma) and deng == e:
                    if (not self.same_engine_sync) or e == "pe":
                        continue
                if need.get(dkey, 0) < dval:
                    need[dkey] = dval
            if e == "pe" and o.get("serial") and self.ecount["pe"] > 1:
                need[("e", "pe")] = max(need.get(("e", "pe"), 0), self.ecount["pe"] - 1)
            wl = []
            for dkey, dval in need.items():
                if self.known[e].get(dkey, 0) >= dval:
                    continue
                self.known[e][dkey] = dval
                wl.append((dkey, dval))
            plan[e].append((wl, o["fn"], mysig[0], o["dma"]))

        def run_engine(ename, eng):
            for wl, fn, key, is_dma in plan[ename]:
                for dkey, dval in wl[:-1]:
                    eng.wait_ge(self.semof(dkey), dval)
                ins = fn(eng)
                if wl:
                    ins._wait_ge(self.semof(wl[-1][0]), wl[-1][1])
                ins.then_inc(self.semof(key), 16 if is_dma else 1)
            if final and ename == "sp":
                for key, val in self.final_dma.items():
                    eng.wait_ge(self.semof(key), val)
                for e2 in self.esem:
                    if self.ecount[e2] > 0:
                        eng.wait_ge(self.esem[e2], self.ecount[e2])

        with nc.Block() as block:
            if plan["pe"]:
                @block.tensor
                def _(eng):
                    run_engine("pe", eng)
            if plan["act"]:
                @block.scalar
                def _(eng):
                    run_engine("act", eng)
            if plan["dve"]:
                @block.vector
                def _(eng):
                    run_engine("dve", eng)
            if plan["pool"]:
                @block.gpsimd
                def _(eng):
                    run_engine("pool", eng)
            if plan["sp"] or final:
                @block.sync
                def _(eng):
                    run_engine("sp", eng)
        return n


D = 1024
KC = 8
T_PROMPT = 4096
T_SAMPLE = 16
NSEQ_S = 2
P_FOX = 2048
P_CHUNK = 512
A_COLS = 1792
AB_COLS = 3336
CD_COLS = 3584
DFF = 2816
RMS_EPS = 1e-6
GN_EPS = 64e-5
DECAY_C = float(np.exp(-0.5))


class Ctx:
    def __init__(self, nc, P, st):
        self.nc = nc
        self.P = P
        self.st = st
        self.uid = 0
        self.dram = {}

    def name(self, base):
        self.uid += 1
        return "%s_%d" % (base, self.uid)

    def sb(self, st, base, shape, dt=F32):
        return st.enter_context(self.nc.sbuf_tensor(self.name(base), list(shape), dt))

    def ps(self, st, base, shape, dt=F32):
        return st.enter_context(self.nc.psum_tensor(self.name(base), list(shape), dt))

    def dr(self, base, shape, dt=F32, kind=None):
        if kind is None:
            t = self.nc.dram_tensor(base, list(shape), dt)
        else:
            t = self.nc.dram_tensor(base, list(shape), dt, kind=kind)
        ap = t.ap()
        self.dram[base] = ap
        return ap


class Rot:
    def __init__(self, C, st, base, shape, dt, n, psum=False):
        self.tiles = [(C.ps if psum else C.sb)(st, base, shape, dt) for _ in range(n)]
        self.keys = [(base, C.uid, i) for i in range(n)]
        self.i = -1

    def next(self):
        self.i = (self.i + 1) % len(self.tiles)
        return self.tiles[self.i], self.keys[self.i]


def build_consts(C):
    P, st = C.P, C.st
    k = {}
    identF = C.sb(st, "identF", [128, 128], F32)
    P.memset("pool", identF[:], 1.0, writes=["identF"])
    P.op("pool", lambda e: e.affine_select(identF[:], identF[:], [[-1, 128]], ALU.is_equal, 0.0, base=0, channel_multiplier=1),
         reads=["identF"], writes=["identF"])
    identB = C.sb(st, "identB", [128, 128], BF16)
    P.copy("pool", identB[:], identF[:], reads=["identF"], writes=["identB"])
    flipF = C.sb(st, "flipF", [128, 128], F32)
    P.memset("pool", flipF[:], 1.0, writes=["flipF"])
    P.op("pool", lambda e: e.affine_select(flipF[:], flipF[:], [[1, 128]], ALU.is_equal, 0.0, base=-127, channel_multiplier=1),
         reads=["flipF"], writes=["flipF"])
    onesB = C.sb(st, "onesB", [128, 128], BF16)
    P.memset("pool", onesB[:], 1.0, writes=["onesB"])
    onesF = C.sb(st, "onesF", [128, 128], F32)
    P.memset("pool", onesF[:], 1.0, writes=["onesF"])
    bo = C.sb(st, "blockones", [128, 128], BF16)
    P.memset("pool", bo[:], 0.0, writes=["bo"])
    P.memset("pool", bo[0:64, 0:64], 1.0, writes=["bo"])
    P.memset("pool", bo[64:128, 64:128], 1.0, writes=["bo"])
    triF = C.sb(st, "triF", [128, 128], F32)
    P.memset("pool", triF[:], 1.0, writes=["triF"])
    P.op("pool", lambda e: e.affine_select(triF[:], triF[:], [[1, 128]], ALU.is_ge, 0.0, base=0, channel_multiplier=-1),
         reads=["triF"], writes=["triF"])
    def cmask(nm, pattern, base, cm, op):
        m = C.sb(st, nm, [64, 8, 64], F32)
        P.memset("pool", m[:], 1.0, writes=[nm])
        for h in range(8):
            P.op("pool", lambda e, h=h: e.affine_select(m[:, h, :], m[:, h, :], pattern, op, 0.0, base=base, channel_multiplier=cm),
                 reads=[nm], writes=[nm])
        return m
    k["m_su"] = cmask("m_su", [[1, 64]], 0, -1, ALU.is_gt)
    k["m_ui"] = cmask("m_ui", [[1, 64]], 0, -1, ALU.is_ge)
    k["m_sl"] = cmask("m_sl", [[-1, 64]], 0, 1, ALU.is_gt)
    k["i8"] = cmask("i8", [[-1, 64]], 0, 1, ALU.is_equal)
    cm_f = C.sb(st, "cm_f", [128, 4, 512], F32)
    P.memset("pool", cm_f[:], 1.0, writes=["cm_f"])
    for d in range(4):
        P.op("pool", lambda e, d=d: e.affine_select(cm_f[:, d, :], cm_f[:, d, :], [[1, 512]], ALU.is_ge, 0.0, base=-128 * d, channel_multiplier=-1),
             reads=["cm_f"], writes=["cm_f"])
    cmB = C.sb(st, "cmB", [128, 4, 512], BF16)
    P.copy("pool", cmB[:], cm_f[:], reads=["cm_f"], writes=["cmB"])
    k.update(identF=identF, identB=identB, flipF=flipF, onesB=onesB, onesF=onesF, bo=bo, triF=triF, cmB=cmB)
    C.k = k
    P.flush()


import contextlib


def cast_weight(C, W, K, N, name):
    P = C.P
    NB = (N + 127) // 128
    kc_n = K // 128
    Wb = C.dr(name, [NB, 128, kc_n, 128], BF16)
    nfull = N // 128
    rem = N - nfull * 128
    with contextlib.ExitStack() as st:
        ld = Rot(C, st, "cw_ld", [128, NB * 128], F32, 2)
        cb = Rot(C, st, "cw_cb", [128, NB * 128], BF16, 2)
        for kc in range(kc_n):
            t, tk = ld.next()
            b, bk = cb.next()
            P.dma(t[:, 0:N], W[kc * 128:(kc + 1) * 128, :], writes=[tk])
            eng = ("dve", "act")[kc % 2]
            P.copy(eng, b[:, 0:N], t[:, 0:N], reads=[tk], writes=[bk])
            if nfull:
                dst = Wb[0:nfull, :, kc, :].rearrange("nb p n -> p nb n")
                src = b[:, 0:nfull * 128].rearrange("p (nb n) -> p nb n", n=128)
                P.dma(dst, src, reads=[bk], writes=[(name, kc)], q="pool")
            if rem:
                P.dma(Wb[nfull, :, kc, 0:rem], b[:, nfull * 128:N], reads=[bk], writes=[(name, kc, "r")], q="pool")
        P.flush()
    return Wb, [(name, kc) for kc in range(kc_n)] + ([(name, kc, "r") for kc in range(kc_n)] if rem else [])


def adaln_phase(C, I, ncol, c_cols):
    P, st, k = C.P, C.st, C.k
    mods = {}
    for l in range(2):
        for j in range(2):
            mods[(l, j)] = C.sb(st, "mod", [128, 24, ncol], F32)
    with contextlib.ExitStack() as ts:
        cT = C.sb(ts, "cT", [128, KC, ncol], F32)
        for j, cap in enumerate(c_cols):
            P.dma(cT[:, :, j], cap.rearrange("(c p) -> p c", p=128), writes=["cT"], allow_slow_non_contiguous=True)
        csT = C.sb(ts, "csT", [128, KC, ncol], F32)
        P.act(csT[:], cT[:], AF.Silu, reads=["cT"], writes=["csT"])
        wl = Rot(C, ts, "ada_w", [128, 3072], F32, 3)
        pacc = [C.ps(ts, "ada_acc", [128, 512], F32) for _ in range(6)]
        ptr = Rot(C, ts, "ada_ptr", [128, 24, 4], F32, 1, psum=True)
        mrow = Rot(C, ts, "ada_mrow", [4, 3072], F32, 2)
        brow = Rot(C, ts, "ada_brow", [4, 3072], F32, 2)
        for l in range(2):
            for j in range(2):
                m = mods[(l, j)]
                bt, btk = brow.next()
                for cc in range(ncol):
                    P.dma(bt[cc:cc + 1, :], I["ada_b"][l, j:j + 1, :], writes=[btk])
                for kc in range(KC):
                    w, wk = wl.next()
                    P.dma(w[:], I["ada_w"][l, j, kc * 128:(kc + 1) * 128, :], writes=[wk])
                    for c6 in range(6):
                        P.mm(pacc[c6][0:ncol, :], csT[:, kc, 0:ncol], w[:, c6 * 512:(c6 + 1) * 512], start=(kc == 0), stop=(kc == KC - 1),
                             reads=[wk, "csT"], writes=[("ada_acc", c6)])
                mr, mrk = mrow.next()
                for c6 in range(6):
                    P.tt("dve", mr[0:ncol, c6 * 512:(c6 + 1) * 512], pacc[c6][0:ncol, :], bt[0:ncol, c6 * 512:(c6 + 1) * 512], ALU.add,
                         reads=[("ada_acc", c6), btk], writes=[mrk])
                pt, ptk = ptr.next()
                for nb in range(24):
                    P.tr(pt[:, nb, 0:ncol], mr[0:ncol, nb * 128:(nb + 1) * 128], k["identF"][0:ncol, 0:ncol], reads=[mrk, "identF"], writes=[ptk])
                P.copy("act", m[:, :, :], pt[:, :, 0:ncol], reads=[ptk], writes=[("mod", l, j)])
        P.flush()
    C.mods = mods


def load_vec_fm(C, st, ap, n, name, q="sp"):
    t = C.sb(st, name, [128, n], F32)
    key = C.name(name)
    C.P.dma(t[:], ap.rearrange("(c p) -> p c", p=128), writes=[key], allow_slow_non_contiguous=True, q=q)
    return t, key


class Job:
    pass


def seq_groups(job, G):
    out = []
    for s in range(job.nseq):
        T = job.Ts[s]
        for t0 in range(0, T, G):
            gs = min(G, T - t0)
            out.append((s, t0, gs, job.bases[s] + t0))
    return out


def x_to_fm(C, job):
    P, k = C.P, C.k
    with contextlib.ExitStack() as st:
        xin = Rot(C, st, "xin", [128, 1024], F32, 2)
        pst = Rot(C, st, "x_ps", [128, 4, 128], F32, 4, psum=True)
        xo = Rot(C, st, "xo", [128, KC, 128], F32, 2)
        for (s, t0, gs, col0) in seq_groups(job, 128):
            t, tk = xin.next()
            P.dma(t[0:gs, :], job.x_in[s][t0:t0 + gs, :], writes=[tk])
            o, ok = xo.next()
            for half in range(2):
                ps, pk = pst.next()
                for c4 in range(4):
                    c = half * 4 + c4
                    P.tr(ps[:, c4, 0:gs], t[0:gs, c * 128:(c + 1) * 128], k["identF"][0:gs, 0:gs], reads=[tk, "identF"], writes=[pk])
                P.copy("act" if half else "dve", o[:, half * 4:half * 4 + 4, 0:gs], ps[:, :, 0:gs], reads=[pk], writes=[ok])
            P.dma(job.xT[:, col0:col0 + gs].rearrange("(c p) t -> p c t", p=128), o[:, :, 0:gs], reads=[ok], writes=[("xT", job.name, c, col0 // 512) for c in range(8)], q="pool")
        P.flush()


def xkeys(job, col0, gs):
    return [("xT", job.name, i) for i in range(col0 // 128, (col0 + gs - 1) // 128 + 1)]


def norm_to_hT(C, st, job, gT, l, j, hT, hkey):
    P, k = C.P, C.k
    m = C.mods[(l, j)]
    with contextlib.ExitStack() as ts:
        gsc = C.sb(ts, "gsc", [128, KC, 3], F32)
        for col in range(3):
            P.stt(gsc[:, :, col], m[:, 8:16, col], 1.0, gT[:], ALU.add, ALU.mult, reads=[("mod", l, j), "smallprm"], writes=["gsc"])
        xg = Rot(C, ts, "xg", [128, KC, 512], F32, 2)
        sq = Rot(C, ts, "sq", [128, KC, 512], BF16, 2)
        pss = Rot(C, ts, "ss_ps", [128, 512], F32, 2, psum=True)
        rs = Rot(C, ts, "rstd", [128, 512], F32, 2)
        tmp = Rot(C, ts, "htmp", [128, 512], F32, 3)
        for (s, t0, gs, col0) in seq_groups(job, 512):
            x, xk = xg.next()
            P.dma(x[:, :, 0:gs], job.xT[:, col0:col0 + gs].rearrange("(c p) t -> p c t", p=128),
                  reads=[key for c in range(8) for key in xk1(job, c, col0, gs)], writes=[xk])
            q, qk = sq.next()
            P.act(q[:, :, 0:gs], x[:, :, 0:gs], AF.Square, reads=[xk], writes=[qk])
            ps, pk = pss.next()
            for c in range(KC):
                P.mm(ps[:, 0:gs], k["onesB"][:], q[:, c, 0:gs], start=(c == 0), stop=(c == KC - 1), reads=[qk, "onesB"], writes=[pk])
            r, rk = rs.next()
            P.act(r[:, 0:gs], ps[:, 0:gs], AF.Ln, bias=k["eps_rms"][:, 0:1], scale=1.0 / D, reads=[pk, "epsv"], writes=[rk])
            P.act(r[:, 0:gs], r[:, 0:gs], AF.Exp, scale=-0.5, reads=[rk], writes=[rk])
            col = job.modcol[s]
            for c in range(KC):
                t_, tk_ = tmp.next()
                P.stt(t_[:, 0:gs], x[:, c, 0:gs], gsc[:, c, col:col + 1], r[:, 0:gs], ALU.mult, ALU.mult, reads=[xk, rk, "gsc"], writes=[tk_])
                P.act(hT[:, c, col0:col0 + gs], t_[:, 0:gs], AF.Identity, bias=m[:, c, col:col + 1], reads=[tk_, ("mod", l, j)], writes=[hkey])
        P.flush()


def proj(C, job, srcT, skey, Wb, wkeys, nbs, kcn, post, G=512, Wf=None, ncols=None):
    P = C.P
    with contextlib.ExitStack() as ts:
        wr = Rot(C, ts, "pw", [128, kcn, 128], BF16, 3)
        wfr = Rot(C, ts, "pwf", [128, kcn, 128], F32, 2) if Wf is not None else None
        pr = Rot(C, ts, "pp", [128, 512], F32, 3, psum=True)
        groups = seq_groups(job, G)
        for nb in nbs:
            w, wk = wr.next()
            if Wf is None:
                P.dma(w[:], Wb[nb], reads=wkeys, writes=[wk])
            else:
                nc_ = min(128, ncols - nb * 128)
                wf, wfk = wfr.next()
                P.dma(wf[:, :, 0:nc_], Wf[:, nb * 128:nb * 128 + nc_].rearrange("(c p) n -> p c n", p=128), writes=[wfk])
                P.copy("pool", w[:, :, 0:nc_], wf[:, :, 0:nc_], reads=[wfk], writes=[wk])
            for (s, t0, gs, col0) in groups:
                ps, pk = pr.next()
                for kc in range(kcn):
                    P.mm(ps[:, 0:gs], w[:, kc, :], srcT[:, kc, col0:col0 + gs], start=(kc == 0), stop=(kc == kcn - 1),
                         reads=[wk, skey], writes=[pk])
                post(nb, s, t0, gs, col0, ps, pk)


def chunk_scan(C, ts, nseg_cols, L, delta, rT, kT, vT, aT, bT, epos, S32, Spad, yT, keys, pools):
    P, k = C.P, C.k
    P.pe_serial = True
    psT, psH = pools["psT"], pools["psH"]
    nlev = int(np.log2(L))
    kin = [keys[n] for n in ("rT", "kT", "vT")] + ([keys["aT"], keys["bT"]] if delta else [])
    for c in range(nseg_cols // L):
        cs = slice(c * L, (c + 1) * L)
        pads = {}
        for nm, src in (("K", kT), ("V", vT)) + ((("B", bT),) if delta else ()):
            ps, pk = psT.next()
            for blk in range(4):
                P.tr(ps[0:L, blk, :], src[:, blk, cs], k["identB"][:, :], reads=kin + ["identB"], writes=[pk])
            pad, padk = pools["pad" + nm].next()
            for e in range(2):
                P.copy("act" if e else "dve", pad[0:L, :, e, e * 64:(e + 1) * 64], ps[0:L, :, e * 64:(e + 1) * 64], reads=[pk], writes=[padk])
            pads[nm] = (pad, padk)
        Kp, Kk = pads["K"]
        Vp, Vk = pads["V"]

        def hsl(h):
            return h // 2, h % 2, (h % 2) * 64

        def mm8(lhs_of, rhs_of, reads):
            ps, pk = psH.next()
            for h in range(8):
                P.mm(ps[0:L, h, 0:L], lhs_of(h), rhs_of(h), reads=reads, writes=[pk])
            return ps, pk

        def fm(t, h):
            pr, e, pb = hsl(h)
            return t[pb:pb + 64, pr, cs]

        if delta:
            psA, pkA = mm8(lambda h: fm(bT, h), lambda h: fm(aT, h), kin)
            psB, pkB = mm8(lambda h: fm(aT, h), lambda h: fm(bT, h), kin)
            PT, PTk = pools["mb"].next()
            Pm, Pmk = pools["mb"].next()
            TTf, TTfk = pools["ttf"].next()
            TTb, TTbk = pools["mb"].next()
            P.tt("dve", TTf[0:L, :, 0:L], psA[0:L, :, 0:L], k["m_su"][0:L, :, 0:L], ALU.mult, reads=[pkA, "m_su"], writes=[TTfk])
            P.copy("act", PT[0:L, :, 0:L], TTf[0:L, :, 0:L], reads=[TTfk], writes=[PTk])
            P.tt("dve", Pm[0:L, :, 0:L], psB[0:L, :, 0:L], k["m_sl"][0:L, :, 0:L], ALU.mult, reads=[pkB, "m_sl"], writes=[Pmk])
            P.tt("pool", TTf[0:L, :, 0:L], TTf[0:L, :, 0:L], k["i8"][0:L, :, 0:L], ALU.add, reads=[TTfk, "i8"], writes=[TTfk])
            P.copy("act", TTb[0:L, :, 0:L], TTf[0:L, :, 0:L], reads=[TTfk], writes=[TTbk])
            for j in range(1, nlev):
                psA, pkA = mm8(lambda h: PT[0:L, h, 0:L], lambda h: Pm[0:L, h, 0:L], [PTk, Pmk])
                last = (j == nlev - 1)
                if not last:
                    psB, pkB = mm8(lambda h: Pm[0:L, h, 0:L], lambda h: PT[0:L, h, 0:L], [PTk, Pmk])
                Pm2, Pm2k = pools["mb"].next()
                P.copy("act", Pm2[0:L, :, 0:L], psA[0:L, :, 0:L], reads=[pkA], writes=[Pm2k])
                if not last:
                    PT2, PT2k = pools["mb"].next()
                    P.copy("dve", PT2[0:L, :, 0:L], psB[0:L, :, 0:L], reads=[pkB], writes=[PT2k])
                    PT, PTk = PT2, PT2k
                Pm, Pmk = Pm2, Pm2k
                psC, pkC = mm8(lambda h: Pm[0:L, h, 0:L], lambda h: TTb[0:L, h, 0:L], [Pmk, TTbk])
                P.tt("dve", TTf[0:L, :, 0:L], TTf[0:L, :, 0:L], psC[0:L, :, 0:L], ALU.add, reads=[pkC, TTfk], writes=[TTfk])
                TTb, TTbk = pools["mb"].next()
                P.copy("act", TTb[0:L, :, 0:L], TTf[0:L, :, 0:L], reads=[TTfk], writes=[TTbk])
            psA, pkA = mm8(lambda h: fm(kT, h), lambda h: fm(aT, h), kin)
            Mak, Makk = pools["mb"].next()
            P.tt("dve", Mak[0:L, :, 0:L], psA[0:L, :, 0:L], k["m_su"][0:L, :, 0:L], ALU.mult, reads=[pkA, "m_su"], writes=[Makk])
            psA, pkA = mm8(lambda h: fm(bT, h), lambda h: fm(rT, h), kin)
            Mrb, Mrbk = pools["mb"].next()
            P.tt("dve", Mrb[0:L, :, 0:L], psA[0:L, :, 0:L], k["m_ui"][0:L, :, 0:L], ALU.mult, reads=[pkA, "m_ui"], writes=[Mrbk])
        psA, pkA = mm8(lambda h: fm(kT, h), lambda h: fm(rT, h), kin)
        Mrk, Mrkk = pools["mb"].next()
        P.tt("dve", Mrk[0:L, :, 0:L], psA[0:L, :, 0:L], k["m_ui"][0:L, :, 0:L], ALU.mult, reads=[pkA, "m_ui"], writes=[Mrkk])
        if delta:
            Bp, Bk = pads["B"]
            ps, pk = psH.next()
            for h in range(8):
                pr, e, pb = hsl(h)
                P.mm(ps[0:L, h, :], fm(aT, h), Spad[pb:pb + 64, pr, pb:pb + 64], start=True, stop=False, reads=kin + ["Spad"], writes=[pk])
                P.mm(ps[0:L, h, :], Mak[0:L, h, 0:L], Vp[0:L, pr, e, pb:pb + 64], start=False, stop=True, reads=[Makk, Vk], writes=[pk])
            W1, W1k = pools["mb"].next()
            P.copy("act", W1[0:L, :, :], ps[0:L, :, :], reads=[pk], writes=[W1k])
            ps, pk = psH.next()
            for h in range(8):
                P.mm(ps[0:L, h, :], TTb[0:L, h, 0:L], W1[0:L, h, :], reads=[TTbk, W1k], writes=[pk])
            Up, Uk = pools["padU"].next()
            for e in range(2):
                P.copy("act" if e else "dve", Up[0:L, :, e, e * 64:(e + 1) * 64],
                       ps[0:L, :, :].rearrange("p (a b) v -> p a b v", b=2)[:, :, e, :], reads=[pk], writes=[Uk])
        ps, pk = psH.next()
        for pr in range(4):
            n_mm = (3 if delta else 2) * 2
            i_mm = 0
            for e in range(2):
                pb = e * 64
                h = pr * 2 + e
                lst = [(Spad[pb:pb + 64, pr, :], rT[pb:pb + 64, pr, cs], kin + ["Spad"])]
                if delta:
                    lst.append((Up[0:L, pr, e, :], Mrb[0:L, h, 0:L], [Uk, Mrbk]))
                lst.append((Vp[0:L, pr, e, :], Mrk[0:L, h, 0:L], [Vk, Mrkk]))
                for (lt, rh, rd) in lst:
                    P.mm(ps[:, pr, 0:L], lt, rh, start=(i_mm == 0), stop=(i_mm == n_mm - 1), reads=rd, writes=[pk])
                    i_mm += 1
        P.copy("act", yT[:, :, cs], ps[:, 0:4, 0:L], reads=[pk], writes=[keys["yT"]])
        ps, pk = psH.next()
        for pr in range(4):
            n_mm = (2 if delta else 1) * 2
            i_mm = 0
            for e in range(2):
                pb = e * 64
                lst = []
                if delta:
                    lst.append((Bp[0:L, pr, e, :], Up[0:L, pr, e, pb:pb + 64], [Bk, Uk]))
                lst.append((Kp[0:L, pr, e, :], Vp[0:L, pr, e, pb:pb + 64], [Kk, Vk]))
                for (lt, rh, rd) in lst:
                    P.mm(ps[:, pr, :], lt, rh, start=(i_mm == 0), stop=(i_mm == n_mm - 1), reads=rd, writes=[pk])
                    i_mm += 1
        P.tt("dve", S32[:, :, :], S32[:, :, :], ps[:, 0:4, :], ALU.add, reads=[pk, "S32"], writes=["S32"])
        for pr in range(4):
            P.ts("dve", S32[:, pr, :], S32[:, pr, :], epos[:, pr, (c + 1) * L - 1:(c + 1) * L], None, ALU.mult, reads=["S32", keys["epos"]], writes=["S32"])
        for e in range(2):
            pb = e * 64
            P.copy("act" if e else "pool", Spad[pb:pb + 64, :, pb:pb + 64], S32[pb:pb + 64, :, :], reads=["S32"], writes=["Spad"])
    P.pe_serial = False


def scan_pools(C, ts, delta):
    pools = {}
    pools["psT"] = Rot(C, ts, "cs_psT", [64, 4, 128], BF16, 2, psum=True)
    pools["psH"] = Rot(C, ts, "cs_psH", [128, 8, 64], F32, 4, psum=True)
    names = ["K", "V"] + (["B", "U"] if delta else [])
    for nm in names:
        r = Rot(C, ts, "pad" + nm, [64, 4, 2, 128], BF16, 2)
        for t_, tk_ in zip(r.tiles, r.keys):
            C.P.memset("pool", t_[:], 0.0, writes=[tk_])
        pools["pad" + nm] = r
    pools["mb"] = Rot(C, ts, "cs_mb", [64, 8, 64], BF16, 10)
    pools["ttf"] = Rot(C, ts, "cs_ttf", [64, 8, 64], F32, 2)
    return pools


def more_consts(C):
    P, st, k = C.P, C.st, C.k
    for L in (64, 32, 16):
        m = C.sb(st, "rmask%d" % L, [128, 512], F32)
        P.memset("pool", m[:], 1.0, writes=["rmask%d" % L])
        P.memset("pool", m[:, :].rearrange("p (c l) -> p c l", l=L)[:, :, 0:1], 0.0, writes=["rmask%d" % L])
        k["rmask%d" % L] = m
    for nm, val in (("eps_rms", RMS_EPS), ("eps_gn", GN_EPS), ("tiny", 1e-12), ("one", 1.0), ("lnc", -40.0 * float(np.log(2.0)))):
        t = C.sb(st, nm, [128, 1], F32)
        P.memset("pool", t[:], val, writes=["epsv"])
        k[nm] = t
    P.flush()


def block_norm_stats(C, P, ps_pool, tmpb, src, srckey, SEG, scale):
    pass


def rwkv_mixer(C, job, s, I, prm):
    P, k = C.P, C.k
    T = job.T
    L = min(64, T)
    SEG = min(256, T)
    base = s * T
    with contextlib.ExitStack() as ts:
        pools = scan_pools(C, ts, True)
        psP = Rot(C, ts, "rw_psP", [128, 512], F32, 2, psum=True)
        S32 = C.sb(ts, "S32", [128, 4, 64], F32)
        Spad = C.sb(ts, "Spad", [128, 4, 128], BF16)
        P.memset("pool", Spad[:], 0.0, writes=["Spad"])
        if job.rwkv_s0 is None:
            P.memset("pool", S32[:], 0.0, writes=["S32"])
        else:
            s0t = C.sb(ts, "s0t", [64, 8, 64], F32)
            P.dma(s0t[:], job.rwkv_s0[s].rearrange("h v k -> v h k"), writes=["s0t"])
            for pr in range(4):
                ps, pk = psP.next()
                P.tr(ps[:, 0:64], s0t[0:64, 2 * pr:2 * pr + 2, :].rearrange("v e k -> v (e k)"), k["identF"][0:64, 0:64], reads=["s0t", "identF"], writes=[pk])
                P.copy("dve", S32[:, pr, :], ps[:, 0:64], reads=[pk], writes=["S32"])
            for e in range(2):
                pb = e * 64
                P.copy("act", Spad[pb:pb + 64, :, pb:pb + 64], S32[pb:pb + 64, :, :], reads=["S32"], writes=["Spad"])
        zt = C.sb(ts, "zt", [128, 14, SEG + 1], F32)
        dd = C.sb(ts, "dd", [128, 14, SEG], F32)
        zs = C.sb(ts, "zs", [128, 14, SEG], F32)
        f4 = lambda nm: C.sb(ts, nm, [128, 4, SEG], F32)
        b4 = lambda nm: C.sb(ts, nm, [128, 4, SEG], BF16)
        lw, asig, gT, kkn, kmod, cc, epos, eneg, eprev, tmpa, tmpb_, rkb, yT = [f4(n) for n in
            ("lw", "asig", "gT", "kkn", "kmod", "cc", "epos", "eneg", "eprev", "tmpa", "tmpb", "rkb", "yT")]
        rT, kT, vT, aT, bT, sqb, yb = [b4(n) for n in ("rT", "kT", "vT", "aT", "bT", "sqb", "yb")]
        tw = C.sb(ts, "tw", [128, SEG], BF16)
        sg = C.sb(ts, "sg", [128, SEG], BF16)
        outb = C.sb(ts, "outb", [128, 4, SEG], BF16)
        keys = dict(rT="rT", kT="kT", vT="vT", aT="aT", bT="bT", epos="epos", yT="yT")
        mu, w0, a0, k_k, k_a, lng, lnb, r_k, omka, wa2, g2 = [prm[n] for n in
            ("mu", "w0", "a0", "k_k", "k_a", "lng", "lnb", "r_k", "omka", "wa2", "g2")]
        pk_ = ["rwprm", "rwprm2"]
        for t0 in range(0, T, SEG):
            col0 = base + t0
            zrows = job.zT[0:A_COLS, :].rearrange("(c p) t -> p c t", p=128)
            zk = [("zT", job.name, nb, i) for nb in range(14) for i in range(col0 // 512, (col0 + SEG - 1) // 512 + 1)]
            if t0 == 0:
                P.dma(zt[:, :, 1:SEG + 1], zrows[:, :, col0:col0 + SEG], reads=zk, writes=["zt"])
                if job.rwkv_shift0 is None:
                    P.memset("pool", zt[:, :, 0:1], 0.0, writes=["zt"])
                else:
                    P.dma(zt[:, :, 0], job.rwkv_shift0[s].rearrange("(c p) -> p c", p=128), writes=["zt"], allow_slow_non_contiguous=True)
            else:
                zk2 = zk + [("zT", job.name, nb, (col0 - 1) // 512) for nb in range(14)]
                P.dma(zt[:, :, 0:SEG + 1], zrows[:, :, col0 - 1:col0 + SEG], reads=zk2, writes=["zt"])
            if t0 + SEG == T:
                P.dma(job.rwkv_shift_out[s].rearrange("(c p) -> p c", p=128), zt[:, :, SEG], reads=["zt"], writes=[("rwsh", job.name, s)],
                      q="pool", allow_slow_non_contiguous=True)
            P.tt("pool", dd[:], zt[:, :, 0:SEG], zt[:, :, 1:SEG + 1], ALU.subtract, reads=["zt"], writes=["dd"])
            for blk in range(14):
                P.stt(zs[:, blk, :], dd[:, blk, :], mu[:, blk:blk + 1], zt[:, blk, 1:SEG + 1], ALU.mult, ALU.add, reads=["dd", "zt"] + pk_, writes=["zs"])
            P.act(tw[0:64, :], zs[0:64, 12, :], AF.Tanh, reads=["zs"], writes=["tw"])
            P.copy("dve", tw[64:128, :], zs[64:128, 12, :], reads=["zs"], writes=["tw"])
            P.act(sg[:], zs[:, 13, :], AF.Sigmoid, reads=["zs"], writes=["sg"])
            for blk in range(4):
                bs = slice(blk * 128, (blk + 1) * 128)
                ps, pk = psP.next()
                P.mm(ps[:, 0:SEG], wa2[0:64, bs], tw[0:64, :], reads=["tw"] + pk_, writes=[pk])
                P.act(lw[:, blk, :], ps[:, 0:SEG], AF.Sigmoid, bias=w0[:, blk:blk + 1], reads=[pk] + pk_, writes=["lw"])
                ps, pk = psP.next()
                P.mm(ps[:, 0:SEG], wa2[64:128, bs], tw[64:128, :], reads=["tw"] + pk_, writes=[pk])
                P.act(asig[:, blk, :], ps[:, 0:SEG], AF.Sigmoid, bias=a0[:, blk:blk + 1], reads=[pk] + pk_, writes=["asig"])
                ps, pk = psP.next()
                P.mm(ps[:, 0:SEG], g2[:, bs], sg[:], reads=["sg"] + pk_, writes=[pk])
                P.copy("dve", gT[:, blk, :], ps[:, 0:SEG], reads=[pk], writes=["gT"])
            P.ts("dve", lw[:], lw[:], -DECAY_C, None, ALU.mult, reads=["lw"], writes=["lw"])
            for blk in range(4):
                P.ts("dve", tmpa[:, blk, :], zs[:, 4 + blk, :], k_k[:, blk:blk + 1], None, ALU.mult, reads=["zs"] + pk_, writes=["tmpa"])
            P.act(sqb[:], tmpa[:], AF.Square, reads=["tmpa"], writes=["sqb"])
            for blk in range(4):
                ps, pk = psP.next()
                P.mm(ps[:, 0:SEG], k["bo"][:], sqb[:, blk, :], reads=["sqb", "bo"], writes=[pk])
                P.act(tmpb_[:, blk, :], ps[:, 0:SEG], AF.Sqrt, reads=[pk], writes=["tmpb"])
            P.ts("dve", tmpb_[:], tmpb_[:], 1e-12, None, ALU.max, reads=["tmpb"], writes=["tmpb"])
            P.op("dve", lambda e: e.reciprocal(tmpb_[:], tmpb_[:]), reads=["tmpb"], writes=["tmpb"])
            P.tt("dve", kkn[:], tmpa[:], tmpb_[:], ALU.mult, reads=["tmpa", "tmpb"], writes=["kkn"])
            for blk in range(4):
                P.ts("dve", tmpa[:, blk, :], asig[:, blk, :], k_a[:, blk:blk + 1], omka[:, blk:blk + 1], ALU.mult, ALU.add,
                     reads=["asig"] + pk_, writes=["tmpa"])
            P.tt("dve", kmod[:], tmpa[:], zs[:, 4:8, :], ALU.mult, reads=["tmpa", "zs"], writes=["kmod"])
            rm = k["rmask%d" % L]
            for blk in range(4):
                P.op("dve", lambda e, blk=blk: e.tensor_tensor_scan(cc[:, blk, :], rm[:, 0:SEG], lw[:, blk, :], 0.0, ALU.mult, ALU.add),
                     reads=["lw", "rmask%d" % L], writes=["cc"])
            P.act(epos[:], cc[:], AF.Exp, reads=["cc"], writes=["epos"])
            P.act(eneg[:], cc[:], AF.Exp, scale=-1.0, reads=["cc"], writes=["eneg"])
            P.tt("pool", tmpa[:], cc[:], lw[:], ALU.subtract, reads=["cc", "lw", "kmod"], writes=["tmpa"])
            P.act(eprev[:], tmpa[:], AF.Exp, reads=["tmpa"], writes=["eprev"])
            P.tt("dve", rT[:], zs[:, 0:4, :], epos[:], ALU.mult, reads=["zs", "epos"], writes=["rT"])
            P.stt(aT[:], kkn[:], -1.0, eprev[:], ALU.mult, ALU.mult, reads=["kkn", "eprev"], writes=["aT"])
            P.tt("pool", tmpb_[:], kkn[:], asig[:], ALU.mult, reads=["kkn", "asig"], writes=["tmpb"])
            P.tt("dve", bT[:], tmpb_[:], eneg[:], ALU.mult, reads=["tmpb", "eneg"], writes=["bT"])
            P.tt("dve", kT[:], kmod[:], eneg[:], ALU.mult, reads=["kmod", "eneg"], writes=["kT"])
            P.copy("act", vT[:], zs[:, 8:12, :], reads=["zs"], writes=["vT"])
            for blk in range(4):
                P.stt(sqb[:, blk, :], zs[:, blk, :], r_k[:, blk:blk + 1], kmod[:, blk, :], ALU.mult, ALU.mult, reads=["zs", "kmod"] + pk_, writes=["sqb"])
            for blk in range(4):
                ps, pk = psP.next()
                P.mm(ps[:, 0:SEG], k["bo"][:], sqb[:, blk, :], reads=["sqb", "bo"], writes=[pk])
                P.copy("act", rkb[:, blk, :], ps[:, 0:SEG], reads=[pk], writes=["rkb"])
            chunk_scan(C, ts, SEG, L, True, rT, kT, vT, aT, bT, epos, S32, Spad, yT, keys, pools)
            P.copy("act", yb[:], yT[:], reads=["yT"], writes=["yb"])
            for blk in range(4):
                ps, pk = psP.next()
                P.mm(ps[:, 0:SEG], k["bo"][:], yb[:, blk, :], reads=["yb", "bo"], writes=[pk])
                P.stt(tmpa[:, blk, :], ps[:, 0:SEG], -1.0 / 64, yT[:, blk, :], ALU.mult, ALU.add, reads=[pk, "yT"], writes=["tmpa"])
            P.act(sqb[:], tmpa[:], AF.Square, reads=["tmpa"], writes=["sqb"])
            for blk in range(4):
                ps, pk = psP.next()
                P.mm(ps[:, 0:SEG], k["bo"][:], sqb[:, blk, :], reads=["sqb", "bo"], writes=[pk])
                P.act(tmpb_[:, blk, :], ps[:, 0:SEG], AF.Sqrt, bias=k["eps_gn"][:, 0:1], scale=1.0 / 64, reads=[pk, "epsv"], writes=["tmpb"])
            P.op("dve", lambda e: e.reciprocal(tmpb_[:], tmpb_[:]), reads=["tmpb"], writes=["tmpb"])
            P.tt("dve", tmpa[:], tmpa[:], tmpb_[:], ALU.mult, reads=["tmpa", "tmpb"], writes=["tmpa"])
            for blk in range(4):
                P.ts("dve", tmpa[:, blk, :], tmpa[:, blk, :], lng[:, blk:blk + 1], lnb[:, blk:blk + 1], ALU.mult, ALU.add, reads=["tmpa"] + pk_, writes=["tmpa"])
            P.tt("pool", tmpb_[:], rkb[:], zs[:, 8:12, :], ALU.mult, reads=["rkb", "zs", "tmpb"], writes=["tmpb"])
            P.tt("dve", tmpa[:], tmpa[:], tmpb_[:], ALU.add, reads=["tmpa", "tmpb"], writes=["tmpa"])
            P.tt("dve", outb[:], tmpa[:], gT[:], ALU.mult, reads=["tmpa", "gT"], writes=["outb"])
            P.dma(job.mixT[0:512, col0:col0 + SEG].rearrange("(c p) t -> p c t", p=128), outb[:], reads=["outb"],
                  writes=[("mixT", job.name, c, col0 // 512, hh) for c in range(4) for hh in range(2)], q="pool")
        so = C.sb(ts, "so", [64, 4, 128], F32)
        for pr in range(4):
            ps, pk = psP.next()
            P.tr(ps[0:64, 0:128], S32[:, pr, :], k["identF"][:, :], reads=["S32", "identF"], writes=[pk])
            P.copy("dve", so[:, pr, :], ps[0:64, 0:128], reads=[pk], writes=["so"])
        P.dma(job.rwkv_out[s].rearrange("(a e) v k -> v a e k", e=2), so[:, :, :].rearrange("v a (e k) -> v a e k", e=2), reads=["so"],
              writes=[("rwst", job.name, s)], q="pool")
        P.flush()


def load_rwkv_params(C, I):
    P, st = C.P, C.st
    prm = {}
    def vec(nm, ap, n):
        t = C.sb(st, "p_" + nm, [128, n], F32)
        P.dma(t[:], ap.rearrange("(c p) -> p c", p=128), writes=["rwprm"], allow_slow_non_contiguous=True)
        prm[nm] = t
    vec("mu", I["rwkv_mu"][0], 14)
    vec("w0", I["rwkv_w0"][0], 4)
    vec("a0", I["rwkv_a0"][0], 4)
    vec("k_k", I["rwkv_k_k"][0], 4)
    vec("k_a", I["rwkv_k_a"][0], 4)
    vec("lng", I["rwkv_lnx_g"][0], 4)
    vec("lnb", I["rwkv_lnx_b"][0], 4)
    vec("r_k", I["rwkv_r_k"][0].rearrange("h k -> (h k)"), 4)
    for nm, src in (("lng8", "rwkv_lnx_g"), ("lnb8", "rwkv_lnx_b")):
        t = C.sb(st, "p_" + nm, [64, 8], F32)
        P.dma(t[:], I[src][0].rearrange("(h k) -> k h", k=64), writes=["rwprm"], allow_slow_non_contiguous=True)
        prm[nm] = t
    omka = C.sb(st, "p_omka", [128, 4], F32)
    P.ts("dve", omka[:], prm["k_a"][:], -1.0, 1.0, ALU.mult, ALU.add, reads=["rwprm"], writes=["rwprm2"])
    prm["omka"] = omka
    wa2 = C.sb(st, "p_wa2", [128, 512], BF16)
    g2 = C.sb(st, "p_g2", [128, 512], BF16)
    with contextlib.ExitStack() as ts:
        wa2f = C.sb(ts, "wa2f", [128, 512], F32)
        g2f = C.sb(ts, "g2f", [128, 512], F32)
        P.dma(wa2f[0:64, :], I["rwkv_w2"][0], writes=["wa2f"])
        P.dma(wa2f[64:128, :], I["rwkv_a2"][0], writes=["wa2f"])
        P.dma(g2f[:], I["rwkv_g2"][0], writes=["g2f"])
        P.copy("dve", wa2[:], wa2f[:], reads=["wa2f"], writes=["rwprm2"])
        P.copy("dve", g2[:], g2f[:], reads=["g2f"], writes=["rwprm2"])
        prm["wa2"], prm["g2"] = wa2, g2
        P.flush()
    return prm


def attn_mixer(C, job, s, I, kind):
    P, k = C.P, C.k
    T = job.Ts[s]
    base = job.bases[s]
    fox = (kind == "fox")
    zoff = A_COLS if fox else 0
    Pc = (job.P_fox[s] if fox else job.P_chunk[s])
    NPt = Pc // 128
    NTt = (T + 127) // 128
    NKT = NPt + NTt
    QG = min(512, T)
    NG = T // QG
    mix_off = 512 if fox else 0
    ck = (job.fox_ck if fox else job.chunk_ck)
    cv = (job.fox_cv if fox else job.chunk_cv)
    kout = (job.fox_kout if fox else job.chunk_kout)
    vout = (job.fox_vout if fox else job.chunk_vout)
    with contextlib.ExitStack() as ts:
        QT = C.sb(ts, "QT", [128, 4, T], BF16)
        KT = C.sb(ts, "KT", [128, 4, NKT * 128], BF16)
        Vt = C.sb(ts, "Vt", [128, NKT, 8, 65], BF16)
        P.memset("pool", Vt[:, :, :, 64:65], 1.0, writes=["Vt"])
        psS = Rot(C, ts, "at_psS", [128, 512], F32, 4, psum=True)
        psM = Rot(C, ts, "at_psM", [128, 512], F32, 1, psum=True)
        psN = Rot(C, ts, "at_psN", [128, 512], F32, 2, psum=True)
        psD = Rot(C, ts, "at_psD", [64, 512], F32, 1, psum=True)
        zrows = lambda off: job.zT[zoff + off:zoff + off + 512, :].rearrange("(c p) t -> p c t", p=128)
        zrows64 = lambda off, a: job.zT[zoff + off + a * 256:zoff + off + a * 256 + 256, :].rearrange("(h d) t -> d h t", d=64)
        zkey = lambda off, col: [("zT", job.name, (zoff + off) // 128 + c, col // 512) for c in range(4)]
        with contextlib.ExitStack() as t2:
            ldq = Rot(C, t2, "at_ldq", [128, 4, 128], F32, 2)
            ldk2 = Rot(C, t2, "at_ldk2", [128, 4, 128], F32, 2)
            ldk = Rot(C, t2, "at_ldk", [128, 4, 128], F32, 2)
            ldv = Rot(C, t2, "at_ldv", [128, 4, 128], F32, 2)
            stg = Rot(C, t2, "at_stg", [128, 512], F32, 4)
            ldc = Rot(C, t2, "at_ldc", [128, 512], F32, 3)
            for ct in range(NPt):
                kc_, kck = ldc.next()
                for a_ in range(2):
                    P.dma(kc_[:, :].rearrange("p (b a d) -> p b a d", b=4, a=2)[:, :, a_, :],
                          ck[s][ct * 128:(ct + 1) * 128, a_ * 256:(a_ + 1) * 256].rearrange("t (b d) -> t b d", b=4), writes=[kck])
                ps, pk = psS.next()
                for h4 in range(4):
                    P.tr(ps[:, h4 * 128:(h4 + 1) * 128], kc_[:, h4 * 128:(h4 + 1) * 128], k["identF"][:, :], reads=[kck, "identF"], writes=[pk])
                P.copy("act", KT[:, :, ct * 128:(ct + 1) * 128], ps[:, :].rearrange("p (c t) -> p c t", t=128), reads=[pk], writes=["KT"])
                vc_, vck = ldc.next()
                P.dma(vc_[:], cv[s][ct * 128:(ct + 1) * 128, :], writes=[vck])
                P.copy("dve", Vt[:, ct, :, 0:64], vc_[:, :].rearrange("p (h d) -> p h d", d=64), reads=[vck], writes=["Vt"])
            for it in range(NTt):
                rows = min(128, T - it * 128)
                col0 = base + it * 128
                q_, qk_ = ldq.next()
                k_, kk_ = ldk.next()
                v_, vk_ = ldv.next()
                k2_, k2k_ = ldk2.next()
                for a in range(2):
                    P.dma(q_[a * 64:(a + 1) * 64, :, 0:rows], zrows64(0, a)[:, :, col0:col0 + rows], reads=zkey(0, col0), writes=[qk_])
                    P.dma(k2_[a * 64:(a + 1) * 64, :, 0:rows], zrows64(512, a)[:, :, col0:col0 + rows], reads=zkey(512, col0), writes=[k2k_])
                P.dma(k_[:, :, 0:rows], zrows(512)[:, :, col0:col0 + rows], reads=zkey(512, col0), writes=[kk_])
                P.dma(v_[:, :, 0:rows], zrows(1024)[:, :, col0:col0 + rows], reads=zkey(1024, col0), writes=[vk_])
                P.copy("act", QT[:, :, it * 128:it * 128 + rows], q_[:, :, 0:rows], reads=[qk_], writes=["QT"])
                P.copy("pool", KT[:, :, (NPt + it) * 128:(NPt + it) * 128 + rows], k2_[:, :, 0:rows], reads=[k2k_], writes=["KT"])
                for src, sk, outap, isv in ((k_, kk_, kout, False), (v_, vk_, vout, True)):
                    ps, pk = psS.next()
                    for blk in range(4):
                        P.tr(ps[0:rows, blk * 128:(blk + 1) * 128], src[:, blk, 0:rows], k["identF"][:, :], reads=[sk, "identF"], writes=[pk])
                    sg_, sgk = stg.next()
                    P.copy("dve" if isv else "act", sg_[0:rows, :], ps[0:rows, :], reads=[pk], writes=[sgk])
                    if isv:
                        P.copy("pool", Vt[0:rows, NPt + it, :, 0:64], sg_[0:rows, :].rearrange("p (h d) -> p h d", d=64), reads=[sgk], writes=["Vt"])
                    if fox or T <= 512:
                        P.dma(outap[s][it * 128:it * 128 + rows, :], sg_[0:rows, :], reads=[sgk], writes=[("kvout", kind, job.name, s, it, isv)], q="pool")
                    elif it * 128 >= T - 512:
                        r0 = it * 128 - (T - 512)
                        P.dma(outap[s][r0:r0 + rows, :], sg_[0:rows, :], reads=[sgk], writes=[("kvout", kind, job.name, s, it, isv)], q="pool")
            P.flush()
        biasT = None
        if fox:
            biasT = C.sb(ts, "biasT", [128, 8, NG, NKT], F32)
            with contextlib.ExitStack() as t2:
                lfT = C.sb(t2, "lfT", [8, T], F32)
                nbf = C.sb(t2, "nbf", [8, 1], F32)
                P.dma(nbf[:], I["fox_b_f"][0:1, :].rearrange("o h -> h o"), writes=["nbf"], allow_slow_non_contiguous=True)
                P.ts("dve", nbf[:], nbf[:], -1.0, None, ALU.mult, reads=["nbf"], writes=["nbf"])
                gk = [("zT", job.name, (A_COLS + 1536) // 128, i) for i in range(base // 512, (base + T - 1) // 512 + 1)]
                P.dma(lfT[:], job.zT[A_COLS + 1536:A_COLS + 1544, base:base + T], reads=gk, writes=["lfT"])
                P.act(lfT[:], lfT[:], AF.Exp, bias=nbf[:, 0:1], scale=-1.0, reads=["lfT", "nbf"], writes=["lfT"])
                P.act(lfT[:], lfT[:], AF.Ln, bias=k["one"][0:8, 0:1], reads=["lfT", "epsv"], writes=["lfT"])
                P.ts("dve", lfT[:], lfT[:], -1.0, None, ALU.mult, reads=["lfT"], writes=["lfT"])
                lft = C.sb(t2, "lft", [128, NKT, 8], F32)
                P.memset("pool", lft[:], 0.0, writes=["lft"])
                if NPt:
                    P.dma(lft[:, 0:NPt, :], job.fox_clogf[s].rearrange("(n p) h -> p n h", p=128), writes=["lft"])
                ps, pk = psS.next()
                for it in range(NTt):
                    rows = min(128, T - it * 128)
                    P.tr(ps[0:rows, it * 8:(it + 1) * 8], lfT[0:8, it * 128:it * 128 + rows], k["identF"][0:8, 0:8], reads=["lfT", "identF"], writes=[pk])
                rows_l = min(128, T)
                P.copy("dve", lft[0:rows_l, NPt:NKT, :], ps[0:rows_l, 0:NTt * 8].rearrange("p (n h) -> p n h", h=8), reads=[pk], writes=["lft"])
                if T >= 128:
                    P.dma(job.fox_logf_out[s].rearrange("(n p) h -> p n h", p=128), lft[:, NPt:NKT, :], reads=["lft"], writes=[("lfout", job.name, s)], q="pool")
                else:
                    P.dma(job.fox_logf_out[s], lft[0:T, NPt, :], reads=["lft"], writes=[("lfout", job.name, s)], q="pool")
                Wt = C.sb(t2, "Wt", [128, 8, NKT], F32)
                TOTt = C.sb(t2, "TOTt", [128, 8, NKT], F32)
                offs = C.sb(t2, "offs", [128, 8, NKT], F32)
                rmk = C.sb(t2, "rmk", [128, 8, NKT], F32)
                P.memset("pool", rmk[:], 1.0, writes=["rmk"])
                P.memset("pool", rmk[:, :, 0:1], 0.0, writes=["rmk"])
                lft2 = lft[:, :, :].rearrange("p n h -> p (n h)")
                ps, pk = psS.next()
                P.mm(ps[:, 0:NKT * 8], k["triF"][:], lft2, reads=["lft", "triF"], writes=[pk])
                P.copy("dve", Wt[:], ps[:, 0:NKT * 8].rearrange("p (n h) -> p h n", h=8), reads=[pk], writes=["Wt"])
                ps, pk = psS.next()
                P.mm(ps[:, 0:NKT * 8], k["onesF"][:], lft2, reads=["lft", "onesF"], writes=[pk])
                P.copy("dve", TOTt[:], ps[:, 0:NKT * 8].rearrange("p (n h) -> p h n", h=8), reads=[pk], writes=["TOTt"])
                P.op("dve", lambda e: e.tensor_tensor_scan(offs[:, :, :].rearrange("p h n -> p (h n)"), rmk[:, :, :].rearrange("p h n -> p (h n)"),
                                                           TOTt[:, :, :].rearrange("p h n -> p (h n)"), 0.0, ALU.mult, ALU.add),
                     reads=["TOTt", "rmk"], writes=["offs"])
                P.tt("dve", offs[:], offs[:], TOTt[:], ALU.subtract, reads=["offs", "TOTt"], writes=["offs"])
                P.tt("dve", Wt[:], Wt[:], offs[:], ALU.add, reads=["offs", "Wt"], writes=["Wt"])
                for h in range(8):
                    for g in range(NG):
                        nq0 = NPt + g * (QG // 128)
                        P.ts("dve", biasT[:, h, g, :], Wt[:, h, :], offs[:, h, nq0:nq0 + 1], -1.0, ALU.subtract, ALU.mult,
                             reads=["Wt", "offs"], writes=["biasT"])
                P.flush()
        pb_ = Rot(C, ts, "at_pb", [128, 512], BF16, 6)
        rd_ = Rot(C, ts, "at_rd", [128, 512], F32, 3)
        nb_ = Rot(C, ts, "at_nb", [64, 512], F32, 3)
        ob_ = Rot(C, ts, "at_ob", [64, 512], BF16, 2)
        MEs = None
        if not fox:
            MEs = [C.sb(ts, "ME", [128, 8, 512], BF16) for _ in range(2)]
            hk = Rot(C, ts, "at_hk", [128, 512], F32, 2)
            me32 = Rot(C, ts, "at_me32", [128, 512], F32, 2)

        def build_me(h):
            ME = MEs[h % 2]
            for rt in range(8):
                if T <= 16 and rt >= NKT:
                    continue
                hh, hhk = hk.next()
                src = bass.AP(C.dram["relext"].tensor, h * 1536 + 896 - 128 * rt, [[1, 128], [1, QG]])
                P.dma(hh[:, 0:QG], src, reads=["relext"], writes=[hhk])
                ps, pk = psM.next()
                P.mm(ps[:, 0:QG], k["flipF"][:], hh[:, 0:QG], reads=[hhk, "flipF"], writes=[pk])
                m32, m32k = me32.next()
                P.act(m32[:, 0:QG], ps[:, 0:QG], AF.Exp, reads=[pk], writes=[m32k])
                P.tt("dve", ME[:, rt, 0:QG], m32[:, 0:QG], k["bandB"][:, rt, 0:QG], ALU.mult, reads=[m32k, "bandB"], writes=[("ME", h % 2)])

        its = []
        for h in range(8):
            for g in range(NG):
                q0 = g * QG
                tiles = []
                if fox:
                    last_kt = NPt + (q0 + QG - 1) // 128
                    for kt in range(0, last_kt + 1):
                        rows = 128 if kt < NPt else min(128, T - (kt - NPt) * 128)
                        d = kt - NPt - q0 // 128
                        mfn = (lambda c0, c1, d=d, rows=rows: k["cmB"][0:rows, d, c0:c1]) if d >= 0 else None
                        c0 = min(128 * d, QG - 1) if d > 0 else 0
                        tiles.append([kt, rows, biasT[0:rows, h, g, kt:kt + 1], mfn, "cmB", c0, QG])
                else:
                    order = [3, 2, 5, 1, 6, 0, 7, 4] if T > 16 else list(range(8))
                    for rt in order:
                        kt = (q0 // 128 - 4 + rt) if T > 16 else rt
                        if kt < 0 or kt >= NKT:
                            continue
                        rows = 128 if kt < NPt else min(128, T - (kt - NPt) * 128)
                        mfn = (lambda c0, c1, rt=rt, rows=rows, hh_=h: MEs[hh_ % 2][0:rows, rt, c0:c1])
                        if T > 16:
                            lo, hi = max(0, 2 * rt - 8), min(7, 2 * rt + 1)
                            c0, c1 = lo * 64, (hi + 1) * 64
                        else:
                            c0, c1 = 0, QG
                        tiles.append([kt, rows, 0.0, mfn, ("ME", h % 2), c0, c1])
                tiles[0][5], tiles[0][6] = 0, QG
                tiles[-1][5], tiles[-1][6] = 0, QG
                G = dict(h=h, g=g, q0=q0, n=len(tiles))
                for i, tl in enumerate(tiles):
                    its.append((G, i, tl))
        D = 3
        qk = {}
        pend = []

        def finalize1(G):
            pn, pnk = G["pn"]
            rd, rdk = rd_.next()
            P.act(rd[64:65, 0:QG], pn[64:65, 0:QG], AF.Ln, scale=float(2.0 ** -40), reads=[pnk], writes=[rdk])
            P.act(rd[64:65, 0:QG], rd[64:65, 0:QG], AF.Exp, bias=k["lnc"][64:65, 0:1], scale=-1.0, reads=[rdk, "epsv"], writes=[rdk])
            nb, nbk = nb_.next()
            P.copy("act", nb[:, 0:QG], pn[0:64, 0:QG], reads=[pnk], writes=[nbk])
            G["rd"], G["nb"] = (rd, rdk), (nb, nbk)

        def finalize(G):
            h, q0 = G["h"], G["q0"]
            rd, rdk = G["rd"]
            nb, nbk = G["nb"]
            pd, pdk = psD.next()
            P.mm(pd[:, 0:QG], k["onesF"][64:65, 0:64], rd[64:65, 0:QG], reads=["onesF", rdk], writes=[pdk])
            ob, obk = ob_.next()
            P.tt("dve", ob[:, 0:QG], nb[:, 0:QG], pd[:, 0:QG], ALU.mult, reads=[nbk, pdk], writes=[obk])
            r0 = mix_off + h * 64
            P.dma(job.mixT[r0:r0 + 64, base + q0:base + q0 + QG], ob[:, 0:QG], reads=[obk],
                  writes=[("mixT", job.name, r0 // 128, (base + q0) // 512, h % 2)], q="pool")

        for n in range(len(its) + D):
            if n < len(its):
                G, i, (kt, rows, bias, mfn, mkey, c0, c1) = its[n]
                h = G["h"]
                hb = (h // 4) * 64
                if (not fox) and G["g"] == 0 and i == 0:
                    build_me(h)
                ps, pk = psS.next()
                P.mm(ps[0:rows, c0:c1], KT[hb:hb + 64, h % 4, kt * 128:kt * 128 + rows], QT[hb:hb + 64, h % 4, G["q0"] + c0:G["q0"] + c1],
                     reads=["KT", "QT"], writes=[pk])
                qk[n] = (ps, pk)
            m = n - D
            if m >= 0:
                G, i, (kt, rows, bias, mfn, mkey, c0, c1) = its[m]
                h = G["h"]
                ps, pk = qk.pop(m)
                if i == 0:
                    while len(pend) > 1:
                        finalize(pend.pop(0))
                    G["pn"] = psN.next()
                pn, pnk = G["pn"]
                pt, ptk = pb_.next()
                P.act(pt[0:rows, c0:c1], ps[0:rows, c0:c1], AF.Exp, bias=bias, scale=0.125, reads=[pk, "biasT"], writes=[ptk])
                if mfn is not None:
                    P.tt("dve", pt[0:rows, c0:c1], pt[0:rows, c0:c1], mfn(c0, c1), ALU.mult, reads=[ptk, mkey], writes=[ptk])
                P.mm(pn[0:65, c0:c1], Vt[0:rows, kt, h, :], pt[0:rows, c0:c1], start=(i == 0), stop=(i == G["n"] - 1), reads=["Vt", ptk], writes=[pnk])
                if i == G["n"] - 1:
                    finalize1(G)
                    G["due"] = m + 4
                    pend.append(G)
                while pend and pend[0].get("due", 1 << 30) <= m and pend[0] is not G:
                    finalize(pend.pop(0))
        while pend:
            finalize(pend.pop(0))
        P.pe_serial = False
        P.flush()


def build_rel_tables(C, I):
    P, st, k = C.P, C.st, C.k
    ext = C.dr("relext", [8, 1536], F32)
    rb = I["chunk_rel_bias"]
    P.dma(ext[:, 384:639], rb[0, :, 1:256], writes=["relext"])
    P.dma(bass.AP(ext.tensor, 0, [[1536, 8], [1, 384], [1, 1]]), bass.AP(rb.tensor, 0, [[257, 8], [0, 384], [1, 1]]), writes=["relext"])
    P.dma(bass.AP(ext.tensor, 639, [[1536, 8], [1, 897], [1, 1]]), bass.AP(rb.tensor, 256, [[257, 8], [0, 897], [1, 1]]), writes=["relext"])
    band = C.sb(st, "bandB", [128, 8, 512], BF16)
    P.memset("pool", band[:], 0.0, writes=["bandB"])
    for rt in range(8):
        for e in range(2):
            kcr = 2 * rt + e
            lo, hi = max(0, kcr - 8), min(7, kcr)
            if lo <= hi:
                P.memset("pool", band[e * 64:(e + 1) * 64, rt, lo * 64:(hi + 1) * 64], 1.0, writes=["bandB"])
    k["bandB"] = band
    P.flush()


def hgrn_mixer(C, job, s, I, prm):
    P, k = C.P, C.k
    T = job.T
    L = min(16, T)
    SEG = min(256, T)
    base = s * T
    with contextlib.ExitStack() as ts:
        pools = scan_pools(C, ts, False)
        psP = Rot(C, ts, "hg_psP", [128, 512], F32, 2, psum=True)
        S32 = C.sb(ts, "S32", [128, 4, 64], F32)
        Spad = C.sb(ts, "Spad", [128, 4, 128], BF16)
        P.memset("pool", Spad[:], 0.0, writes=["Spad"])
        if job.hgrn_s0 is None:
            P.memset("pool", S32[:], 0.0, writes=["S32"])
        else:
            P.dma(S32[:], job.hgrn_s0[s].rearrange("(a e) k v -> (e k) a v", e=2), writes=["S32"])
            for e in range(2):
                pb = e * 64
                P.copy("act", Spad[pb:pb + 64, :, pb:pb + 64], S32[pb:pb + 64, :, :], reads=["S32"], writes=["Spad"])
        z4 = C.sb(ts, "z4", [128, 16, SEG], F32)
        f4 = lambda nm: C.sb(ts, nm, [128, 4, SEG], F32)
        b4 = lambda nm: C.sb(ts, nm, [128, 4, SEG], BF16)
        qf, sgf, lw, kin, cc, epos, eneg, yT, tmpa = [f4(n) for n in ("qf", "sgf", "lw", "kin", "cc", "epos", "eneg", "yT", "tmpa")]
        rT, kT, vT, sqb, outb = [b4(n) for n in ("rT", "kT", "vT", "sqb", "outb")]
        keys = dict(rT="rT", kT="kT", vT="vT", epos="epos", yT="yT")
        lb, oml, noml, ng = prm["lb"], prm["oml"], prm["noml"], prm["ng"]
        pk_ = ["hgprm"]
        rm = k["rmask%d" % L]
        for t0 in range(0, T, SEG):
            col0 = base + t0
            zk = [("zT", job.name, nb, i) for nb in range(12, 28) for i in range(col0 // 512, (col0 + SEG - 1) // 512 + 1)]
            P.dma(z4[:], job.zT[1536:3584, col0:col0 + SEG].rearrange("(c p) t -> p c t", p=128), reads=zk, writes=["z4"])
            P.act(qf[:], z4[:, 0:4, :], AF.Silu, reads=["z4"], writes=["qf"])
            P.act(sgf[:], z4[:, 4:8, :], AF.Sigmoid, reads=["z4"], writes=["sgf"])
            for blk in range(4):
                P.ts("dve", lw[:, blk, :], sgf[:, blk, :], oml[:, blk:blk + 1], lb[:, blk:blk + 1], ALU.mult, ALU.add, reads=["sgf"] + pk_, writes=["lw"])
                P.ts("dve", kin[:, blk, :], sgf[:, blk, :], noml[:, blk:blk + 1], oml[:, blk:blk + 1], ALU.mult, ALU.add, reads=["sgf"] + pk_, writes=["kin"])
            P.act(lw[:], lw[:], AF.Ln, reads=["lw"], writes=["lw"])
            for blk in range(4):
                P.op("dve", lambda e, blk=blk: e.tensor_tensor_scan(cc[:, blk, :], rm[:, 0:SEG], lw[:, blk, :], 0.0, ALU.mult, ALU.add),
                     reads=["lw", "rmask%d" % L], writes=["cc"])
            P.act(epos[:], cc[:], AF.Exp, reads=["cc"], writes=["epos"])
            P.act(eneg[:], cc[:], AF.Exp, scale=-1.0, reads=["cc"], writes=["eneg"])
            P.tt("dve", rT[:], qf[:], epos[:], ALU.mult, reads=["qf", "epos"], writes=["rT"])
            P.tt("dve", kT[:], kin[:], eneg[:], ALU.mult, reads=["kin", "eneg"], writes=["kT"])
            P.copy("pool", vT[:], z4[:, 8:12, :], reads=["z4"], writes=["vT"])
            chunk_scan(C, ts, SEG, L, False, rT, kT, vT, None, None, epos, S32, Spad, yT, keys, pools)
            P.act(sqb[:], yT[:], AF.Square, reads=["yT"], writes=["sqb"])
            for blk in range(4):
                ps, pk = psP.next()
                P.mm(ps[:, 0:SEG], k["bo"][:], sqb[:, blk, :], reads=["sqb", "bo"], writes=[pk])
                P.act(tmpa[:, blk, :], ps[:, 0:SEG], AF.Sqrt, bias=k["eps_rms"][:, 0:1], scale=1.0 / 64, reads=[pk, "epsv"], writes=["tmpa"])
            P.op("dve", lambda e: e.reciprocal(tmpa[:], tmpa[:]), reads=["tmpa"], writes=["tmpa"])
            P.tt("dve", tmpa[:], tmpa[:], yT[:], ALU.mult, reads=["tmpa", "yT"], writes=["tmpa"])
            P.act(qf[:], z4[:, 12:16, :], AF.Silu, reads=["z4", "rT"], writes=["qf"])
            for blk in range(4):
                P.stt(outb[:, blk, :], tmpa[:, blk, :], ng[:, blk:blk + 1], qf[:, blk, :], ALU.mult, ALU.mult, reads=["tmpa", "qf"] + pk_, writes=["outb"])
            P.dma(job.mixT[512:1024, col0:col0 + SEG].rearrange("(c p) t -> p c t", p=128), outb[:], reads=["outb"],
                  writes=[("mixT", job.name, 4 + c, col0 // 512, hh) for c in range(4) for hh in range(2)], q="pool")
        P.dma(job.hgrn_out[s].rearrange("(a e) k v -> (e k) a v", e=2), S32[:], reads=["S32"], writes=[("hgst", job.name, s)], q="pool")
        P.flush()


def load_hgrn_params(C, I):
    P, st = C.P, C.st
    prm = {}
    t0 = C.sb(st, "hg_t0", [128, 4], F32)
    t1 = C.sb(st, "hg_t1", [128, 4], F32)
    P.dma(t0[:], I["hgrn_lb_table"][0].rearrange("(c p) -> p c", p=128), writes=["hg_t0"], allow_slow_non_contiguous=True)
    P.dma(t1[:], I["hgrn_lb_table"][1].rearrange("(c p) -> p c", p=128), writes=["hg_t1"], allow_slow_non_contiguous=True)
    P.act(t0[:], t0[:], AF.Exp, reads=["hg_t0"], writes=["hg_t0"])
    P.act(t1[:], t1[:], AF.Exp, reads=["hg_t1"], writes=["hg_t1"])
    P.tt("dve", t0[:], t0[:], t1[:], ALU.add, reads=["hg_t0", "hg_t1"], writes=["hg_t0"])
    P.op("dve", lambda e: e.reciprocal(t0[:], t0[:]), reads=["hg_t0"], writes=["hg_t0"])
    lb = C.sb(st, "hg_lb", [128, 4], F32)
    oml = C.sb(st, "hg_oml", [128, 4], F32)
    noml = C.sb(st, "hg_noml", [128, 4], F32)
    ng = C.sb(st, "hg_ng", [128, 4], F32)
    P.tt("dve", lb[:], t1[:], t0[:], ALU.mult, reads=["hg_t0", "hg_t1"], writes=["hgprm"])
    P.ts("dve", oml[:], lb[:], -1.0, 1.0, ALU.mult, ALU.add, reads=["hgprm"], writes=["hgprm"])
    P.ts("dve", noml[:], oml[:], -1.0, None, ALU.mult, reads=["hgprm"], writes=["hgprm"])
    P.dma(ng[:], I["hgrn_norm_g"][0].rearrange("(c p) -> p c", p=128), writes=["hgprm"], allow_slow_non_contiguous=True)
    ng8 = C.sb(st, "hg_ng8", [64, 8], F32)
    P.dma(ng8[:], I["hgrn_norm_g"][0].rearrange("(h k) -> k h", k=64), writes=["hgprm"], allow_slow_non_contiguous=True)
    prm.update(lb=lb, oml=oml, noml=noml, ng=ng, ng8=ng8)
    P.flush()
    return prm


def xk1(job, c, col0, gs):
    return [("xT", job.name, c, i) for i in range(col0 // 512, (col0 + gs - 1) // 512 + 1)]


def out_proj(C, job, Wb, wkeys, l, j, srcT_dram, kcn, src_keys_fn):
    P = C.P
    m = C.mods[(l, j)]
    with contextlib.ExitStack() as ts:
        wr = Rot(C, ts, "op_w", [128, kcn, 128], BF16, 8)
        ws = []
        for nb in range(8):
            w, wk = wr.next()
            P.dma(w[:], Wb[nb], reads=wkeys, writes=[wk])
            ws.append((w, wk))
        sr = Rot(C, ts, "op_src", [128, kcn, 512], BF16, 2)
        pr = Rot(C, ts, "op_ps", [128, 512], F32, 3, psum=True)
        xr = Rot(C, ts, "op_x", [128, 512], F32, 3)
        for (s, t0, gs, col0) in seq_groups(job, 512):
            sT, sk = sr.next()
            P.dma(sT[:, :, 0:gs], srcT_dram[:, col0:col0 + gs].rearrange("(c p) t -> p c t", p=128), reads=src_keys_fn(col0), writes=[sk])
            col = job.modcol[s]
            for nb in range(8):
                w, wk = ws[nb]
                ps, pk = pr.next()
                for kc in range(kcn):
                    P.mm(ps[:, 0:gs], w[:, kc, :], sT[:, kc, 0:gs], start=(kc == 0), stop=(kc == kcn - 1), reads=[wk, sk], writes=[pk])
                x, xk = xr.next()
                P.dma(x[:, 0:gs], job.xT[nb * 128:(nb + 1) * 128, col0:col0 + gs], reads=xk1(job, nb, col0, gs), writes=[xk])
                P.stt(x[:, 0:gs], ps[:, 0:gs], m[:, 16 + nb, col:col + 1], x[:, 0:gs], ALU.mult, ALU.add, reads=[pk, xk, ("mod", l, j)], writes=[xk])
                P.dma(job.xT[nb * 128:(nb + 1) * 128, col0:col0 + gs], x[:, 0:gs], reads=[xk], writes=xk1(job, nb, col0, gs), q="pool")
        P.flush()


def ffn_up(C, job, l, I, Wup, wkeys, hT, hkey):
    P, k = C.P, C.k
    with contextlib.ExitStack() as ts:
        cw = C.sp[("cw", l)]
        cbv = C.sp[("cb", l)]
        wr = Rot(C, ts, "fu_w", [128, KC, 128], BF16, 4)
        wfr = Rot(C, ts, "fu_wf", [128, KC, 128], F32, 4)
        pr = Rot(C, ts, "fu_ps", [128, 512], F32, 4, psum=True)
        Er = [Rot(C, ts, "fu_E%d" % i, [128, 514], F32, 3) for i in range(2)]
        t1r = Rot(C, ts, "fu_t1", [128, 512], F32, 4)
        gr = Rot(C, ts, "fu_g", [128, 512], BF16, 3)
        groups = seq_groups(job, 512)
        for jb in range(22):
            wts = []
            for half in range(2):
                w, wk = wr.next()
                blk_ = jb + 22 * half
                wf, wfk = wfr.next()
                P.dma(wf[:], I["ffn_w_up"][l][:, blk_ * 128:(blk_ + 1) * 128].rearrange("(c p) n -> p c n", p=128), writes=[wfk])
                P.copy("pool", w[:], wf[:], reads=[wfk], writes=[wk])
                wts.append((w, wk))
            prevE = [None, None]
            for (s, t0, gs, col0) in groups:
                res = []
                for half in range(2):
                    blk = jb + 22 * half
                    w, wk = wts[half]
                    ps, pk = pr.next()
                    for kc in range(KC):
                        P.mm(ps[:, 0:gs], w[:, kc, :], hT[:, kc, col0:col0 + gs], start=(kc == 0), stop=(kc == KC - 1), reads=[wk, hkey], writes=[pk])
                    E, Ek = Er[half].next()
                    if t0 == 0:
                        if job.ffn_buf[l][s] is None:
                            P.memset("pool", E[:, 0:2], 0.0, writes=[Ek])
                        else:
                            P.dma(E[:, 0:2], job.ffn_buf[l][s][:, blk * 128:(blk + 1) * 128].rearrange("r f -> f r"), writes=[Ek], allow_slow_non_contiguous=True)
                    else:
                        pE, pEk, pgs = prevE[half]
                        P.copy("pool", E[:, 0:2], pE[:, pgs:pgs + 2], reads=[pEk], writes=[Ek])
                    P.copy("act", E[:, 2:2 + gs], ps[:, 0:gs], reads=[pk], writes=[Ek])
                    prevE[half] = (E, Ek, gs)
                    if t0 + gs == job.Ts[s]:
                        P.dma(job.ffn_out[l][s][:, blk * 128:(blk + 1) * 128].rearrange("r f -> f r"), E[:, gs:gs + 2], reads=[Ek],
                              writes=[("ffo", job.name, l, s, blk)], q="pool", allow_slow_non_contiguous=True)
                    t1, t1k = t1r.next()
                    P.act(t1[:, 0:gs], E[:, 0:gs], AF.Identity, bias=cbv[:, blk:blk + 1], scale=cw[:, 0, blk:blk + 1], reads=[Ek, "smallprm"], writes=[t1k])
                    P.stt(t1[:, 0:gs], E[:, 1:gs + 1], cw[:, 1, blk:blk + 1], t1[:, 0:gs], ALU.mult, ALU.add, reads=[Ek, "smallprm", t1k], writes=[t1k])
                    P.stt(t1[:, 0:gs], E[:, 2:gs + 2], cw[:, 2, blk:blk + 1], t1[:, 0:gs], ALU.mult, ALU.add, reads=[Ek, "smallprm", t1k], writes=[t1k])
                    res.append((t1, t1k))
                (ta, tak), (tb, tbk) = res
                P.act(ta[:, 0:gs], ta[:, 0:gs], AF.Silu, reads=[tak], writes=[tak])
                g, gk = gr.next()
                P.tt("pool", g[:, 0:gs], ta[:, 0:gs], tb[:, 0:gs], ALU.mult, reads=[tak, tbk], writes=[gk])
                P.dma(job.gT[jb * 128:(jb + 1) * 128, col0:col0 + gs], g[:, 0:gs], reads=[gk], writes=[("gT", job.name, jb, col0 // 512)], q="pool")
        P.flush()


def final_norm(C, job, I):
    P, k = C.P, C.k
    with contextlib.ExitStack() as ts:
        gf, gfk = C.sp["gfin"], "smallprm"
        xg = Rot(C, ts, "fn_x", [128, KC, 128], F32, 3)
        sq = Rot(C, ts, "fn_sq", [128, KC, 128], BF16, 2)
        pss = Rot(C, ts, "fn_ss", [128, 128], F32, 2, psum=True)
        rs = Rot(C, ts, "fn_r", [128, 128], F32, 2)
        pst = Rot(C, ts, "fn_pt", [128, 512], F32, 4, psum=True)
        yo = Rot(C, ts, "fn_yo", [128, 1024], F32, 2)

        def stage_a(grp):
            (s, t0, gs, col0) = grp
            x, xk = xg.next()
            rk_ = [key for c in range(8) for key in xk1(job, c, col0, gs)]
            P.dma(x[:, :, 0:gs], job.xT[:, col0:col0 + gs].rearrange("(c p) t -> p c t", p=128), reads=rk_, writes=[xk])
            q, qk = sq.next()
            P.act(q[:, :, 0:gs], x[:, :, 0:gs], AF.Square, reads=[xk], writes=[qk])
            ps, pk = pss.next()
            for c in range(KC):
                P.mm(ps[:, 0:gs], k["onesB"][:], q[:, c, 0:gs], start=(c == 0), stop=(c == KC - 1), reads=[qk, "onesB"], writes=[pk])
            r, rk = rs.next()
            P.act(r[:, 0:gs], ps[:, 0:gs], AF.Ln, bias=k["eps_rms"][:, 0:1], scale=1.0 / D, reads=[pk, "epsv"], writes=[rk])
            P.act(r[:, 0:gs], r[:, 0:gs], AF.Exp, scale=-0.5, reads=[rk], writes=[rk])
            for c in range(KC):
                P.stt(x[:, c, 0:gs], x[:, c, 0:gs], gf[:, c:c + 1], r[:, 0:gs], ALU.mult, ALU.mult, reads=[xk, rk, gfk], writes=[xk])
            return (x, xk, grp)

        def stage_b(st_):
            x, xk, (s, t0, gs, col0) = st_
            y, yk = yo.next()
            for half in range(2):
                pt, ptk = pst.next()
                for c4 in range(4):
                    c = half * 4 + c4
                    P.tr(pt[0:gs, c4 * 128:(c4 + 1) * 128], x[:, c, 0:gs], k["identF"][:, :], reads=[xk, "identF"], writes=[ptk])
                P.copy("act" if half else "dve", y[0:gs, half * 512:(half + 1) * 512], pt[0:gs, :], reads=[ptk], writes=[yk])
            P.dma(job.y_out[s][t0:t0 + gs, :], y[0:gs, :], reads=[yk], writes=[("yout", job.name, s, t0)], q="pool")

        groups = seq_groups(job, 128)
        prev = None
        for grp in groups:
            cur = stage_a(grp)
            if prev is not None:
                stage_b(prev)
            prev = cur
        stage_b(prev)
        P.flush()


def load_small_params(C, I):
    P, st, k = C.P, C.st, C.k
    sp = {}
    specs = []
    for l in range(2):
        specs.append((("gmix", l), I["norm_mix_g"][l].rearrange("(c p) -> c p", p=128), 8))
        specs.append((("gffn", l), I["norm_ffn_g"][l].rearrange("(c p) -> c p", p=128), 8))
        specs.append((("cw", l), I["ffn_conv_w"][l].rearrange("j (c p) -> (j c) p", p=128), 132))
        specs.append((("cb", l), I["ffn_conv_b"][l].rearrange("(c p) -> c p", p=128), 44))
    specs.append(("gfin", I["final_norm_g"].rearrange("(c p) -> c p", p=128), 8))
    tiles = {}
    for key, ap, R in specs:
        tiles[key] = C.sb(st, "sp_%s" % str(key).replace(" ", ""), [128, R], F32)
    with contextlib.ExitStack() as ts:
        ld = Rot(C, ts, "sp_ld", [128, 128], F32, 3)
        pp = Rot(C, ts, "sp_ps", [128, 128], F32, 2, psum=True)
        for key, ap, R in specs:
            dst = tiles[key]
            for r0 in range(0, R, 128):
                rows = min(128, R - r0)
                t, tk = ld.next()
                P.dma(t[0:rows, :], ap[r0:r0 + rows, :], writes=[tk])
                ps, pk = pp.next()
                P.tr(ps[:, 0:rows], t[0:rows, :], k["identF"][0:rows, 0:rows], reads=[tk, "identF"], writes=[pk])
                P.copy("dve", dst[:, r0:r0 + rows], ps[:, 0:rows], reads=[pk], writes=["smallprm"])
        P.flush()
    for l in range(2):
        sp[("gmix", l)] = tiles[("gmix", l)]
        sp[("gffn", l)] = tiles[("gffn", l)]
        sp[("cw", l)] = tiles[("cw", l)][:, :].rearrange("p (j c) -> p j c", j=3)
        sp[("cb", l)] = tiles[("cb", l)]
    sp["gfin"] = tiles["gfin"]
    C.sp = sp


def run_layer(C, job, l, I, W, prm):
    P = C.P
    wname_in = "ab_in" if l == 0 else "cd_in"
    wname_out = "ab_out" if l == 0 else "cd_out"
    nb_in = 27 if l == 0 else 28
    Ttot = job.Ttot
    with contextlib.ExitStack() as ts:
        hT = C.sb(ts, "hT", [128, KC, Ttot], BF16)
        hkey = C.name("hT")
        with contextlib.ExitStack() as t2:
            norm_to_hT(C, t2, job, C.sp[("gmix", l)], l, 0, hT, hkey)
        with contextlib.ExitStack() as t2:
            stg = Rot(C, t2, "pj_stg", [128, 512], F32, 4)
            cnt = [0]

            def post(nb, s, t0, gs, col0, ps, pk):
                st_, sk_ = stg.next()
                cnt[0] += 1
                P.copy("act" if cnt[0] % 2 else "dve", st_[:, 0:gs], ps[:, 0:gs], reads=[pk], writes=[sk_])
                P.dma(job.zT[nb * 128:(nb + 1) * 128, col0:col0 + gs], st_[:, 0:gs], reads=[sk_], writes=[("zT", job.name, nb, col0 // 512)], q="pool")
            if l == 0:
                proj(C, job, hT, hkey, None, [], range(nb_in), KC, post, Wf=I["ab_w_in"][0], ncols=AB_COLS)
            else:
                proj(C, job, hT, hkey, None, [], range(nb_in), KC, post, Wf=I["cd_w_in"][0], ncols=CD_COLS)
            P.flush()
    stage("inproj %s %d" % (job.name, l))
    for s in range(job.nseq):
        if l == 0:
            rwkv_mixer2(C, job, s, I, prm["rwkv"])
            stage("rwkv")
            attn_mixer(C, job, s, I, "fox")
            stage("fox")
        else:
            attn_mixer(C, job, s, I, "chunk")
            stage("chunk")
            hgrn_mixer2(C, job, s, I, prm["hgrn"])
            stage("hgrn")
    Wb, wk = W[wname_out]
    out_proj(C, job, Wb, wk, l, 0, job.mixT, 8,
             lambda col0: [("mixT", job.name, c, col0 // 512, hh) for c in range(8) for hh in range(2)])
    stage("outproj")
    with contextlib.ExitStack() as ts:
        hT = C.sb(ts, "hT2", [128, KC, Ttot], BF16)
        hkey = C.name("hT2")
        with contextlib.ExitStack() as t2:
            norm_to_hT(C, t2, job, C.sp[("gffn", l)], l, 1, hT, hkey)
        Wb, wk = W["up%d" % l]
        ffn_up(C, job, l, I, Wb, wk, hT, hkey)
    stage("ffn_up")
    Wb, wk = W["down%d" % l]
    out_proj(C, job, Wb, wk, l, 1, job.gT, 22, lambda col0: [("gT", job.name, jb, col0 // 512) for jb in range(22)])


class StopBuild(Exception):
    pass


import os
_STOP = int(os.environ.get("K_STOP", "999"))
_stage = [0]


def stage(msg=""):
    _stage[0] += 1
    if os.environ.get("K_VERBOSE"):
        print("stage", _stage[0], msg)
    if _stage[0] >= _STOP:
        raise StopBuild()


def build_program(Tp):
    nc = bass.Bass("TRN2", target_bir_lowering=False)
    _stage[0] = 0
    I = {}
    O = {}

    def inp(name, shape):
        I[name] = nc.dram_tensor(name, list(shape), F32, kind="ExternalInput").ap()

    def outp(name, shape):
        O[name] = nc.dram_tensor(name, list(shape), F32, kind="ExternalOutput").ap()
    S2 = NSEQ_S
    inp("xp", [Tp, D]); inp("xs", [S2, 16, D]); inp("cp", [D]); inp("csv", [S2, D])
    inp("fox_ck", [S2, P_FOX, 512]); inp("fox_cv", [S2, P_FOX, 512]); inp("fox_clogf", [S2, P_FOX, 8])
    inp("rw_s0", [S2, 8, 64, 64]); inp("rw_sh0", [S2, A_COLS])
    inp("ch_ck", [S2, P_CHUNK, 512]); inp("ch_cv", [S2, P_CHUNK, 512]); inp("hg_s0", [S2, 8, 64, 64])
    inp("ffn_buf", [2, S2, 2, 2 * DFF])
    for nm, shp in (("ada_w", [2, 2, D, 3 * D]), ("ada_b", [2, 2, 3 * D]), ("norm_mix_g", [2, D]), ("norm_ffn_g", [2, D]),
                    ("ab_w_in", [1, D, AB_COLS]), ("rwkv_mu", [1, A_COLS]), ("rwkv_w0", [1, 512]), ("rwkv_w2", [1, 64, 512]),
                    ("rwkv_a0", [1, 512]), ("rwkv_a2", [1, 64, 512]), ("rwkv_g2", [1, 128, 512]), ("rwkv_k_k", [1, 512]),
                    ("rwkv_k_a", [1, 512]), ("rwkv_r_k", [1, 8, 64]), ("rwkv_lnx_g", [1, 512]), ("rwkv_lnx_b", [1, 512]),
                    ("fox_b_f", [1, 8]), ("ab_w_out", [1, D, D]), ("cd_w_in", [1, D, CD_COLS]), ("chunk_rel_bias", [1, 8, 257]),
                    ("hgrn_lb_table", [2, 512]), ("hgrn_norm_g", [1, 512]), ("cd_w_out", [1, D, D]), ("ffn_w_up", [2, D, 2 * DFF]),
                    ("ffn_conv_w", [2, 3, 2 * DFF]), ("ffn_conv_b", [2, 2 * DFF]), ("ffn_w_down", [2, DFF, D]), ("final_norm_g", [D])):
        inp(nm, shp)
    cK = min(512, Tp)
    outp("y_p", [Tp, D]); outp("y_s", [S2, 16, D])
    outp("fox_k_p", [Tp, 512]); outp("fox_v_p", [Tp, 512]); outp("fox_logf_p", [Tp, 8])
    outp("rwkv_p", [8, 64, 64]); outp("rwkv_shift_p", [A_COLS])
    outp("chunk_k_p", [cK, 512]); outp("chunk_v_p", [cK, 512]); outp("hgrn_p", [8, 64, 64]); outp("ffn_conv_p", [2, 2, 2 * DFF])
    outp("fox_k_s", [S2, 16, 512]); outp("fox_v_s", [S2, 16, 512]); outp("fox_logf_s", [S2, 16, 8])
    outp("rwkv_s", [S2, 8, 64, 64]); outp("rwkv_shift_s", [S2, A_COLS])
    outp("chunk_k_s", [S2, 16, 512]); outp("chunk_v_s", [S2, 16, 512]); outp("hgrn_s", [S2, 8, 64, 64]); outp("ffn_conv_s", [2, S2, 2, 2 * DFF])

    with contextlib.ExitStack() as st:
        P = Prog(nc, st)
        C = Ctx(nc, P, st)
        try:
          build_consts(C)
          more_consts(C)
          load_small_params(C, I)
          stage("consts")
          W = {}
          W["ab_out"] = cast_weight(C, I["ab_w_out"][0], D, D, "Wab_out")
          W["cd_out"] = cast_weight(C, I["cd_w_out"][0], D, D, "Wcd_out")
          for l in range(2):
              W["up%d" % l] = (None, [])
              W["down%d" % l] = cast_weight(C, I["ffn_w_down"][l], DFF, D, "Wdown%d" % l)
          stage("cast")
          adaln_phase(C, I, 1 + S2, [I["cp"]] + [I["csv"][s] for s in range(S2)])
          stage("adaln")
          prm = dict(rwkv=load_rwkv_params(C, I), hgrn=load_hgrn_params(C, I))
          build_rel_tables(C, I)
          stage("params")

          S3 = 1 + S2
          jb = Job()
          jb.name, jb.nseq = "m", S3
          jb.Ts = [Tp] + [16] * S2
          jb.bases = [0] + [Tp + 16 * i for i in range(S2)]
          jb.Ttot = Tp + 16 * S2
          jb.modcol = list(range(S3))
          jb.x_in = [I["xp"]] + [I["xs"][s] for s in range(S2)]
          jb.y_out = [O["y_p"]] + [O["y_s"][s] for s in range(S2)]
          jb.fox_kout = [O["fox_k_p"]] + [O["fox_k_s"][s] for s in range(S2)]
          jb.fox_vout = [O["fox_v_p"]] + [O["fox_v_s"][s] for s in range(S2)]
          jb.fox_logf_out = [O["fox_logf_p"]] + [O["fox_logf_s"][s] for s in range(S2)]
          jb.rwkv_out = [O["rwkv_p"]] + [O["rwkv_s"][s] for s in range(S2)]
          jb.rwkv_shift_out = [O["rwkv_shift_p"]] + [O["rwkv_shift_s"][s] for s in range(S2)]
          jb.chunk_kout = [O["chunk_k_p"]] + [O["chunk_k_s"][s] for s in range(S2)]
          jb.chunk_vout = [O["chunk_v_p"]] + [O["chunk_v_s"][s] for s in range(S2)]
          jb.hgrn_out = [O["hgrn_p"]] + [O["hgrn_s"][s] for s in range(S2)]
          jb.ffn_out = [[O["ffn_conv_p"][l]] + [O["ffn_conv_s"][l, s] for s in range(S2)] for l in range(2)]
          jb.P_fox = [0] + [P_FOX] * S2
          jb.P_chunk = [0] + [P_CHUNK] * S2
          jb.fox_ck = [None] + [I["fox_ck"][s] for s in range(S2)]
          jb.fox_cv = [None] + [I["fox_cv"][s] for s in range(S2)]
          jb.fox_clogf = [None] + [I["fox_clogf"][s] for s in range(S2)]
          jb.chunk_ck = [None] + [I["ch_ck"][s] for s in range(S2)]
          jb.chunk_cv = [None] + [I["ch_cv"][s] for s in range(S2)]
          jb.rwkv_s0 = [None] + [I["rw_s0"][s] for s in range(S2)]
          jb.rwkv_shift0 = [None] + [I["rw_sh0"][s] for s in range(S2)]
          jb.hgrn_s0 = [None] + [I["hg_s0"][s] for s in range(S2)]
          jb.ffn_buf = [[None] + [I["ffn_buf"][l, s] for s in range(S2)] for l in range(2)]
          for job in (jb,):
              Ttot = job.Ttot
              job.xT = C.dr("xT_" + job.name, [D, Ttot], F32)
              job.zT = C.dr("zT_" + job.name, [28 * 128, Ttot], F32)
              job.mixT = C.dr("mixT_" + job.name, [D, Ttot], BF16)
              job.gT = C.dr("gT_" + job.name, [DFF, Ttot], BF16)
              x_to_fm(C, job)
              stage("x_to_fm " + job.name)
              for l in range(2):
                  run_layer(C, job, l, I, W, prm)
              final_norm(C, job, I)
              stage("final " + job.name)
        except StopBuild:
            P.ops = []
        P.flush(final=True)
        print("ops:", P.n_total)
        if os.environ.get("K_FLUSHLOG"):
            import json as _json
            _json.dump(P.flush_log, open(os.environ["K_FLUSHLOG"], "w"))
    return nc


_CACHE = {}


def kernel(**inp):
    f = lambda a: np.ascontiguousarray(np.asarray(a, dtype=np.float32))
    xpr = f(inp["x_prompt"])
    B, Tp, _ = xpr.shape
    if Tp not in _CACHE:
        _CACHE[Tp] = build_program(Tp)
    nc = _CACHE[Tp]
    wnames = ["ada_w", "ada_b", "norm_mix_g", "norm_ffn_g", "ab_w_in", "rwkv_mu", "rwkv_w0", "rwkv_w2", "rwkv_a0", "rwkv_a2",
              "rwkv_g2", "rwkv_k_k", "rwkv_k_a", "rwkv_r_k", "rwkv_lnx_g", "rwkv_lnx_b", "fox_b_f", "ab_w_out", "cd_w_in",
              "chunk_rel_bias", "hgrn_lb_table", "hgrn_norm_g", "cd_w_out", "ffn_w_up", "ffn_conv_w", "ffn_conv_b", "ffn_w_down",
              "final_norm_g"]
    wts = {n: f(inp[n]) for n in wnames}
    xs = f(inp["x_sample"]); cp = f(inp["c_prompt"]); csv = f(inp["c_sample"])
    fk = f(inp["cache_fox_k"])[0].reshape(16, P_FOX, 512); fv = f(inp["cache_fox_v"])[0].reshape(16, P_FOX, 512)
    fl = f(inp["cache_fox_logf"])[0]
    rs0 = f(inp["state_rwkv"])[0]; rsh = f(inp["state_rwkv_shift"])[0]
    ckk = f(inp["cache_chunk_k"])[0].reshape(16, P_CHUNK, 512); ckv = f(inp["cache_chunk_v"])[0].reshape(16, P_CHUNK, 512)
    hs0 = f(inp["state_hgrn"])[0]; fb = f(inp["state_ffn_conv"])
    in_maps = []
    for c in range(N_CORES):
        b = c % B
        sl = slice(2 * c, 2 * c + 2)
        m = dict(xp=xpr[b], xs=xs[sl], cp=cp[b], csv=csv[sl], fox_ck=fk[sl], fox_cv=fv[sl], fox_clogf=fl[sl], rw_s0=rs0[sl], rw_sh0=rsh[sl],
                 ch_ck=ckk[sl], ch_cv=ckv[sl], hg_s0=hs0[sl], ffn_buf=np.ascontiguousarray(fb[:, sl]))
        m.update(wts)
        in_maps.append({k_: np.ascontiguousarray(v_) for k_, v_ in m.items()})
    res = run_bass_kernel_spmd(nc, in_maps, core_ids=list(range(N_CORES)))
    R = res.results
    pc = lambda name: np.stack([R[b][name] for b in range(B)])
    sc = lambda name, ax=0: np.concatenate([R[c][name] for c in range(N_CORES)], axis=ax)
    cK = min(512, Tp)
    outs = (
        pc("y_p"), sc("y_s"),
        pc("fox_k_p").reshape(1, B, Tp, 8, 64), pc("fox_v_p").reshape(1, B, Tp, 8, 64), pc("fox_logf_p").reshape(1, B, Tp, 8),
        pc("rwkv_p")[None], pc("rwkv_shift_p")[None],
        pc("chunk_k_p").reshape(1, B, cK, 8, 64), pc("chunk_v_p").reshape(1, B, cK, 8, 64), pc("hgrn_p")[None],
        np.stack([R[b]["ffn_conv_p"] for b in range(B)], axis=1),
        sc("fox_k_s").reshape(1, 16, 16, 8, 64), sc("fox_v_s").reshape(1, 16, 16, 8, 64), sc("fox_logf_s").reshape(1, 16, 16, 8),
        sc("rwkv_s")[None], sc("rwkv_shift_s")[None],
        sc("chunk_k_s").reshape(1, 16, 16, 8, 64), sc("chunk_v_s").reshape(1, 16, 16, 8, 64), sc("hgrn_s")[None],
        sc("ffn_conv_s", ax=1),
    )
    return tuple(np.ascontiguousarray(o.astype(np.float32)) for o in outs)


def chunk_scan2(C, ts, SEG, L, delta, ops8, tok4, eposL, S32, Sb, yT8):
    P, k = C.P, C.k
    NCH = SEG // L
    nlev = int(np.log2(L))
    psT = Rot(C, ts, "c2_psT", [64, 4, 128], BF16, 2, psum=True)
    psH = Rot(C, ts, "c2_psH", [64, 8, 64], F32, 6, psum=True)
    r8, r8k = ops8["r"]
    k8, k8k = ops8["k"]
    if delta:
        a8, a8k = ops8["a"]
        b8, b8k = ops8["b"]

    def bt(nm, dt=BF16):
        return [(C.sb(ts, "c2_%s%d" % (nm, c), [64, 8, 64], dt), C.name("c2_" + nm)) for c in range(NCH)]
    Ktok, Vtok = bt("Ktok"), bt("Vtok")
    Mrk = bt("Mrk")
    if delta:
        Btok, Mak, Mrb = bt("Btok"), bt("Mak"), bt("Mrb")
        PT = [bt("PTa"), bt("PTb")]
        Pm = [bt("Pma"), bt("Pmb")]
        TTb = [bt("TTba"), bt("TTbb")]
        TTf = bt("TTf", F32)
    cs_of = lambda c: slice(c * L, (c + 1) * L)

    def mm8(c, lhs_of, rhs_of, reads):
        ps, pk = psH.next()
        for h in range(8):
            P.mm(ps[0:L, h, 0:L], lhs_of(h), rhs_of(h), reads=reads, writes=[pk])
        return ps, pk

    for c in range(NCH):
        cs = cs_of(c)
        lst = [("k", Ktok), ("v", Vtok)] + ([("b", Btok)] if delta else [])
        for nm, dstl in lst:
            src, sk = tok4[nm]
            ps, pk = psT.next()
            for blk in range(4):
                P.tr(ps[0:L, blk, :], src[:, blk, cs], k["identB"][:, :], reads=[sk, "identB"], writes=[pk])
            dst, dk = dstl[c]
            P.copy("act" if nm == "v" else "dve", dst[0:L, :, :], ps[0:L, :, :].rearrange("p a (e x) -> p (a e) x", e=2), reads=[pk], writes=[dk])
    if delta:
        for c in range(NCH):
            cs = cs_of(c)
            psA, pkA = mm8(c, lambda h: b8[:, h, cs], lambda h: a8[:, h, cs], [a8k, b8k])
            psB, pkB = mm8(c, lambda h: a8[:, h, cs], lambda h: b8[:, h, cs], [a8k, b8k])
            tf, tfk = TTf[c]
            pt, ptk = PT[0][c]
            pm, pmk = Pm[0][c]
            tb, tbk = TTb[0][c]
            P.tt("dve", tf[0:L, :, 0:L], psA[0:L, :, 0:L], k["m_su"][0:L, :, 0:L], ALU.mult, reads=[pkA, "m_su"], writes=[tfk])
            P.copy("act", pt[0:L, :, 0:L], tf[0:L, :, 0:L], reads=[tfk], writes=[ptk])
            P.tt("dve", pm[0:L, :, 0:L], psB[0:L, :, 0:L], k["m_sl"][0:L, :, 0:L], ALU.mult, reads=[pkB, "m_sl"], writes=[pmk])
            P.tt("pool", tf[0:L, :, 0:L], tf[0:L, :, 0:L], k["i8"][0:L, :, 0:L], ALU.add, reads=[tfk, "i8"], writes=[tfk])
            P.copy("act", tb[0:L, :, 0:L], tf[0:L, :, 0:L], reads=[tfk], writes=[tbk])
        cur = 0
        for j in range(1, nlev):
            last = (j == nlev - 1)
            nxt = 1 - cur
            pend_ = []
            for c in range(NCH):
                pt, ptk = PT[cur][c]
                pm, pmk = Pm[cur][c]
                psA, pkA = mm8(c, lambda h: pt[0:L, h, 0:L], lambda h: pm[0:L, h, 0:L], [ptk, pmk])
                pm2, pm2k = Pm[nxt][c]
                P.copy("act", pm2[0:L, :, 0:L], psA[0:L, :, 0:L], reads=[pkA], writes=[pm2k])
                if not last:
                    psB, pkB = mm8(c, lambda h: pm[0:L, h, 0:L], lambda h: pt[0:L, h, 0:L], [ptk, pmk])
                    pt2, pt2k = PT[nxt][c]
                    P.copy("dve", pt2[0:L, :, 0:L], psB[0:L, :, 0:L], reads=[pkB], writes=[pt2k])
                if c >= 1:
                    pend_.append(c - 1)
                    cc_ = pend_.pop(0)
                    pm2_, pm2k_ = Pm[nxt][cc_]
                    tb, tbk = TTb[cur][cc_]
                    psC, pkC = mm8(cc_, lambda h: pm2_[0:L, h, 0:L], lambda h: tb[0:L, h, 0:L], [pm2k_, tbk])
                    tf, tfk = TTf[cc_]
                    P.tt("dve", tf[0:L, :, 0:L], tf[0:L, :, 0:L], psC[0:L, :, 0:L], ALU.add, reads=[pkC, tfk], writes=[tfk])
                    tb2, tb2k = TTb[nxt][cc_]
                    P.copy("act", tb2[0:L, :, 0:L], tf[0:L, :, 0:L], reads=[tfk], writes=[tb2k])
            cc_ = NCH - 1
            pm2_, pm2k_ = Pm[nxt][cc_]
            tb, tbk = TTb[cur][cc_]
            psC, pkC = mm8(cc_, lambda h: pm2_[0:L, h, 0:L], lambda h: tb[0:L, h, 0:L], [pm2k_, tbk])
            tf, tfk = TTf[cc_]
            P.tt("dve", tf[0:L, :, 0:L], tf[0:L, :, 0:L], psC[0:L, :, 0:L], ALU.add, reads=[pkC, tfk], writes=[tfk])
            tb2, tb2k = TTb[nxt][cc_]
            P.copy("act", tb2[0:L, :, 0:L], tf[0:L, :, 0:L], reads=[tfk], writes=[tb2k])
            cur = nxt
        TTfin = TTb[cur]
    for c in range(NCH):
        cs = cs_of(c)
        if delta:
            psA, pkA = mm8(c, lambda h: k8[:, h, cs], lambda h: a8[:, h, cs], [k8k, a8k])
            m_, mk_ = Mak[c]
            P.tt("dve", m_[0:L, :, 0:L], psA[0:L, :, 0:L], k["m_su"][0:L, :, 0:L], ALU.mult, reads=[pkA, "m_su"], writes=[mk_])
            psA, pkA = mm8(c, lambda h: b8[:, h, cs], lambda h: r8[:, h, cs], [b8k, r8k])
            m_, mk_ = Mrb[c]
            P.tt("dve", m_[0:L, :, 0:L], psA[0:L, :, 0:L], k["m_ui"][0:L, :, 0:L], ALU.mult, reads=[pkA, "m_ui"], writes=[mk_])
        psA, pkA = mm8(c, lambda h: k8[:, h, cs], lambda h: r8[:, h, cs], [k8k, r8k])
        m_, mk_ = Mrk[c]
        P.tt("dve", m_[0:L, :, 0:L], psA[0:L, :, 0:L], k["m_ui"][0:L, :, 0:L], ALU.mult, reads=[pkA, "m_ui"], writes=[mk_])
    W1r = Rot(C, ts, "c2_W1", [64, 8, 64], BF16, 2)
    Ur = Rot(C, ts, "c2_U", [64, 8, 64], BF16, 2)
    yt, ytk = yT8
    el, elk = eposL
    for c in range(NCH):
        cs = cs_of(c)
        Kt, Ktk = Ktok[c]
        Vt_, Vtk = Vtok[c]
        mrk, mrkk = Mrk[c]
        if delta:
            Bt, Btk = Btok[c]
            mak, makk = Mak[c]
            mrb, mrbk = Mrb[c]
            tb, tbk = TTfin[c]
            ps, pk = psH.next()
            for h in range(8):
                P.mm(ps[0:L, h, :], a8[:, h, cs], Sb[:, h, :], start=True, stop=False, reads=[a8k, "Sb"], writes=[pk])
                P.mm(ps[0:L, h, :], mak[0:L, h, 0:L], Vt_[0:L, h, :], start=False, stop=True, reads=[makk, Vtk], writes=[pk])
            W1, W1k = W1r.next()
            P.copy("act", W1[0:L, :, :], ps[0:L, :, :], reads=[pk], writes=[W1k])
            ps, pk = psH.next()
            for h in range(8):
                P.mm(ps[0:L, h, :], tb[0:L, h, 0:L], W1[0:L, h, :], reads=[tbk, W1k], writes=[pk])
            U, Uk = Ur.next()
            P.copy("dve", U[0:L, :, :], ps[0:L, :, :], reads=[pk], writes=[Uk])
        ps, pk = psH.next()
        for h in range(8):
            P.mm(ps[:, h, 0:L], Sb[:, h, :], r8[:, h, cs], start=True, stop=False, reads=["Sb", r8k], writes=[pk])
            if delta:
                P.mm(ps[:, h, 0:L], U[0:L, h, :], mrb[0:L, h, 0:L], start=False, stop=False, reads=[Uk, mrbk], writes=[pk])
            P.mm(ps[:, h, 0:L], Vt_[0:L, h, :], mrk[0:L, h, 0:L], start=False, stop=True, reads=[Vtk, mrkk], writes=[pk])
        P.copy("act", yt[:, :, cs], ps[:, :, 0:L], reads=[pk], writes=[ytk])
        ps, pk = psH.next()
        for h in range(8):
            if delta:
                P.mm(ps[:, h, :], Bt[0:L, h, :], U[0:L, h, :], start=True, stop=False, reads=[Btk, Uk], writes=[pk])
            P.mm(ps[:, h, :], Kt[0:L, h, :], Vt_[0:L, h, :], start=(not delta), stop=True, reads=[Ktk, Vtk], writes=[pk])
        skeys = [("S32", h) for h in range(8)]
        P.tt("dve", S32[:, :, :], S32[:, :, :], ps[:, :, :], ALU.add, reads=[pk] + skeys, writes=skeys)
        for h in range(8):
            P.ts("dve", S32[:, h, :], S32[:, h, :], el[:, h, c:c + 1], None, ALU.mult, reads=[("S32", h), elk], writes=[("S32", h)])
        P.copy("act", Sb[:, :, :], S32[:, :, :], reads=skeys, writes=["Sb"])


def to8(P, dst8, dkey, src4, skey, q="sp"):
    d4 = dst8[:, :, :].rearrange("p (a e) t -> p a e t", e=2)
    for e in range(2):
        P.dma(d4[:, :, e, :], src4[e * 64:(e + 1) * 64, :, :], reads=[skey], writes=[dkey], q=q)


def rwkv_mixer2(C, job, s, I, prm):
    P, k = C.P, C.k
    T = job.Ts[s]
    L = min(64, T)
    SEG = min(256, T)
    NCH = SEG // L
    base = job.bases[s]
    S8K = [("S32", h) for h in range(8)]
    with contextlib.ExitStack() as ts:
        S32 = C.sb(ts, "S32", [64, 8, 64], F32)
        Sb = C.sb(ts, "Sb", [64, 8, 64], BF16)
        k4, v4, b4 = [C.sb(ts, n, [128, 4, SEG], BF16) for n in ("k4", "v4", "b4")]
        r8, a8, b8, k8, v8, t8 = [C.sb(ts, n, [64, 8, SEG], BF16) for n in ("r8", "a8", "b8", "k8", "v8", "t8")]
        el = C.sb(ts, "el", [64, 8, NCH], F32)
        yT8 = C.sb(ts, "yT8", [64, 8, SEG], F32)
        sg = C.sb(ts, "sg", [128, SEG], BF16)
        zlast = C.sb(ts, "zlast", [128, 14], F32)
        mu, w0, a0, k_k, k_a, r_k, omka, wa2, g2, lng8, lnb8 = [prm[n] for n in
            ("mu", "w0", "a0", "k_k", "k_a", "r_k", "omka", "wa2", "g2", "lng8", "lnb8")]
        pk_ = ["rwprm", "rwprm2"]
        with contextlib.ExitStack() as t0s:
            if job.rwkv_s0[s] is None:
                P.memset("pool", S32[:], 0.0, writes=S8K)
            else:
                s0t = C.sb(t0s, "s0t", [64, 8, 64], F32)
                psI = C.ps(t0s, "rw_psI", [64, 8, 64], F32)
                P.dma(s0t[:], job.rwkv_s0[s].rearrange("h v k -> v h k"), writes=["s0t"])
                for h in range(8):
                    P.tr(psI[:, h, :], s0t[:, h, :], k["identF"][0:64, 0:64], reads=["s0t", "identF"], writes=["psI"])
                P.copy("dve", S32[:], psI[:], reads=["psI"], writes=S8K)
            P.copy("act", Sb[:], S32[:], reads=S8K, writes=["Sb"])
            P.flush()
        for t0 in range(0, T, SEG):
            col0 = base + t0
            with contextlib.ExitStack() as tp:
                psP = Rot(C, tp, "rw_psP", [128, 512], F32, 4, psum=True)
                zt = C.sb(tp, "zt", [128, 14, SEG + 1], F32)
                dd = C.sb(tp, "dd", [128, 14, SEG], F32)
                zs = C.sb(tp, "zs", [128, 14, SEG], F32)
                f4 = lambda nm: C.sb(tp, nm, [128, 4, SEG], F32)
                b4_ = lambda nm: C.sb(tp, nm, [128, 4, SEG], BF16)
                lw, asig, kkn, kmod, cc, epos, eneg, eprev, tmpa, tmpb_ = [f4(n) for n in
                    ("lw", "asig", "kkn", "kmod", "cc", "epos", "eneg", "eprev", "tmpa", "tmpb")]
                rT4, aT4, t4, sqb = [b4_(n) for n in ("rT4", "aT4", "t4", "sqb")]
                tw = C.sb(tp, "tw", [128, SEG], BF16)
                zrows = job.zT[0:A_COLS, :].rearrange("(c p) t -> p c t", p=128)
                zk = [("zT", job.name, nb, i) for nb in range(14) for i in range(col0 // 512, (col0 + SEG - 1) // 512 + 1)]
                if t0 == 0:
                    P.dma(zt[:, :, 1:SEG + 1], zrows[:, :, col0:col0 + SEG], reads=zk, writes=["zt"])
                    if job.rwkv_shift0[s] is None:
                        P.memset("pool", zt[:, :, 0:1], 0.0, writes=["zt"])
                    else:
                        P.dma(zt[:, :, 0], job.rwkv_shift0[s].rearrange("(c p) -> p c", p=128), writes=["zt"], allow_slow_non_contiguous=True)
                else:
                    P.dma(zt[:, :, 1:SEG + 1], zrows[:, :, col0:col0 + SEG], reads=zk, writes=["zt"])
                    P.copy("pool", zt[:, :, 0], zlast[:, :], reads=["zlast"], writes=["zt"])
                P.copy("pool", zlast[:, :], zt[:, :, SEG], reads=["zt"], writes=["zlast"])
                if t0 + SEG == T:
                    P.dma(job.rwkv_shift_out[s].rearrange("(c p) -> p c", p=128), zlast[:, :], reads=["zlast"], writes=[("rwsh", job.name, s)],
                          q="pool", allow_slow_non_contiguous=True)
                P.tt("pool", dd[:], zt[:, :, 0:SEG], zt[:, :, 1:SEG + 1], ALU.subtract, reads=["zt"], writes=["dd"])
                for blk in range(14):
                    P.stt(zs[:, blk, :], dd[:, blk, :], mu[:, blk:blk + 1], zt[:, blk, 1:SEG + 1], ALU.mult, ALU.add, reads=["dd", "zt"] + pk_, writes=["zs"])
                P.act(tw[0:64, :], zs[0:64, 12, :], AF.Tanh, reads=["zs"], writes=["tw"])
                P.copy("dve", tw[64:128, :], zs[64:128, 12, :], reads=["zs"], writes=["tw"])
                P.act(sg[:], zs[:, 13, :], AF.Sigmoid, reads=["zs"], writes=["sg"])
                for blk in range(4):
                    bs = slice(blk * 128, (blk + 1) * 128)
                    ps, pk = psP.next()
                    P.mm(ps[:, 0:SEG], wa2[0:64, bs], tw[0:64, :], reads=["tw"] + pk_, writes=[pk])
                    P.act(lw[:, blk, :], ps[:, 0:SEG], AF.Sigmoid, bias=w0[:, blk:blk + 1], reads=[pk] + pk_, writes=["lw"])
                    ps, pk = psP.next()
                    P.mm(ps[:, 0:SEG], wa2[64:128, bs], tw[64:128, :], reads=["tw"] + pk_, writes=[pk])
                    P.act(asig[:, blk, :], ps[:, 0:SEG], AF.Sigmoid, bias=a0[:, blk:blk + 1], reads=[pk] + pk_, writes=["asig"])
                P.ts("dve", lw[:], lw[:], -DECAY_C, None, ALU.mult, reads=["lw"], writes=["lw"])
                for blk in range(4):
                    P.ts("dve", tmpa[:, blk, :], zs[:, 4 + blk, :], k_k[:, blk:blk + 1], None, ALU.mult, reads=["zs"] + pk_, writes=["tmpa"])
                P.act(sqb[:], tmpa[:], AF.Square, reads=["tmpa"], writes=["sqb"])
                for blk in range(4):
                    ps, pk = psP.next()
                    P.mm(ps[:, 0:SEG], k["bo"][:], sqb[:, blk, :], reads=["sqb", "bo"], writes=[pk])
                    P.act(tmpb_[:, blk, :], ps[:, 0:SEG], AF.Sqrt, reads=[pk], writes=["tmpb"])
                P.ts("dve", tmpb_[:], tmpb_[:], 1e-12, None, ALU.max, reads=["tmpb"], writes=["tmpb"])
                P.op("dve", lambda e: e.reciprocal(tmpb_[:], tmpb_[:]), reads=["tmpb"], writes=["tmpb"])
                P.tt("dve", kkn[:], tmpa[:], tmpb_[:], ALU.mult, reads=["tmpa", "tmpb"], writes=["kkn"])
                for blk in range(4):
                    P.ts("dve", tmpa[:, blk, :], asig[:, blk, :], k_a[:, blk:blk + 1], omka[:, blk:blk + 1], ALU.mult, ALU.add,
                         reads=["asig"] + pk_, writes=["tmpa"])
                P.tt("dve", kmod[:], tmpa[:], zs[:, 4:8, :], ALU.mult, reads=["tmpa", "zs"], writes=["kmod"])
                rm = k["rmask%d" % L]
                for blk in range(4):
                    P.op("dve", lambda e, blk=blk: e.tensor_tensor_scan(cc[:, blk, :], rm[:, 0:SEG], lw[:, blk, :], 0.0, ALU.mult, ALU.add),
                         reads=["lw", "rmask%d" % L], writes=["cc"])
                P.act(epos[:], cc[:], AF.Exp, reads=["cc"], writes=["epos"])
                P.act(eneg[:], cc[:], AF.Exp, scale=-1.0, reads=["cc"], writes=["eneg"])
                P.tt("pool", tmpa[:], cc[:], lw[:], ALU.subtract, reads=["cc", "lw", "kmod"], writes=["tmpa"])
                P.act(eprev[:], tmpa[:], AF.Exp, reads=["tmpa"], writes=["eprev"])
                P.tt("dve", rT4[:], zs[:, 0:4, :], epos[:], ALU.mult, reads=["zs", "epos"], writes=["rT4"])
                P.stt(aT4[:], kkn[:], -1.0, eprev[:], ALU.mult, ALU.mult, reads=["kkn", "eprev"], writes=["aT4"])
                P.tt("pool", tmpb_[:], kkn[:], asig[:], ALU.mult, reads=["kkn", "asig"], writes=["tmpb"])
                P.tt("dve", b4[:], tmpb_[:], eneg[:], ALU.mult, reads=["tmpb", "eneg"], writes=["b4"])
                P.tt("dve", k4[:], kmod[:], eneg[:], ALU.mult, reads=["kmod", "eneg"], writes=["k4"])
                P.copy("act", v4[:], zs[:, 8:12, :], reads=["zs"], writes=["v4"])
                for blk in range(4):
                    P.stt(t4[:, blk, :], zs[:, blk, :], r_k[:, blk:blk + 1], kmod[:, blk, :], ALU.mult, ALU.mult, reads=["zs", "kmod"] + pk_, writes=["t4"])
                for dst, dkey, src, skey in ((r8, "r8", rT4, "rT4"), (a8, "a8", aT4, "aT4"), (b8, "b8", b4, "b4"), (k8, "k8", k4, "k4"),
                                             (v8, "v8", v4, "v4"), (t8, "t8", t4, "t4")):
                    to8(P, dst, dkey, src, skey)
                ec = epos[:, :, :].rearrange("p a (c l) -> p a c l", l=L)
                e4 = el[:, :, :].rearrange("p (a e) c -> p a e c", e=2)
                ecomp = C.sb(tp, "ecomp", [128, 4, NCH], F32)
                P.copy("pool", ecomp[:], ec[:, :, :, L - 1], reads=["epos"], writes=["ecomp"])
                for e in range(2):
                    P.dma(e4[:, :, e, :], ecomp[e * 64:(e + 1) * 64, :, :], reads=["ecomp"], writes=["el"], allow_slow_non_contiguous=True)
                P.flush()
            with contextlib.ExitStack() as tsc:
                ops8 = dict(r=(r8, "r8"), k=(k8, "k8"), a=(a8, "a8"), b=(b8, "b8"))
                tok4 = dict(k=(k4, "k4"), v=(v4, "v4"), b=(b4, "b4"))
                chunk_scan2(C, tsc, SEG, L, True, ops8, tok4, (el, "el"), S32, Sb, (yT8, "yT8"))
                P.flush()
            with contextlib.ExitStack() as tq:
                psQ = Rot(C, tq, "rw_psQ", [64, 4, SEG], F32, 3 if SEG > 128 else 6, psum=True)
                yb8 = C.sb(tq, "yb8", [64, 8, SEG], BF16)
                d8 = C.sb(tq, "d8", [64, 8, SEG], F32)
                rs8 = C.sb(tq, "rs8", [64, 8, SEG], F32)
                tm8 = C.sb(tq, "tm8", [64, 8, SEG], F32)
                o8 = C.sb(tq, "o8", [64, 8, SEG], BF16)
                one64 = k["onesB"][0:64, 0:64]
                P.copy("act", yb8[:], yT8[:], reads=["yT8"], writes=["yb8"])
                for half in range(2):
                    hs = slice(half * 4, half * 4 + 4)
                    ps, pk = psQ.next()
                    for i in range(4):
                        P.mm(ps[:, i, :], one64, yb8[:, half * 4 + i, :], reads=["yb8", "onesB"], writes=[pk])
                    P.stt(d8[:, hs, :], ps[:, :, :], -1.0 / 64, yT8[:, hs, :], ALU.mult, ALU.add, reads=[pk, "yT8"], writes=[("d8", half)])
                P.act(yb8[:], d8[:], AF.Square, reads=[("d8", 0), ("d8", 1), "yb8"], writes=["yb8"])
                for half in range(2):
                    hs = slice(half * 4, half * 4 + 4)
                    ps, pk = psQ.next()
                    for i in range(4):
                        P.mm(ps[:, i, :], one64, yb8[:, half * 4 + i, :], reads=["yb8", "onesB"], writes=[pk])
                    P.act(rs8[:, hs, :], ps[:, :, :], AF.Ln, bias=k["eps_gn"][0:64, 0:1], scale=1.0 / 64, reads=[pk, "epsv"], writes=[("rs8", half)])
                P.act(rs8[:], rs8[:], AF.Exp, scale=-0.5, reads=[("rs8", 0), ("rs8", 1)], writes=["rs8r"])
                P.tt("dve", d8[:], d8[:], rs8[:], ALU.mult, reads=[("d8", 0), ("d8", 1), "rs8r"], writes=["d8n"])
                for h in range(8):
                    P.ts("dve", d8[:, h, :], d8[:, h, :], lng8[:, h:h + 1], lnb8[:, h:h + 1], ALU.mult, ALU.add, reads=["d8n"] + pk_, writes=[("d8a", h)])
                for half in range(2):
                    hs = slice(half * 4, half * 4 + 4)
                    ps, pk = psQ.next()
                    for i in range(4):
                        P.mm(ps[:, i, :], one64, t8[:, half * 4 + i, :], reads=["t8", "onesB"], writes=[pk])
                    P.tt("dve", tm8[:, hs, :], ps[:, :, :], v8[:, hs, :], ALU.mult, reads=[pk, "v8"], writes=[("tm8", half)])
                P.tt("dve", d8[:], d8[:], tm8[:], ALU.add, reads=[("d8a", h) for h in range(8)] + [("tm8", 0), ("tm8", 1)], writes=["d8f"])
                for half in range(2):
                    hs = slice(half * 4, half * 4 + 4)
                    ps, pk = psQ.next()
                    for i in range(4):
                        h = half * 4 + i
                        P.mm(ps[:, i, :], g2[:, h * 64:(h + 1) * 64], sg[:, :], reads=["sg"] + pk_, writes=[pk])
                    P.tt("dve", o8[:, hs, :], ps[:, :, :], d8[:, hs, :], ALU.mult, reads=[pk, "d8f"], writes=[("o8", half)])
                P.dma(job.mixT[0:512, col0:col0 + SEG].rearrange("(h k) t -> k h t", k=64), o8[:], reads=[("o8", 0), ("o8", 1)],
                      writes=[("mixT", job.name, c, col0 // 512, hh) for c in range(4) for hh in range(2)], q="pool")
                P.flush()
        with contextlib.ExitStack() as tf_:
            psO = C.ps(tf_, "rw_psO", [64, 8, 64], F32)
            so = C.sb(tf_, "so", [64, 8, 64], F32)
            for h in range(8):
                P.tr(psO[:, h, :], S32[:, h, :], k["identF"][0:64, 0:64], reads=S8K + ["identF"], writes=["psO"])
            P.copy("dve", so[:], psO[:], reads=["psO"], writes=["so"])
            P.dma(job.rwkv_out[s].rearrange("h v k -> v h k"), so[:], reads=["so"], writes=[("rwst", job.name, s)], q="pool")
            P.flush()


def hgrn_mixer2(C, job, s, I, prm):
    P, k = C.P, C.k
    T = job.Ts[s]
    L = min(32, T)
    SEG = min(256, T)
    NCH = SEG // L
    base = job.bases[s]
    S8K = [("S32", h) for h in range(8)]
    with contextlib.ExitStack() as ts:
        S32 = C.sb(ts, "S32", [64, 8, 64], F32)
        Sb = C.sb(ts, "Sb", [64, 8, 64], BF16)
        k4, v4 = [C.sb(ts, n, [128, 4, SEG], BF16) for n in ("k4", "v4")]
        r8, k8 = [C.sb(ts, n, [64, 8, SEG], BF16) for n in ("r8", "k8")]
        el = C.sb(ts, "el", [64, 8, NCH], F32)
        yT8 = C.sb(ts, "yT8", [64, 8, SEG], F32)
        lb, oml, noml, ng8 = prm["lb"], prm["oml"], prm["noml"], prm["ng8"]
        pk_ = ["hgprm"]
        if job.hgrn_s0[s] is None:
            P.memset("pool", S32[:], 0.0, writes=S8K)
        else:
            P.dma(S32[:], job.hgrn_s0[s].rearrange("h k v -> k h v"), writes=S8K)
        P.copy("act", Sb[:], S32[:], reads=S8K, writes=["Sb"])
        P.flush()
        rm = k["rmask%d" % L]
        for t0 in range(0, T, SEG):
            col0 = base + t0
            with contextlib.ExitStack() as tp:
                z4 = C.sb(tp, "z4", [128, 12, SEG], F32)
                f4 = lambda nm: C.sb(tp, nm, [128, 4, SEG], F32)
                qf, sgf, lw, kin, cc, epos, eneg = [f4(n) for n in ("qf", "sgf", "lw", "kin", "cc", "epos", "eneg")]
                rT4 = C.sb(tp, "rT4", [128, 4, SEG], BF16)
                zk = [("zT", job.name, nb, i) for nb in range(12, 24) for i in range(col0 // 512, (col0 + SEG - 1) // 512 + 1)]
                P.dma(z4[:], job.zT[1536:3072, col0:col0 + SEG].rearrange("(c p) t -> p c t", p=128), reads=zk, writes=["z4"])
                P.act(qf[:], z4[:, 0:4, :], AF.Silu, reads=["z4"], writes=["qf"])
                P.act(sgf[:], z4[:, 4:8, :], AF.Sigmoid, reads=["z4"], writes=["sgf"])
                for blk in range(4):
                    P.ts("dve", lw[:, blk, :], sgf[:, blk, :], oml[:, blk:blk + 1], lb[:, blk:blk + 1], ALU.mult, ALU.add, reads=["sgf"] + pk_, writes=["lw"])
                    P.ts("dve", kin[:, blk, :], sgf[:, blk, :], noml[:, blk:blk + 1], oml[:, blk:blk + 1], ALU.mult, ALU.add, reads=["sgf"] + pk_, writes=["kin"])
                P.act(lw[:], lw[:], AF.Ln, reads=["lw"], writes=["lw"])
                for blk in range(4):
                    P.op("dve", lambda e, blk=blk: e.tensor_tensor_scan(cc[:, blk, :], rm[:, 0:SEG], lw[:, blk, :], 0.0, ALU.mult, ALU.add),
                         reads=["lw", "rmask%d" % L], writes=["cc"])
                P.act(epos[:], cc[:], AF.Exp, reads=["cc"], writes=["epos"])
                P.act(eneg[:], cc[:], AF.Exp, scale=-1.0, reads=["cc"], writes=["eneg"])
                P.tt("dve", rT4[:], qf[:], epos[:], ALU.mult, reads=["qf", "epos"], writes=["rT4"])
                P.tt("dve", k4[:], kin[:], eneg[:], ALU.mult, reads=["kin", "eneg"], writes=["k4"])
                P.copy("pool", v4[:], z4[:, 8:12, :], reads=["z4"], writes=["v4"])
                to8(P, r8, "r8", rT4, "rT4")
                to8(P, k8, "k8", k4, "k4")
                ec = epos[:, :, :].rearrange("p a (c l) -> p a c l", l=L)
                e4 = el[:, :, :].rearrange("p (a e) c -> p a e c", e=2)
                ecomp = C.sb(tp, "ecomp", [128, 4, NCH], F32)
                P.copy("pool", ecomp[:], ec[:, :, :, L - 1], reads=["epos"], writes=["ecomp"])
                for e in range(2):
                    P.dma(e4[:, :, e, :], ecomp[e * 64:(e + 1) * 64, :, :], reads=["ecomp"], writes=["el"], allow_slow_non_contiguous=True)
                P.flush()
            with contextlib.ExitStack() as tsc:
                chunk_scan2(C, tsc, SEG, L, False, dict(r=(r8, "r8"), k=(k8, "k8")), dict(k=(k4, "k4"), v=(v4, "v4")),
                            (el, "el"), S32, Sb, (yT8, "yT8"))
                P.flush()
            with contextlib.ExitStack() as tq:
                psQ = Rot(C, tq, "hg_psQ", [64, 4, SEG], F32, 3 if SEG > 128 else 6, psum=True)
                yb8 = C.sb(tq, "yb8", [64, 8, SEG], BF16)
                rs8 = C.sb(tq, "rs8", [64, 8, SEG], F32)
                zg8 = C.sb(tq, "zg8", [64, 8, SEG], F32)
                o8 = C.sb(tq, "o8", [64, 8, SEG], BF16)
                one64 = k["onesB"][0:64, 0:64]
                zkg = [("zT", job.name, nb, i) for nb in range(24, 28) for i in range(col0 // 512, (col0 + SEG - 1) // 512 + 1)]
                P.dma(zg8[:], job.zT[3072:3584, col0:col0 + SEG].rearrange("(h k) t -> k h t", k=64), reads=zkg, writes=["zg8"])
                P.act(zg8[:], zg8[:], AF.Silu, reads=["zg8"], writes=["zg8"])
                P.act(yb8[:], yT8[:], AF.Square, reads=["yT8"], writes=["yb8"])
                for half in range(2):
                    hs = slice(half * 4, half * 4 + 4)
                    ps, pk = psQ.next()
                    for i in range(4):
                        P.mm(ps[:, i, :], one64, yb8[:, half * 4 + i, :], reads=["yb8", "onesB"], writes=[pk])
                    P.act(rs8[:, hs, :], ps[:, :, :], AF.Ln, bias=k["eps_rms"][0:64, 0:1], scale=1.0 / 64, reads=[pk, "epsv"], writes=[("rs8", half)])
                P.act(rs8[:], rs8[:], AF.Exp, scale=-0.5, reads=[("rs8", 0), ("rs8", 1)], writes=["rs8r"])
                P.tt("dve", rs8[:], rs8[:], yT8[:], ALU.mult, reads=["rs8r", "yT8"], writes=["rs8y"])
                for h in range(8):
                    P.stt(o8[:, h, :], rs8[:, h, :], ng8[:, h:h + 1], zg8[:, h, :], ALU.mult, ALU.mult, reads=["rs8y", "zg8"] + pk_, writes=[("o8", h)])
                P.dma(job.mixT[512:1024, col0:col0 + SEG].rearrange("(h k) t -> k h t", k=64), o8[:], reads=[("o8", h) for h in range(8)],
                      writes=[("mixT", job.name, 4 + c, col0 // 512, hh) for c in range(4) for hh in range(2)], q="pool")
                P.flush()
        P.dma(job.hgrn_out[s].rearrange("h k v -> k h v"), S32[:], reads=S8K, writes=[("hgst", job.name, s)], q="pool")
        P.flush()
```

```python
import os
import numpy as np
import concourse.bass as bass
import concourse.mybir as mybir
from concourse.bass_utils import run_bass_kernel_spmd

F32 = mybir.dt.float32
BF16 = mybir.dt.bfloat16
AF = mybir.ActivationFunctionType
ALU = mybir.AluOpType
AX = mybir.AxisListType

N_CORES = 8


class Prog:
    ENGS = ("pe", "act", "dve", "pool", "sp")
    DMA_SLOTS = {"sp": 20, "pool": 12, "act": 8}

    def __init__(self, nc, stack):
        self.nc = nc
        self.ops = []
        self.same_engine_sync = True
        self.esem = {e: stack.enter_context(nc.semaphore("s_" + e)) for e in ("pe", "act", "dve", "pool")}
        self.dsem = {q: [stack.enter_context(nc.semaphore("d_%s%d" % (q, i))) for i in range(k)]
                     for q, k in self.DMA_SLOTS.items()}
        self.ecount = {e: 0 for e in self.esem}
        self.dcount = {q: 0 for q in self.dsem}
        self.slot_last = {}
        self.last_w = {}
        self.readers = {}
        self.known = {e: {} for e in self.ENGS}
        self.final_dma = {}
        self.n_total = 0

    def op(self, eng, fn, reads=(), writes=(), dma=False):
        self.n_rec = getattr(self, "n_rec", 0) + 1
        if self.n_rec == int(os.environ.get("K_SHOW", "-1")):
            import traceback
            traceback.print_stack(limit=4)
            print("SHOW op", eng, reads, writes)
        if self.n_rec > int(os.environ.get("K_MAXOPS", "100000000")):
            return
        self.ops.append(dict(eng=eng, fn=fn, reads=tuple(reads), writes=tuple(writes), dma=dma, serial=getattr(self, "pe_serial", False)))

    def mm(self, out, lhsT, rhs, start=True, stop=True, reads=(), writes=()):
        self.op("pe", lambda e: e.matmul(out, lhsT, rhs, start=start, stop=stop), reads, writes)

    def tr(self, out, in_, ident, reads=(), writes=()):
        self.op("pe", lambda e: e.transpose(out, in_, ident), reads, writes)

    def act(self, out, in_, func, bias=0.0, scale=1.0, reads=(), writes=(), accum_out=None):
        if accum_out is None:
            self.op("act", lambda e: e.activation(out, in_, func, bias=bias, scale=scale), reads, writes)
        else:
            self.op("act", lambda e: e.activation(out, in_, func, bias=bias, scale=scale, accum_out=accum_out), reads, writes)

    def tt(self, eng, out, in0, in1, op, reads=(), writes=()):
        self.op(eng, lambda e: e.tensor_tensor(out, in0, in1, op), reads, writes)

    def ts(self, eng, out, in0, s1, s2, op0, op1=None, reads=(), writes=()):
        if op1 is None:
            self.op(eng, lambda e: e.tensor_scalar(out, in0, s1, None, op0), reads, writes)
        else:
            self.op(eng, lambda e: e.tensor_scalar(out, in0, s1, s2, op0, op1), reads, writes)

    def stt(self, out, in0, scalar, in1, op0, op1, reads=(), writes=()):
        self.op("dve", lambda e: e.scalar_tensor_tensor(out, in0, scalar, in1, op0, op1), reads, writes)

    def copy(self, eng, out, in_, reads=(), writes=()):
        if eng == "act":
            self.op("act", lambda e: e.copy(out, in_), reads, writes)
        else:
            self.op(eng, lambda e: e.tensor_copy(out, in_), reads, writes)

    def memset(self, eng, ap, val, writes=()):
        self.op(eng, lambda e: e.memset(ap, val), (), writes)

    def dma(self, out, in_, reads=(), writes=(), q="sp", **kw):
        self.op(q, lambda e: e.dma_start(out=out, in_=in_, **kw), reads, writes, dma=True)

    def semof(self, key):
        return self.esem[key[1]] if key[0] == "e" else self.dsem[key[1]][key[2]]

    def flush(self, final=False):
        import inspect
        fr = inspect.stack()[1]
        if not hasattr(self, "flush_log"):
            self.flush_log = []
        _cnt = {}
        for _o in self.ops:
            _cnt[_o["eng"]] = _cnt.get(_o["eng"], 0) + 1
        self.flush_log.append(("%s:%d" % (fr.function, fr.lineno), len(self.ops), _cnt))
        nc = self.nc
        ops = self.ops
        self.ops = []
        n = len(ops)
        self.n_total += n
        plan = {e: [] for e in self.ENGS}
        for o in ops:
            e = o["eng"]
            deps = []
            if o["dma"]:
                j = self.dcount[e]
                self.dcount[e] += 1
                k = len(self.dsem[e])
                key = ("d", e, j % k)
                val = 16 * (j // k + 1)
                if key in self.slot_last:
                    deps.append(self.slot_last[key])
                mysig = (key, val, e, True)
                self.slot_last[key] = mysig
                self.final_dma[key] = val
            else:
                self.ecount[e] += 1
                mysig = (("e", e), self.ecount[e], e, False)
            for r in o["reads"]:
                if r in self.last_w:
                    deps.append(self.last_w[r])
            for w in o["writes"]:
                if w in self.last_w:
                    deps.append(self.last_w[w])
                deps.extend(self.readers.get(w, ()))
            for r in o["reads"]:
                self.readers.setdefault(r, []).append(mysig)
            for w in o["writes"]:
                self.last_w[w] = mysig
                self.readers[w] = []
            need = {}
            for (dkey, dval, deng, ddma) in deps:
                if (not ddma) and deng == e:
                    if (not self.same_engine_sync) or e == "pe":
                        continue
                if need.get(dkey, 0) < dval:
                    need[dkey] = dval
            if e == "pe" and o.get("serial") and self.ecount["pe"] > 1:
                need[("e", "pe")] = max(need.get(("e", "pe"), 0), self.ecount["pe"] - 1)
            wl = []
            for dkey, dval in need.items():
                if self.known[e].get(dkey, 0) >= dval:
                    continue
                self.known[e][dkey] = dval
                wl.append((dkey, dval))
            plan[e].append((wl, o["fn"], mysig[0], o["dma"]))

        def run_engine(ename, eng):
            for wl, fn, key, is_dma in plan[ename]:
                for dkey, dval in wl[:-1]:
                    eng.wait_ge(self.semof(dkey), dval)
                ins = fn(eng)
                if wl:
                    ins._wait_ge(self.semof(wl[-1][0]), wl[-1][1])
                ins.then_inc(self.semof(key), 16 if is_dma else 1)
            if final and ename == "sp":
                for key, val in self.final_dma.items():
                    eng.wait_ge(self.semof(key), val)
                for e2 in self.esem:
                    if self.ecount[e2] > 0:
                        eng.wait_ge(self.esem[e2], self.ecount[e2])

        with nc.Block() as block:
            if plan["pe"]:
                @block.tensor
                def _(eng):
                    run_engine("pe", eng)
            if plan["act"]:
                @block.scalar
                def _(eng):
                    run_engine("act", eng)
            if plan["dve"]:
                @block.vector
                def _(eng):
                    run_engine("dve", eng)
            if plan["pool"]:
                @block.gpsimd
                def _(eng):
                    run_engine("pool", eng)
            if plan["sp"] or final:
                @block.sync
                def _(eng):
                    run_engine("sp", eng)
        return n


D = 1024
KC = 8
T_PROMPT = 4096
T_SAMPLE = 16
NSEQ_S = 2
P_FOX = 2048
P_CHUNK = 512
A_COLS = 1792
AB_COLS = 3336
CD_COLS = 3584
DFF = 2816
RMS_EPS = 1e-6
GN_EPS = 64e-5
DECAY_C = float(np.exp(-0.5))


class Ctx:
    def __init__(self, nc, P, st):
        self.nc = nc
        self.P = P
        self.st = st
        self.uid = 0
        self.dram = {}

    def name(self, base):
        self.uid += 1
        return "%s_%d" % (base, self.uid)

    def sb(self, st, base, shape, dt=F32):
        return st.enter_context(self.nc.sbuf_tensor(self.name(base), list(shape), dt))

    def ps(self, st, base, shape, dt=F32):
        return st.enter_context(self.nc.psum_tensor(self.name(base), list(shape), dt))

    def dr(self, base, shape, dt=F32, kind=None):
        if kind is None:
            t = self.nc.dram_tensor(base, list(shape), dt)
        else:
            t = self.nc.dram_tensor(base, list(shape), dt, kind=kind)
        ap = t.ap()
        self.dram[base] = ap
        return ap


class Rot:
    def __init__(self, C, st, base, shape, dt, n, psum=False):
        self.tiles = [(C.ps if psum else C.sb)(st, base, shape, dt) for _ in range(n)]
        self.keys = [(base, C.uid, i) for i in range(n)]
        self.i = -1

    def next(self):
        self.i = (self.i + 1) % len(self.tiles)
        return self.tiles[self.i], self.keys[self.i]


def build_consts(C):
    P, st = C.P, C.st
    k = {}
    identF = C.sb(st, "identF", [128, 128], F32)
    P.memset("pool", identF[:], 1.0, writes=["identF"])
    P.op("pool", lambda e: e.affine_select(identF[:], identF[:], [[-1, 128]], ALU.is_equal, 0.0, base=0, channel_multiplier=1),
         reads=["identF"], writes=["identF"])
    identB = C.sb(st, "identB", [128, 128], BF16)
    P.copy("pool", identB[:], identF[:], reads=["identF"], writes=["identB"])
    flipF = C.sb(st, "flipF", [128, 128], F32)
    P.memset("pool", flipF[:], 1.0, writes=["flipF"])
    P.op("pool", lambda e: e.affine_select(flipF[:], flipF[:], [[1, 128]], ALU.is_equal, 0.0, base=-127, channel_multiplier=1),
         reads=["flipF"], writes=["flipF"])
    onesB = C.sb(st, "onesB", [128, 128], BF16)
    P.memset("pool", onesB[:], 1.0, writes=["onesB"])
    onesF = C.sb(st, "onesF", [128, 128], F32)
    P.memset("pool", onesF[:], 1.0, writes=["onesF"])
    bo = C.sb(st, "blockones", [128, 128], BF16)
    P.memset("pool", bo[:], 0.0, writes=["bo"])
    P.memset("pool", bo[0:64, 0:64], 1.0, writes=["bo"])
    P.memset("pool", bo[64:128, 64:128], 1.0, writes=["bo"])
    triF = C.sb(st, "triF", [128, 128], F32)
    P.memset("pool", triF[:], 1.0, writes=["triF"])
    P.op("pool", lambda e: e.affine_select(triF[:], triF[:], [[1, 128]], ALU.is_ge, 0.0, base=0, channel_multiplier=-1),
         reads=["triF"], writes=["triF"])
    def cmask(nm, pattern, base, cm, op):
        m = C.sb(st, nm, [64, 8, 64], F32)
        P.memset("pool", m[:], 1.0, writes=[nm])
        for h in range(8):
            P.op("pool", lambda e, h=h: e.affine_select(m[:, h, :], m[:, h, :], pattern, op, 0.0, base=base, channel_multiplier=cm),
                 reads=[nm], writes=[nm])
        return m
    k["m_su"] = cmask("m_su", [[1, 64]], 0, -1, ALU.is_gt)
    k["m_ui"] = cmask("m_ui", [[1, 64]], 0, -1, ALU.is_ge)
    k["m_sl"] = cmask("m_sl", [[-1, 64]], 0, 1, ALU.is_gt)
    k["i8"] = cmask("i8", [[-1, 64]], 0, 1, ALU.is_equal)
    cm_f = C.sb(st, "cm_f", [128, 4, 512], F32)
    P.memset("pool", cm_f[:], 1.0, writes=["cm_f"])
    for d in range(4):
        P.op("pool", lambda e, d=d: e.affine_select(cm_f[:, d, :], cm_f[:, d, :], [[1, 512]], ALU.is_ge, 0.0, base=-128 * d, channel_multiplier=-1),
             reads=["cm_f"], writes=["cm_f"])
    cmB = C.sb(st, "cmB", [128, 4, 512], BF16)
    P.copy("pool", cmB[:], cm_f[:], reads=["cm_f"], writes=["cmB"])
    k.update(identF=identF, identB=identB, flipF=flipF, onesB=onesB, onesF=onesF, bo=bo, triF=triF, cmB=cmB)
    C.k = k
    P.flush()


import contextlib


def cast_weight(C, W, K, N, name):
    P = C.P
    NB = (N + 127) // 128
    kc_n = K // 128
    Wb = C.dr(name, [NB, 128, kc_n, 128], BF16)
    nfull = N // 128
    rem = N - nfull * 128
    with contextlib.ExitStack() as st:
        ld = Rot(C, st, "cw_ld", [128, NB * 128], F32, 2)
        cb = Rot(C, st, "cw_cb", [128, NB * 128], BF16, 2)
        for kc in range(kc_n):
            t, tk = ld.next()
            b, bk = cb.next()
            P.dma(t[:, 0:N], W[kc * 128:(kc + 1) * 128, :], writes=[tk])
            eng = ("dve", "act")[kc % 2]
            P.copy(eng, b[:, 0:N], t[:, 0:N], reads=[tk], writes=[bk])
            if nfull:
                dst = Wb[0:nfull, :, kc, :].rearrange("nb p n -> p nb n")
                src = b[:, 0:nfull * 128].rearrange("p (nb n) -> p nb n", n=128)
                P.dma(dst, src, reads=[bk], writes=[(name, kc)], q="pool")
            if rem:
                P.dma(Wb[nfull, :, kc, 0:rem], b[:, nfull * 128:N], reads=[bk], writes=[(name, kc, "r")], q="pool")
        P.flush()
    return Wb, [(name, kc) for kc in range(kc_n)] + ([(name, kc, "r") for kc in range(kc_n)] if rem else [])


def adaln_phase(C, I, ncol, c_cols):
    P, st, k = C.P, C.st, C.k
    mods = {}
    for l in range(2):
        for j in range(2):
            mods[(l, j)] = C.sb(st, "mod", [128, 24, ncol], F32)
    with contextlib.ExitStack() as ts:
        cT = C.sb(ts, "cT", [128, KC, ncol], F32)
        for j, cap in enumerate(c_cols):
            P.dma(cT[:, :, j], cap.rearrange("(c p) -> p c", p=128), writes=["cT"], allow_slow_non_contiguous=True)
        csT = C.sb(ts, "csT", [128, KC, ncol], F32)
        P.act(csT[:], cT[:], AF.Silu, reads=["cT"], writes=["csT"])
        wl = Rot(C, ts, "ada_w", [128, 3072], F32, 3)
        pacc = [C.ps(ts, "ada_acc", [128, 512], F32) for _ in range(6)]
        ptr = Rot(C, ts, "ada_ptr", [128, 24, 4], F32, 1, psum=True)
        mrow = Rot(C, ts, "ada_mrow", [4, 3072], F32, 2)
        brow = Rot(C, ts, "ada_brow", [4, 3072], F32, 2)
        for l in range(2):
            for j in range(2):
                m = mods[(l, j)]
                bt, btk = brow.next()
                for cc in range(ncol):
                    P.dma(bt[cc:cc + 1, :], I["ada_b"][l, j:j + 1, :], writes=[btk])
                for kc in range(KC):
                    w, wk = wl.next()
                    P.dma(w[:], I["ada_w"][l, j, kc * 128:(kc + 1) * 128, :], writes=[wk])
                    for c6 in range(6):
                        P.mm(pacc[c6][0:ncol, :], csT[:, kc, 0:ncol], w[:, c6 * 512:(c6 + 1) * 512], start=(kc == 0), stop=(kc == KC - 1),
                             reads=[wk, "csT"], writes=[("ada_acc", c6)])
                mr, mrk = mrow.next()
                for c6 in range(6):
                    P.tt("dve", mr[0:ncol, c6 * 512:(c6 + 1) * 512], pacc[c6][0:ncol, :], bt[0:ncol, c6 * 512:(c6 + 1) * 512], ALU.add,
                         reads=[("ada_acc", c6), btk], writes=[mrk])
                pt, ptk = ptr.next()
                for nb in range(24):
                    P.tr(pt[:, nb, 0:ncol], mr[0:ncol, nb * 128:(nb + 1) * 128], k["identF"][0:ncol, 0:ncol], reads=[mrk, "identF"], writes=[ptk])
                P.copy("act", m[:, :, :], pt[:, :, 0:ncol], reads=[ptk], writes=[("mod", l, j)])
        P.flush()
    C.mods = mods


def load_vec_fm(C, st, ap, n, name, q="sp"):
    t = C.sb(st, name, [128, n], F32)
    key = C.name(name)
    C.P.dma(t[:], ap.rearrange("(c p) -> p c", p=128), writes=[key], allow_slow_non_contiguous=True, q=q)
    return t, key


class Job:
    pass


def seq_groups(job, G):
    out = []
    for s in range(job.nseq):
        T = job.Ts[s]
        for t0 in range(0, T, G):
            gs = min(G, T - t0)
            out.append((s, t0, gs, job.bases[s] + t0))
    return out


def x_to_fm(C, job):
    P, k = C.P, C.k
    with contextlib.ExitStack() as st:
        xin = Rot(C, st, "xin", [128, 1024], F32, 2)
        pst = Rot(C, st, "x_ps", [128, 4, 128], F32, 4, psum=True)
        xo = Rot(C, st, "xo", [128, KC, 128], F32, 2)
        for (s, t0, gs, col0) in seq_groups(job, 128):
            t, tk = xin.next()
            P.dma(t[0:gs, :], job.x_in[s][t0:t0 + gs, :], writes=[tk])
            o, ok = xo.next()
            for half in range(2):
                ps, pk = pst.next()
                for c4 in range(4):
                    c = half * 4 + c4
                    P.tr(ps[:, c4, 0:gs], t[0:gs, c * 128:(c + 1) * 128], k["identF"][0:gs, 0:gs], reads=[tk, "identF"], writes=[pk])
                P.copy("act" if half else "dve", o[:, half * 4:half * 4 + 4, 0:gs], ps[:, :, 0:gs], reads=[pk], writes=[ok])
            P.dma(job.xT[:, col0:col0 + gs].rearrange("(c p) t -> p c t", p=128), o[:, :, 0:gs], reads=[ok], writes=[("xT", job.name, c, col0 // 512) for c in range(8)], q="pool")
        P.flush()


def xkeys(job, col0, gs):
    return [("xT", job.name, i) for i in range(col0 // 128, (col0 + gs - 1) // 128 + 1)]


def norm_to_hT(C, st, job, gT, l, j, hT, hkey):
    P, k = C.P, C.k
    m = C.mods[(l, j)]
    with contextlib.ExitStack() as ts:
        gsc = C.sb(ts, "gsc", [128, KC, 3], F32)
        for col in range(3):
            P.stt(gsc[:, :, col], m[:, 8:16, col], 1.0, gT[:], ALU.add, ALU.mult, reads=[("mod", l, j), "smallprm"], writes=["gsc"])
        xg = Rot(C, ts, "xg", [128, KC, 512], F32, 2)
        sq = Rot(C, ts, "sq", [128, KC, 512], BF16, 2)
        pss = Rot(C, ts, "ss_ps", [128, 512], F32, 2, psum=True)
        rs = Rot(C, ts, "rstd", [128, 512], F32, 2)
        tmp = Rot(C, ts, "htmp", [128, 512], F32, 3)
        for (s, t0, gs, col0) in seq_groups(job, 512):
            x, xk = xg.next()
            P.dma(x[:, :, 0:gs], job.xT[:, col0:col0 + gs].rearrange("(c p) t -> p c t", p=128),
                  reads=[key for c in range(8) for key in xk1(job, c, col0, gs)], writes=[xk])
            q, qk = sq.next()
            P.act(q[:, :, 0:gs], x[:, :, 0:gs], AF.Square, reads=[xk], writes=[qk])
            ps, pk = pss.next()
            for c in range(KC):
                P.mm(ps[:, 0:gs], k["onesB"][:], q[:, c, 0:gs], start=(c == 0), stop=(c == KC - 1), reads=[qk, "onesB"], writes=[pk])
            r, rk = rs.next()
            P.act(r[:, 0:gs], ps[:, 0:gs], AF.Ln, bias=k["eps_rms"][:, 0:1], scale=1.0 / D, reads=[pk, "epsv"], writes=[rk])
            P.act(r[:, 0:gs], r[:, 0:gs], AF.Exp, scale=-0.5, reads=[rk], writes=[rk])
            col = job.modcol[s]
            for c in range(KC):
                t_, tk_ = tmp.next()
                P.stt(t_[:, 0:gs], x[:, c, 0:gs], gsc[:, c, col:col + 1], r[:, 0:gs], ALU.mult, ALU.mult, reads=[xk, rk, "gsc"], writes=[tk_])
                P.act(hT[:, c, col0:col0 + gs], t_[:, 0:gs], AF.Identity, bias=m[:, c, col:col + 1], reads=[tk_, ("mod", l, j)], writes=[hkey])
        P.flush()


def proj(C, job, srcT, skey, Wb, wkeys, nbs, kcn, post, G=512):
    P = C.P
    with contextlib.ExitStack() as ts:
        wr = Rot(C, ts, "pw", [128, kcn, 128], BF16, 3)
        pr = Rot(C, ts, "pp", [128, 512], F32, 3, psum=True)
        groups = seq_groups(job, G)
        for nb in nbs:
            w, wk = wr.next()
            P.dma(w[:], Wb[nb], reads=wkeys, writes=[wk])
            for (s, t0, gs, col0) in groups:
                ps, pk = pr.next()
                for kc in range(kcn):
                    P.mm(ps[:, 0:gs], w[:, kc, :], srcT[:, kc, col0:col0 + gs], start=(kc == 0), stop=(kc == kcn - 1),
                         reads=[wk, skey], writes=[pk])
                post(nb, s, t0, gs, col0, ps, pk)


def chunk_scan(C, ts, nseg_cols, L, delta, rT, kT, vT, aT, bT, epos, S32, Spad, yT, keys, pools):
    P, k = C.P, C.k
    P.pe_serial = True
    psT, psH = pools["psT"], pools["psH"]
    nlev = int(np.log2(L))
    kin = [keys[n] for n in ("rT", "kT", "vT")] + ([keys["aT"], keys["bT"]] if delta else [])
    for c in range(nseg_cols // L):
        cs = slice(c * L, (c + 1) * L)
        pads = {}
        for nm, src in (("K", kT), ("V", vT)) + ((("B", bT),) if delta else ()):
            ps, pk = psT.next()
            for blk in range(4):
                P.tr(ps[0:L, blk, :], src[:, blk, cs], k["identB"][:, :], reads=kin + ["identB"], writes=[pk])
            pad, padk = pools["pad" + nm].next()
            for e in range(2):
                P.copy("act" if e else "dve", pad[0:L, :, e, e * 64:(e + 1) * 64], ps[0:L, :, e * 64:(e + 1) * 64], reads=[pk], writes=[padk])
            pads[nm] = (pad, padk)
        Kp, Kk = pads["K"]
        Vp, Vk = pads["V"]

        def hsl(h):
            return h // 2, h % 2, (h % 2) * 64

        def mm8(lhs_of, rhs_of, reads):
            ps, pk = psH.next()
            for h in range(8):
                P.mm(ps[0:L, h, 0:L], lhs_of(h), rhs_of(h), reads=reads, writes=[pk])
            return ps, pk

        def fm(t, h):
            pr, e, pb = hsl(h)
            return t[pb:pb + 64, pr, cs]

        if delta:
            psA, pkA = mm8(lambda h: fm(bT, h), lambda h: fm(aT, h), kin)
            psB, pkB = mm8(lambda h: fm(aT, h), lambda h: fm(bT, h), kin)
            PT, PTk = pools["mb"].next()
            Pm, Pmk = pools["mb"].next()
            TTf, TTfk = pools["ttf"].next()
            TTb, TTbk = pools["mb"].next()
            P.tt("dve", TTf[0:L, :, 0:L], psA[0:L, :, 0:L], k["m_su"][0:L, :, 0:L], ALU.mult, reads=[pkA, "m_su"], writes=[TTfk])
            P.copy("act", PT[0:L, :, 0:L], TTf[0:L, :, 0:L], reads=[TTfk], writes=[PTk])
            P.tt("dve", Pm[0:L, :, 0:L], psB[0:L, :, 0:L], k["m_sl"][0:L, :, 0:L], ALU.mult, reads=[pkB, "m_sl"], writes=[Pmk])
            P.tt("pool", TTf[0:L, :, 0:L], TTf[0:L, :, 0:L], k["i8"][0:L, :, 0:L], ALU.add, reads=[TTfk, "i8"], writes=[TTfk])
            P.copy("act", TTb[0:L, :, 0:L], TTf[0:L, :, 0:L], reads=[TTfk], writes=[TTbk])
            for j in range(1, nlev):
                psA, pkA = mm8(lambda h: PT[0:L, h, 0:L], lambda h: Pm[0:L, h, 0:L], [PTk, Pmk])
                last = (j == nlev - 1)
                if not last:
                    psB, pkB = mm8(lambda h: Pm[0:L, h, 0:L], lambda h: PT[0:L, h, 0:L], [PTk, Pmk])
                Pm2, Pm2k = pools["mb"].next()
                P.copy("act", Pm2[0:L, :, 0:L], psA[0:L, :, 0:L], reads=[pkA], writes=[Pm2k])
                if not last:
                    PT2, PT2k = pools["mb"].next()
                    P.copy("dve", PT2[0:L, :, 0:L], psB[0:L, :, 0:L], reads=[pkB], writes=[PT2k])
                    PT, PTk = PT2, PT2k
                Pm, Pmk = Pm2, Pm2k
                psC, pkC = mm8(lambda h: Pm[0:L, h, 0:L], lambda h: TTb[0:L, h, 0:L], [Pmk, TTbk])
                P.tt("dve", TTf[0:L, :, 0:L], TTf[0:L, :, 0:L], psC[0:L, :, 0:L], ALU.add, reads=[pkC, TTfk], writes=[TTfk])
                TTb, TTbk = pools["mb"].next()
                P.copy("act", TTb[0:L, :, 0:L], TTf[0:L, :, 0:L], reads=[TTfk], writes=[TTbk])
            psA, pkA = mm8(lambda h: fm(kT, h), lambda h: fm(aT, h), kin)
            Mak, Makk = pools["mb"].next()
            P.tt("dve", Mak[0:L, :, 0:L], psA[0:L, :, 0:L], k["m_su"][0:L, :, 0:L], ALU.mult, reads=[pkA, "m_su"], writes=[Makk])
            psA, pkA = mm8(lambda h: fm(bT, h), lambda h: fm(rT, h), kin)
            Mrb, Mrbk = pools["mb"].next()
            P.tt("dve", Mrb[0:L, :, 0:L], psA[0:L, :, 0:L], k["m_ui"][0:L, :, 0:L], ALU.mult, reads=[pkA, "m_ui"], writes=[Mrbk])
        psA, pkA = mm8(lambda h: fm(kT, h), lambda h: fm(rT, h), kin)
        Mrk, Mrkk = pools["mb"].next()
        P.tt("dve", Mrk[0:L, :, 0:L], psA[0:L, :, 0:L], k["m_ui"][0:L, :, 0:L], ALU.mult, reads=[pkA, "m_ui"], writes=[Mrkk])
        if delta:
            Bp, Bk = pads["B"]
            ps, pk = psH.next()
            for h in range(8):
                pr, e, pb = hsl(h)
                P.mm(ps[0:L, h, :], fm(aT, h), Spad[pb:pb + 64, pr, pb:pb + 64], start=True, stop=False, reads=kin + ["Spad"], writes=[pk])
                P.mm(ps[0:L, h, :], Mak[0:L, h, 0:L], Vp[0:L, pr, e, pb:pb + 64], start=False, stop=True, reads=[Makk, Vk], writes=[pk])
            W1, W1k = pools["mb"].next()
            P.copy("act", W1[0:L, :, :], ps[0:L, :, :], reads=[pk], writes=[W1k])
            ps, pk = psH.next()
            for h in range(8):
                P.mm(ps[0:L, h, :], TTb[0:L, h, 0:L], W1[0:L, h, :], reads=[TTbk, W1k], writes=[pk])
            Up, Uk = pools["padU"].next()
            for e in range(2):
                P.copy("act" if e else "dve", Up[0:L, :, e, e * 64:(e + 1) * 64],
                       ps[0:L, :, :].rearrange("p (a b) v -> p a b v", b=2)[:, :, e, :], reads=[pk], writes=[Uk])
        ps, pk = psH.next()
        for pr in range(4):
            n_mm = (3 if delta else 2) * 2
            i_mm = 0
            for e in range(2):
                pb = e * 64
                h = pr * 2 + e
                lst = [(Spad[pb:pb + 64, pr, :], rT[pb:pb + 64, pr, cs], kin + ["Spad"])]
                if delta:
                    lst.append((Up[0:L, pr, e, :], Mrb[0:L, h, 0:L], [Uk, Mrbk]))
                lst.append((Vp[0:L, pr, e, :], Mrk[0:L, h, 0:L], [Vk, Mrkk]))
                for (lt, rh, rd) in lst:
                    P.mm(ps[:, pr, 0:L], lt, rh, start=(i_mm == 0), stop=(i_mm == n_mm - 1), reads=rd, writes=[pk])
                    i_mm += 1
        P.copy("act", yT[:, :, cs], ps[:, 0:4, 0:L], reads=[pk], writes=[keys["yT"]])
        ps, pk = psH.next()
        for pr in range(4):
            n_mm = (2 if delta else 1) * 2
            i_mm = 0
            for e in range(2):
                pb = e * 64
                lst = []
                if delta:
                    lst.append((Bp[0:L, pr, e, :], Up[0:L, pr, e, pb:pb + 64], [Bk, Uk]))
                lst.append((Kp[0:L, pr, e, :], Vp[0:L, pr, e, pb:pb + 64], [Kk, Vk]))
                for (lt, rh, rd) in lst:
                    P.mm(ps[:, pr, :], lt, rh, start=(i_mm == 0), stop=(i_mm == n_mm - 1), reads=rd, writes=[pk])
                    i_mm += 1
        P.tt("dve", S32[:, :, :], S32[:, :, :], ps[:, 0:4, :], ALU.add, reads=[pk, "S32"], writes=["S32"])
        for pr in range(4):
            P.ts("dve", S32[:, pr, :], S32[:, pr, :], epos[:, pr, (c + 1) * L - 1:(c + 1) * L], None, ALU.mult, reads=["S32", keys["epos"]], writes=["S32"])
        for e in range(2):
            pb = e * 64
            P.copy("act" if e else "pool", Spad[pb:pb + 64, :, pb:pb + 64], S32[pb:pb + 64, :, :], reads=["S32"], writes=["Spad"])
    P.pe_serial = False


def scan_pools(C, ts, delta):
    pools = {}
    pools["psT"] = Rot(C, ts, "cs_psT", [64, 4, 128], BF16, 2, psum=True)
    pools["psH"] = Rot(C, ts, "cs_psH", [128, 8, 64], F32, 4, psum=True)
    names = ["K", "V"] + (["B", "U"] if delta else [])
    for nm in names:
        r = Rot(C, ts, "pad" + nm, [64, 4, 2, 128], BF16, 2)
        for t_, tk_ in zip(r.tiles, r.keys):
            C.P.memset("pool", t_[:], 0.0, writes=[tk_])
        pools["pad" + nm] = r
    pools["mb"] = Rot(C, ts, "cs_mb", [64, 8, 64], BF16, 10)
    pools["ttf"] = Rot(C, ts, "cs_ttf", [64, 8, 64], F32, 2)
    return pools


def more_consts(C):
    P, st, k = C.P, C.st, C.k
    for L in (64, 32, 16):
        m = C.sb(st, "rmask%d" % L, [128, 512], F32)
        P.memset("pool", m[:], 1.0, writes=["rmask%d" % L])
        P.memset("pool", m[:, :].rearrange("p (c l) -> p c l", l=L)[:, :, 0:1], 0.0, writes=["rmask%d" % L])
        k["rmask%d" % L] = m
    for nm, val in (("eps_rms", RMS_EPS), ("eps_gn", GN_EPS), ("tiny", 1e-12), ("one", 1.0), ("lnc", -40.0 * float(np.log(2.0)))):
        t = C.sb(st, nm, [128, 1], F32)
        P.memset("pool", t[:], val, writes=["epsv"])
        k[nm] = t
    P.flush()


def block_norm_stats(C, P, ps_pool, tmpb, src, srckey, SEG, scale):
    pass


def rwkv_mixer(C, job, s, I, prm):
    P, k = C.P, C.k
    T = job.T
    L = min(64, T)
    SEG = min(256, T)
    base = s * T
    with contextlib.ExitStack() as ts:
        pools = scan_pools(C, ts, True)
        psP = Rot(C, ts, "rw_psP", [128, 512], F32, 2, psum=True)
        S32 = C.sb(ts, "S32", [128, 4, 64], F32)
        Spad = C.sb(ts, "Spad", [128, 4, 128], BF16)
        P.memset("pool", Spad[:], 0.0, writes=["Spad"])
        if job.rwkv_s0 is None:
            P.memset("pool", S32[:], 0.0, writes=["S32"])
        else:
            s0t = C.sb(ts, "s0t", [64, 8, 64], F32)
            P.dma(s0t[:], job.rwkv_s0[s].rearrange("h v k -> v h k"), writes=["s0t"])
            for pr in range(4):
                ps, pk = psP.next()
                P.tr(ps[:, 0:64], s0t[0:64, 2 * pr:2 * pr + 2, :].rearrange("v e k -> v (e k)"), k["identF"][0:64, 0:64], reads=["s0t", "identF"], writes=[pk])
                P.copy("dve", S32[:, pr, :], ps[:, 0:64], reads=[pk], writes=["S32"])
            for e in range(2):
                pb = e * 64
                P.copy("act", Spad[pb:pb + 64, :, pb:pb + 64], S32[pb:pb + 64, :, :], reads=["S32"], writes=["Spad"])
        zt = C.sb(ts, "zt", [128, 14, SEG + 1], F32)
        dd = C.sb(ts, "dd", [128, 14, SEG], F32)
        zs = C.sb(ts, "zs", [128, 14, SEG], F32)
        f4 = lambda nm: C.sb(ts, nm, [128, 4, SEG], F32)
        b4 = lambda nm: C.sb(ts, nm, [128, 4, SEG], BF16)
        lw, asig, gT, kkn, kmod, cc, epos, eneg, eprev, tmpa, tmpb_, rkb, yT = [f4(n) for n in
            ("lw", "asig", "gT", "kkn", "kmod", "cc", "epos", "eneg", "eprev", "tmpa", "tmpb", "rkb", "yT")]
        rT, kT, vT, aT, bT, sqb, yb = [b4(n) for n in ("rT", "kT", "vT", "aT", "bT", "sqb", "yb")]
        tw = C.sb(ts, "tw", [128, SEG], BF16)
        sg = C.sb(ts, "sg", [128, SEG], BF16)
        outb = C.sb(ts, "outb", [128, 4, SEG], BF16)
        keys = dict(rT="rT", kT="kT", vT="vT", aT="aT", bT="bT", epos="epos", yT="yT")
        mu, w0, a0, k_k, k_a, lng, lnb, r_k, omka, wa2, g2 = [prm[n] for n in
            ("mu", "w0", "a0", "k_k", "k_a", "lng", "lnb", "r_k", "omka", "wa2", "g2")]
        pk_ = ["rwprm", "rwprm2"]
        for t0 in range(0, T, SEG):
            col0 = base + t0
            zrows = job.zT[0:A_COLS, :].rearrange("(c p) t -> p c t", p=128)
            zk = [("zT", job.name, nb, i) for nb in range(14) for i in range(col0 // 512, (col0 + SEG - 1) // 512 + 1)]
            if t0 == 0:
                P.dma(zt[:, :, 1:SEG + 1], zrows[:, :, col0:col0 + SEG], reads=zk, writes=["zt"])
                if job.rwkv_shift0 is None:
                    P.memset("pool", zt[:, :, 0:1], 0.0, writes=["zt"])
                else:
                    P.dma(zt[:, :, 0], job.rwkv_shift0[s].rearrange("(c p) -> p c", p=128), writes=["zt"], allow_slow_non_contiguous=True)
            else:
                zk2 = zk + [("zT", job.name, nb, (col0 - 1) // 512) for nb in range(14)]
                P.dma(zt[:, :, 0:SEG + 1], zrows[:, :, col0 - 1:col0 + SEG], reads=zk2, writes=["zt"])
            if t0 + SEG == T:
                P.dma(job.rwkv_shift_out[s].rearrange("(c p) -> p c", p=128), zt[:, :, SEG], reads=["zt"], writes=[("rwsh", job.name, s)],
                      q="pool", allow_slow_non_contiguous=True)
            P.tt("pool", dd[:], zt[:, :, 0:SEG], zt[:, :, 1:SEG + 1], ALU.subtract, reads=["zt"], writes=["dd"])
            for blk in range(14):
                P.stt(zs[:, blk, :], dd[:, blk, :], mu[:, blk:blk + 1], zt[:, blk, 1:SEG + 1], ALU.mult, ALU.add, reads=["dd", "zt"] + pk_, writes=["zs"])
            P.act(tw[0:64, :], zs[0:64, 12, :], AF.Tanh, reads=["zs"], writes=["tw"])
            P.copy("dve", tw[64:128, :], zs[64:128, 12, :], reads=["zs"], writes=["tw"])
            P.act(sg[:], zs[:, 13, :], AF.Sigmoid, reads=["zs"], writes=["sg"])
            for blk in range(4):
                bs = slice(blk * 128, (blk + 1) * 128)
                ps, pk = psP.next()
                P.mm(ps[:, 0:SEG], wa2[0:64, bs], tw[0:64, :], reads=["tw"] + pk_, writes=[pk])
                P.act(lw[:, blk, :], ps[:, 0:SEG], AF.Sigmoid, bias=w0[:, blk:blk + 1], reads=[pk] + pk_, writes=["lw"])
                ps, pk = psP.next()
                P.mm(ps[:, 0:SEG], wa2[64:128, bs], tw[64:128, :], reads=["tw"] + pk_, writes=[pk])
                P.act(asig[:, blk, :], ps[:, 0:SEG], AF.Sigmoid, bias=a0[:, blk:blk + 1], reads=[pk] + pk_, writes=["asig"])
                ps, pk = psP.next()
                P.mm(ps[:, 0:SEG], g2[:, bs], sg[:], reads=["sg"] + pk_, writes=[pk])
                P.copy("dve", gT[:, blk, :], ps[:, 0:SEG], reads=[pk], writes=["gT"])
            P.ts("dve", lw[:], lw[:], -DECAY_C, None, ALU.mult, reads=["lw"], writes=["lw"])
            for blk in range(4):
                P.ts("dve", tmpa[:, blk, :], zs[:, 4 + blk, :], k_k[:, blk:blk + 1], None, ALU.mult, reads=["zs"] + pk_, writes=["tmpa"])
            P.act(sqb[:], tmpa[:], AF.Square, reads=["tmpa"], writes=["sqb"])
            for blk in range(4):
                ps, pk = psP.next()
                P.mm(ps[:, 0:SEG], k["bo"][:], sqb[:, blk, :], reads=["sqb", "bo"], writes=[pk])
                P.act(tmpb_[:, blk, :], ps[:, 0:SEG], AF.Sqrt, reads=[pk], writes=["tmpb"])
            P.ts("dve", tmpb_[:], tmpb_[:], 1e-12, None, ALU.max, reads=["tmpb"], writes=["tmpb"])
            P.op("dve", lambda e: e.reciprocal(tmpb_[:], tmpb_[:]), reads=["tmpb"], writes=["tmpb"])
            P.tt("dve", kkn[:], tmpa[:], tmpb_[:], ALU.mult, reads=["tmpa", "tmpb"], writes=["kkn"])
            for blk in range(4):
                P.ts("dve", tmpa[:, blk, :], asig[:, blk, :], k_a[:, blk:blk + 1], omka[:, blk:blk + 1], ALU.mult, ALU.add,
                     reads=["asig"] + pk_, writes=["tmpa"])
            P.tt("dve", kmod[:], tmpa[:], zs[:, 4:8, :], ALU.mult, reads=["tmpa", "zs"], writes=["kmod"])
            rm = k["rmask%d" % L]
            for blk in range(4):
                P.op("dve", lambda e, blk=blk: e.tensor_tensor_scan(cc[:, blk, :], rm[:, 0:SEG], lw[:, blk, :], 0.0, ALU.mult, ALU.add),
                     reads=["lw", "rmask%d" % L], writes=["cc"])
            P.act(epos[:], cc[:], AF.Exp, reads=["cc"], writes=["epos"])
            P.act(eneg[:], cc[:], AF.Exp, scale=-1.0, reads=["cc"], writes=["eneg"])
            P.tt("pool", tmpa[:], cc[:], lw[:], ALU.subtract, reads=["cc", "lw", "kmod"], writes=["tmpa"])
            P.act(eprev[:], tmpa[:], AF.Exp, reads=["tmpa"], writes=["eprev"])
            P.tt("dve", rT[:], zs[:, 0:4, :], epos[:], ALU.mult, reads=["zs", "epos"], writes=["rT"])
            P.stt(aT[:], kkn[:], -1.0, eprev[:], ALU.mult, ALU.mult, reads=["kkn", "eprev"], writes=["aT"])
            P.tt("pool", tmpb_[:], kkn[:], asig[:], ALU.mult, reads=["kkn", "asig"], writes=["tmpb"])
            P.tt("dve", bT[:], tmpb_[:], eneg[:], ALU.mult, reads=["tmpb", "eneg"], writes=["bT"])
            P.tt("dve", kT[:], kmod[:], eneg[:], ALU.mult, reads=["kmod", "eneg"], writes=["kT"])
            P.copy("act", vT[:], zs[:, 8:12, :], reads=["zs"], writes=["vT"])
            for blk in range(4):
                P.stt(sqb[:, blk, :], zs[:, blk, :], r_k[:, blk:blk + 1], kmod[:, blk, :], ALU.mult, ALU.mult, reads=["zs", "kmod"] + pk_, writes=["sqb"])
            for blk in range(4):
                ps, pk = psP.next()
                P.mm(ps[:, 0:SEG], k["bo"][:], sqb[:, blk, :], reads=["sqb", "bo"], writes=[pk])
                P.copy("act", rkb[:, blk, :], ps[:, 0:SEG], reads=[pk], writes=["rkb"])
            chunk_scan(C, ts, SEG, L, True, rT, kT, vT, aT, bT, epos, S32, Spad, yT, keys, pools)
            P.copy("act", yb[:], yT[:], reads=["yT"], writes=["yb"])
            for blk in range(4):
                ps, pk = psP.next()
                P.mm(ps[:, 0:SEG], k["bo"][:], yb[:, blk, :], reads=["yb", "bo"], writes=[pk])
                P.stt(tmpa[:, blk, :], ps[:, 0:SEG], -1.0 / 64, yT[:, blk, :], ALU.mult, ALU.add, reads=[pk, "yT"], writes=["tmpa"])
            P.act(sqb[:], tmpa[:], AF.Square, reads=["tmpa"], writes=["sqb"])
            for blk in range(4):
                ps, pk = psP.next()
                P.mm(ps[:, 0:SEG], k["bo"][:], sqb[:, blk, :], reads=["sqb", "bo"], writes=[pk])
                P.act(tmpb_[:, blk, :], ps[:, 0:SEG], AF.Sqrt, bias=k["eps_gn"][:, 0:1], scale=1.0 / 64, reads=[pk, "epsv"], writes=["tmpb"])
            P.op("dve", lambda e: e.reciprocal(tmpb_[:], tmpb_[:]), reads=["tmpb"], writes=["tmpb"])
            P.tt("dve", tmpa[:], tmpa[:], tmpb_[:], ALU.mult, reads=["tmpa", "tmpb"], writes=["tmpa"])
            for blk in range(4):
                P.ts("dve", tmpa[:, blk, :], tmpa[:, blk, :], lng[:, blk:blk + 1], lnb[:, blk:blk + 1], ALU.mult, ALU.add, reads=["tmpa"] + pk_, writes=["tmpa"])
            P.tt("pool", tmpb_[:], rkb[:], zs[:, 8:12, :], ALU.mult, reads=["rkb", "zs", "tmpb"], writes=["tmpb"])
            P.tt("dve", tmpa[:], tmpa[:], tmpb_[:], ALU.add, reads=["tmpa", "tmpb"], writes=["tmpa"])
            P.tt("dve", outb[:], tmpa[:], gT[:], ALU.mult, reads=["tmpa", "gT"], writes=["outb"])
            P.dma(job.mixT[0:512, col0:col0 + SEG].rearrange("(c p) t -> p c t", p=128), outb[:], reads=["outb"],
                  writes=[("mixT", job.name, c, col0 // 512, hh) for c in range(4) for hh in range(2)], q="pool")
        so = C.sb(ts, "so", [64, 4, 128], F32)
        for pr in range(4):
            ps, pk = psP.next()
            P.tr(ps[0:64, 0:128], S32[:, pr, :], k["identF"][:, :], reads=["S32", "identF"], writes=[pk])
            P.copy("dve", so[:, pr, :], ps[0:64, 0:128], reads=[pk], writes=["so"])
        P.dma(job.rwkv_out[s].rearrange("(a e) v k -> v a e k", e=2), so[:, :, :].rearrange("v a (e k) -> v a e k", e=2), reads=["so"],
              writes=[("rwst", job.name, s)], q="pool")
        P.flush()


def load_rwkv_params(C, I):
    P, st = C.P, C.st
    prm = {}
    def vec(nm, ap, n):
        t = C.sb(st, "p_" + nm, [128, n], F32)
        P.dma(t[:], ap.rearrange("(c p) -> p c", p=128), writes=["rwprm"], allow_slow_non_contiguous=True)
        prm[nm] = t
    vec("mu", I["rwkv_mu"][0], 14)
    vec("w0", I["rwkv_w0"][0], 4)
    vec("a0", I["rwkv_a0"][0], 4)
    vec("k_k", I["rwkv_k_k"][0], 4)
    vec("k_a", I["rwkv_k_a"][0], 4)
    vec("lng", I["rwkv_lnx_g"][0], 4)
    vec("lnb", I["rwkv_lnx_b"][0], 4)
    vec("r_k", I["rwkv_r_k"][0].rearrange("h k -> (h k)"), 4)
    for nm, src in (("lng8", "rwkv_lnx_g"), ("lnb8", "rwkv_lnx_b")):
        t = C.sb(st, "p_" + nm, [64, 8], F32)
        P.dma(t[:], I[src][0].rearrange("(h k) -> k h", k=64), writes=["rwprm"], allow_slow_non_contiguous=True)
        prm[nm] = t
    omka = C.sb(st, "p_omka", [128, 4], F32)
    P.ts("dve", omka[:], prm["k_a"][:], -1.0, 1.0, ALU.mult, ALU.add, reads=["rwprm"], writes=["rwprm2"])
    prm["omka"] = omka
    wa2 = C.sb(st, "p_wa2", [128, 512], BF16)
    g2 = C.sb(st, "p_g2", [128, 512], BF16)
    with contextlib.ExitStack() as ts:
        wa2f = C.sb(ts, "wa2f", [128, 512], F32)
        g2f = C.sb(ts, "g2f", [128, 512], F32)
        P.dma(wa2f[0:64, :], I["rwkv_w2"][0], writes=["wa2f"])
        P.dma(wa2f[64:128, :], I["rwkv_a2"][0], writes=["wa2f"])
        P.dma(g2f[:], I["rwkv_g2"][0], writes=["g2f"])
        P.copy("dve", wa2[:], wa2f[:], reads=["wa2f"], writes=["rwprm2"])
        P.copy("dve", g2[:], g2f[:], reads=["g2f"], writes=["rwprm2"])
        prm["wa2"], prm["g2"] = wa2, g2
        P.flush()
    return prm


def attn_mixer(C, job, s, I, kind):
    P, k = C.P, C.k
    T = job.Ts[s]
    base = job.bases[s]
    fox = (kind == "fox")
    zoff = A_COLS if fox else 0
    Pc = (job.P_fox[s] if fox else job.P_chunk[s])
    NPt = Pc // 128
    NTt = (T + 127) // 128
    NKT = NPt + NTt
    QG = min(512, T)
    NG = T // QG
    mix_off = 512 if fox else 0
    ck = (job.fox_ck if fox else job.chunk_ck)
    cv = (job.fox_cv if fox else job.chunk_cv)
    kout = (job.fox_kout if fox else job.chunk_kout)
    vout = (job.fox_vout if fox else job.chunk_vout)
    with contextlib.ExitStack() as ts:
        QT = C.sb(ts, "QT", [128, 4, T], BF16)
        KT = C.sb(ts, "KT", [128, 4, NKT * 128], BF16)
        Vt = C.sb(ts, "Vt", [128, NKT, 8, 65], BF16)
        P.memset("pool", Vt[:, :, :, 64:65], 1.0, writes=["Vt"])
        psS = Rot(C, ts, "at_psS", [128, 512], F32, 4, psum=True)
        psM = Rot(C, ts, "at_psM", [128, 512], F32, 1, psum=True)
        psN = Rot(C, ts, "at_psN", [128, 512], F32, 2, psum=True)
        psD = Rot(C, ts, "at_psD", [64, 512], F32, 1, psum=True)
        zrows = lambda off: job.zT[zoff + off:zoff + off + 512, :].rearrange("(c p) t -> p c t", p=128)
        zrows64 = lambda off, a: job.zT[zoff + off + a * 256:zoff + off + a * 256 + 256, :].rearrange("(h d) t -> d h t", d=64)
        zkey = lambda off, col: [("zT", job.name, (zoff + off) // 128 + c, col // 512) for c in range(4)]
        with contextlib.ExitStack() as t2:
            ldq = Rot(C, t2, "at_ldq", [128, 4, 128], F32, 2)
            ldk2 = Rot(C, t2, "at_ldk2", [128, 4, 128], F32, 2)
            ldk = Rot(C, t2, "at_ldk", [128, 4, 128], F32, 2)
            ldv = Rot(C, t2, "at_ldv", [128, 4, 128], F32, 2)
            stg = Rot(C, t2, "at_stg", [128, 512], F32, 4)
            ldc = Rot(C, t2, "at_ldc", [128, 512], F32, 3)
            for ct in range(NPt):
                kc_, kck = ldc.next()
                for a_ in range(2):
                    P.dma(kc_[:, :].rearrange("p (b a d) -> p b a d", b=4, a=2)[:, :, a_, :],
                          ck[s][ct * 128:(ct + 1) * 128, a_ * 256:(a_ + 1) * 256].rearrange("t (b d) -> t b d", b=4), writes=[kck])
                ps, pk = psS.next()
                for h4 in range(4):
                    P.tr(ps[:, h4 * 128:(h4 + 1) * 128], kc_[:, h4 * 128:(h4 + 1) * 128], k["identF"][:, :], reads=[kck, "identF"], writes=[pk])
                P.copy("act", KT[:, :, ct * 128:(ct + 1) * 128], ps[:, :].rearrange("p (c t) -> p c t", t=128), reads=[pk], writes=["KT"])
                vc_, vck = ldc.next()
                P.dma(vc_[:], cv[s][ct * 128:(ct + 1) * 128, :], writes=[vck])
                P.copy("dve", Vt[:, ct, :, 0:64], vc_[:, :].rearrange("p (h d) -> p h d", d=64), reads=[vck], writes=["Vt"])
            for it in range(NTt):
                rows = min(128, T - it * 128)
                col0 = base + it * 128
                q_, qk_ = ldq.next()
                k_, kk_ = ldk.next()
                v_, vk_ = ldv.next()
                k2_, k2k_ = ldk2.next()
                for a in range(2):
                    P.dma(q_[a * 64:(a + 1) * 64, :, 0:rows], zrows64(0, a)[:, :, col0:col0 + rows], reads=zkey(0, col0), writes=[qk_])
                    P.dma(k2_[a * 64:(a + 1) * 64, :, 0:rows], zrows64(512, a)[:, :, col0:col0 + rows], reads=zkey(512, col0), writes=[k2k_])
                P.dma(k_[:, :, 0:rows], zrows(512)[:, :, col0:col0 + rows], reads=zkey(512, col0), writes=[kk_])
                P.dma(v_[:, :, 0:rows], zrows(1024)[:, :, col0:col0 + rows], reads=zkey(1024, col0), writes=[vk_])
                P.copy("act", QT[:, :, it * 128:it * 128 + rows], q_[:, :, 0:rows], reads=[qk_], writes=["QT"])
                P.copy("pool", KT[:, :, (NPt + it) * 128:(NPt + it) * 128 + rows], k2_[:, :, 0:rows], reads=[k2k_], writes=["KT"])
                for src, sk, outap, isv in ((k_, kk_, kout, False), (v_, vk_, vout, True)):
                    ps, pk = psS.next()
                    for blk in range(4):
                        P.tr(ps[0:rows, blk * 128:(blk + 1) * 128], src[:, blk, 0:rows], k["identF"][:, :], reads=[sk, "identF"], writes=[pk])
                    sg_, sgk = stg.next()
                    P.copy("dve" if isv else "act", sg_[0:rows, :], ps[0:rows, :], reads=[pk], writes=[sgk])
                    if isv:
                        P.copy("pool", Vt[0:rows, NPt + it, :, 0:64], sg_[0:rows, :].rearrange("p (h d) -> p h d", d=64), reads=[sgk], writes=["Vt"])
                    if fox or T <= 512:
                        P.dma(outap[s][it * 128:it * 128 + rows, :], sg_[0:rows, :], reads=[sgk], writes=[("kvout", kind, job.name, s, it, isv)], q="pool")
                    elif it * 128 >= T - 512:
                        r0 = it * 128 - (T - 512)
                        P.dma(outap[s][r0:r0 + rows, :], sg_[0:rows, :], reads=[sgk], writes=[("kvout", kind, job.name, s, it, isv)], q="pool")
            P.flush()
        biasT = None
        if fox:
            biasT = C.sb(ts, "biasT", [128, 8, NG, NKT], F32)
            with contextlib.ExitStack() as t2:
                lfT = C.sb(t2, "lfT", [8, T], F32)
                nbf = C.sb(t2, "nbf", [8, 1], F32)
                P.dma(nbf[:], I["fox_b_f"][0:1, :].rearrange("o h -> h o"), writes=["nbf"], allow_slow_non_contiguous=True)
                P.ts("dve", nbf[:], nbf[:], -1.0, None, ALU.mult, reads=["nbf"], writes=["nbf"])
                gk = [("zT", job.name, (A_COLS + 1536) // 128, i) for i in range(base // 512, (base + T - 1) // 512 + 1)]
                P.dma(lfT[:], job.zT[A_COLS + 1536:A_COLS + 1544, base:base + T], reads=gk, writes=["lfT"])
                P.act(lfT[:], lfT[:], AF.Exp, bias=nbf[:, 0:1], scale=-1.0, reads=["lfT", "nbf"], writes=["lfT"])
                P.act(lfT[:], lfT[:], AF.Ln, bias=k["one"][0:8, 0:1], reads=["lfT", "epsv"], writes=["lfT"])
                P.ts("dve", lfT[:], lfT[:], -1.0, None, ALU.mult, reads=["lfT"], writes=["lfT"])
                lft = C.sb(t2, "lft", [128, NKT, 8], F32)
                P.memset("pool", lft[:], 0.0, writes=["lft"])
                if NPt:
                    P.dma(lft[:, 0:NPt, :], job.fox_clogf[s].rearrange("(n p) h -> p n h", p=128), writes=["lft"])
                ps, pk = psS.next()
                for it in range(NTt):
                    rows = min(128, T - it * 128)
                    P.tr(ps[0:rows, it * 8:(it + 1) * 8], lfT[0:8, it * 128:it * 128 + rows], k["identF"][0:8, 0:8], reads=["lfT", "identF"], writes=[pk])
                rows_l = min(128, T)
                P.copy("dve", lft[0:rows_l, NPt:NKT, :], ps[0:rows_l, 0:NTt * 8].rearrange("p (n h) -> p n h", h=8), reads=[pk], writes=["lft"])
                if T >= 128:
                    P.dma(job.fox_logf_out[s].rearrange("(n p) h -> p n h", p=128), lft[:, NPt:NKT, :], reads=["lft"], writes=[("lfout", job.name, s)], q="pool")
                else:
                    P.dma(job.fox_logf_out[s], lft[0:T, NPt, :], reads=["lft"], writes=[("lfout", job.name, s)], q="pool")
                Wt = C.sb(t2, "Wt", [128, 8, NKT], F32)
                TOTt = C.sb(t2, "TOTt", [128, 8, NKT], F32)
                offs = C.sb(t2, "offs", [128, 8, NKT], F32)
                rmk = C.sb(t2, "rmk", [128, 8, NKT], F32)
                P.memset("pool", rmk[:], 1.0, writes=["rmk"])
                P.memset("pool", rmk[:, :, 0:1], 0.0, writes=["rmk"])
                lft2 = lft[:, :, :].rearrange("p n h -> p (n h)")
                ps, pk = psS.next()
                P.mm(ps[:, 0:NKT * 8], k["triF"][:], lft2, reads=["lft", "triF"], writes=[pk])
                P.copy("dve", Wt[:], ps[:, 0:NKT * 8].rearrange("p (n h) -> p h n", h=8), reads=[pk], writes=["Wt"])
                ps, pk = psS.next()
                P.mm(ps[:, 0:NKT * 8], k["onesF"][:], lft2, reads=["lft", "onesF"], writes=[pk])
                P.copy("dve", TOTt[:], ps[:, 0:NKT * 8].rearrange("p (n h) -> p h n", h=8), reads=[pk], writes=["TOTt"])
                P.op("dve", lambda e: e.tensor_tensor_scan(offs[:, :, :].rearrange("p h n -> p (h n)"), rmk[:, :, :].rearrange("p h n -> p (h n)"),
                                                           TOTt[:, :, :].rearrange("p h n -> p (h n)"), 0.0, ALU.mult, ALU.add),
                     reads=["TOTt", "rmk"], writes=["offs"])
                P.tt("dve", offs[:], offs[:], TOTt[:], ALU.subtract, reads=["offs", "TOTt"], writes=["offs"])
                P.tt("dve", Wt[:], Wt[:], offs[:], ALU.add, reads=["offs", "Wt"], writes=["Wt"])
                for h in range(8):
                    for g in range(NG):
                        nq0 = NPt + g * (QG // 128)
                        P.ts("dve", biasT[:, h, g, :], Wt[:, h, :], offs[:, h, nq0:nq0 + 1], -1.0, ALU.subtract, ALU.mult,
                             reads=["Wt", "offs"], writes=["biasT"])
                P.flush()
        pb_ = Rot(C, ts, "at_pb", [128, 512], BF16, 6)
        rd_ = Rot(C, ts, "at_rd", [128, 512], F32, 3)
        nb_ = Rot(C, ts, "at_nb", [64, 512], F32, 3)
        ob_ = Rot(C, ts, "at_ob", [64, 512], BF16, 2)
        MEs = None
        if not fox:
            MEs = [C.sb(ts, "ME", [128, 8, 512], BF16) for _ in range(2)]
            hk = Rot(C, ts, "at_hk", [128, 512], F32, 2)
            me32 = Rot(C, ts, "at_me32", [128, 512], F32, 2)

        def build_me(h):
            ME = MEs[h % 2]
            for rt in range(8):
                if T <= 16 and rt >= NKT:
                    continue
                hh, hhk = hk.next()
                src = bass.AP(C.dram["relext"].tensor, h * 1536 + 896 - 128 * rt, [[1, 128], [1, QG]])
                P.dma(hh[:, 0:QG], src, reads=["relext"], writes=[hhk])
                ps, pk = psM.next()
                P.mm(ps[:, 0:QG], k["flipF"][:], hh[:, 0:QG], reads=[hhk, "flipF"], writes=[pk])
                m32, m32k = me32.next()
                P.act(m32[:, 0:QG], ps[:, 0:QG], AF.Exp, reads=[pk], writes=[m32k])
                P.tt("dve", ME[:, rt, 0:QG], m32[:, 0:QG], k["bandB"][:, rt, 0:QG], ALU.mult, reads=[m32k, "bandB"], writes=[("ME", h % 2)])

        its = []
        for h in range(8):
            for g in range(NG):
                q0 = g * QG
                tiles = []
                if fox:
                    last_kt = NPt + (q0 + QG - 1) // 128
                    for kt in range(0, last_kt + 1):
                        rows = 128 if kt < NPt else min(128, T - (kt - NPt) * 128)
                        d = kt - NPt - q0 // 128
                        mfn = (lambda c0, c1, d=d, rows=rows: k["cmB"][0:rows, d, c0:c1]) if d >= 0 else None
                        c0 = min(128 * d, QG - 1) if d > 0 else 0
                        tiles.append([kt, rows, biasT[0:rows, h, g, kt:kt + 1], mfn, "cmB", c0, QG])
                else:
                    order = [3, 2, 5, 1, 6, 0, 7, 4] if T > 16 else list(range(8))
                    for rt in order:
                        kt = (q0 // 128 - 4 + rt) if T > 16 else rt
                        if kt < 0 or kt >= NKT:
                            continue
                        rows = 128 if kt < NPt else min(128, T - (kt - NPt) * 128)
                        mfn = (lambda c0, c1, rt=rt, rows=rows, hh_=h: MEs[hh_ % 2][0:rows, rt, c0:c1])
                        if T > 16:
                            lo, hi = max(0, 2 * rt - 8), min(7, 2 * rt + 1)
                            c0, c1 = lo * 64, (hi + 1) * 64
                        else:
                            c0, c1 = 0, QG
                        tiles.append([kt, rows, 0.0, mfn, ("ME", h % 2), c0, c1])
                tiles[0][5], tiles[0][6] = 0, QG
                tiles[-1][5], tiles[-1][6] = 0, QG
                G = dict(h=h, g=g, q0=q0, n=len(tiles))
                for i, tl in enumerate(tiles):
                    its.append((G, i, tl))
        D = 3
        qk = {}
        pend = []

        def finalize1(G):
            pn, pnk = G["pn"]
            rd, rdk = rd_.next()
            P.act(rd[64:65, 0:QG], pn[64:65, 0:QG], AF.Ln, scale=float(2.0 ** -40), reads=[pnk], writes=[rdk])
            P.act(rd[64:65, 0:QG], rd[64:65, 0:QG], AF.Exp, bias=k["lnc"][64:65, 0:1], scale=-1.0, reads=[rdk, "epsv"], writes=[rdk])
            nb, nbk = nb_.next()
            P.copy("act", nb[:, 0:QG], pn[0:64, 0:QG], reads=[pnk], writes=[nbk])
            G["rd"], G["nb"] = (rd, rdk), (nb, nbk)

        def finalize(G):
            h, q0 = G["h"], G["q0"]
            rd, rdk = G["rd"]
            nb, nbk = G["nb"]
            pd, pdk = psD.next()
            P.mm(pd[:, 0:QG], k["onesF"][64:65, 0:64], rd[64:65, 0:QG], reads=["onesF", rdk], writes=[pdk])
            ob, obk = ob_.next()
            P.tt("dve", ob[:, 0:QG], nb[:, 0:QG], pd[:, 0:QG], ALU.mult, reads=[nbk, pdk], writes=[obk])
            r0 = mix_off + h * 64
            P.dma(job.mixT[r0:r0 + 64, base + q0:base + q0 + QG], ob[:, 0:QG], reads=[obk],
                  writes=[("mixT", job.name, r0 // 128, (base + q0) // 512, h % 2)], q="pool")

        for n in range(len(its) + D):
            if n < len(its):
                G, i, (kt, rows, bias, mfn, mkey, c0, c1) = its[n]
                h = G["h"]
                hb = (h // 4) * 64
                if (not fox) and G["g"] == 0 and i == 0:
                    build_me(h)
                ps, pk = psS.next()
                P.mm(ps[0:rows, c0:c1], KT[hb:hb + 64, h % 4, kt * 128:kt * 128 + rows], QT[hb:hb + 64, h % 4, G["q0"] + c0:G["q0"] + c1],
                     reads=["KT", "QT"], writes=[pk])
                qk[n] = (ps, pk)
            m = n - D
            if m >= 0:
                G, i, (kt, rows, bias, mfn, mkey, c0, c1) = its[m]
                h = G["h"]
                ps, pk = qk.pop(m)
                if i == 0:
                    while len(pend) > 1:
                        finalize(pend.pop(0))
                    G["pn"] = psN.next()
                pn, pnk = G["pn"]
                pt, ptk = pb_.next()
                P.act(pt[0:rows, c0:c1], ps[0:rows, c0:c1], AF.Exp, bias=bias, scale=0.125, reads=[pk, "biasT"], writes=[ptk])
                if mfn is not None:
                    P.tt("dve", pt[0:rows, c0:c1], pt[0:rows, c0:c1], mfn(c0, c1), ALU.mult, reads=[ptk, mkey], writes=[ptk])
                P.mm(pn[0:65, c0:c1], Vt[0:rows, kt, h, :], pt[0:rows, c0:c1], start=(i == 0), stop=(i == G["n"] - 1), reads=["Vt", ptk], writes=[pnk])
                if i == G["n"] - 1:
                    finalize1(G)
                    G["due"] = m + 4
                    pend.append(G)
                while pend and pend[0].get("due", 1 << 30) <= m and pend[0] is not G:
                    finalize(pend.pop(0))
        while pend:
            finalize(pend.pop(0))
        P.pe_serial = False
        P.flush()


def build_rel_tables(C, I):
    P, st, k = C.P, C.st, C.k
    ext = C.dr("relext", [8, 1536], F32)
    rb = I["chunk_rel_bias"]
    P.dma(ext[:, 384:639], rb[0, :, 1:256], writes=["relext"])
    P.dma(bass.AP(ext.tensor, 0, [[1536, 8], [1, 384], [1, 1]]), bass.AP(rb.tensor, 0, [[257, 8], [0, 384], [1, 1]]), writes=["relext"])
    P.dma(bass.AP(ext.tensor, 639, [[1536, 8], [1, 897], [1, 1]]), bass.AP(rb.tensor, 256, [[257, 8], [0, 897], [1, 1]]), writes=["relext"])
    band = C.sb(st, "bandB", [128, 8, 512], BF16)
    P.memset("pool", band[:], 0.0, writes=["bandB"])
    for rt in range(8):
        for e in range(2):
            kcr = 2 * rt + e
            lo, hi = max(0, kcr - 8), min(7, kcr)
            if lo <= hi:
                P.memset("pool", band[e * 64:(e + 1) * 64, rt, lo * 64:(hi + 1) * 64], 1.0, writes=["bandB"])
    k["bandB"] = band
    P.flush()


def hgrn_mixer(C, job, s, I, prm):
    P, k = C.P, C.k
    T = job.T
    L = min(16, T)
    SEG = min(256, T)
    base = s * T
    with contextlib.ExitStack() as ts:
        pools = scan_pools(C, ts, False)
        psP = Rot(C, ts, "hg_psP", [128, 512], F32, 2, psum=True)
        S32 = C.sb(ts, "S32", [128, 4, 64], F32)
        Spad = C.sb(ts, "Spad", [128, 4, 128], BF16)
        P.memset("pool", Spad[:], 0.0, writes=["Spad"])
        if job.hgrn_s0 is None:
            P.memset("pool", S32[:], 0.0, writes=["S32"])
        else:
            P.dma(S32[:], job.hgrn_s0[s].rearrange("(a e) k v -> (e k) a v", e=2), writes=["S32"])
            for e in range(2):
                pb = e * 64
                P.copy("act", Spad[pb:pb + 64, :, pb:pb + 64], S32[pb:pb + 64, :, :], reads=["S32"], writes=["Spad"])
        z4 = C.sb(ts, "z4", [128, 16, SEG], F32)
        f4 = lambda nm: C.sb(ts, nm, [128, 4, SEG], F32)
        b4 = lambda nm: C.sb(ts, nm, [128, 4, SEG], BF16)
        qf, sgf, lw, kin, cc, epos, eneg, yT, tmpa = [f4(n) for n in ("qf", "sgf", "lw", "kin", "cc", "epos", "eneg", "yT", "tmpa")]
        rT, kT, vT, sqb, outb = [b4(n) for n in ("rT", "kT", "vT", "sqb", "outb")]
        keys = dict(rT="rT", kT="kT", vT="vT", epos="epos", yT="yT")
        lb, oml, noml, ng = prm["lb"], prm["oml"], prm["noml"], prm["ng"]
        pk_ = ["hgprm"]
        rm = k["rmask%d" % L]
        for t0 in range(0, T, SEG):
            col0 = base + t0
            zk = [("zT", job.name, nb, i) for nb in range(12, 28) for i in range(col0 // 512, (col0 + SEG - 1) // 512 + 1)]
            P.dma(z4[:], job.zT[1536:3584, col0:col0 + SEG].rearrange("(c p) t -> p c t", p=128), reads=zk, writes=["z4"])
            P.act(qf[:], z4[:, 0:4, :], AF.Silu, reads=["z4"], writes=["qf"])
            P.act(sgf[:], z4[:, 4:8, :], AF.Sigmoid, reads=["z4"], writes=["sgf"])
            for blk in range(4):
                P.ts("dve", lw[:, blk, :], sgf[:, blk, :], oml[:, blk:blk + 1], lb[:, blk:blk + 1], ALU.mult, ALU.add, reads=["sgf"] + pk_, writes=["lw"])
                P.ts("dve", kin[:, blk, :], sgf[:, blk, :], noml[:, blk:blk + 1], oml[:, blk:blk + 1], ALU.mult, ALU.add, reads=["sgf"] + pk_, writes=["kin"])
            P.act(lw[:], lw[:], AF.Ln, reads=["lw"], writes=["lw"])
            for blk in range(4):
                P.op("dve", lambda e, blk=blk: e.tensor_tensor_scan(cc[:, blk, :], rm[:, 0:SEG], lw[:, blk, :], 0.0, ALU.mult, ALU.add),
                     reads=["lw", "rmask%d" % L], writes=["cc"])
            P.act(epos[:], cc[:], AF.Exp, reads=["cc"], writes=["epos"])
            P.act(eneg[:], cc[:], AF.Exp, scale=-1.0, reads=["cc"], writes=["eneg"])
            P.tt("dve", rT[:], qf[:], epos[:], ALU.mult, reads=["qf", "epos"], writes=["rT"])
            P.tt("dve", kT[:], kin[:], eneg[:], ALU.mult, reads=["kin", "eneg"], writes=["kT"])
            P.copy("pool", vT[:], z4[:, 8:12, :], reads=["z4"], writes=["vT"])
            chunk_scan(C, ts, SEG, L, False, rT, kT, vT, None, None, epos, S32, Spad, yT, keys, pools)
            P.act(sqb[:], yT[:], AF.Square, reads=["yT"], writes=["sqb"])
            for blk in range(4):
                ps, pk = psP.next()
                P.mm(ps[:, 0:SEG], k["bo"][:], sqb[:, blk, :], reads=["sqb", "bo"], writes=[pk])
                P.act(tmpa[:, blk, :], ps[:, 0:SEG], AF.Sqrt, bias=k["eps_rms"][:, 0:1], scale=1.0 / 64, reads=[pk, "epsv"], writes=["tmpa"])
            P.op("dve", lambda e: e.reciprocal(tmpa[:], tmpa[:]), reads=["tmpa"], writes=["tmpa"])
            P.tt("dve", tmpa[:], tmpa[:], yT[:], ALU.mult, reads=["tmpa", "yT"], writes=["tmpa"])
            P.act(qf[:], z4[:, 12:16, :], AF.Silu, reads=["z4", "rT"], writes=["qf"])
            for blk in range(4):
                P.stt(outb[:, blk, :], tmpa[:, blk, :], ng[:, blk:blk + 1], qf[:, blk, :], ALU.mult, ALU.mult, reads=["tmpa", "qf"] + pk_, writes=["outb"])
            P.dma(job.mixT[512:1024, col0:col0 + SEG].rearrange("(c p) t -> p c t", p=128), outb[:], reads=["outb"],
                  writes=[("mixT", job.name, 4 + c, col0 // 512, hh) for c in range(4) for hh in range(2)], q="pool")
        P.dma(job.hgrn_out[s].rearrange("(a e) k v -> (e k) a v", e=2), S32[:], reads=["S32"], writes=[("hgst", job.name, s)], q="pool")
        P.flush()


def load_hgrn_params(C, I):
    P, st = C.P, C.st
    prm = {}
    t0 = C.sb(st, "hg_t0", [128, 4], F32)
    t1 = C.sb(st, "hg_t1", [128, 4], F32)
    P.dma(t0[:], I["hgrn_lb_table"][0].rearrange("(c p) -> p c", p=128), writes=["hg_t0"], allow_slow_non_contiguous=True)
    P.dma(t1[:], I["hgrn_lb_table"][1].rearrange("(c p) -> p c", p=128), writes=["hg_t1"], allow_slow_non_contiguous=True)
    P.act(t0[:], t0[:], AF.Exp, reads=["hg_t0"], writes=["hg_t0"])
    P.act(t1[:], t1[:], AF.Exp, reads=["hg_t1"], writes=["hg_t1"])
    P.tt("dve", t0[:], t0[:], t1[:], ALU.add, reads=["hg_t0", "hg_t1"], writes=["hg_t0"])
    P.op("dve", lambda e: e.reciprocal(t0[:], t0[:]), reads=["hg_t0"], writes=["hg_t0"])
    lb = C.sb(st, "hg_lb", [128, 4], F32)
    oml = C.sb(st, "hg_oml", [128, 4], F32)
    noml = C.sb(st, "hg_noml", [128, 4], F32)
    ng = C.sb(st, "hg_ng", [128, 4], F32)
    P.tt("dve", lb[:], t1[:], t0[:], ALU.mult, reads=["hg_t0", "hg_t1"], writes=["hgprm"])
    P.ts("dve", oml[:], lb[:], -1.0, 1.0, ALU.mult, ALU.add, reads=["hgprm"], writes=["hgprm"])
    P.ts("dve", noml[:], oml[:], -1.0, None, ALU.mult, reads=["hgprm"], writes=["hgprm"])
    P.dma(ng[:], I["hgrn_norm_g"][0].rearrange("(c p) -> p c", p=128), writes=["hgprm"], allow_slow_non_contiguous=True)
    ng8 = C.sb(st, "hg_ng8", [64, 8], F32)
    P.dma(ng8[:], I["hgrn_norm_g"][0].rearrange("(h k) -> k h", k=64), writes=["hgprm"], allow_slow_non_contiguous=True)
    prm.update(lb=lb, oml=oml, noml=noml, ng=ng, ng8=ng8)
    P.flush()
    return prm


def xk1(job, c, col0, gs):
    return [("xT", job.name, c, i) for i in range(col0 // 512, (col0 + gs - 1) // 512 + 1)]


def out_proj(C, job, Wb, wkeys, l, j, srcT_dram, kcn, src_keys_fn):
    P = C.P
    m = C.mods[(l, j)]
    with contextlib.ExitStack() as ts:
        wr = Rot(C, ts, "op_w", [128, kcn, 128], BF16, 8)
        ws = []
        for nb in range(8):
            w, wk = wr.next()
            P.dma(w[:], Wb[nb], reads=wkeys, writes=[wk])
            ws.append((w, wk))
        sr = Rot(C, ts, "op_src", [128, kcn, 512], BF16, 2)
        pr = Rot(C, ts, "op_ps", [128, 512], F32, 3, psum=True)
        xr = Rot(C, ts, "op_x", [128, 512], F32, 3)
        for (s, t0, gs, col0) in seq_groups(job, 512):
            sT, sk = sr.next()
            P.dma(sT[:, :, 0:gs], srcT_dram[:, col0:col0 + gs].rearrange("(c p) t -> p c t", p=128), reads=src_keys_fn(col0), writes=[sk])
            col = job.modcol[s]
            for nb in range(8):
                w, wk = ws[nb]
                ps, pk = pr.next()
                for kc in range(kcn):
                    P.mm(ps[:, 0:gs], w[:, kc, :], sT[:, kc, 0:gs], start=(kc == 0), stop=(kc == kcn - 1), reads=[wk, sk], writes=[pk])
                x, xk = xr.next()
                P.dma(x[:, 0:gs], job.xT[nb * 128:(nb + 1) * 128, col0:col0 + gs], reads=xk1(job, nb, col0, gs), writes=[xk])
                P.stt(x[:, 0:gs], ps[:, 0:gs], m[:, 16 + nb, col:col + 1], x[:, 0:gs], ALU.mult, ALU.add, reads=[pk, xk, ("mod", l, j)], writes=[xk])
                P.dma(job.xT[nb * 128:(nb + 1) * 128, col0:col0 + gs], x[:, 0:gs], reads=[xk], writes=xk1(job, nb, col0, gs), q="pool")
        P.flush()


def ffn_up(C, job, l, I, Wup, wkeys, hT, hkey):
    P, k = C.P, C.k
    with contextlib.ExitStack() as ts:
        cw = C.sp[("cw", l)]
        cbv = C.sp[("cb", l)]
        wr = Rot(C, ts, "fu_w", [128, KC, 128], BF16, 4)
        pr = Rot(C, ts, "fu_ps", [128, 512], F32, 4, psum=True)
        Er = [Rot(C, ts, "fu_E%d" % i, [128, 514], F32, 3) for i in range(2)]
        t1r = Rot(C, ts, "fu_t1", [128, 512], F32, 4)
        gr = Rot(C, ts, "fu_g", [128, 512], BF16, 3)
        groups = seq_groups(job, 512)
        for jb in range(22):
            wts = []
            for half in range(2):
                w, wk = wr.next()
                P.dma(w[:], Wup[jb + 22 * half], reads=wkeys, writes=[wk])
                wts.append((w, wk))
            prevE = [None, None]
            for (s, t0, gs, col0) in groups:
                res = []
                for half in range(2):
                    blk = jb + 22 * half
                    w, wk = wts[half]
                    ps, pk = pr.next()
                    for kc in range(KC):
                        P.mm(ps[:, 0:gs], w[:, kc, :], hT[:, kc, col0:col0 + gs], start=(kc == 0), stop=(kc == KC - 1), reads=[wk, hkey], writes=[pk])
                    E, Ek = Er[half].next()
                    if t0 == 0:
                        if job.ffn_buf[l][s] is None:
                            P.memset("pool", E[:, 0:2], 0.0, writes=[Ek])
                        else:
                            P.dma(E[:, 0:2], job.ffn_buf[l][s][:, blk * 128:(blk + 1) * 128].rearrange("r f -> f r"), writes=[Ek], allow_slow_non_contiguous=True)
                    else:
                        pE, pEk, pgs = prevE[half]
                        P.copy("pool", E[:, 0:2], pE[:, pgs:pgs + 2], reads=[pEk], writes=[Ek])
                    P.copy("act", E[:, 2:2 + gs], ps[:, 0:gs], reads=[pk], writes=[Ek])
                    prevE[half] = (E, Ek, gs)
                    if t0 + gs == job.Ts[s]:
                        P.dma(job.ffn_out[l][s][:, blk * 128:(blk + 1) * 128].rearrange("r f -> f r"), E[:, gs:gs + 2], reads=[Ek],
                              writes=[("ffo", job.name, l, s, blk)], q="pool", allow_slow_non_contiguous=True)
                    t1, t1k = t1r.next()
                    P.act(t1[:, 0:gs], E[:, 0:gs], AF.Identity, bias=cbv[:, blk:blk + 1], scale=cw[:, 0, blk:blk + 1], reads=[Ek, "smallprm"], writes=[t1k])
                    P.stt(t1[:, 0:gs], E[:, 1:gs + 1], cw[:, 1, blk:blk + 1], t1[:, 0:gs], ALU.mult, ALU.add, reads=[Ek, "smallprm", t1k], writes=[t1k])
                    P.stt(t1[:, 0:gs], E[:, 2:gs + 2], cw[:, 2, blk:blk + 1], t1[:, 0:gs], ALU.mult, ALU.add, reads=[Ek, "smallprm", t1k], writes=[t1k])
                    res.append((t1, t1k))
                (ta, tak), (tb, tbk) = res
                P.act(ta[:, 0:gs], ta[:, 0:gs], AF.Silu, reads=[tak], writes=[tak])
                g, gk = gr.next()
                P.tt("pool", g[:, 0:gs], ta[:, 0:gs], tb[:, 0:gs], ALU.mult, reads=[tak, tbk], writes=[gk])
                P.dma(job.gT[jb * 128:(jb + 1) * 128, col0:col0 + gs], g[:, 0:gs], reads=[gk], writes=[("gT", job.name, jb, col0 // 512)], q="pool")
        P.flush()


def final_norm(C, job, I):
    P, k = C.P, C.k
    with contextlib.ExitStack() as ts:
        gf, gfk = C.sp["gfin"], "smallprm"
        xg = Rot(C, ts, "fn_x", [128, KC, 128], F32, 3)
        sq = Rot(C, ts, "fn_sq", [128, KC, 128], BF16, 2)
        pss = Rot(C, ts, "fn_ss", [128, 128], F32, 2, psum=True)
        rs = Rot(C, ts, "fn_r", [128, 128], F32, 2)
        pst = Rot(C, ts, "fn_pt", [128, 512], F32, 4, psum=True)
        yo = Rot(C, ts, "fn_yo", [128, 1024], F32, 2)

        def stage_a(grp):
            (s, t0, gs, col0) = grp
            x, xk = xg.next()
            rk_ = [key for c in range(8) for key in xk1(job, c, col0, gs)]
            P.dma(x[:, :, 0:gs], job.xT[:, col0:col0 + gs].rearrange("(c p) t -> p c t", p=128), reads=rk_, writes=[xk])
            q, qk = sq.next()
            P.act(q[:, :, 0:gs], x[:, :, 0:gs], AF.Square, reads=[xk], writes=[qk])
            ps, pk = pss.next()
            for c in range(KC):
                P.mm(ps[:, 0:gs], k["onesB"][:], q[:, c, 0:gs], start=(c == 0), stop=(c == KC - 1), reads=[qk, "onesB"], writes=[pk])
            r, rk = rs.next()
            P.act(r[:, 0:gs], ps[:, 0:gs], AF.Ln, bias=k["eps_rms"][:, 0:1], scale=1.0 / D, reads=[pk, "epsv"], writes=[rk])
            P.act(r[:, 0:gs], r[:, 0:gs], AF.Exp, scale=-0.5, reads=[rk], writes=[rk])
            for c in range(KC):
                P.stt(x[:, c, 0:gs], x[:, c, 0:gs], gf[:, c:c + 1], r[:, 0:gs], ALU.mult, ALU.mult, reads=[xk, rk, gfk], writes=[xk])
            return (x, xk, grp)

        def stage_b(st_):
            x, xk, (s, t0, gs, col0) = st_
            y, yk = yo.next()
            for half in range(2):
                pt, ptk = pst.next()
                for c4 in range(4):
                    c = half * 4 + c4
                    P.tr(pt[0:gs, c4 * 128:(c4 + 1) * 128], x[:, c, 0:gs], k["identF"][:, :], reads=[xk, "identF"], writes=[ptk])
                P.copy("act" if half else "dve", y[0:gs, half * 512:(half + 1) * 512], pt[0:gs, :], reads=[ptk], writes=[yk])
            P.dma(job.y_out[s][t0:t0 + gs, :], y[0:gs, :], reads=[yk], writes=[("yout", job.name, s, t0)], q="pool")

        groups = seq_groups(job, 128)
        prev = None
        for grp in groups:
            cur = stage_a(grp)
            if prev is not None:
                stage_b(prev)
            prev = cur
        stage_b(prev)
        P.flush()


def load_small_params(C, I):
    P, st, k = C.P, C.st, C.k
    sp = {}
    specs = []
    for l in range(2):
        specs.append((("gmix", l), I["norm_mix_g"][l].rearrange("(c p) -> c p", p=128), 8))
        specs.append((("gffn", l), I["norm_ffn_g"][l].rearrange("(c p) -> c p", p=128), 8))
        specs.append((("cw", l), I["ffn_conv_w"][l].rearrange("j (c p) -> (j c) p", p=128), 132))
        specs.append((("cb", l), I["ffn_conv_b"][l].rearrange("(c p) -> c p", p=128), 44))
    specs.append(("gfin", I["final_norm_g"].rearrange("(c p) -> c p", p=128), 8))
    tiles = {}
    for key, ap, R in specs:
        tiles[key] = C.sb(st, "sp_%s" % str(key).replace(" ", ""), [128, R], F32)
    with contextlib.ExitStack() as ts:
        ld = Rot(C, ts, "sp_ld", [128, 128], F32, 3)
        pp = Rot(C, ts, "sp_ps", [128, 128], F32, 2, psum=True)
        for key, ap, R in specs:
            dst = tiles[key]
            for r0 in range(0, R, 128):
                rows = min(128, R - r0)
                t, tk = ld.next()
                P.dma(t[0:rows, :], ap[r0:r0 + rows, :], writes=[tk])
                ps, pk = pp.next()
                P.tr(ps[:, 0:rows], t[0:rows, :], k["identF"][0:rows, 0:rows], reads=[tk, "identF"], writes=[pk])
                P.copy("dve", dst[:, r0:r0 + rows], ps[:, 0:rows], reads=[pk], writes=["smallprm"])
        P.flush()
    for l in range(2):
        sp[("gmix", l)] = tiles[("gmix", l)]
        sp[("gffn", l)] = tiles[("gffn", l)]
        sp[("cw", l)] = tiles[("cw", l)][:, :].rearrange("p (j c) -> p j c", j=3)
        sp[("cb", l)] = tiles[("cb", l)]
    sp["gfin"] = tiles["gfin"]
    C.sp = sp


def run_layer(C, job, l, I, W, prm):
    P = C.P
    wname_in = "ab_in" if l == 0 else "cd_in"
    wname_out = "ab_out" if l == 0 else "cd_out"
    nb_in = 27 if l == 0 else 28
    Ttot = job.Ttot
    with contextlib.ExitStack() as ts:
        hT = C.sb(ts, "hT", [128, KC, Ttot], BF16)
        hkey = C.name("hT")
        with contextlib.ExitStack() as t2:
            norm_to_hT(C, t2, job, C.sp[("gmix", l)], l, 0, hT, hkey)
        with contextlib.ExitStack() as t2:
            stg = Rot(C, t2, "pj_stg", [128, 512], F32, 4)
            cnt = [0]

            def post(nb, s, t0, gs, col0, ps, pk):
                st_, sk_ = stg.next()
                cnt[0] += 1
                P.copy("act" if cnt[0] % 2 else "dve", st_[:, 0:gs], ps[:, 0:gs], reads=[pk], writes=[sk_])
                P.dma(job.zT[nb * 128:(nb + 1) * 128, col0:col0 + gs], st_[:, 0:gs], reads=[sk_], writes=[("zT", job.name, nb, col0 // 512)], q="pool")
            Wb, wk = W[wname_in]
            proj(C, job, hT, hkey, Wb, wk, range(nb_in), KC, post)
            P.flush()
    stage("inproj %s %d" % (job.name, l))
    for s in range(job.nseq):
        if l == 0:
            rwkv_mixer2(C, job, s, I, prm["rwkv"])
            stage("rwkv")
            attn_mixer(C, job, s, I, "fox")
            stage("fox")
        else:
            attn_mixer(C, job, s, I, "chunk")
            stage("chunk")
            hgrn_mixer2(C, job, s, I, prm["hgrn"])
            stage("hgrn")
    Wb, wk = W[wname_out]
    out_proj(C, job, Wb, wk, l, 0, job.mixT, 8,
             lambda col0: [("mixT", job.name, c, col0 // 512, hh) for c in range(8) for hh in range(2)])
    stage("outproj")
    with contextlib.ExitStack() as ts:
        hT = C.sb(ts, "hT2", [128, KC, Ttot], BF16)
        hkey = C.name("hT2")
        with contextlib.ExitStack() as t2:
            norm_to_hT(C, t2, job, C.sp[("gffn", l)], l, 1, hT, hkey)
        Wb, wk = W["up%d" % l]
        ffn_up(C, job, l, I, Wb, wk, hT, hkey)
    stage("ffn_up")
    Wb, wk = W["down%d" % l]
    out_proj(C, job, Wb, wk, l, 1, job.gT, 22, lambda col0: [("gT", job.name, jb, col0 // 512) for jb in range(22)])


class StopBuild(Exception):
    pass


import os
_STOP = int(os.environ.get("K_STOP", "999"))
_stage = [0]


def stage(msg=""):
    _stage[0] += 1
    if os.environ.get("K_VERBOSE"):
        print("stage", _stage[0], msg)
    if _stage[0] >= _STOP:
        raise StopBuild()


def build_program(Tp):
    nc = bass.Bass("TRN2", target_bir_lowering=False)
    _stage[0] = 0
    I = {}
    O = {}

    def inp(name, shape):
        I[name] = nc.dram_tensor(name, list(shape), F32, kind="ExternalInput").ap()

    def outp(name, shape):
        O[name] = nc.dram_tensor(name, list(shape), F32, kind="ExternalOutput").ap()
    S2 = NSEQ_S
    inp("xp", [Tp, D]); inp("xs", [S2, 16, D]); inp("cp", [D]); inp("csv", [S2, D])
    inp("fox_ck", [S2, P_FOX, 512]); inp("fox_cv", [S2, P_FOX, 512]); inp("fox_clogf", [S2, P_FOX, 8])
    inp("rw_s0", [S2, 8, 64, 64]); inp("rw_sh0", [S2, A_COLS])
    inp("ch_ck", [S2, P_CHUNK, 512]); inp("ch_cv", [S2, P_CHUNK, 512]); inp("hg_s0", [S2, 8, 64, 64])
    inp("ffn_buf", [2, S2, 2, 2 * DFF])
    for nm, shp in (("ada_w", [2, 2, D, 3 * D]), ("ada_b", [2, 2, 3 * D]), ("norm_mix_g", [2, D]), ("norm_ffn_g", [2, D]),
                    ("ab_w_in", [1, D, AB_COLS]), ("rwkv_mu", [1, A_COLS]), ("rwkv_w0", [1, 512]), ("rwkv_w2", [1, 64, 512]),
                    ("rwkv_a0", [1, 512]), ("rwkv_a2", [1, 64, 512]), ("rwkv_g2", [1, 128, 512]), ("rwkv_k_k", [1, 512]),
                    ("rwkv_k_a", [1, 512]), ("rwkv_r_k", [1, 8, 64]), ("rwkv_lnx_g", [1, 512]), ("rwkv_lnx_b", [1, 512]),
                    ("fox_b_f", [1, 8]), ("ab_w_out", [1, D, D]), ("cd_w_in", [1, D, CD_COLS]), ("chunk_rel_bias", [1, 8, 257]),
                    ("hgrn_lb_table", [2, 512]), ("hgrn_norm_g", [1, 512]), ("cd_w_out", [1, D, D]), ("ffn_w_up", [2, D, 2 * DFF]),
                    ("ffn_conv_w", [2, 3, 2 * DFF]), ("ffn_conv_b", [2, 2 * DFF]), ("ffn_w_down", [2, DFF, D]), ("final_norm_g", [D])):
        inp(nm, shp)
    cK = min(512, Tp)
    outp("y_p", [Tp, D]); outp("y_s", [S2, 16, D])
    outp("fox_k_p", [Tp, 512]); outp("fox_v_p", [Tp, 512]); outp("fox_logf_p", [Tp, 8])
    outp("rwkv_p", [8, 64, 64]); outp("rwkv_shift_p", [A_COLS])
    outp("chunk_k_p", [cK, 512]); outp("chunk_v_p", [cK, 512]); outp("hgrn_p", [8, 64, 64]); outp("ffn_conv_p", [2, 2, 2 * DFF])
    outp("fox_k_s", [S2, 16, 512]); outp("fox_v_s", [S2, 16, 512]); outp("fox_logf_s", [S2, 16, 8])
    outp("rwkv_s", [S2, 8, 64, 64]); outp("rwkv_shift_s", [S2, A_COLS])
    outp("chunk_k_s", [S2, 16, 512]); outp("chunk_v_s", [S2, 16, 512]); outp("hgrn_s", [S2, 8, 64, 64]); outp("ffn_conv_s", [2, S2, 2, 2 * DFF])

    with contextlib.ExitStack() as st:
        P = Prog(nc, st)
        C = Ctx(nc, P, st)
        try:
          build_consts(C)
          more_consts(C)
          load_small_params(C, I)
          stage("consts")
          W = {}
          W["ab_in"] = cast_weight(C, I["ab_w_in"][0], D, AB_COLS, "Wab_in")
          W["ab_out"] = cast_weight(C, I["ab_w_out"][0], D, D, "Wab_out")
          W["cd_in"] = cast_weight(C, I["cd_w_in"][0], D, CD_COLS, "Wcd_in")
          W["cd_out"] = cast_weight(C, I["cd_w_out"][0], D, D, "Wcd_out")
          for l in range(2):
              W["up%d" % l] = cast_weight(C, I["ffn_w_up"][l], D, 2 * DFF, "Wup%d" % l)
              W["down%d" % l] = cast_weight(C, I["ffn_w_down"][l], DFF, D, "Wdown%d" % l)
          stage("cast")
          adaln_phase(C, I, 1 + S2, [I["cp"]] + [I["csv"][s] for s in range(S2)])
          stage("adaln")
          prm = dict(rwkv=load_rwkv_params(C, I), hgrn=load_hgrn_params(C, I))
          build_rel_tables(C, I)
          stage("params")

          S3 = 1 + S2
          jb = Job()
          jb.name, jb.nseq = "m", S3
          jb.Ts = [Tp] + [16] * S2
          jb.bases = [0] + [Tp + 16 * i for i in range(S2)]
          jb.Ttot = Tp + 16 * S2
          jb.modcol = list(range(S3))
          jb.x_in = [I["xp"]] + [I["xs"][s] for s in range(S2)]
          jb.y_out = [O["y_p"]] + [O["y_s"][s] for s in range(S2)]
          jb.fox_kout = [O["fox_k_p"]] + [O["fox_k_s"][s] for s in range(S2)]
          jb.fox_vout = [O["fox_v_p"]] + [O["fox_v_s"][s] for s in range(S2)]
          jb.fox_logf_out = [O["fox_logf_p"]] + [O["fox_logf_s"][s] for s in range(S2)]
          jb.rwkv_out = [O["rwkv_p"]] + [O["rwkv_s"][s] for s in range(S2)]
          jb.rwkv_shift_out = [O["rwkv_shift_p"]] + [O["rwkv_shift_s"][s] for s in range(S2)]
          jb.chunk_kout = [O["chunk_k_p"]] + [O["chunk_k_s"][s] for s in range(S2)]
          jb.chunk_vout = [O["chunk_v_p"]] + [O["chunk_v_s"][s] for s in range(S2)]
          jb.hgrn_out = [O["hgrn_p"]] + [O["hgrn_s"][s] for s in range(S2)]
          jb.ffn_out = [[O["ffn_conv_p"][l]] + [O["ffn_conv_s"][l, s] for s in range(S2)] for l in range(2)]
          jb.P_fox = [0] + [P_FOX] * S2
          jb.P_chunk = [0] + [P_CHUNK] * S2
          jb.fox_ck = [None] + [I["fox_ck"][s] for s in range(S2)]
          jb.fox_cv = [None] + [I["fox_cv"][s] for s in range(S2)]
          jb.fox_clogf = [None] + [I["fox_clogf"][s] for s in range(S2)]
          jb.chunk_ck = [None] + [I["ch_ck"][s] for s in range(S2)]
          jb.chunk_cv = [None] + [I["ch_cv"][s] for s in range(S2)]
          jb.rwkv_s0 = [None] + [I["rw_s0"][s] for s in range(S2)]
          jb.rwkv_shift0 = [None] + [I["rw_sh0"][s] for s in range(S2)]
          jb.hgrn_s0 = [None] + [I["hg_s0"][s] for s in range(S2)]
          jb.ffn_buf = [[None] + [I["ffn_buf"][l, s] for s in range(S2)] for l in range(2)]
          for job in (jb,):
              Ttot = job.Ttot
              job.xT = C.dr("xT_" + job.name, [D, Ttot], F32)
              job.zT = C.dr("zT_" + job.name, [28 * 128, Ttot], F32)
              job.mixT = C.dr("mixT_" + job.name, [D, Ttot], BF16)
              job.gT = C.dr("gT_" + job.name, [DFF, Ttot], BF16)
              x_to_fm(C, job)
              stage("x_to_fm " + job.name)
              for l in range(2):
                  run_layer(C, job, l, I, W, prm)
              final_norm(C, job, I)
              stage("final " + job.name)
        except StopBuild:
            P.ops = []
        P.flush(final=True)
        print("ops:", P.n_total)
        if os.environ.get("K_FLUSHLOG"):
            import json as _json
            _json.dump(P.flush_log, open(os.environ["K_FLUSHLOG"], "w"))
    return nc


_CACHE = {}


def kernel(**inp):
    f = lambda a: np.ascontiguousarray(np.asarray(a, dtype=np.float32))
    xpr = f(inp["x_prompt"])
    B, Tp, _ = xpr.shape
    if Tp not in _CACHE:
        _CACHE[Tp] = build_program(Tp)
    nc = _CACHE[Tp]
    wnames = ["ada_w", "ada_b", "norm_mix_g", "norm_ffn_g", "ab_w_in", "rwkv_mu", "rwkv_w0", "rwkv_w2", "rwkv_a0", "rwkv_a2",
              "rwkv_g2", "rwkv_k_k", "rwkv_k_a", "rwkv_r_k", "rwkv_lnx_g", "rwkv_lnx_b", "fox_b_f", "ab_w_out", "cd_w_in",
              "chunk_rel_bias", "hgrn_lb_table", "hgrn_norm_g", "cd_w_out", "ffn_w_up", "ffn_conv_w", "ffn_conv_b", "ffn_w_down",
              "final_norm_g"]
    wts = {n: f(inp[n]) for n in wnames}
    xs = f(inp["x_sample"]); cp = f(inp["c_prompt"]); csv = f(inp["c_sample"])
    fk = f(inp["cache_fox_k"])[0].reshape(16, P_FOX, 512); fv = f(inp["cache_fox_v"])[0].reshape(16, P_FOX, 512)
    fl = f(inp["cache_fox_logf"])[0]
    rs0 = f(inp["state_rwkv"])[0]; rsh = f(inp["state_rwkv_shift"])[0]
    ckk = f(inp["cache_chunk_k"])[0].reshape(16, P_CHUNK, 512); ckv = f(inp["cache_chunk_v"])[0].reshape(16, P_CHUNK, 512)
    hs0 = f(inp["state_hgrn"])[0]; fb = f(inp["state_ffn_conv"])
    in_maps = []
    for c in range(N_CORES):
        b = c % B
        sl = slice(2 * c, 2 * c + 2)
        m = dict(xp=xpr[b], xs=xs[sl], cp=cp[b], csv=csv[sl], fox_ck=fk[sl], fox_cv=fv[sl], fox_clogf=fl[sl], rw_s0=rs0[sl], rw_sh0=rsh[sl],
                 ch_ck=ckk[sl], ch_cv=ckv[sl], hg_s0=hs0[sl], ffn_buf=np.ascontiguousarray(fb[:, sl]))
        m.update(wts)
        in_maps.append({k_: np.ascontiguousarray(v_) for k_, v_ in m.items()})
    res = run_bass_kernel_spmd(nc, in_maps, core_ids=list(range(N_CORES)))
    R = res.results
    pc = lambda name: np.stack([R[b][name] for b in range(B)])
    sc = lambda name, ax=0: np.concatenate([R[c][name] for c in range(N_CORES)], axis=ax)
    cK = min(512, Tp)
    outs = (
        pc("y_p"), sc("y_s"),
        pc("fox_k_p").reshape(1, B, Tp, 8, 64), pc("fox_v_p").reshape(1, B, Tp, 8, 64), pc("fox_logf_p").reshape(1, B, Tp, 8),
        pc("rwkv_p")[None], pc("rwkv_shift_p")[None],
        pc("chunk_k_p").reshape(1, B, cK, 8, 64), pc("chunk_v_p").reshape(1, B, cK, 8, 64), pc("hgrn_p")[None],
        np.stack([R[b]["ffn_conv_p"] for b in range(B)], axis=1),
        sc("fox_k_s").reshape(1, 16, 16, 8, 64), sc("fox_v_s").reshape(1, 16, 16, 8, 64), sc("fox_logf_s").reshape(1, 16, 16, 8),
        sc("rwkv_s")[None], sc("rwkv_shift_s")[None],
        sc("chunk_k_s").reshape(1, 16, 16, 8, 64), sc("chunk_v_s").reshape(1, 16, 16, 8, 64), sc("hgrn_s")[None],
        sc("ffn_conv_s", ax=1),
    )
    return tuple(np.ascontiguousarray(o.astype(np.float32)) for o in outs)


def chunk_scan2(C, ts, SEG, L, delta, ops8, tok4, eposL, S32, Sb, yT8):
    P, k = C.P, C.k
    NCH = SEG // L
    nlev = int(np.log2(L))
    psT = Rot(C, ts, "c2_psT", [64, 4, 128], BF16, 2, psum=True)
    psH = Rot(C, ts, "c2_psH", [64, 8, 64], F32, 6, psum=True)
    r8, r8k = ops8["r"]
    k8, k8k = ops8["k"]
    if delta:
        a8, a8k = ops8["a"]
        b8, b8k = ops8["b"]

    def bt(nm, dt=BF16):
        return [(C.sb(ts, "c2_%s%d" % (nm, c), [64, 8, 64], dt), C.name("c2_" + nm)) for c in range(NCH)]
    Ktok, Vtok = bt("Ktok"), bt("Vtok")
    Mrk = bt("Mrk")
    if delta:
        Btok, Mak, Mrb = bt("Btok"), bt("Mak"), bt("Mrb")
        PT = [bt("PTa"), bt("PTb")]
        Pm = [bt("Pma"), bt("Pmb")]
        TTb = [bt("TTba"), bt("TTbb")]
        TTf = bt("TTf", F32)
    cs_of = lambda c: slice(c * L, (c + 1) * L)

    def mm8(c, lhs_of, rhs_of, reads):
        ps, pk = psH.next()
        for h in range(8):
            P.mm(ps[0:L, h, 0:L], lhs_of(h), rhs_of(h), reads=reads, writes=[pk])
        return ps, pk

    for c in range(NCH):
        cs = cs_of(c)
        lst = [("k", Ktok), ("v", Vtok)] + ([("b", Btok)] if delta else [])
        for nm, dstl in lst:
            src, sk = tok4[nm]
            ps, pk = psT.next()
            for blk in range(4):
                P.tr(ps[0:L, blk, :], src[:, blk, cs], k["identB"][:, :], reads=[sk, "identB"], writes=[pk])
            dst, dk = dstl[c]
            P.copy("act" if nm == "v" else "dve", dst[0:L, :, :], ps[0:L, :, :].rearrange("p a (e x) -> p (a e) x", e=2), reads=[pk], writes=[dk])
    if delta:
        for c in range(NCH):
            cs = cs_of(c)
            psA, pkA = mm8(c, lambda h: b8[:, h, cs], lambda h: a8[:, h, cs], [a8k, b8k])
            psB, pkB = mm8(c, lambda h: a8[:, h, cs], lambda h: b8[:, h, cs], [a8k, b8k])
            tf, tfk = TTf[c]
            pt, ptk = PT[0][c]
            pm, pmk = Pm[0][c]
            tb, tbk = TTb[0][c]
            P.tt("dve", tf[0:L, :, 0:L], psA[0:L, :, 0:L], k["m_su"][0:L, :, 0:L], ALU.mult, reads=[pkA, "m_su"], writes=[tfk])
            P.copy("act", pt[0:L, :, 0:L], tf[0:L, :, 0:L], reads=[tfk], writes=[ptk])
            P.tt("dve", pm[0:L, :, 0:L], psB[0:L, :, 0:L], k["m_sl"][0:L, :, 0:L], ALU.mult, reads=[pkB, "m_sl"], writes=[pmk])
            P.tt("pool", tf[0:L, :, 0:L], tf[0:L, :, 0:L], k["i8"][0:L, :, 0:L], ALU.add, reads=[tfk, "i8"], writes=[tfk])
            P.copy("act", tb[0:L, :, 0:L], tf[0:L, :, 0:L], reads=[tfk], writes=[tbk])
        cur = 0
        for j in range(1, nlev):
            last = (j == nlev - 1)
            nxt = 1 - cur
            pend_ = []
            for c in range(NCH):
                pt, ptk = PT[cur][c]
                pm, pmk = Pm[cur][c]
                psA, pkA = mm8(c, lambda h: pt[0:L, h, 0:L], lambda h: pm[0:L, h, 0:L], [ptk, pmk])
                pm2, pm2k = Pm[nxt][c]
                P.copy("act", pm2[0:L, :, 0:L], psA[0:L, :, 0:L], reads=[pkA], writes=[pm2k])
                if not last:
                    psB, pkB = mm8(c, lambda h: pm[0:L, h, 0:L], lambda h: pt[0:L, h, 0:L], [ptk, pmk])
                    pt2, pt2k = PT[nxt][c]
                    P.copy("dve", pt2[0:L, :, 0:L], psB[0:L, :, 0:L], reads=[pkB], writes=[pt2k])
                if c >= 1:
                    pend_.append(c - 1)
                    cc_ = pend_.pop(0)
                    pm2_, pm2k_ = Pm[nxt][cc_]
                    tb, tbk = TTb[cur][cc_]
                    psC, pkC = mm8(cc_, lambda h: pm2_[0:L, h, 0:L], lambda h: tb[0:L, h, 0:L], [pm2k_, tbk])
                    tf, tfk = TTf[cc_]
                    P.tt("dve", tf[0:L, :, 0:L], tf[0:L, :, 0:L], psC[0:L, :, 0:L], ALU.add, reads=[pkC, tfk], writes=[tfk])
                    tb2, tb2k = TTb[nxt][cc_]
                    P.copy("act", tb2[0:L, :, 0:L], tf[0:L, :, 0:L], reads=[tfk], writes=[tb2k])
            cc_ = NCH - 1
            pm2_, pm2k_ = Pm[nxt][cc_]
            tb, tbk = TTb[cur][cc_]
            psC, pkC = mm8(cc_, lambda h: pm2_[0:L, h, 0:L], lambda h: tb[0:L, h, 0:L], [pm2k_, tbk])
            tf, tfk = TTf[cc_]
            P.tt("dve", tf[0:L, :, 0:L], tf[0:L, :, 0:L], psC[0:L, :, 0:L], ALU.add, reads=[pkC, tfk], writes=[tfk])
            tb2, tb2k = TTb[nxt][cc_]
            P.copy("act", tb2[0:L, :, 0:L], tf[0:L, :, 0:L], reads=[tfk], writes=[tb2k])
            cur = nxt
        TTfin = TTb[cur]
    for c in range(NCH):
        cs = cs_of(c)
        if delta:
            psA, pkA = mm8(c, lambda h: k8[:, h, cs], lambda h: a8[:, h, cs], [k8k, a8k])
            m_, mk_ = Mak[c]
            P.tt("dve", m_[0:L, :, 0:L], psA[0:L, :, 0:L], k["m_su"][0:L, :, 0:L], ALU.mult, reads=[pkA, "m_su"], writes=[mk_])
            psA, pkA = mm8(c, lambda h: b8[:, h, cs], lambda h: r8[:, h, cs], [b8k, r8k])
            m_, mk_ = Mrb[c]
            P.tt("dve", m_[0:L, :, 0:L], psA[0:L, :, 0:L], k["m_ui"][0:L, :, 0:L], ALU.mult, reads=[pkA, "m_ui"], writes=[mk_])
        psA, pkA = mm8(c, lambda h: k8[:, h, cs], lambda h: r8[:, h, cs], [k8k, r8k])
        m_, mk_ = Mrk[c]
        P.tt("dve", m_[0:L, :, 0:L], psA[0:L, :, 0:L], k["m_ui"][0:L, :, 0:L], ALU.mult, reads=[pkA, "m_ui"], writes=[mk_])
    W1r = Rot(C, ts, "c2_W1", [64, 8, 64], BF16, 2)
    Ur = Rot(C, ts, "c2_U", [64, 8, 64], BF16, 2)
    yt, ytk = yT8
    el, elk = eposL
    for c in range(NCH):
        cs = cs_of(c)
        Kt, Ktk = Ktok[c]
        Vt_, Vtk = Vtok[c]
        mrk, mrkk = Mrk[c]
        if delta:
            Bt, Btk = Btok[c]
            mak, makk = Mak[c]
            mrb, mrbk = Mrb[c]
            tb, tbk = TTfin[c]
            ps, pk = psH.next()
            for h in range(8):
                P.mm(ps[0:L, h, :], a8[:, h, cs], Sb[:, h, :], start=True, stop=False, reads=[a8k, "Sb"], writes=[pk])
                P.mm(ps[0:L, h, :], mak[0:L, h, 0:L], Vt_[0:L, h, :], start=False, stop=True, reads=[makk, Vtk], writes=[pk])
            W1, W1k = W1r.next()
            P.copy("act", W1[0:L, :, :], ps[0:L, :, :], reads=[pk], writes=[W1k])
            ps, pk = psH.next()
            for h in range(8):
                P.mm(ps[0:L, h, :], tb[0:L, h, 0:L], W1[0:L, h, :], reads=[tbk, W1k], writes=[pk])
            U, Uk = Ur.next()
            P.copy("dve", U[0:L, :, :], ps[0:L, :, :], reads=[pk], writes=[Uk])
        ps, pk = psH.next()
        for h in range(8):
            P.mm(ps[:, h, 0:L], Sb[:, h, :], r8[:, h, cs], start=True, stop=False, reads=["Sb", r8k], writes=[pk])
            if delta:
                P.mm(ps[:, h, 0:L], U[0:L, h, :], mrb[0:L, h, 0:L], start=False, stop=False, reads=[Uk, mrbk], writes=[pk])
            P.mm(ps[:, h, 0:L], Vt_[0:L, h, :], mrk[0:L, h, 0:L], start=False, stop=True, reads=[Vtk, mrkk], writes=[pk])
        P.copy("act", yt[:, :, cs], ps[:, :, 0:L], reads=[pk], writes=[ytk])
        ps, pk = psH.next()
        for h in range(8):
            if delta:
                P.mm(ps[:, h, :], Bt[0:L, h, :], U[0:L, h, :], start=True, stop=False, reads=[Btk, Uk], writes=[pk])
            P.mm(ps[:, h, :], Kt[0:L, h, :], Vt_[0:L, h, :], start=(not delta), stop=True, reads=[Ktk, Vtk], writes=[pk])
        skeys = [("S32", h) for h in range(8)]
        P.tt("dve", S32[:, :, :], S32[:, :, :], ps[:, :, :], ALU.add, reads=[pk] + skeys, writes=skeys)
        for h in range(8):
            P.ts("dve", S32[:, h, :], S32[:, h, :], el[:, h, c:c + 1], None, ALU.mult, reads=[("S32", h), elk], writes=[("S32", h)])
        P.copy("act", Sb[:, :, :], S32[:, :, :], reads=skeys, writes=["Sb"])


def to8(P, dst8, dkey, src4, skey, q="sp"):
    d4 = dst8[:, :, :].rearrange("p (a e) t -> p a e t", e=2)
    for e in range(2):
        P.dma(d4[:, :, e, :], src4[e * 64:(e + 1) * 64, :, :], reads=[skey], writes=[dkey], q=q)


def rwkv_mixer2(C, job, s, I, prm):
    P, k = C.P, C.k
    T = job.Ts[s]
    L = min(64, T)
    SEG = min(256, T)
    NCH = SEG // L
    base = job.bases[s]
    S8K = [("S32", h) for h in range(8)]
    with contextlib.ExitStack() as ts:
        S32 = C.sb(ts, "S32", [64, 8, 64], F32)
        Sb = C.sb(ts, "Sb", [64, 8, 64], BF16)
        k4, v4, b4 = [C.sb(ts, n, [128, 4, SEG], BF16) for n in ("k4", "v4", "b4")]
        r8, a8, b8, k8, v8, t8 = [C.sb(ts, n, [64, 8, SEG], BF16) for n in ("r8", "a8", "b8", "k8", "v8", "t8")]
        el = C.sb(ts, "el", [64, 8, NCH], F32)
        yT8 = C.sb(ts, "yT8", [64, 8, SEG], F32)
        sg = C.sb(ts, "sg", [128, SEG], BF16)
        zlast = C.sb(ts, "zlast", [128, 14], F32)
        mu, w0, a0, k_k, k_a, r_k, omka, wa2, g2, lng8, lnb8 = [prm[n] for n in
            ("mu", "w0", "a0", "k_k", "k_a", "r_k", "omka", "wa2", "g2", "lng8", "lnb8")]
        pk_ = ["rwprm", "rwprm2"]
        with contextlib.ExitStack() as t0s:
            if job.rwkv_s0[s] is None:
                P.memset("pool", S32[:], 0.0, writes=S8K)
            else:
                s0t = C.sb(t0s, "s0t", [64, 8, 64], F32)
                psI = C.ps(t0s, "rw_psI", [64, 8, 64], F32)
                P.dma(s0t[:], job.rwkv_s0[s].rearrange("h v k -> v h k"), writes=["s0t"])
                for h in range(8):
                    P.tr(psI[:, h, :], s0t[:, h, :], k["identF"][0:64, 0:64], reads=["s0t", "identF"], writes=["psI"])
                P.copy("dve", S32[:], psI[:], reads=["psI"], writes=S8K)
            P.copy("act", Sb[:], S32[:], reads=S8K, writes=["Sb"])
            P.flush()
        for t0 in range(0, T, SEG):
            col0 = base + t0
            with contextlib.ExitStack() as tp:
                psP = Rot(C, tp, "rw_psP", [128, 512], F32, 4, psum=True)
                zt = C.sb(tp, "zt", [128, 14, SEG + 1], F32)
                dd = C.sb(tp, "dd", [128, 14, SEG], F32)
                zs = C.sb(tp, "zs", [128, 14, SEG], F32)
                f4 = lambda nm: C.sb(tp, nm, [128, 4, SEG], F32)
                b4_ = lambda nm: C.sb(tp, nm, [128, 4, SEG], BF16)
                lw, asig, kkn, kmod, cc, epos, eneg, eprev, tmpa, tmpb_ = [f4(n) for n in
                    ("lw", "asig", "kkn", "kmod", "cc", "epos", "eneg", "eprev", "tmpa", "tmpb")]
                rT4, aT4, t4, sqb = [b4_(n) for n in ("rT4", "aT4", "t4", "sqb")]
                tw = C.sb(tp, "tw", [128, SEG], BF16)
                zrows = job.zT[0:A_COLS, :].rearrange("(c p) t -> p c t", p=128)
                zk = [("zT", job.name, nb, i) for nb in range(14) for i in range(col0 // 512, (col0 + SEG - 1) // 512 + 1)]
                if t0 == 0:
                    P.dma(zt[:, :, 1:SEG + 1], zrows[:, :, col0:col0 + SEG], reads=zk, writes=["zt"])
                    if job.rwkv_shift0[s] is None:
                        P.memset("pool", zt[:, :, 0:1], 0.0, writes=["zt"])
                    else:
                        P.dma(zt[:, :, 0], job.rwkv_shift0[s].rearrange("(c p) -> p c", p=128), writes=["zt"], allow_slow_non_contiguous=True)
                else:
                    P.dma(zt[:, :, 1:SEG + 1], zrows[:, :, col0:col0 + SEG], reads=zk, writes=["zt"])
                    P.copy("pool", zt[:, :, 0], zlast[:, :], reads=["zlast"], writes=["zt"])
                P.copy("pool", zlast[:, :], zt[:, :, SEG], reads=["zt"], writes=["zlast"])
                if t0 + SEG == T:
                    P.dma(job.rwkv_shift_out[s].rearrange("(c p) -> p c", p=128), zlast[:, :], reads=["zlast"], writes=[("rwsh", job.name, s)],
                          q="pool", allow_slow_non_contiguous=True)
                P.tt("dve", dd[:], zt[:, :, 0:SEG], zt[:, :, 1:SEG + 1], ALU.subtract, reads=["zt"], writes=["dd"])
                for blk in range(14):
                    P.stt(zs[:, blk, :], dd[:, blk, :], mu[:, blk:blk + 1], zt[:, blk, 1:SEG + 1], ALU.mult, ALU.add, reads=["dd", "zt"] + pk_, writes=["zs"])
                P.act(tw[0:64, :], zs[0:64, 12, :], AF.Tanh, reads=["zs"], writes=["tw"])
                P.copy("dve", tw[64:128, :], zs[64:128, 12, :], reads=["zs"], writes=["tw"])
                P.act(sg[:], zs[:, 13, :], AF.Sigmoid, reads=["zs"], writes=["sg"])
                for blk in range(4):
                    bs = slice(blk * 128, (blk + 1) * 128)
                    ps, pk = psP.next()
                    P.mm(ps[:, 0:SEG], wa2[0:64, bs], tw[0:64, :], reads=["tw"] + pk_, writes=[pk])
                    P.act(lw[:, blk, :], ps[:, 0:SEG], AF.Sigmoid, bias=w0[:, blk:blk + 1], reads=[pk] + pk_, writes=["lw"])
                    ps, pk = psP.next()
                    P.mm(ps[:, 0:SEG], wa2[64:128, bs], tw[64:128, :], reads=["tw"] + pk_, writes=[pk])
                    P.act(asig[:, blk, :], ps[:, 0:SEG], AF.Sigmoid, bias=a0[:, blk:blk + 1], reads=[pk] + pk_, writes=["asig"])
                P.ts("dve", lw[:], lw[:], -DECAY_C, None, ALU.mult, reads=["lw"], writes=["lw"])
                for blk in range(4):
                    P.ts("dve", tmpa[:, blk, :], zs[:, 4 + blk, :], k_k[:, blk:blk + 1], None, ALU.mult, reads=["zs"] + pk_, writes=["tmpa"])
                P.act(sqb[:], tmpa[:], AF.Square, reads=["tmpa"], writes=["sqb"])
                for blk in range(4):
                    ps, pk = psP.next()
                    P.mm(ps[:, 0:SEG], k["bo"][:], sqb[:, blk, :], reads=["sqb", "bo"], writes=[pk])
                    P.act(tmpb_[:, blk, :], ps[:, 0:SEG], AF.Sqrt, reads=[pk], writes=["tmpb"])
                P.ts("dve", tmpb_[:], tmpb_[:], 1e-12, None, ALU.max, reads=["tmpb"], writes=["tmpb"])
                P.op("dve", lambda e: e.reciprocal(tmpb_[:], tmpb_[:]), reads=["tmpb"], writes=["tmpb"])
                P.tt("dve", kkn[:], tmpa[:], tmpb_[:], ALU.mult, reads=["tmpa", "tmpb"], writes=["kkn"])
                for blk in range(4):
                    P.ts("dve", tmpa[:, blk, :], asig[:, blk, :], k_a[:, blk:blk + 1], omka[:, blk:blk + 1], ALU.mult, ALU.add,
                         reads=["asig"] + pk_, writes=["tmpa"])
                P.tt("dve", kmod[:], tmpa[:], zs[:, 4:8, :], ALU.mult, reads=["tmpa", "zs"], writes=["kmod"])
                rm = k["rmask%d" % L]
                for blk in range(4):
                    P.op("dve", lambda e, blk=blk: e.tensor_tensor_scan(cc[:, blk, :], rm[:, 0:SEG], lw[:, blk, :], 0.0, ALU.mult, ALU.add),
                         reads=["lw", "rmask%d" % L], writes=["cc"])
                P.act(epos[:], cc[:], AF.Exp, reads=["cc"], writes=["epos"])
                P.act(eneg[:], cc[:], AF.Exp, scale=-1.0, reads=["cc"], writes=["eneg"])
                P.tt("dve", tmpa[:], cc[:], lw[:], ALU.subtract, reads=["cc", "lw", "kmod"], writes=["tmpa"])
                P.act(eprev[:], tmpa[:], AF.Exp, reads=["tmpa"], writes=["eprev"])
                P.tt("dve", rT4[:], zs[:, 0:4, :], epos[:], ALU.mult, reads=["zs", "epos"], writes=["rT4"])
                P.stt(aT4[:], kkn[:], -1.0, eprev[:], ALU.mult, ALU.mult, reads=["kkn", "eprev"], writes=["aT4"])
                P.tt("pool", tmpb_[:], kkn[:], asig[:], ALU.mult, reads=["kkn", "asig"], writes=["tmpb"])
                P.tt("dve", b4[:], tmpb_[:], eneg[:], ALU.mult, reads=["tmpb", "eneg"], writes=["b4"])
                P.tt("dve", k4[:], kmod[:], eneg[:], ALU.mult, reads=["kmod", "eneg"], writes=["k4"])
                P.copy("act", v4[:], zs[:, 8:12, :], reads=["zs"], writes=["v4"])
                for blk in range(4):
                    P.stt(t4[:, blk, :], zs[:, blk, :], r_k[:, blk:blk + 1], kmod[:, blk, :], ALU.mult, ALU.mult, reads=["zs", "kmod"] + pk_, writes=["t4"])
                for dst, dkey, src, skey in ((r8, "r8", rT4, "rT4"), (a8, "a8", aT4, "aT4"), (b8, "b8", b4, "b4"), (k8, "k8", k4, "k4"),
                                             (v8, "v8", v4, "v4"), (t8, "t8", t4, "t4")):
                    to8(P, dst, dkey, src, skey)
                ec = epos[:, :, :].rearrange("p a (c l) -> p a c l", l=L)
                e4 = el[:, :, :].rearrange("p (a e) c -> p a e c", e=2)
                ecomp = C.sb(tp, "ecomp", [128, 4, NCH], F32)
                P.copy("pool", ecomp[:], ec[:, :, :, L - 1], reads=["epos"], writes=["ecomp"])
                for e in range(2):
                    P.dma(e4[:, :, e, :], ecomp[e * 64:(e + 1) * 64, :, :], reads=["ecomp"], writes=["el"], allow_slow_non_contiguous=True)
                P.flush()
            with contextlib.ExitStack() as tsc:
                ops8 = dict(r=(r8, "r8"), k=(k8, "k8"), a=(a8, "a8"), b=(b8, "b8"))
                tok4 = dict(k=(k4, "k4"), v=(v4, "v4"), b=(b4, "b4"))
                chunk_scan2(C, tsc, SEG, L, True, ops8, tok4, (el, "el"), S32, Sb, (yT8, "yT8"))
                P.flush()
            with contextlib.ExitStack() as tq:
                psQ = Rot(C, tq, "rw_psQ", [64, 4, SEG], F32, 3 if SEG > 128 else 6, psum=True)
                yb8 = C.sb(tq, "yb8", [64, 8, SEG], BF16)
                d8 = C.sb(tq, "d8", [64, 8, SEG], F32)
                rs8 = C.sb(tq, "rs8", [64, 8, SEG], F32)
                tm8 = C.sb(tq, "tm8", [64, 8, SEG], F32)
                o8 = C.sb(tq, "o8", [64, 8, SEG], BF16)
                one64 = k["onesB"][0:64, 0:64]
                P.copy("act", yb8[:], yT8[:], reads=["yT8"], writes=["yb8"])
                for half in range(2):
                    hs = slice(half * 4, half * 4 + 4)
                    ps, pk = psQ.next()
                    for i in range(4):
                        P.mm(ps[:, i, :], one64, yb8[:, half * 4 + i, :], reads=["yb8", "onesB"], writes=[pk])
                    P.stt(d8[:, hs, :], ps[:, :, :], -1.0 / 64, yT8[:, hs, :], ALU.mult, ALU.add, reads=[pk, "yT8"], writes=[("d8", half)])
                P.act(yb8[:], d8[:], AF.Square, reads=[("d8", 0), ("d8", 1), "yb8"], writes=["yb8"])
                for half in range(2):
                    hs = slice(half * 4, half * 4 + 4)
                    ps, pk = psQ.next()
                    for i in range(4):
                        P.mm(ps[:, i, :], one64, yb8[:, half * 4 + i, :], reads=["yb8", "onesB"], writes=[pk])
                    P.act(rs8[:, hs, :], ps[:, :, :], AF.Ln, bias=k["eps_gn"][0:64, 0:1], scale=1.0 / 64, reads=[pk, "epsv"], writes=[("rs8", half)])
                P.act(rs8[:], rs8[:], AF.Exp, scale=-0.5, reads=[("rs8", 0), ("rs8", 1)], writes=["rs8r"])
                P.tt("dve", d8[:], d8[:], rs8[:], ALU.mult, reads=[("d8", 0), ("d8", 1), "rs8r"], writes=["d8n"])
                for h in range(8):
                    P.ts("dve", d8[:, h, :], d8[:, h, :], lng8[:, h:h + 1], lnb8[:, h:h + 1], ALU.mult, ALU.add, reads=["d8n"] + pk_, writes=[("d8a", h)])
                for half in range(2):
                    hs = slice(half * 4, half * 4 + 4)
                    ps, pk = psQ.next()
                    for i in range(4):
                        P.mm(ps[:, i, :], one64, t8[:, half * 4 + i, :], reads=["t8", "onesB"], writes=[pk])
                    P.tt("dve", tm8[:, hs, :], ps[:, :, :], v8[:, hs, :], ALU.mult, reads=[pk, "v8"], writes=[("tm8", half)])
                P.tt("dve", d8[:], d8[:], tm8[:], ALU.add, reads=[("d8a", h) for h in range(8)] + [("tm8", 0), ("tm8", 1)], writes=["d8f"])
                for half in range(2):
                    hs = slice(half * 4, half * 4 + 4)
                    ps, pk = psQ.next()
                    for i in range(4):
                        h = half * 4 + i
                        P.mm(ps[:, i, :], g2[:, h * 64:(h + 1) * 64], sg[:, :], reads=["sg"] + pk_, writes=[pk])
                    P.tt("dve", o8[:, hs, :], ps[:, :, :], d8[:, hs, :], ALU.mult, reads=[pk, "d8f"], writes=[("o8", half)])
                P.dma(job.mixT[0:512, col0:col0 + SEG].rearrange("(h k) t -> k h t", k=64), o8[:], reads=[("o8", 0), ("o8", 1)],
                      writes=[("mixT", job.name, c, col0 // 512, hh) for c in range(4) for hh in range(2)], q="pool")
                P.flush()
        with contextlib.ExitStack() as tf_:
            psO = C.ps(tf_, "rw_psO", [64, 8, 64], F32)
            so = C.sb(tf_, "so", [64, 8, 64], F32)
            for h in range(8):
                P.tr(psO[:, h, :], S32[:, h, :], k["identF"][0:64, 0:64], reads=S8K + ["identF"], writes=["psO"])
            P.copy("dve", so[:], psO[:], reads=["psO"], writes=["so"])
            P.dma(job.rwkv_out[s].rearrange("h v k -> v h k"), so[:], reads=["so"], writes=[("rwst", job.name, s)], q="pool")
            P.flush()


def hgrn_mixer2(C, job, s, I, prm):
    P, k = C.P, C.k
    T = job.Ts[s]
    L = min(32, T)
    SEG = min(256, T)
    NCH = SEG // L
    base = job.bases[s]
    S8K = [("S32", h) for h in range(8)]
    with contextlib.ExitStack() as ts:
        S32 = C.sb(ts, "S32", [64, 8, 64], F32)
        Sb = C.sb(ts, "Sb", [64, 8, 64], BF16)
        k4, v4 = [C.sb(ts, n, [128, 4, SEG], BF16) for n in ("k4", "v4")]
        r8, k8 = [C.sb(ts, n, [64, 8, SEG], BF16) for n in ("r8", "k8")]
        el = C.sb(ts, "el", [64, 8, NCH], F32)
        yT8 = C.sb(ts, "yT8", [64, 8, SEG], F32)
        lb, oml, noml, ng8 = prm["lb"], prm["oml"], prm["noml"], prm["ng8"]
        pk_ = ["hgprm"]
        if job.hgrn_s0[s] is None:
            P.memset("pool", S32[:], 0.0, writes=S8K)
        else:
            P.dma(S32[:], job.hgrn_s0[s].rearrange("h k v -> k h v"), writes=S8K)
        P.copy("act", Sb[:], S32[:], reads=S8K, writes=["Sb"])
        P.flush()
        rm = k["rmask%d" % L]
        for t0 in range(0, T, SEG):
            col0 = base + t0
            with contextlib.ExitStack() as tp:
                z4 = C.sb(tp, "z4", [128, 12, SEG], F32)
                f4 = lambda nm: C.sb(tp, nm, [128, 4, SEG], F32)
                qf, sgf, lw, kin, cc, epos, eneg = [f4(n) for n in ("qf", "sgf", "lw", "kin", "cc", "epos", "eneg")]
                rT4 = C.sb(tp, "rT4", [128, 4, SEG], BF16)
                zk = [("zT", job.name, nb, i) for nb in range(12, 24) for i in range(col0 // 512, (col0 + SEG - 1) // 512 + 1)]
                P.dma(z4[:], job.zT[1536:3072, col0:col0 + SEG].rearrange("(c p) t -> p c t", p=128), reads=zk, writes=["z4"])
                P.act(qf[:], z4[:, 0:4, :], AF.Silu, reads=["z4"], writes=["qf"])
                P.act(sgf[:], z4[:, 4:8, :], AF.Sigmoid, reads=["z4"], writes=["sgf"])
                for blk in range(4):
                    P.ts("dve", lw[:, blk, :], sgf[:, blk, :], oml[:, blk:blk + 1], lb[:, blk:blk + 1], ALU.mult, ALU.add, reads=["sgf"] + pk_, writes=["lw"])
                    P.ts("dve", kin[:, blk, :], sgf[:, blk, :], noml[:, blk:blk + 1], oml[:, blk:blk + 1], ALU.mult, ALU.add, reads=["sgf"] + pk_, writes=["kin"])
                P.act(lw[:], lw[:], AF.Ln, reads=["lw"], writes=["lw"])
                for blk in range(4):
                    P.op("dve", lambda e, blk=blk: e.tensor_tensor_scan(cc[:, blk, :], rm[:, 0:SEG], lw[:, blk, :], 0.0, ALU.mult, ALU.add),
                         reads=["lw", "rmask%d" % L], writes=["cc"])
                P.act(epos[:], cc[:], AF.Exp, reads=["cc"], writes=["epos"])
                P.act(eneg[:], cc[:], AF.Exp, scale=-1.0, reads=["cc"], writes=["eneg"])
                P.tt("dve", rT4[:], qf[:], epos[:], ALU.mult, reads=["qf", "epos"], writes=["rT4"])
                P.tt("dve", k4[:], kin[:], eneg[:], ALU.mult, reads=["kin", "eneg"], writes=["k4"])
                P.copy("pool", v4[:], z4[:, 8:12, :], reads=["z4"], writes=["v4"])
                to8(P, r8, "r8", rT4, "rT4")
                to8(P, k8, "k8", k4, "k4")
                ec = epos[:, :, :].rearrange("p a (c l) -> p a c l", l=L)
                e4 = el[:, :, :].rearrange("p (a e) c -> p a e c", e=2)
                ecomp = C.sb(tp, "ecomp", [128, 4, NCH], F32)
                P.copy("pool", ecomp[:], ec[:, :, :, L - 1], reads=["epos"], writes=["ecomp"])
                for e in range(2):
                    P.dma(e4[:, :, e, :], ecomp[e * 64:(e + 1) * 64, :, :], reads=["ecomp"], writes=["el"], allow_slow_non_contiguous=True)
                P.flush()
            with contextlib.ExitStack() as tsc:
                chunk_scan2(C, tsc, SEG, L, False, dict(r=(r8, "r8"), k=(k8, "k8")), dict(k=(k4, "k4"), v=(v4, "v4")),
                            (el, "el"), S32, Sb, (yT8, "yT8"))
                P.flush()
            with contextlib.ExitStack() as tq:
                psQ = Rot(C, tq, "hg_psQ", [64, 4, SEG], F32, 3 if SEG > 128 else 6, psum=True)
                yb8 = C.sb(tq, "yb8", [64, 8, SEG], BF16)
                rs8 = C.sb(tq, "rs8", [64, 8, SEG], F32)
                zg8 = C.sb(tq, "zg8", [64, 8, SEG], F32)
                o8 = C.sb(tq, "o8", [64, 8, SEG], BF16)
                one64 = k["onesB"][0:64, 0:64]
                zkg = [("zT", job.name, nb, i) for nb in range(24, 28) for i in range(col0 // 512, (col0 + SEG - 1) // 512 + 1)]
                P.dma(zg8[:], job.zT[3072:3584, col0:col0 + SEG].rearrange("(h k) t -> k h t", k=64), reads=zkg, writes=["zg8"])
                P.act(zg8[:], zg8[:], AF.Silu, reads=["zg8"], writes=["zg8"])
                P.act(yb8[:], yT8[:], AF.Square, reads=["yT8"], writes=["yb8"])
                for half in range(2):
                    hs = slice(half * 4, half * 4 + 4)
                    ps, pk = psQ.next()
                    for i in range(4):
                        P.mm(ps[:, i, :], one64, yb8[:, half * 4 + i, :], reads=["yb8", "onesB"], writes=[pk])
                    P.act(rs8[:, hs, :], ps[:, :, :], AF.Ln, bias=k["eps_rms"][0:64, 0:1], scale=1.0 / 64, reads=[pk, "epsv"], writes=[("rs8", half)])
                P.act(rs8[:], rs8[:], AF.Exp, scale=-0.5, reads=[("rs8", 0), ("rs8", 1)], writes=["rs8r"])
                P.tt("dve", rs8[:], rs8[:], yT8[:], ALU.mult, reads=["rs8r", "yT8"], writes=["rs8y"])
                for h in range(8):
                    P.stt(o8[:, h, :], rs8[:, h, :], ng8[:, h:h + 1], zg8[:, h, :], ALU.mult, ALU.mult, reads=["rs8y", "zg8"] + pk_, writes=[("o8", h)])
                P.dma(job.mixT[512:1024, col0:col0 + SEG].rearrange("(h k) t -> k h t", k=64), o8[:], reads=[("o8", h) for h in range(8)],
                      writes=[("mixT", job.name, 4 + c, col0 // 512, hh) for c in range(4) for hh in range(2)], q="pool")
                P.flush()
        P.dma(job.hgrn_out[s].rearrange("h k v -> k h v"), S32[:], reads=S8K, writes=[("hgst", job.name, s)], q="pool")
        P.flush()
```

```python
import os
import numpy as np
import concourse.bass as bass
import concourse.mybir as mybir
from concourse.bass_utils import run_bass_kernel_spmd

F32 = mybir.dt.float32
BF16 = mybir.dt.bfloat16
AF = mybir.ActivationFunctionType
ALU = mybir.AluOpType
AX = mybir.AxisListType

N_CORES = 8


class Prog:
    ENGS = ("pe", "act", "dve", "pool", "sp")
    DMA_SLOTS = {"sp": 20, "pool": 12, "act": 8}

    def __init__(self, nc, stack):
        self.nc = nc
        self.ops = []
        self.same_engine_sync = True
        self.esem = {e: stack.enter_context(nc.semaphore("s_" + e)) for e in ("pe", "act", "dve", "pool")}
        self.dsem = {q: [stack.enter_context(nc.semaphore("d_%s%d" % (q, i))) for i in range(k)]
                     for q, k in self.DMA_SLOTS.items()}
        self.ecount = {e: 0 for e in self.esem}
        self.dcount = {q: 0 for q in self.dsem}
        self.slot_last = {}
        self.last_w = {}
        self.readers = {}
        self.known = {e: {} for e in self.ENGS}
        self.final_dma = {}
        self.n_total = 0

    def op(self, eng, fn, reads=(), writes=(), dma=False):
        self.n_rec = getattr(self, "n_rec", 0) + 1
        if self.n_rec == int(os.environ.get("K_SHOW", "-1")):
            import traceback
            traceback.print_stack(limit=4)
            print("SHOW op", eng, reads, writes)
        if self.n_rec > int(os.environ.get("K_MAXOPS", "100000000")):
            return
        self.ops.append(dict(eng=eng, fn=fn, reads=tuple(reads), writes=tuple(writes), dma=dma, serial=getattr(self, "pe_serial", False)))

    def mm(self, out, lhsT, rhs, start=True, stop=True, reads=(), writes=()):
        self.op("pe", lambda e: e.matmul(out, lhsT, rhs, start=start, stop=stop), reads, writes)

    def tr(self, out, in_, ident, reads=(), writes=()):
        self.op("pe", lambda e: e.transpose(out, in_, ident), reads, writes)

    def act(self, out, in_, func, bias=0.0, scale=1.0, reads=(), writes=(), accum_out=None):
        if accum_out is None:
            self.op("act", lambda e: e.activation(out, in_, func, bias=bias, scale=scale), reads, writes)
        else:
            self.op("act", lambda e: e.activation(out, in_, func, bias=bias, scale=scale, accum_out=accum_out), reads, writes)

    def tt(self, eng, out, in0, in1, op, reads=(), writes=()):
        self.op(eng, lambda e: e.tensor_tensor(out, in0, in1, op), reads, writes)

    def ts(self, eng, out, in0, s1, s2, op0, op1=None, reads=(), writes=()):
        if op1 is None:
            self.op(eng, lambda e: e.tensor_scalar(out, in0, s1, None, op0), reads, writes)
        else:
            self.op(eng, lambda e: e.tensor_scalar(out, in0, s1, s2, op0, op1), reads, writes)

    def stt(self, out, in0, scalar, in1, op0, op1, reads=(), writes=()):
        self.op("dve", lambda e: e.scalar_tensor_tensor(out, in0, scalar, in1, op0, op1), reads, writes)

    def copy(self, eng, out, in_, reads=(), writes=()):
        if eng == "act":
            self.op("act", lambda e: e.copy(out, in_), reads, writes)
        else:
            self.op(eng, lambda e: e.tensor_copy(out, in_), reads, writes)

    def memset(self, eng, ap, val, writes=()):
        self.op(eng, lambda e: e.memset(ap, val), (), writes)

    def dma(self, out, in_, reads=(), writes=(), q="sp", **kw):
        self.op(q, lambda e: e.dma_start(out=out, in_=in_, **kw), reads, writes, dma=True)

    def semof(self, key):
        return self.esem[key[1]] if key[0] == "e" else self.dsem[key[1]][key[2]]

    def flush(self, final=False):
        import inspect
        fr = inspect.stack()[1]
        if not hasattr(self, "flush_log"):
            self.flush_log = []
        _cnt = {}
        for _o in self.ops:
            _cnt[_o["eng"]] = _cnt.get(_o["eng"], 0) + 1
        self.flush_log.append(("%s:%d" % (fr.function, fr.lineno), len(self.ops), _cnt))
        nc = self.nc
        ops = self.ops
        self.ops = []
        n = len(ops)
        self.n_total += n
        plan = {e: [] for e in self.ENGS}
        for o in ops:
            e = o["eng"]
            deps = []
            if o["dma"]:
                j = self.dcount[e]
                self.dcount[e] += 1
                k = len(self.dsem[e])
                key = ("d", e, j % k)
                val = 16 * (j // k + 1)
                if key in self.slot_last:
                    deps.append(self.slot_last[key])
                mysig = (key, val, e, True)
                self.slot_last[key] = mysig
                self.final_dma[key] = val
            else:
                self.ecount[e] += 1
                mysig = (("e", e), self.ecount[e], e, False)
            for r in o["reads"]:
                if r in self.last_w:
                    deps.append(self.last_w[r])
            for w in o["writes"]:
                if w in self.last_w:
                    deps.append(self.last_w[w])
                deps.extend(self.readers.get(w, ()))
            for r in o["reads"]:
                self.readers.setdefault(r, []).append(mysig)
            for w in o["writes"]:
                self.last_w[w] = mysig
                self.readers[w] = []
            need = {}
            for (dkey, dval, deng, ddma) in deps:
                if (not ddma) and deng == e:
                    if (not self.same_engine_sync) or e == "pe":
                        continue
                if need.get(dkey, 0) < dval:
                    need[dkey] = dval
            if e == "pe" and o.get("serial") and self.ecount["pe"] > 1:
                need[("e", "pe")] = max(need.get(("e", "pe"), 0), self.ecount["pe"] - 1)
            wl = []
            for dkey, dval in need.items():
                if self.known[e].get(dkey, 0) >= dval:
                    continue
                self.known[e][dkey] = dval
                wl.append((dkey, dval))
            plan[e].append((wl, o["fn"], mysig[0], o["dma"]))

        def run_engine(ename, eng):
            for wl, fn, key, is_dma in plan[ename]:
                for dkey, dval in wl[:-1]:
                    eng.wait_ge(self.semof(dkey), dval)
                ins = fn(eng)
                if wl:
                    ins._wait_ge(self.semof(wl[-1][0]), wl[-1][1])
                ins.then_inc(self.semof(key), 16 if is_dma else 1)
            if final and ename == "sp":
                for key, val in self.final_dma.items():
                    eng.wait_ge(self.semof(key), val)
                for e2 in self.esem:
                    if self.ecount[e2] > 0:
                        eng.wait_ge(self.esem[e2], self.ecount[e2])

        with nc.Block() as block:
            if plan["pe"]:
                @block.tensor
                def _(eng):
                    run_engine("pe", eng)
            if plan["act"]:
                @block.scalar
                def _(eng):
                    run_engine("act", eng)
            if plan["dve"]:
                @block.vector
                def _(eng):
                    run_engine("dve", eng)
            if plan["pool"]:
                @block.gpsimd
                def _(eng):
                    run_engine("pool", eng)
            if plan["sp"] or final:
                @block.sync
                def _(eng):
                    run_engine("sp", eng)
        return n


D = 1024
KC = 8
T_PROMPT = 4096
T_SAMPLE = 16
NSEQ_S = 2
P_FOX = 2048
P_CHUNK = 512
A_COLS = 1792
AB_COLS = 3336
CD_COLS = 3584
DFF = 2816
RMS_EPS = 1e-6
GN_EPS = 64e-5
DECAY_C = float(np.exp(-0.5))


class Ctx:
    def __init__(self, nc, P, st):
        self.nc = nc
        self.P = P
        self.st = st
        self.uid = 0
        self.dram = {}

    def name(self, base):
        self.uid += 1
        return "%s_%d" % (base, self.uid)

    def sb(self, st, base, shape, dt=F32):
        return st.enter_context(self.nc.sbuf_tensor(self.name(base), list(shape), dt))

    def ps(self, st, base, shape, dt=F32):
        return st.enter_context(self.nc.psum_tensor(self.name(base), list(shape), dt))

    def dr(self, base, shape, dt=F32, kind=None):
        if kind is None:
            t = self.nc.dram_tensor(base, list(shape), dt)
        else:
            t = self.nc.dram_tensor(base, list(shape), dt, kind=kind)
        ap = t.ap()
        self.dram[base] = ap
        return ap


class Rot:
    def __init__(self, C, st, base, shape, dt, n, psum=False):
        self.tiles = [(C.ps if psum else C.sb)(st, base, shape, dt) for _ in range(n)]
        self.keys = [(base, C.uid, i) for i in range(n)]
        self.i = -1

    def next(self):
        self.i = (self.i + 1) % len(self.tiles)
        return self.tiles[self.i], self.keys[self.i]


def build_consts(C):
    P, st = C.P, C.st
    k = {}
    identF = C.sb(st, "identF", [128, 128], F32)
    P.memset("pool", identF[:], 1.0, writes=["identF"])
    P.op("pool", lambda e: e.affine_select(identF[:], identF[:], [[-1, 128]], ALU.is_equal, 0.0, base=0, channel_multiplier=1),
         reads=["identF"], writes=["identF"])
    identB = C.sb(st, "identB", [128, 128], BF16)
    P.copy("pool", identB[:], identF[:], reads=["identF"], writes=["identB"])
    flipF = C.sb(st, "flipF", [128, 128], F32)
    P.memset("pool", flipF[:], 1.0, writes=["flipF"])
    P.op("pool", lambda e: e.affine_select(flipF[:], flipF[:], [[1, 128]], ALU.is_equal, 0.0, base=-127, channel_multiplier=1),
         reads=["flipF"], writes=["flipF"])
    onesB = C.sb(st, "onesB", [128, 128], BF16)
    P.memset("pool", onesB[:], 1.0, writes=["onesB"])
    onesF = C.sb(st, "onesF", [128, 128], F32)
    P.memset("pool", onesF[:], 1.0, writes=["onesF"])
    bo = C.sb(st, "blockones", [128, 128], BF16)
    P.memset("pool", bo[:], 0.0, writes=["bo"])
    P.memset("pool", bo[0:64, 0:64], 1.0, writes=["bo"])
    P.memset("pool", bo[64:128, 64:128], 1.0, writes=["bo"])
    triF = C.sb(st, "triF", [128, 128], F32)
    P.memset("pool", triF[:], 1.0, writes=["triF"])
    P.op("pool", lambda e: e.affine_select(triF[:], triF[:], [[1, 128]], ALU.is_ge, 0.0, base=0, channel_multiplier=-1),
         reads=["triF"], writes=["triF"])
    def cmask(nm, pattern, base, cm, op):
        m = C.sb(st, nm, [64, 8, 64], F32)
        P.memset("pool", m[:], 1.0, writes=[nm])
        for h in range(8):
            P.op("pool", lambda e, h=h: e.affine_select(m[:, h, :], m[:, h, :], pattern, op, 0.0, base=base, channel_multiplier=cm),
                 reads=[nm], writes=[nm])
        return m
    k["m_su"] = cmask("m_su", [[1, 64]], 0, -1, ALU.is_gt)
    k["m_ui"] = cmask("m_ui", [[1, 64]], 0, -1, ALU.is_ge)
    k["m_sl"] = cmask("m_sl", [[-1, 64]], 0, 1, ALU.is_gt)
    k["i8"] = cmask("i8", [[-1, 64]], 0, 1, ALU.is_equal)
    cm_f = C.sb(st, "cm_f", [128, 4, 512], F32)
    P.memset("pool", cm_f[:], 1.0, writes=["cm_f"])
    for d in range(4):
        P.op("pool", lambda e, d=d: e.affine_select(cm_f[:, d, :], cm_f[:, d, :], [[1, 512]], ALU.is_ge, 0.0, base=-128 * d, channel_multiplier=-1),
             reads=["cm_f"], writes=["cm_f"])
    cmB = C.sb(st, "cmB", [128, 4, 512], BF16)
    P.copy("pool", cmB[:], cm_f[:], reads=["cm_f"], writes=["cmB"])
    k.update(identF=identF, identB=identB, flipF=flipF, onesB=onesB, onesF=onesF, bo=bo, triF=triF, cmB=cmB)
    C.k = k
    P.flush()


import contextlib


def cast_weight(C, W, K, N, name):
    P = C.P
    NB = (N + 127) // 128
    kc_n = K // 128
    Wb = C.dr(name, [NB, 128, kc_n, 128], BF16)
    nfull = N // 128
    rem = N - nfull * 128
    with contextlib.ExitStack() as st:
        ld = Rot(C, st, "cw_ld", [128, NB * 128], F32, 2)
        cb = Rot(C, st, "cw_cb", [128, NB * 128], BF16, 2)
        for kc in range(kc_n):
            t, tk = ld.next()
            b, bk = cb.next()
            P.dma(t[:, 0:N], W[kc * 128:(kc + 1) * 128, :], writes=[tk])
            eng = ("dve", "act")[kc % 2]
            P.copy(eng, b[:, 0:N], t[:, 0:N], reads=[tk], writes=[bk])
            if nfull:
                dst = Wb[0:nfull, :, kc, :].rearrange("nb p n -> p nb n")
                src = b[:, 0:nfull * 128].rearrange("p (nb n) -> p nb n", n=128)
                P.dma(dst, src, reads=[bk], writes=[(name, kc)], q="pool")
            if rem:
                P.dma(Wb[nfull, :, kc, 0:rem], b[:, nfull * 128:N], reads=[bk], writes=[(name, kc, "r")], q="pool")
        P.flush()
    return Wb, [(name, kc) for kc in range(kc_n)] + ([(name, kc, "r") for kc in range(kc_n)] if rem else [])


def adaln_phase(C, I, ncol, c_cols):
    P, st, k = C.P, C.st, C.k
    mods = {}
    for l in range(2):
        for j in range(2):
            mods[(l, j)] = C.sb(st, "mod", [128, 24, ncol], F32)
    with contextlib.ExitStack() as ts:
        cT = C.sb(ts, "cT", [128, KC, ncol], F32)
        for j, cap in enumerate(c_cols):
            P.dma(cT[:, :, j], cap.rearrange("(c p) -> p c", p=128), writes=["cT"], allow_slow_non_contiguous=True)
        csT = C.sb(ts, "csT", [128, KC, ncol], F32)
        P.act(csT[:], cT[:], AF.Silu, reads=["cT"], writes=["csT"])
        wl = Rot(C, ts, "ada_w", [128, 3072], F32, 3)
        pacc = [C.ps(ts, "ada_acc", [128, 512], F32) for _ in range(6)]
        ptr = Rot(C, ts, "ada_ptr", [128, 24, 4], F32, 1, psum=True)
        mrow = Rot(C, ts, "ada_mrow", [4, 3072], F32, 2)
        brow = Rot(C, ts, "ada_brow", [4, 3072], F32, 2)
        for l in range(2):
            for j in range(2):
                m = mods[(l, j)]
                bt, btk = brow.next()
                for cc in range(ncol):
                    P.dma(bt[cc:cc + 1, :], I["ada_b"][l, j:j + 1, :], writes=[btk])
                for kc in range(KC):
                    w, wk = wl.next()
                    P.dma(w[:], I["ada_w"][l, j, kc * 128:(kc + 1) * 128, :], writes=[wk])
                    for c6 in range(6):
                        P.mm(pacc[c6][0:ncol, :], csT[:, kc, 0:ncol], w[:, c6 * 512:(c6 + 1) * 512], start=(kc == 0), stop=(kc == KC - 1),
                             reads=[wk, "csT"], writes=[("ada_acc", c6)])
                mr, mrk = mrow.next()
                for c6 in range(6):
                    P.tt("dve", mr[0:ncol, c6 * 512:(c6 + 1) * 512], pacc[c6][0:ncol, :], bt[0:ncol, c6 * 512:(c6 + 1) * 512], ALU.add,
                         reads=[("ada_acc", c6), btk], writes=[mrk])
                pt, ptk = ptr.next()
                for nb in range(24):
                    P.tr(pt[:, nb, 0:ncol], mr[0:ncol, nb * 128:(nb + 1) * 128], k["identF"][0:ncol, 0:ncol], reads=[mrk, "identF"], writes=[ptk])
                P.copy("act", m[:, :, :], pt[:, :, 0:ncol], reads=[ptk], writes=[("mod", l, j)])
        P.flush()
    C.mods = mods


def load_vec_fm(C, st, ap, n, name, q="sp"):
    t = C.sb(st, name, [128, n], F32)
    key = C.name(name)
    C.P.dma(t[:], ap.rearrange("(c p) -> p c", p=128), writes=[key], allow_slow_non_contiguous=True, q=q)
    return t, key


class Job:
    pass


def seq_groups(job, G):
    out = []
    for s in range(job.nseq):
        T = job.Ts[s]
        for t0 in range(0, T, G):
            gs = min(G, T - t0)
            out.append((s, t0, gs, job.bases[s] + t0))
    return out


def x_to_fm(C, job):
    P, k = C.P, C.k
    with contextlib.ExitStack() as st:
        xin = Rot(C, st, "xin", [128, 1024], F32, 2)
        pst = Rot(C, st, "x_ps", [128, 4, 128], F32, 4, psum=True)
        xo = Rot(C, st, "xo", [128, KC, 128], F32, 2)
        for (s, t0, gs, col0) in seq_groups(job, 128):
            t, tk = xin.next()
            P.dma(t[0:gs, :], job.x_in[s][t0:t0 + gs, :], writes=[tk])
            o, ok = xo.next()
            for half in range(2):
                ps, pk = pst.next()
                for c4 in range(4):
                    c = half * 4 + c4
                    P.tr(ps[:, c4, 0:gs], t[0:gs, c * 128:(c + 1) * 128], k["identF"][0:gs, 0:gs], reads=[tk, "identF"], writes=[pk])
                P.copy("act" if half else "dve", o[:, half * 4:half * 4 + 4, 0:gs], ps[:, :, 0:gs], reads=[pk], writes=[ok])
            P.dma(job.xT[:, col0:col0 + gs].rearrange("(c p) t -> p c t", p=128), o[:, :, 0:gs], reads=[ok], writes=[("xT", job.name, c, col0 // 512) for c in range(8)], q="pool")
        P.flush()


def xkeys(job, col0, gs):
    return [("xT", job.name, i) for i in range(col0 // 128, (col0 + gs - 1) // 128 + 1)]


def norm_to_hT(C, st, job, gT, l, j, hT, hkey):
    P, k = C.P, C.k
    m = C.mods[(l, j)]
    with contextlib.ExitStack() as ts:
        gsc = C.sb(ts, "gsc", [128, KC, 3], F32)
        for col in range(3):
            P.stt(gsc[:, :, col], m[:, 8:16, col], 1.0, gT[:], ALU.add, ALU.mult, reads=[("mod", l, j), "smallprm"], writes=["gsc"])
        xg = Rot(C, ts, "xg", [128, KC, 512], F32, 2)
        sq = Rot(C, ts, "sq", [128, KC, 512], BF16, 2)
        pss = Rot(C, ts, "ss_ps", [128, 512], F32, 2, psum=True)
        rs = Rot(C, ts, "rstd", [128, 512], F32, 2)
        tmp = Rot(C, ts, "htmp", [128, 512], F32, 3)
        for (s, t0, gs, col0) in seq_groups(job, 512):
            x, xk = xg.next()
            P.dma(x[:, :, 0:gs], job.xT[:, col0:col0 + gs].rearrange("(c p) t -> p c t", p=128),
                  reads=[key for c in range(8) for key in xk1(job, c, col0, gs)], writes=[xk])
            q, qk = sq.next()
            P.act(q[:, :, 0:gs], x[:, :, 0:gs], AF.Square, reads=[xk], writes=[qk])
            ps, pk = pss.next()
            for c in range(KC):
                P.mm(ps[:, 0:gs], k["onesB"][:], q[:, c, 0:gs], start=(c == 0), stop=(c == KC - 1), reads=[qk, "onesB"], writes=[pk])
            r, rk = rs.next()
            P.act(r[:, 0:gs], ps[:, 0:gs], AF.Ln, bias=k["eps_rms"][:, 0:1], scale=1.0 / D, reads=[pk, "epsv"], writes=[rk])
            P.act(r[:, 0:gs], r[:, 0:gs], AF.Exp, scale=-0.5, reads=[rk], writes=[rk])
            col = job.modcol[s]
            for c in range(KC):
                t_, tk_ = tmp.next()
                P.stt(t_[:, 0:gs], x[:, c, 0:gs], gsc[:, c, col:col + 1], r[:, 0:gs], ALU.mult, ALU.mult, reads=[xk, rk, "gsc"], writes=[tk_])
                P.act(hT[:, c, col0:col0 + gs], t_[:, 0:gs], AF.Identity, bias=m[:, c, col:col + 1], reads=[tk_, ("mod", l, j)], writes=[hkey])
        P.flush()


def proj(C, job, srcT, skey, Wb, wkeys, nbs, kcn, post, G=512):
    P = C.P
    with contextlib.ExitStack() as ts:
        wr = Rot(C, ts, "pw", [128, kcn, 128], BF16, 3)
        pr = Rot(C, ts, "pp", [128, 512], F32, 3, psum=True)
        groups = seq_groups(job, G)
        for nb in nbs:
            w, wk = wr.next()
            P.dma(w[:], Wb[nb], reads=wkeys, writes=[wk])
            for (s, t0, gs, col0) in groups:
                ps, pk = pr.next()
                for kc in range(kcn):
                    P.mm(ps[:, 0:gs], w[:, kc, :], srcT[:, kc, col0:col0 + gs], start=(kc == 0), stop=(kc == kcn - 1),
                         reads=[wk, skey], writes=[pk])
                post(nb, s, t0, gs, col0, ps, pk)


def chunk_scan(C, ts, nseg_cols, L, delta, rT, kT, vT, aT, bT, epos, S32, Spad, yT, keys, pools):
    P, k = C.P, C.k
    P.pe_serial = True
    psT, psH = pools["psT"], pools["psH"]
    nlev = int(np.log2(L))
    kin = [keys[n] for n in ("rT", "kT", "vT")] + ([keys["aT"], keys["bT"]] if delta else [])
    for c in range(nseg_cols // L):
        cs = slice(c * L, (c + 1) * L)
        pads = {}
        for nm, src in (("K", kT), ("V", vT)) + ((("B", bT),) if delta else ()):
            ps, pk = psT.next()
            for blk in range(4):
                P.tr(ps[0:L, blk, :], src[:, blk, cs], k["identB"][:, :], reads=kin + ["identB"], writes=[pk])
            pad, padk = pools["pad" + nm].next()
            for e in range(2):
                P.copy("act" if e else "dve", pad[0:L, :, e, e * 64:(e + 1) * 64], ps[0:L, :, e * 64:(e + 1) * 64], reads=[pk], writes=[padk])
            pads[nm] = (pad, padk)
        Kp, Kk = pads["K"]
        Vp, Vk = pads["V"]

        def hsl(h):
            return h // 2, h % 2, (h % 2) * 64

        def mm8(lhs_of, rhs_of, reads):
            ps, pk = psH.next()
            for h in range(8):
                P.mm(ps[0:L, h, 0:L], lhs_of(h), rhs_of(h), reads=reads, writes=[pk])
            return ps, pk

        def fm(t, h):
            pr, e, pb = hsl(h)
            return t[pb:pb + 64, pr, cs]

        if delta:
            psA, pkA = mm8(lambda h: fm(bT, h), lambda h: fm(aT, h), kin)
            psB, pkB = mm8(lambda h: fm(aT, h), lambda h: fm(bT, h), kin)
            PT, PTk = pools["mb"].next()
            Pm, Pmk = pools["mb"].next()
            TTf, TTfk = pools["ttf"].next()
            TTb, TTbk = pools["mb"].next()
            P.tt("dve", TTf[0:L, :, 0:L], psA[0:L, :, 0:L], k["m_su"][0:L, :, 0:L], ALU.mult, reads=[pkA, "m_su"], writes=[TTfk])
            P.copy("act", PT[0:L, :, 0:L], TTf[0:L, :, 0:L], reads=[TTfk], writes=[PTk])
            P.tt("dve", Pm[0:L, :, 0:L], psB[0:L, :, 0:L], k["m_sl"][0:L, :, 0:L], ALU.mult, reads=[pkB, "m_sl"], writes=[Pmk])
            P.tt("pool", TTf[0:L, :, 0:L], TTf[0:L, :, 0:L], k["i8"][0:L, :, 0:L], ALU.add, reads=[TTfk, "i8"], writes=[TTfk])
            P.copy("act", TTb[0:L, :, 0:L], TTf[0:L, :, 0:L], reads=[TTfk], writes=[TTbk])
            for j in range(1, nlev):
                psA, pkA = mm8(lambda h: PT[0:L, h, 0:L], lambda h: Pm[0:L, h, 0:L], [PTk, Pmk])
                last = (j == nlev - 1)
                if not last:
                    psB, pkB = mm8(lambda h: Pm[0:L, h, 0:L], lambda h: PT[0:L, h, 0:L], [PTk, Pmk])
                Pm2, Pm2k = pools["mb"].next()
                P.copy("act", Pm2[0:L, :, 0:L], psA[0:L, :, 0:L], reads=[pkA], writes=[Pm2k])
                if not last:
                    PT2, PT2k = pools["mb"].next()
                    P.copy("dve", PT2[0:L, :, 0:L], psB[0:L, :, 0:L], reads=[pkB], writes=[PT2k])
                    PT, PTk = PT2, PT2k
                Pm, Pmk = Pm2, Pm2k
                psC, pkC = mm8(lambda h: Pm[0:L, h, 0:L], lambda h: TTb[0:L, h, 0:L], [Pmk, TTbk])
                P.tt("dve", TTf[0:L, :, 0:L], TTf[0:L, :, 0:L], psC[0:L, :, 0:L], ALU.add, reads=[pkC, TTfk], writes=[TTfk])
                TTb, TTbk = pools["mb"].next()
                P.copy("act", TTb[0:L, :, 0:L], TTf[0:L, :, 0:L], reads=[TTfk], writes=[TTbk])
            psA, pkA = mm8(lambda h: fm(kT, h), lambda h: fm(aT, h), kin)
            Mak, Makk = pools["mb"].next()
            P.tt("dve", Mak[0:L, :, 0:L], psA[0:L, :, 0:L], k["m_su"][0:L, :, 0:L], ALU.mult, reads=[pkA, "m_su"], writes=[Makk])
            psA, pkA = mm8(lambda h: fm(bT, h), lambda h: fm(rT, h), kin)
            Mrb, Mrbk = pools["mb"].next()
            P.tt("dve", Mrb[0:L, :, 0:L], psA[0:L, :, 0:L], k["m_ui"][0:L, :, 0:L], ALU.mult, reads=[pkA, "m_ui"], writes=[Mrbk])
        psA, pkA = mm8(lambda h: fm(kT, h), lambda h: fm(rT, h), kin)
        Mrk, Mrkk = pools["mb"].next()
        P.tt("dve", Mrk[0:L, :, 0:L], psA[0:L, :, 0:L], k["m_ui"][0:L, :, 0:L], ALU.mult, reads=[pkA, "m_ui"], writes=[Mrkk])
        if delta:
            Bp, Bk = pads["B"]
            ps, pk = psH.next()
            for h in range(8):
                pr, e, pb = hsl(h)
                P.mm(ps[0:L, h, :], fm(aT, h), Spad[pb:pb + 64, pr, pb:pb + 64], start=True, stop=False, reads=kin + ["Spad"], writes=[pk])
                P.mm(ps[0:L, h, :], Mak[0:L, h, 0:L], Vp[0:L, pr, e, pb:pb + 64], start=False, stop=True, reads=[Makk, Vk], writes=[pk])
            W1, W1k = pools["mb"].next()
            P.copy("act", W1[0:L, :, :], ps[0:L, :, :], reads=[pk], writes=[W1k])
            ps, pk = psH.next()
            for h in range(8):
                P.mm(ps[0:L, h, :], TTb[0:L, h, 0:L], W1[0:L, h, :], reads=[TTbk, W1k], writes=[pk])
            Up, Uk = pools["padU"].next()
            for e in range(2):
                P.copy("act" if e else "dve", Up[0:L, :, e, e * 64:(e + 1) * 64],
                       ps[0:L, :, :].rearrange("p (a b) v -> p a b v", b=2)[:, :, e, :], reads=[pk], writes=[Uk])
        ps, pk = psH.next()
        for pr in range(4):
            n_mm = (3 if delta else 2) * 2
            i_mm = 0
            for e in range(2):
                pb = e * 64
                h = pr * 2 + e
                lst = [(Spad[pb:pb + 64, pr, :], rT[pb:pb + 64, pr, cs], kin + ["Spad"])]
                if delta:
                    lst.append((Up[0:L, pr, e, :], Mrb[0:L, h, 0:L], [Uk, Mrbk]))
                lst.append((Vp[0:L, pr, e, :], Mrk[0:L, h, 0:L], [Vk, Mrkk]))
                for (lt, rh, rd) in lst:
                    P.mm(ps[:, pr, 0:L], lt, rh, start=(i_mm == 0), stop=(i_mm == n_mm - 1), reads=rd, writes=[pk])
                    i_mm += 1
        P.copy("act", yT[:, :, cs], ps[:, 0:4, 0:L], reads=[pk], writes=[keys["yT"]])
        ps, pk = psH.next()
        for pr in range(4):
            n_mm = (2 if delta else 1) * 2
            i_mm = 0
            for e in range(2):
                pb = e * 64
                lst = []
                if delta:
                    lst.append((Bp[0:L, pr, e, :], Up[0:L, pr, e, pb:pb + 64], [Bk, Uk]))
                lst.append((Kp[0:L, pr, e, :], Vp[0:L, pr, e, pb:pb + 64], [Kk, Vk]))
                for (lt, rh, rd) in lst:
                    P.mm(ps[:, pr, :], lt, rh, start=(i_mm == 0), stop=(i_mm == n_mm - 1), reads=rd, writes=[pk])
                    i_mm += 1
        P.tt("dve", S32[:, :, :], S32[:, :, :], ps[:, 0:4, :], ALU.add, reads=[pk, "S32"], writes=["S32"])
        for pr in range(4):
            P.ts("dve", S32[:, pr, :], S32[:, pr, :], epos[:, pr, (c + 1) * L - 1:(c + 1) * L], None, ALU.mult, reads=["S32", keys["epos"]], writes=["S32"])
        for e in range(2):
            pb = e * 64
            P.copy("act" if e else "pool", Spad[pb:pb + 64, :, pb:pb + 64], S32[pb:pb + 64, :, :], reads=["S32"], writes=["Spad"])
    P.pe_serial = False


def scan_pools(C, ts, delta):
    pools = {}
    pools["psT"] = Rot(C, ts, "cs_psT", [64, 4, 128], BF16, 2, psum=True)
    pools["psH"] = Rot(C, ts, "cs_psH", [128, 8, 64], F32, 4, psum=True)
    names = ["K", "V"] + (["B", "U"] if delta else [])
    for nm in names:
        r = Rot(C, ts, "pad" + nm, [64, 4, 2, 128], BF16, 2)
        for t_, tk_ in zip(r.tiles, r.keys):
            C.P.memset("pool", t_[:], 0.0, writes=[tk_])
        pools["pad" + nm] = r
    pools["mb"] = Rot(C, ts, "cs_mb", [64, 8, 64], BF16, 10)
    pools["ttf"] = Rot(C, ts, "cs_ttf", [64, 8, 64], F32, 2)
    return pools


def more_consts(C):
    P, st, k = C.P, C.st, C.k
    for L in (64, 32, 16):
        m = C.sb(st, "rmask%d" % L, [128, 512], F32)
        P.memset("pool", m[:], 1.0, writes=["rmask%d" % L])
        P.memset("pool", m[:, :].rearrange("p (c l) -> p c l", l=L)[:, :, 0:1], 0.0, writes=["rmask%d" % L])
        k["rmask%d" % L] = m
    for nm, val in (("eps_rms", RMS_EPS), ("eps_gn", GN_EPS), ("tiny", 1e-12), ("one", 1.0), ("lnc", -40.0 * float(np.log(2.0)))):
        t = C.sb(st, nm, [128, 1], F32)
        P.memset("pool", t[:], val, writes=["epsv"])
        k[nm] = t
    P.flush()


def block_norm_stats(C, P, ps_pool, tmpb, src, srckey, SEG, scale):
    pass


def rwkv_mixer(C, job, s, I, prm):
    P, k = C.P, C.k
    T = job.T
    L = min(64, T)
    SEG = min(256, T)
    base = s * T
    with contextlib.ExitStack() as ts:
        pools = scan_pools(C, ts, True)
        psP = Rot(C, ts, "rw_psP", [128, 512], F32, 2, psum=True)
        S32 = C.sb(ts, "S32", [128, 4, 64], F32)
        Spad = C.sb(ts, "Spad", [128, 4, 128], BF16)
        P.memset("pool", Spad[:], 0.0, writes=["Spad"])
        if job.rwkv_s0 is None:
            P.memset("pool", S32[:], 0.0, writes=["S32"])
        else:
            s0t = C.sb(ts, "s0t", [64, 8, 64], F32)
            P.dma(s0t[:], job.rwkv_s0[s].rearrange("h v k -> v h k"), writes=["s0t"])
            for pr in range(4):
                ps, pk = psP.next()
                P.tr(ps[:, 0:64], s0t[0:64, 2 * pr:2 * pr + 2, :].rearrange("v e k -> v (e k)"), k["identF"][0:64, 0:64], reads=["s0t", "identF"], writes=[pk])
                P.copy("dve", S32[:, pr, :], ps[:, 0:64], reads=[pk], writes=["S32"])
            for e in range(2):
                pb = e * 64
                P.copy("act", Spad[pb:pb + 64, :, pb:pb + 64], S32[pb:pb + 64, :, :], reads=["S32"], writes=["Spad"])
        zt = C.sb(ts, "zt", [128, 14, SEG + 1], F32)
        dd = C.sb(ts, "dd", [128, 14, SEG], F32)
        zs = C.sb(ts, "zs", [128, 14, SEG], F32)
        f4 = lambda nm: C.sb(ts, nm, [128, 4, SEG], F32)
        b4 = lambda nm: C.sb(ts, nm, [128, 4, SEG], BF16)
        lw, asig, gT, kkn, kmod, cc, epos, eneg, eprev, tmpa, tmpb_, rkb, yT = [f4(n) for n in
            ("lw", "asig", "gT", "kkn", "kmod", "cc", "epos", "eneg", "eprev", "tmpa", "tmpb", "rkb", "yT")]
        rT, kT, vT, aT, bT, sqb, yb = [b4(n) for n in ("rT", "kT", "vT", "aT", "bT", "sqb", "yb")]
        tw = C.sb(ts, "tw", [128, SEG], BF16)
        sg = C.sb(ts, "sg", [128, SEG], BF16)
        outb = C.sb(ts, "outb", [128, 4, SEG], BF16)
        keys = dict(rT="rT", kT="kT", vT="vT", aT="aT", bT="bT", epos="epos", yT="yT")
        mu, w0, a0, k_k, k_a, lng, lnb, r_k, omka, wa2, g2 = [prm[n] for n in
            ("mu", "w0", "a0", "k_k", "k_a", "lng", "lnb", "r_k", "omka", "wa2", "g2")]
        pk_ = ["rwprm", "rwprm2"]
        for t0 in range(0, T, SEG):
            col0 = base + t0
            zrows = job.zT[0:A_COLS, :].rearrange("(c p) t -> p c t", p=128)
            zk = [("zT", job.name, nb, i) for nb in range(14) for i in range(col0 // 512, (col0 + SEG - 1) // 512 + 1)]
            if t0 == 0:
                P.dma(zt[:, :, 1:SEG + 1], zrows[:, :, col0:col0 + SEG], reads=zk, writes=["zt"])
                if job.rwkv_shift0 is None:
                    P.memset("pool", zt[:, :, 0:1], 0.0, writes=["zt"])
                else:
                    P.dma(zt[:, :, 0], job.rwkv_shift0[s].rearrange("(c p) -> p c", p=128), writes=["zt"], allow_slow_non_contiguous=True)
            else:
                zk2 = zk + [("zT", job.name, nb, (col0 - 1) // 512) for nb in range(14)]
                P.dma(zt[:, :, 0:SEG + 1], zrows[:, :, col0 - 1:col0 + SEG], reads=zk2, writes=["zt"])
            if t0 + SEG == T:
                P.dma(job.rwkv_shift_out[s].rearrange("(c p) -> p c", p=128), zt[:, :, SEG], reads=["zt"], writes=[("rwsh", job.name, s)],
                      q="pool", allow_slow_non_contiguous=True)
            P.tt("pool", dd[:], zt[:, :, 0:SEG], zt[:, :, 1:SEG + 1], ALU.subtract, reads=["zt"], writes=["dd"])
            for blk in range(14):
                P.stt(zs[:, blk, :], dd[:, blk, :], mu[:, blk:blk + 1], zt[:, blk, 1:SEG + 1], ALU.mult, ALU.add, reads=["dd", "zt"] + pk_, writes=["zs"])
            P.act(tw[0:64, :], zs[0:64, 12, :], AF.Tanh, reads=["zs"], writes=["tw"])
            P.copy("dve", tw[64:128, :], zs[64:128, 12, :], reads=["zs"], writes=["tw"])
            P.act(sg[:], zs[:, 13, :], AF.Sigmoid, reads=["zs"], writes=["sg"])
            for blk in range(4):
                bs = slice(blk * 128, (blk + 1) * 128)
                ps, pk = psP.next()
                P.mm(ps[:, 0:SEG], wa2[0:64, bs], tw[0:64, :], reads=["tw"] + pk_, writes=[pk])
                P.act(lw[:, blk, :], ps[:, 0:SEG], AF.Sigmoid, bias=w0[:, blk:blk + 1], reads=[pk] + pk_, writes=["lw"])
                ps, pk = psP.next()
                P.mm(ps[:, 0:SEG], wa2[64:128, bs], tw[64:128, :], reads=["tw"] + pk_, writes=[pk])
                P.act(asig[:, blk, :], ps[:, 0:SEG], AF.Sigmoid, bias=a0[:, blk:blk + 1], reads=[pk] + pk_, writes=["asig"])
                ps, pk = psP.next()
                P.mm(ps[:, 0:SEG], g2[:, bs], sg[:], reads=["sg"] + pk_, writes=[pk])
                P.copy("dve", gT[:, blk, :], ps[:, 0:SEG], reads=[pk], writes=["gT"])
            P.ts("dve", lw[:], lw[:], -DECAY_C, None, ALU.mult, reads=["lw"], writes=["lw"])
            for blk in range(4):
                P.ts("dve", tmpa[:, blk, :], zs[:, 4 + blk, :], k_k[:, blk:blk + 1], None, ALU.mult, reads=["zs"] + pk_, writes=["tmpa"])
            P.act(sqb[:], tmpa[:], AF.Square, reads=["tmpa"], writes=["sqb"])
            for blk in range(4):
                ps, pk = psP.next()
                P.mm(ps[:, 0:SEG], k["bo"][:], sqb[:, blk, :], reads=["sqb", "bo"], writes=[pk])
                P.act(tmpb_[:, blk, :], ps[:, 0:SEG], AF.Sqrt, reads=[pk], writes=["tmpb"])
            P.ts("dve", tmpb_[:], tmpb_[:], 1e-12, None, ALU.max, reads=["tmpb"], writes=["tmpb"])
            P.op("dve", lambda e: e.reciprocal(tmpb_[:], tmpb_[:]), reads=["tmpb"], writes=["tmpb"])
            P.tt("dve", kkn[:], tmpa[:], tmpb_[:], ALU.mult, reads=["tmpa", "tmpb"], writes=["kkn"])
            for blk in range(4):
                P.ts("dve", tmpa[:, blk, :], asig[:, blk, :], k_a[:, blk:blk + 1], omka[:, blk:blk + 1], ALU.mult, ALU.add,
                     reads=["asig"] + pk_, writes=["tmpa"])
            P.tt("dve", kmod[:], tmpa[:], zs[:, 4:8, :], ALU.mult, reads=["tmpa", "zs"], writes=["kmod"])
            rm = k["rmask%d" % L]
            for blk in range(4):
                P.op("dve", lambda e, blk=blk: e.tensor_tensor_scan(cc[:, blk, :], rm[:, 0:SEG], lw[:, blk, :], 0.0, ALU.mult, ALU.add),
                     reads=["lw", "rmask%d" % L], writes=["cc"])
            P.act(epos[:], cc[:], AF.Exp, reads=["cc"], writes=["epos"])
            P.act(eneg[:], cc[:], AF.Exp, scale=-1.0, reads=["cc"], writes=["eneg"])
            P.tt("pool", tmpa[:], cc[:], lw[:], ALU.subtract, reads=["cc", "lw", "kmod"], writes=["tmpa"])
            P.act(eprev[:], tmpa[:], AF.Exp, reads=["tmpa"], writes=["eprev"])
            P.tt("dve", rT[:], zs[:, 0:4, :], epos[:], ALU.mult, reads=["zs", "epos"], writes=["rT"])
            P.stt(aT[:], kkn[:], -1.0, eprev[:], ALU.mult, ALU.mult, reads=["kkn", "eprev"], writes=["aT"])
            P.tt("pool", tmpb_[:], kkn[:], asig[:], ALU.mult, reads=["kkn", "asig"], writes=["tmpb"])
            P.tt("dve", bT[:], tmpb_[:], eneg[:], ALU.mult, reads=["tmpb", "eneg"], writes=["bT"])
            P.tt("dve", kT[:], kmod[:], eneg[:], ALU.mult, reads=["kmod", "eneg"], writes=["kT"])
            P.copy("act", vT[:], zs[:, 8:12, :], reads=["zs"], writes=["vT"])
            for blk in range(4):
                P.stt(sqb[:, blk, :], zs[:, blk, :], r_k[:, blk:blk + 1], kmod[:, blk, :], ALU.mult, ALU.mult, reads=["zs", "kmod"] + pk_, writes=["sqb"])
            for blk in range(4):
                ps, pk = psP.next()
                P.mm(ps[:, 0:SEG], k["bo"][:], sqb[:, blk, :], reads=["sqb", "bo"], writes=[pk])
                P.copy("act", rkb[:, blk, :], ps[:, 0:SEG], reads=[pk], writes=["rkb"])
            chunk_scan(C, ts, SEG, L, True, rT, kT, vT, aT, bT, epos, S32, Spad, yT, keys, pools)
            P.copy("act", yb[:], yT[:], reads=["yT"], writes=["yb"])
            for blk in range(4):
                ps, pk = psP.next()
                P.mm(ps[:, 0:SEG], k["bo"][:], yb[:, blk, :], reads=["yb", "bo"], writes=[pk])
                P.stt(tmpa[:, blk, :], ps[:, 0:SEG], -1.0 / 64, yT[:, blk, :], ALU.mult, ALU.add, reads=[pk, "yT"], writes=["tmpa"])
            P.act(sqb[:], tmpa[:], AF.Square, reads=["tmpa"], writes=["sqb"])
            for blk in range(4):
                ps, pk = psP.next()
                P.mm(ps[:, 0:SEG], k["bo"][:], sqb[:, blk, :], reads=["sqb", "bo"], writes=[pk])
                P.act(tmpb_[:, blk, :], ps[:, 0:SEG], AF.Sqrt, bias=k["eps_gn"][:, 0:1], scale=1.0 / 64, reads=[pk, "epsv"], writes=["tmpb"])
            P.op("dve", lambda e: e.reciprocal(tmpb_[:], tmpb_[:]), reads=["tmpb"], writes=["tmpb"])
            P.tt("dve", tmpa[:], tmpa[:], tmpb_[:], ALU.mult, reads=["tmpa", "tmpb"], writes=["tmpa"])
            for blk in range(4):
                P.ts("dve", tmpa[:, blk, :], tmpa[:, blk, :], lng[:, blk:blk + 1], lnb[:, blk:blk + 1], ALU.mult, ALU.add, reads=["tmpa"] + pk_, writes=["tmpa"])
            P.tt("pool", tmpb_[:], rkb[:], zs[:, 8:12, :], ALU.mult, reads=["rkb", "zs", "tmpb"], writes=["tmpb"])
            P.tt("dve", tmpa[:], tmpa[:], tmpb_[:], ALU.add, reads=["tmpa", "tmpb"], writes=["tmpa"])
            P.tt("dve", outb[:], tmpa[:], gT[:], ALU.mult, reads=["tmpa", "gT"], writes=["outb"])
            P.dma(job.mixT[0:512, col0:col0 + SEG].rearrange("(c p) t -> p c t", p=128), outb[:], reads=["outb"],
                  writes=[("mixT", job.name, c, col0 // 512, hh) for c in range(4) for hh in range(2)], q="pool")
        so = C.sb(ts, "so", [64, 4, 128], F32)
        for pr in range(4):
            ps, pk = psP.next()
            P.tr(ps[0:64, 0:128], S32[:, pr, :], k["identF"][:, :], reads=["S32", "identF"], writes=[pk])
            P.copy("dve", so[:, pr, :], ps[0:64, 0:128], reads=[pk], writes=["so"])
        P.dma(job.rwkv_out[s].rearrange("(a e) v k -> v a e k", e=2), so[:, :, :].rearrange("v a (e k) -> v a e k", e=2), reads=["so"],
              writes=[("rwst", job.name, s)], q="pool")
        P.flush()


def load_rwkv_params(C, I):
    P, st = C.P, C.st
    prm = {}
    def vec(nm, ap, n):
        t = C.sb(st, "p_" + nm, [128, n], F32)
        P.dma(t[:], ap.rearrange("(c p) -> p c", p=128), writes=["rwprm"], allow_slow_non_contiguous=True)
        prm[nm] = t
    vec("mu", I["rwkv_mu"][0], 14)
    vec("w0", I["rwkv_w0"][0], 4)
    vec("a0", I["rwkv_a0"][0], 4)
    vec("k_k", I["rwkv_k_k"][0], 4)
    vec("k_a", I["rwkv_k_a"][0], 4)
    vec("lng", I["rwkv_lnx_g"][0], 4)
    vec("lnb", I["rwkv_lnx_b"][0], 4)
    vec("r_k", I["rwkv_r_k"][0].rearrange("h k -> (h k)"), 4)
    for nm, src in (("lng8", "rwkv_lnx_g"), ("lnb8", "rwkv_lnx_b")):
        t = C.sb(st, "p_" + nm, [64, 8], F32)
        P.dma(t[:], I[src][0].rearrange("(h k) -> k h", k=64), writes=["rwprm"], allow_slow_non_contiguous=True)
        prm[nm] = t
    omka = C.sb(st, "p_omka", [128, 4], F32)
    P.ts("dve", omka[:], prm["k_a"][:], -1.0, 1.0, ALU.mult, ALU.add, reads=["rwprm"], writes=["rwprm2"])
    prm["omka"] = omka
    wa2 = C.sb(st, "p_wa2", [128, 512], BF16)
    g2 = C.sb(st, "p_g2", [128, 512], BF16)
    with contextlib.ExitStack() as ts:
        wa2f = C.sb(ts, "wa2f", [128, 512], F32)
        g2f = C.sb(ts, "g2f", [128, 512], F32)
        P.dma(wa2f[0:64, :], I["rwkv_w2"][0], writes=["wa2f"])
        P.dma(wa2f[64:128, :], I["rwkv_a2"][0], writes=["wa2f"])
        P.dma(g2f[:], I["rwkv_g2"][0], writes=["g2f"])
        P.copy("dve", wa2[:], wa2f[:], reads=["wa2f"], writes=["rwprm2"])
        P.copy("dve", g2[:], g2f[:], reads=["g2f"], writes=["rwprm2"])
        prm["wa2"], prm["g2"] = wa2, g2
        P.flush()
    return prm


def attn_mixer(C, job, s, I, kind):
    P, k = C.P, C.k
    T = job.Ts[s]
    base = job.bases[s]
    fox = (kind == "fox")
    zoff = A_COLS if fox else 0
    Pc = (job.P_fox[s] if fox else job.P_chunk[s])
    NPt = Pc // 128
    NTt = (T + 127) // 128
    NKT = NPt + NTt
    QG = min(512, T)
    NG = T // QG
    mix_off = 512 if fox else 0
    ck = (job.fox_ck if fox else job.chunk_ck)
    cv = (job.fox_cv if fox else job.chunk_cv)
    kout = (job.fox_kout if fox else job.chunk_kout)
    vout = (job.fox_vout if fox else job.chunk_vout)
    with contextlib.ExitStack() as ts:
        QT = C.sb(ts, "QT", [128, 4, T], BF16)
        KT = C.sb(ts, "KT", [128, 4, NKT * 128], BF16)
        Vt = C.sb(ts, "Vt", [128, NKT, 8, 65], BF16)
        P.memset("pool", Vt[:, :, :, 64:65], 1.0, writes=["Vt"])
        psS = Rot(C, ts, "at_psS", [128, 512], F32, 4, psum=True)
        psM = Rot(C, ts, "at_psM", [128, 512], F32, 1, psum=True)
        psN = Rot(C, ts, "at_psN", [128, 512], F32, 2, psum=True)
        psD = Rot(C, ts, "at_psD", [64, 512], F32, 1, psum=True)
        zrows = lambda off: job.zT[zoff + off:zoff + off + 512, :].rearrange("(c p) t -> p c t", p=128)
        zrows64 = lambda off, a: job.zT[zoff + off + a * 256:zoff + off + a * 256 + 256, :].rearrange("(h d) t -> d h t", d=64)
        zkey = lambda off, col: [("zT", job.name, (zoff + off) // 128 + c, col // 512) for c in range(4)]
        with contextlib.ExitStack() as t2:
            ldq = Rot(C, t2, "at_ldq", [128, 4, 128], F32, 2)
            ldk2 = Rot(C, t2, "at_ldk2", [128, 4, 128], F32, 2)
            ldk = Rot(C, t2, "at_ldk", [128, 4, 128], F32, 2)
            ldv = Rot(C, t2, "at_ldv", [128, 4, 128], F32, 2)
            stg = Rot(C, t2, "at_stg", [128, 512], F32, 4)
            ldc = Rot(C, t2, "at_ldc", [128, 512], F32, 3)
            for ct in range(NPt):
                kc_, kck = ldc.next()
                for a_ in range(2):
                    P.dma(kc_[:, :].rearrange("p (b a d) -> p b a d", b=4, a=2)[:, :, a_, :],
                          ck[s][ct * 128:(ct + 1) * 128, a_ * 256:(a_ + 1) * 256].rearrange("t (b d) -> t b d", b=4), writes=[kck])
                ps, pk = psS.next()
                for h4 in range(4):
                    P.tr(ps[:, h4 * 128:(h4 + 1) * 128], kc_[:, h4 * 128:(h4 + 1) * 128], k["identF"][:, :], reads=[kck, "identF"], writes=[pk])
                P.copy("act", KT[:, :, ct * 128:(ct + 1) * 128], ps[:, :].rearrange("p (c t) -> p c t", t=128), reads=[pk], writes=["KT"])
                vc_, vck = ldc.next()
                P.dma(vc_[:], cv[s][ct * 128:(ct + 1) * 128, :], writes=[vck])
                P.copy("dve", Vt[:, ct, :, 0:64], vc_[:, :].rearrange("p (h d) -> p h d", d=64), reads=[vck], writes=["Vt"])
            for it in range(NTt):
                rows = min(128, T - it * 128)
                col0 = base + it * 128
                q_, qk_ = ldq.next()
                k_, kk_ = ldk.next()
                v_, vk_ = ldv.next()
                k2_, k2k_ = ldk2.next()
                for a in range(2):
                    P.dma(q_[a * 64:(a + 1) * 64, :, 0:rows], zrows64(0, a)[:, :, col0:col0 + rows], reads=zkey(0, col0), writes=[qk_])
                    P.dma(k2_[a * 64:(a + 1) * 64, :, 0:rows], zrows64(512, a)[:, :, col0:col0 + rows], reads=zkey(512, col0), writes=[k2k_])
                P.dma(k_[:, :, 0:rows], zrows(512)[:, :, col0:col0 + rows], reads=zkey(512, col0), writes=[kk_])
                P.dma(v_[:, :, 0:rows], zrows(1024)[:, :, col0:col0 + rows], reads=zkey(1024, col0), writes=[vk_])
                P.copy("act", QT[:, :, it * 128:it * 128 + rows], q_[:, :, 0:rows], reads=[qk_], writes=["QT"])
                P.copy("pool", KT[:, :, (NPt + it) * 128:(NPt + it) * 128 + rows], k2_[:, :, 0:rows], reads=[k2k_], writes=["KT"])
                for src, sk, outap, isv in ((k_, kk_, kout, False), (v_, vk_, vout, True)):
                    ps, pk = psS.next()
                    for blk in range(4):
                        P.tr(ps[0:rows, blk * 128:(blk + 1) * 128], src[:, blk, 0:rows], k["identF"][:, :], reads=[sk, "identF"], writes=[pk])
                    sg_, sgk = stg.next()
                    P.copy("dve" if isv else "act", sg_[0:rows, :], ps[0:rows, :], reads=[pk], writes=[sgk])
                    if isv:
                        P.copy("pool", Vt[0:rows, NPt + it, :, 0:64], sg_[0:rows, :].rearrange("p (h d) -> p h d", d=64), reads=[sgk], writes=["Vt"])
                    if fox or T <= 512:
                        P.dma(outap[s][it * 128:it * 128 + rows, :], sg_[0:rows, :], reads=[sgk], writes=[("kvout", kind, job.name, s, it, isv)], q="pool")
                    elif it * 128 >= T - 512:
                        r0 = it * 128 - (T - 512)
                        P.dma(outap[s][r0:r0 + rows, :], sg_[0:rows, :], reads=[sgk], writes=[("kvout", kind, job.name, s, it, isv)], q="pool")
            P.flush()
        biasT = None
        if fox:
            biasT = C.sb(ts, "biasT", [128, 8, NG, NKT], F32)
            with contextlib.ExitStack() as t2:
                lfT = C.sb(t2, "lfT", [8, T], F32)
                nbf = C.sb(t2, "nbf", [8, 1], F32)
                P.dma(nbf[:], I["fox_b_f"][0:1, :].rearrange("o h -> h o"), writes=["nbf"], allow_slow_non_contiguous=True)
                P.ts("dve", nbf[:], nbf[:], -1.0, None, ALU.mult, reads=["nbf"], writes=["nbf"])
                gk = [("zT", job.name, (A_COLS + 1536) // 128, i) for i in range(base // 512, (base + T - 1) // 512 + 1)]
                P.dma(lfT[:], job.zT[A_COLS + 1536:A_COLS + 1544, base:base + T], reads=gk, writes=["lfT"])
                P.act(lfT[:], lfT[:], AF.Exp, bias=nbf[:, 0:1], scale=-1.0, reads=["lfT", "nbf"], writes=["lfT"])
                P.act(lfT[:], lfT[:], AF.Ln, bias=k["one"][0:8, 0:1], reads=["lfT", "epsv"], writes=["lfT"])
                P.ts("dve", lfT[:], lfT[:], -1.0, None, ALU.mult, reads=["lfT"], writes=["lfT"])
                lft = C.sb(t2, "lft", [128, NKT, 8], F32)
                P.memset("pool", lft[:], 0.0, writes=["lft"])
                if NPt:
                    P.dma(lft[:, 0:NPt, :], job.fox_clogf[s].rearrange("(n p) h -> p n h", p=128), writes=["lft"])
                ps, pk = psS.next()
                for it in range(NTt):
                    rows = min(128, T - it * 128)
                    P.tr(ps[0:rows, it * 8:(it + 1) * 8], lfT[0:8, it * 128:it * 128 + rows], k["identF"][0:8, 0:8], reads=["lfT", "identF"], writes=[pk])
                rows_l = min(128, T)
                P.copy("dve", lft[0:rows_l, NPt:NKT, :], ps[0:rows_l, 0:NTt * 8].rearrange("p (n h) -> p n h", h=8), reads=[pk], writes=["lft"])
                if T >= 128:
                    P.dma(job.fox_logf_out[s].rearrange("(n p) h -> p n h", p=128), lft[:, NPt:NKT, :], reads=["lft"], writes=[("lfout", job.name, s)], q="pool")
                else:
                    P.dma(job.fox_logf_out[s], lft[0:T, NPt, :], reads=["lft"], writes=[("lfout", job.name, s)], q="pool")
                Wt = C.sb(t2, "Wt", [128, 8, NKT], F32)
                TOTt = C.sb(t2, "TOTt", [128, 8, NKT], F32)
                offs = C.sb(t2, "offs", [128, 8, NKT], F32)
                rmk = C.sb(t2, "rmk", [128, 8, NKT], F32)
                P.memset("pool", rmk[:], 1.0, writes=["rmk"])
                P.memset("pool", rmk[:, :, 0:1], 0.0, writes=["rmk"])
                lft2 = lft[:, :, :].rearrange("p n h -> p (n h)")
                ps, pk = psS.next()
                P.mm(ps[:, 0:NKT * 8], k["triF"][:], lft2, reads=["lft", "triF"], writes=[pk])
                P.copy("dve", Wt[:], ps[:, 0:NKT * 8].rearrange("p (n h) -> p h n", h=8), reads=[pk], writes=["Wt"])
                ps, pk = psS.next()
                P.mm(ps[:, 0:NKT * 8], k["onesF"][:], lft2, reads=["lft", "onesF"], writes=[pk])
                P.copy("dve", TOTt[:], ps[:, 0:NKT * 8].rearrange("p (n h) -> p h n", h=8), reads=[pk], writes=["TOTt"])
                P.op("dve", lambda e: e.tensor_tensor_scan(offs[:, :, :].rearrange("p h n -> p (h n)"), rmk[:, :, :].rearrange("p h n -> p (h n)"),
                                                           TOTt[:, :, :].rearrange("p h n -> p (h n)"), 0.0, ALU.mult, ALU.add),
                     reads=["TOTt", "rmk"], writes=["offs"])
                P.tt("dve", offs[:], offs[:], TOTt[:], ALU.subtract, reads=["offs", "TOTt"], writes=["offs"])
                P.tt("dve", Wt[:], Wt[:], offs[:], ALU.add, reads=["offs", "Wt"], writes=["Wt"])
                for h in range(8):
                    for g in range(NG):
                        nq0 = NPt + g * (QG // 128)
                        P.ts("dve", biasT[:, h, g, :], Wt[:, h, :], offs[:, h, nq0:nq0 + 1], -1.0, ALU.subtract, ALU.mult,
                             reads=["Wt", "offs"], writes=["biasT"])
                P.flush()
        pb_ = Rot(C, ts, "at_pb", [128, 512], BF16, 6)
        rd_ = Rot(C, ts, "at_rd", [128, 512], F32, 3)
        nb_ = Rot(C, ts, "at_nb", [64, 512], F32, 3)
        ob_ = Rot(C, ts, "at_ob", [64, 512], BF16, 2)
        MEs = None
        if not fox:
            MEs = [C.sb(ts, "ME", [128, 8, 512], BF16) for _ in range(2)]
            hk = Rot(C, ts, "at_hk", [128, 512], F32, 2)
            me32 = Rot(C, ts, "at_me32", [128, 512], F32, 2)

        def build_me(h):
            ME = MEs[h % 2]
            for rt in range(8):
                if T <= 16 and rt >= NKT:
                    continue
                hh, hhk = hk.next()
                src = bass.AP(C.dram["relext"].tensor, h * 1536 + 896 - 128 * rt, [[1, 128], [1, QG]])
                P.dma(hh[:, 0:QG], src, reads=["relext"], writes=[hhk])
                ps, pk = psM.next()
                P.mm(ps[:, 0:QG], k["flipF"][:], hh[:, 0:QG], reads=[hhk, "flipF"], writes=[pk])
                m32, m32k = me32.next()
                P.act(m32[:, 0:QG], ps[:, 0:QG], AF.Exp, reads=[pk], writes=[m32k])
                P.tt("dve", ME[:, rt, 0:QG], m32[:, 0:QG], k["bandB"][:, rt, 0:QG], ALU.mult, reads=[m32k, "bandB"], writes=[("ME", h % 2)])

        its = []
        for h in range(8):
            for g in range(NG):
                q0 = g * QG
                tiles = []
                if fox:
                    last_kt = NPt + (q0 + QG - 1) // 128
                    for kt in range(0, last_kt + 1):
                        rows = 128 if kt < NPt else min(128, T - (kt - NPt) * 128)
                        d = kt - NPt - q0 // 128
                        mfn = (lambda c0, c1, d=d, rows=rows: k["cmB"][0:rows, d, c0:c1]) if d >= 0 else None
                        c0 = min(128 * d, QG - 1) if d > 0 else 0
                        tiles.append([kt, rows, biasT[0:rows, h, g, kt:kt + 1], mfn, "cmB", c0, QG])
                else:
                    order = [3, 2, 5, 1, 6, 0, 7, 4] if T > 16 else list(range(8))
                    for rt in order:
                        kt = (q0 // 128 - 4 + rt) if T > 16 else rt
                        if kt < 0 or kt >= NKT:
                            continue
                        rows = 128 if kt < NPt else min(128, T - (kt - NPt) * 128)
                        mfn = (lambda c0, c1, rt=rt, rows=rows, hh_=h: MEs[hh_ % 2][0:rows, rt, c0:c1])
                        if T > 16:
                            lo, hi = max(0, 2 * rt - 8), min(7, 2 * rt + 1)
                            c0, c1 = lo * 64, (hi + 1) * 64
                        else:
                            c0, c1 = 0, QG
                        tiles.append([kt, rows, 0.0, mfn, ("ME", h % 2), c0, c1])
                tiles[0][5], tiles[0][6] = 0, QG
                tiles[-1][5], tiles[-1][6] = 0, QG
                G = dict(h=h, g=g, q0=q0, n=len(tiles))
                for i, tl in enumerate(tiles):
                    its.append((G, i, tl))
        D = 3
        qk = {}
        pend = []

        def finalize1(G):
            pn, pnk = G["pn"]
            rd, rdk = rd_.next()
            P.act(rd[64:65, 0:QG], pn[64:65, 0:QG], AF.Ln, scale=float(2.0 ** -40), reads=[pnk], writes=[rdk])
            P.act(rd[64:65, 0:QG], rd[64:65, 0:QG], AF.Exp, bias=k["lnc"][64:65, 0:1], scale=-1.0, reads=[rdk, "epsv"], writes=[rdk])
            nb, nbk = nb_.next()
            P.copy("act", nb[:, 0:QG], pn[0:64, 0:QG], reads=[pnk], writes=[nbk])
            G["rd"], G["nb"] = (rd, rdk), (nb, nbk)

        def finalize(G):
            h, q0 = G["h"], G["q0"]
            rd, rdk = G["rd"]
            nb, nbk = G["nb"]
            pd, pdk = psD.next()
            P.mm(pd[:, 0:QG], k["onesF"][64:65, 0:64], rd[64:65, 0:QG], reads=["onesF", rdk], writes=[pdk])
            ob, obk = ob_.next()
            P.tt("dve", ob[:, 0:QG], nb[:, 0:QG], pd[:, 0:QG], ALU.mult, reads=[nbk, pdk], writes=[obk])
            r0 = mix_off + h * 64
            P.dma(job.mixT[r0:r0 + 64, base + q0:base + q0 + QG], ob[:, 0:QG], reads=[obk],
                  writes=[("mixT", job.name, r0 // 128, (base + q0) // 512, h % 2)], q="pool")

        for n in range(len(its) + D):
            if n < len(its):
                G, i, (kt, rows, bias, mfn, mkey, c0, c1) = its[n]
                h = G["h"]
                hb = (h // 4) * 64
                if (not fox) and G["g"] == 0 and i == 0:
                    build_me(h)
                ps, pk = psS.next()
                P.mm(ps[0:rows, c0:c1], KT[hb:hb + 64, h % 4, kt * 128:kt * 128 + rows], QT[hb:hb + 64, h % 4, G["q0"] + c0:G["q0"] + c1],
                     reads=["KT", "QT"], writes=[pk])
                qk[n] = (ps, pk)
            m = n - D
            if m >= 0:
                G, i, (kt, rows, bias, mfn, mkey, c0, c1) = its[m]
                h = G["h"]
                ps, pk = qk.pop(m)
                if i == 0:
                    while len(pend) > 1:
                        finalize(pend.pop(0))
                    G["pn"] = psN.next()
                pn, pnk = G["pn"]
                pt, ptk = pb_.next()
                P.act(pt[0:rows, c0:c1], ps[0:rows, c0:c1], AF.Exp, bias=bias, scale=0.125, reads=[pk, "biasT"], writes=[ptk])
                if mfn is not None:
                    P.tt("dve", pt[0:rows, c0:c1], pt[0:rows, c0:c1], mfn(c0, c1), ALU.mult, reads=[ptk, mkey], writes=[ptk])
                P.mm(pn[0:65, c0:c1], Vt[0:rows, kt, h, :], pt[0:rows, c0:c1], start=(i == 0), stop=(i == G["n"] - 1), reads=["Vt", ptk], writes=[pnk])
                if i == G["n"] - 1:
                    finalize1(G)
                    G["due"] = m + 4
                    pend.append(G)
                while pend and pend[0].get("due", 1 << 30) <= m and pend[0] is not G:
                    finalize(pend.pop(0))
        while pend:
            finalize(pend.pop(0))
        P.pe_serial = False
        P.flush()


def build_rel_tables(C, I):
    P, st, k = C.P, C.st, C.k
    ext = C.dr("relext", [8, 1536], F32)
    rb = I["chunk_rel_bias"]
    P.dma(ext[:, 384:639], rb[0, :, 1:256], writes=["relext"])
    P.dma(bass.AP(ext.tensor, 0, [[1536, 8], [1, 384], [1, 1]]), bass.AP(rb.tensor, 0, [[257, 8], [0, 384], [1, 1]]), writes=["relext"])
    P.dma(bass.AP(ext.tensor, 639, [[1536, 8], [1, 897], [1, 1]]), bass.AP(rb.tensor, 256, [[257, 8], [0, 897], [1, 1]]), writes=["relext"])
    band = C.sb(st, "bandB", [128, 8, 512], BF16)
    P.memset("pool", band[:], 0.0, writes=["bandB"])
    for rt in range(8):
        for e in range(2):
            kcr = 2 * rt + e
            lo, hi = max(0, kcr - 8), min(7, kcr)
            if lo <= hi:
                P.memset("pool", band[e * 64:(e + 1) * 64, rt, lo * 64:(hi + 1) * 64], 1.0, writes=["bandB"])
    k["bandB"] = band
    P.flush()


def hgrn_mixer(C, job, s, I, prm):
    P, k = C.P, C.k
    T = job.T
    L = min(16, T)
    SEG = min(256, T)
    base = s * T
    with contextlib.ExitStack() as ts:
        pools = scan_pools(C, ts, False)
        psP = Rot(C, ts, "hg_psP", [128, 512], F32, 2, psum=True)
        S32 = C.sb(ts, "S32", [128, 4, 64], F32)
        Spad = C.sb(ts, "Spad", [128, 4, 128], BF16)
        P.memset("pool", Spad[:], 0.0, writes=["Spad"])
        if job.hgrn_s0 is None:
            P.memset("pool", S32[:], 0.0, writes=["S32"])
        else:
            P.dma(S32[:], job.hgrn_s0[s].rearrange("(a e) k v -> (e k) a v", e=2), writes=["S32"])
            for e in range(2):
                pb = e * 64
                P.copy("act", Spad[pb:pb + 64, :, pb:pb + 64], S32[pb:pb + 64, :, :], reads=["S32"], writes=["Spad"])
        z4 = C.sb(ts, "z4", [128, 16, SEG], F32)
        f4 = lambda nm: C.sb(ts, nm, [128, 4, SEG], F32)
        b4 = lambda nm: C.sb(ts, nm, [128, 4, SEG], BF16)
        qf, sgf, lw, kin, cc, epos, eneg, yT, tmpa = [f4(n) for n in ("qf", "sgf", "lw", "kin", "cc", "epos", "eneg", "yT", "tmpa")]
        rT, kT, vT, sqb, outb = [b4(n) for n in ("rT", "kT", "vT", "sqb", "outb")]
        keys = dict(rT="rT", kT="kT", vT="vT", epos="epos", yT="yT")
        lb, oml, noml, ng = prm["lb"], prm["oml"], prm["noml"], prm["ng"]
        pk_ = ["hgprm"]
        rm = k["rmask%d" % L]
        for t0 in range(0, T, SEG):
            col0 = base + t0
            zk = [("zT", job.name, nb, i) for nb in range(12, 28) for i in range(col0 // 512, (col0 + SEG - 1) // 512 + 1)]
            P.dma(z4[:], job.zT[1536:3584, col0:col0 + SEG].rearrange("(c p) t -> p c t", p=128), reads=zk, writes=["z4"])
            P.act(qf[:], z4[:, 0:4, :], AF.Silu, reads=["z4"], writes=["qf"])
            P.act(sgf[:], z4[:, 4:8, :], AF.Sigmoid, reads=["z4"], writes=["sgf"])
            for blk in range(4):
                P.ts("dve", lw[:, blk, :], sgf[:, blk, :], oml[:, blk:blk + 1], lb[:, blk:blk + 1], ALU.mult, ALU.add, reads=["sgf"] + pk_, writes=["lw"])
                P.ts("dve", kin[:, blk, :], sgf[:, blk, :], noml[:, blk:blk + 1], oml[:, blk:blk + 1], ALU.mult, ALU.add, reads=["sgf"] + pk_, writes=["kin"])
            P.act(lw[:], lw[:], AF.Ln, reads=["lw"], writes=["lw"])
            for blk in range(4):
                P.op("dve", lambda e, blk=blk: e.tensor_tensor_scan(cc[:, blk, :], rm[:, 0:SEG], lw[:, blk, :], 0.0, ALU.mult, ALU.add),
                     reads=["lw", "rmask%d" % L], writes=["cc"])
            P.act(epos[:], cc[:], AF.Exp, reads=["cc"], writes=["epos"])
            P.act(eneg[:], cc[:], AF.Exp, scale=-1.0, reads=["cc"], writes=["eneg"])
            P.tt("dve", rT[:], qf[:], epos[:], ALU.mult, reads=["qf", "epos"], writes=["rT"])
            P.tt("dve", kT[:], kin[:], eneg[:], ALU.mult, reads=["kin", "eneg"], writes=["kT"])
            P.copy("pool", vT[:], z4[:, 8:12, :], reads=["z4"], writes=["vT"])
            chunk_scan(C, ts, SEG, L, False, rT, kT, vT, None, None, epos, S32, Spad, yT, keys, pools)
            P.act(sqb[:], yT[:], AF.Square, reads=["yT"], writes=["sqb"])
            for blk in range(4):
                ps, pk = psP.next()
                P.mm(ps[:, 0:SEG], k["bo"][:], sqb[:, blk, :], reads=["sqb", "bo"], writes=[pk])
                P.act(tmpa[:, blk, :], ps[:, 0:SEG], AF.Sqrt, bias=k["eps_rms"][:, 0:1], scale=1.0 / 64, reads=[pk, "epsv"], writes=["tmpa"])
            P.op("dve", lambda e: e.reciprocal(tmpa[:], tmpa[:]), reads=["tmpa"], writes=["tmpa"])
            P.tt("dve", tmpa[:], tmpa[:], yT[:], ALU.mult, reads=["tmpa", "yT"], writes=["tmpa"])
            P.act(qf[:], z4[:, 12:16, :], AF.Silu, reads=["z4", "rT"], writes=["qf"])
            for blk in range(4):
                P.stt(outb[:, blk, :], tmpa[:, blk, :], ng[:, blk:blk + 1], qf[:, blk, :], ALU.mult, ALU.mult, reads=["tmpa", "qf"] + pk_, writes=["outb"])
            P.dma(job.mixT[512:1024, col0:col0 + SEG].rearrange("(c p) t -> p c t", p=128), outb[:], reads=["outb"],
                  writes=[("mixT", job.name, 4 + c, col0 // 512, hh) for c in range(4) for hh in range(2)], q="pool")
        P.dma(job.hgrn_out[s].rearrange("(a e) k v -> (e k) a v", e=2), S32[:], reads=["S32"], writes=[("hgst", job.name, s)], q="pool")
        P.flush()


def load_hgrn_params(C, I):
    P, st = C.P, C.st
    prm = {}
    t0 = C.sb(st, "hg_t0", [128, 4], F32)
    t1 = C.sb(st, "hg_t1", [128, 4], F32)
    P.dma(t0[:], I["hgrn_lb_table"][0].rearrange("(c p) -> p c", p=128), writes=["hg_t0"], allow_slow_non_contiguous=True)
    P.dma(t1[:], I["hgrn_lb_table"][1].rearrange("(c p) -> p c", p=128), writes=["hg_t1"], allow_slow_non_contiguous=True)
    P.act(t0[:], t0[:], AF.Exp, reads=["hg_t0"], writes=["hg_t0"])
    P.act(t1[:], t1[:], AF.Exp, reads=["hg_t1"], writes=["hg_t1"])
    P.tt("dve", t0[:], t0[:], t1[:], ALU.add, reads=["hg_t0", "hg_t1"], writes=["hg_t0"])
    P.op("dve", lambda e: e.reciprocal(t0[:], t0[:]), reads=["hg_t0"], writes=["hg_t0"])
    lb = C.sb(st, "hg_lb", [128, 4], F32)
    oml = C.sb(st, "hg_oml", [128, 4], F32)
    noml = C.sb(st, "hg_noml", [128, 4], F32)
    ng = C.sb(st, "hg_ng", [128, 4], F32)
    P.tt("dve", lb[:], t1[:], t0[:], ALU.mult, reads=["hg_t0", "hg_t1"], writes=["hgprm"])
    P.ts("dve", oml[:], lb[:], -1.0, 1.0, ALU.mult, ALU.add, reads=["hgprm"], writes=["hgprm"])
    P.ts("dve", noml[:], oml[:], -1.0, None, ALU.mult, reads=["hgprm"], writes=["hgprm"])
    P.dma(ng[:], I["hgrn_norm_g"][0].rearrange("(c p) -> p c", p=128), writes=["hgprm"], allow_slow_non_contiguous=True)
    ng8 = C.sb(st, "hg_ng8", [64, 8], F32)
    P.dma(ng8[:], I["hgrn_norm_g"][0].rearrange("(h k) -> k h", k=64), writes=["hgprm"], allow_slow_non_contiguous=True)
    prm.update(lb=lb, oml=oml, noml=noml, ng=ng, ng8=ng8)
    P.flush()
    return prm


def xk1(job, c, col0, gs):
    return [("xT", job.name, c, i) for i in range(col0 // 512, (col0 + gs - 1) // 512 + 1)]


def out_proj(C, job, Wb, wkeys, l, j, srcT_dram, kcn, src_keys_fn):
    P = C.P
    m = C.mods[(l, j)]
    with contextlib.ExitStack() as ts:
        wr = Rot(C, ts, "op_w", [128, kcn, 128], BF16, 8)
        ws = []
        for nb in range(8):
            w, wk = wr.next()
            P.dma(w[:], Wb[nb], reads=wkeys, writes=[wk])
            ws.append((w, wk))
        sr = Rot(C, ts, "op_src", [128, kcn, 512], BF16, 2)
        pr = Rot(C, ts, "op_ps", [128, 512], F32, 3, psum=True)
        xr = Rot(C, ts, "op_x", [128, 512], F32, 3)
        for (s, t0, gs, col0) in seq_groups(job, 512):
            sT, sk = sr.next()
            P.dma(sT[:, :, 0:gs], srcT_dram[:, col0:col0 + gs].rearrange("(c p) t -> p c t", p=128), reads=src_keys_fn(col0), writes=[sk])
            col = job.modcol[s]
            for nb in range(8):
                w, wk = ws[nb]
                ps, pk = pr.next()
                for kc in range(kcn):
                    P.mm(ps[:, 0:gs], w[:, kc, :], sT[:, kc, 0:gs], start=(kc == 0), stop=(kc == kcn - 1), reads=[wk, sk], writes=[pk])
                x, xk = xr.next()
                P.dma(x[:, 0:gs], job.xT[nb * 128:(nb + 1) * 128, col0:col0 + gs], reads=xk1(job, nb, col0, gs), writes=[xk])
                P.stt(x[:, 0:gs], ps[:, 0:gs], m[:, 16 + nb, col:col + 1], x[:, 0:gs], ALU.mult, ALU.add, reads=[pk, xk, ("mod", l, j)], writes=[xk])
                P.dma(job.xT[nb * 128:(nb + 1) * 128, col0:col0 + gs], x[:, 0:gs], reads=[xk], writes=xk1(job, nb, col0, gs), q="pool")
        P.flush()


def ffn_up(C, job, l, I, Wup, wkeys, hT, hkey):
    P, k = C.P, C.k
    with contextlib.ExitStack() as ts:
        cw = C.sp[("cw", l)]
        cbv = C.sp[("cb", l)]
        wr = Rot(C, ts, "fu_w", [128, KC, 128], BF16, 4)
        pr = Rot(C, ts, "fu_ps", [128, 512], F32, 8, psum=True)
        Er = [Rot(C, ts, "fu_E%d" % i, [128, 514], F32, 4) for i in range(2)]
        t1r = Rot(C, ts, "fu_t1", [128, 512], F32, 6)
        gr = Rot(C, ts, "fu_g", [128, 512], BF16, 3)
        groups = seq_groups(job, 512)
        for jb in range(22):
            wts = []
            for half in range(2):
                w, wk = wr.next()
                P.dma(w[:], Wup[jb + 22 * half], reads=wkeys, writes=[wk])
                wts.append((w, wk))
            prevE = [None, None]
            for (s, t0, gs, col0) in groups:
                res = []
                for half in range(2):
                    blk = jb + 22 * half
                    w, wk = wts[half]
                    ps, pk = pr.next()
                    for kc in range(KC):
                        P.mm(ps[:, 0:gs], w[:, kc, :], hT[:, kc, col0:col0 + gs], start=(kc == 0), stop=(kc == KC - 1), reads=[wk, hkey], writes=[pk])
                    E, Ek = Er[half].next()
                    if t0 == 0:
                        if job.ffn_buf[l][s] is None:
                            P.memset("pool", E[:, 0:2], 0.0, writes=[Ek])
                        else:
                            P.dma(E[:, 0:2], job.ffn_buf[l][s][:, blk * 128:(blk + 1) * 128].rearrange("r f -> f r"), writes=[Ek], allow_slow_non_contiguous=True)
                    else:
                        pE, pEk, pgs = prevE[half]
                        P.copy("pool", E[:, 0:2], pE[:, pgs:pgs + 2], reads=[pEk], writes=[Ek])
                    P.copy("act", E[:, 2:2 + gs], ps[:, 0:gs], reads=[pk], writes=[Ek])
                    prevE[half] = (E, Ek, gs)
                    if t0 + gs == job.Ts[s]:
                        P.dma(job.ffn_out[l][s][:, blk * 128:(blk + 1) * 128].rearrange("r f -> f r"), E[:, gs:gs + 2], reads=[Ek],
                              writes=[("ffo", job.name, l, s, blk)], q="pool", allow_slow_non_contiguous=True)
                    t1, t1k = t1r.next()
                    P.act(t1[:, 0:gs], E[:, 0:gs], AF.Identity, bias=cbv[:, blk:blk + 1], scale=cw[:, 0, blk:blk + 1], reads=[Ek, "smallprm"], writes=[t1k])
                    P.stt(t1[:, 0:gs], E[:, 1:gs + 1], cw[:, 1, blk:blk + 1], t1[:, 0:gs], ALU.mult, ALU.add, reads=[Ek, "smallprm", t1k], writes=[t1k])
                    P.stt(t1[:, 0:gs], E[:, 2:gs + 2], cw[:, 2, blk:blk + 1], t1[:, 0:gs], ALU.mult, ALU.add, reads=[Ek, "smallprm", t1k], writes=[t1k])
                    res.append((t1, t1k))
                (ta, tak), (tb, tbk) = res
                P.act(ta[:, 0:gs], ta[:, 0:gs], AF.Silu, reads=[tak], writes=[tak])
                g, gk = gr.next()
                P.tt("pool", g[:, 0:gs], ta[:, 0:gs], tb[:, 0:gs], ALU.mult, reads=[tak, tbk], writes=[gk])
                P.dma(job.gT[jb * 128:(jb + 1) * 128, col0:col0 + gs], g[:, 0:gs], reads=[gk], writes=[("gT", job.name, jb, col0 // 512)], q="pool")
        P.flush()


def final_norm(C, job, I):
    P, k = C.P, C.k
    with contextlib.ExitStack() as ts:
        gf, gfk = C.sp["gfin"], "smallprm"
        xg = Rot(C, ts, "fn_x", [128, KC, 128], F32, 3)
        sq = Rot(C, ts, "fn_sq", [128, KC, 128], BF16, 2)
        pss = Rot(C, ts, "fn_ss", [128, 128], F32, 2, psum=True)
        rs = Rot(C, ts, "fn_r", [128, 128], F32, 2)
        pst = Rot(C, ts, "fn_pt", [128, 512], F32, 4, psum=True)
        yo = Rot(C, ts, "fn_yo", [128, 1024], F32, 2)

        def stage_a(grp):
            (s, t0, gs, col0) = grp
            x, xk = xg.next()
            rk_ = [key for c in range(8) for key in xk1(job, c, col0, gs)]
            P.dma(x[:, :, 0:gs], job.xT[:, col0:col0 + gs].rearrange("(c p) t -> p c t", p=128), reads=rk_, writes=[xk])
            q, qk = sq.next()
            P.act(q[:, :, 0:gs], x[:, :, 0:gs], AF.Square, reads=[xk], writes=[qk])
            ps, pk = pss.next()
            for c in range(KC):
                P.mm(ps[:, 0:gs], k["onesB"][:], q[:, c, 0:gs], start=(c == 0), stop=(c == KC - 1), reads=[qk, "onesB"], writes=[pk])
            r, rk = rs.next()
            P.act(r[:, 0:gs], ps[:, 0:gs], AF.Ln, bias=k["eps_rms"][:, 0:1], scale=1.0 / D, reads=[pk, "epsv"], writes=[rk])
            P.act(r[:, 0:gs], r[:, 0:gs], AF.Exp, scale=-0.5, reads=[rk], writes=[rk])
            for c in range(KC):
                P.stt(x[:, c, 0:gs], x[:, c, 0:gs], gf[:, c:c + 1], r[:, 0:gs], ALU.mult, ALU.mult, reads=[xk, rk, gfk], writes=[xk])
            return (x, xk, grp)

        def stage_b(st_):
            x, xk, (s, t0, gs, col0) = st_
            y, yk = yo.next()
            for half in range(2):
                pt, ptk = pst.next()
                for c4 in range(4):
                    c = half * 4 + c4
                    P.tr(pt[0:gs, c4 * 128:(c4 + 1) * 128], x[:, c, 0:gs], k["identF"][:, :], reads=[xk, "identF"], writes=[ptk])
                P.copy("act" if half else "dve", y[0:gs, half * 512:(half + 1) * 512], pt[0:gs, :], reads=[ptk], writes=[yk])
            P.dma(job.y_out[s][t0:t0 + gs, :], y[0:gs, :], reads=[yk], writes=[("yout", job.name, s, t0)], q="pool")

        groups = seq_groups(job, 128)
        prev = None
        for grp in groups:
            cur = stage_a(grp)
            if prev is not None:
                stage_b(prev)
            prev = cur
        stage_b(prev)
        P.flush()


def load_small_params(C, I):
    P, st, k = C.P, C.st, C.k
    sp = {}
    specs = []
    for l in range(2):
        specs.append((("gmix", l), I["norm_mix_g"][l].rearrange("(c p) -> c p", p=128), 8))
        specs.append((("gffn", l), I["norm_ffn_g"][l].rearrange("(c p) -> c p", p=128), 8))
        specs.append((("cw", l), I["ffn_conv_w"][l].rearrange("j (c p) -> (j c) p", p=128), 132))
        specs.append((("cb", l), I["ffn_conv_b"][l].rearrange("(c p) -> c p", p=128), 44))
    specs.append(("gfin", I["final_norm_g"].rearrange("(c p) -> c p", p=128), 8))
    tiles = {}
    for key, ap, R in specs:
        tiles[key] = C.sb(st, "sp_%s" % str(key).replace(" ", ""), [128, R], F32)
    with contextlib.ExitStack() as ts:
        ld = Rot(C, ts, "sp_ld", [128, 128], F32, 3)
        pp = Rot(C, ts, "sp_ps", [128, 128], F32, 2, psum=True)
        for key, ap, R in specs:
            dst = tiles[key]
            for r0 in range(0, R, 128):
                rows = min(128, R - r0)
                t, tk = ld.next()
                P.dma(t[0:rows, :], ap[r0:r0 + rows, :], writes=[tk])
                ps, pk = pp.next()
                P.tr(ps[:, 0:rows], t[0:rows, :], k["identF"][0:rows, 0:rows], reads=[tk, "identF"], writes=[pk])
                P.copy("dve", dst[:, r0:r0 + rows], ps[:, 0:rows], reads=[pk], writes=["smallprm"])
        P.flush()
    for l in range(2):
        sp[("gmix", l)] = tiles[("gmix", l)]
        sp[("gffn", l)] = tiles[("gffn", l)]
        sp[("cw", l)] = tiles[("cw", l)][:, :].rearrange("p (j c) -> p j c", j=3)
        sp[("cb", l)] = tiles[("cb", l)]
    sp["gfin"] = tiles["gfin"]
    C.sp = sp


def run_layer(C, job, l, I, W, prm):
    P = C.P
    wname_in = "ab_in" if l == 0 else "cd_in"
    wname_out = "ab_out" if l == 0 else "cd_out"
    nb_in = 27 if l == 0 else 28
    Ttot = job.Ttot
    with contextlib.ExitStack() as ts:
        hT = C.sb(ts, "hT", [128, KC, Ttot], BF16)
        hkey = C.name("hT")
        with contextlib.ExitStack() as t2:
            norm_to_hT(C, t2, job, C.sp[("gmix", l)], l, 0, hT, hkey)
        with contextlib.ExitStack() as t2:
            stg = Rot(C, t2, "pj_stg", [128, 512], F32, 4)
            cnt = [0]

            def post(nb, s, t0, gs, col0, ps, pk):
                st_, sk_ = stg.next()
                cnt[0] += 1
                P.copy("act" if cnt[0] % 2 else "dve", st_[:, 0:gs], ps[:, 0:gs], reads=[pk], writes=[sk_])
                P.dma(job.zT[nb * 128:(nb + 1) * 128, col0:col0 + gs], st_[:, 0:gs], reads=[sk_], writes=[("zT", job.name, nb, col0 // 512)], q="pool")
            Wb, wk = W[wname_in]
            proj(C, job, hT, hkey, Wb, wk, range(nb_in), KC, post)
            P.flush()
    stage("inproj %s %d" % (job.name, l))
    for s in range(job.nseq):
        if l == 0:
            rwkv_mixer2(C, job, s, I, prm["rwkv"])
            stage("rwkv")
            attn_mixer(C, job, s, I, "fox")
            stage("fox")
        else:
            attn_mixer(C, job, s, I, "chunk")
            stage("chunk")
            hgrn_mixer2(C, job, s, I, prm["hgrn"])
            stage("hgrn")
    Wb, wk = W[wname_out]
    out_proj(C, job, Wb, wk, l, 0, job.mixT, 8,
             lambda col0: [("mixT", job.name, c, col0 // 512, hh) for c in range(8) for hh in range(2)])
    stage("outproj")
    with contextlib.ExitStack() as ts:
        hT = C.sb(ts, "hT2", [128, KC, Ttot], BF16)
        hkey = C.name("hT2")
        with contextlib.ExitStack() as t2:
            norm_to_hT(C, t2, job, C.sp[("gffn", l)], l, 1, hT, hkey)
        Wb, wk = W["up%d" % l]
        ffn_up(C, job, l, I, Wb, wk, hT, hkey)
    stage("ffn_up")
    Wb, wk = W["down%d" % l]
    out_proj(C, job, Wb, wk, l, 1, job.gT, 22, lambda col0: [("gT", job.name, jb, col0 // 512) for jb in range(22)])


class StopBuild(Exception):
    pass


import os
_STOP = int(os.environ.get("K_STOP", "999"))
_stage = [0]


def stage(msg=""):
    _stage[0] += 1
    if os.environ.get("K_VERBOSE"):
        print("stage", _stage[0], msg)
    if _stage[0] >= _STOP:
        raise StopBuild()


def build_program(Tp):
    nc = bass.Bass("TRN2", target_bir_lowering=False)
    _stage[0] = 0
    I = {}
    O = {}

    def inp(name, shape):
        I[name] = nc.dram_tensor(name, list(shape), F32, kind="ExternalInput").ap()

    def outp(name, shape):
        O[name] = nc.dram_tensor(name, list(shape), F32, kind="ExternalOutput").ap()
    S2 = NSEQ_S
    inp("xp", [Tp, D]); inp("xs", [S2, 16, D]); inp("cp", [D]); inp("csv", [S2, D])
    inp("fox_ck", [S2, P_FOX, 512]); inp("fox_cv", [S2, P_FOX, 512]); inp("fox_clogf", [S2, P_FOX, 8])
    inp("rw_s0", [S2, 8, 64, 64]); inp("rw_sh0", [S2, A_COLS])
    inp("ch_ck", [S2, P_CHUNK, 512]); inp("ch_cv", [S2, P_CHUNK, 512]); inp("hg_s0", [S2, 8, 64, 64])
    inp("ffn_buf", [2, S2, 2, 2 * DFF])
    for nm, shp in (("ada_w", [2, 2, D, 3 * D]), ("ada_b", [2, 2, 3 * D]), ("norm_mix_g", [2, D]), ("norm_ffn_g", [2, D]),
                    ("ab_w_in", [1, D, AB_COLS]), ("rwkv_mu", [1, A_COLS]), ("rwkv_w0", [1, 512]), ("rwkv_w2", [1, 64, 512]),
                    ("rwkv_a0", [1, 512]), ("rwkv_a2", [1, 64, 512]), ("rwkv_g2", [1, 128, 512]), ("rwkv_k_k", [1, 512]),
                    ("rwkv_k_a", [1, 512]), ("rwkv_r_k", [1, 8, 64]), ("rwkv_lnx_g", [1, 512]), ("rwkv_lnx_b", [1, 512]),
                    ("fox_b_f", [1, 8]), ("ab_w_out", [1, D, D]), ("cd_w_in", [1, D, CD_COLS]), ("chunk_rel_bias", [1, 8, 257]),
                    ("hgrn_lb_table", [2, 512]), ("hgrn_norm_g", [1, 512]), ("cd_w_out", [1, D, D]), ("ffn_w_up", [2, D, 2 * DFF]),
                    ("ffn_conv_w", [2, 3, 2 * DFF]), ("ffn_conv_b", [2, 2 * DFF]), ("ffn_w_down", [2, DFF, D]), ("final_norm_g", [D])):
        inp(nm, shp)
    cK = min(512, Tp)
    outp("y_p", [Tp, D]); outp("y_s", [S2, 16, D])
    outp("fox_k_p", [Tp, 512]); outp("fox_v_p", [Tp, 512]); outp("fox_logf_p", [Tp, 8])
    outp("rwkv_p", [8, 64, 64]); outp("rwkv_shift_p", [A_COLS])
    outp("chunk_k_p", [cK, 512]); outp("chunk_v_p", [cK, 512]); outp("hgrn_p", [8, 64, 64]); outp("ffn_conv_p", [2, 2, 2 * DFF])
    outp("fox_k_s", [S2, 16, 512]); outp("fox_v_s", [S2, 16, 512]); outp("fox_logf_s", [S2, 16, 8])
    outp("rwkv_s", [S2, 8, 64, 64]); outp("rwkv_shift_s", [S2, A_COLS])
    outp("chunk_k_s", [S2, 16, 512]); outp("chunk_v_s", [S2, 16, 512]); outp("hgrn_s", [S2, 8, 64, 64]); outp("ffn_conv_s", [2, S2, 2, 2 * DFF])

    with contextlib.ExitStack() as st:
        P = Prog(nc, st)
        C = Ctx(nc, P, st)
        try:
          build_consts(C)
          more_consts(C)
          load_small_params(C, I)
          stage("consts")
          W = {}
          W["ab_in"] = cast_weight(C, I["ab_w_in"][0], D, AB_COLS, "Wab_in")
          W["ab_out"] = cast_weight(C, I["ab_w_out"][0], D, D, "Wab_out")
          W["cd_in"] = cast_weight(C, I["cd_w_in"][0], D, CD_COLS, "Wcd_in")
          W["cd_out"] = cast_weight(C, I["cd_w_out"][0], D, D, "Wcd_out")
          for l in range(2):
              W["up%d" % l] = cast_weight(C, I["ffn_w_up"][l], D, 2 * DFF, "Wup%d" % l)
              W["down%d" % l] = cast_weight(C, I["ffn_w_down"][l], DFF, D, "Wdown%d" % l)
          stage("cast")
          adaln_phase(C, I, 1 + S2, [I["cp"]] + [I["csv"][s] for s in range(S2)])
          stage("adaln")
          prm = dict(rwkv=load_rwkv_params(C, I), hgrn=load_hgrn_params(C, I))
          build_rel_tables(C, I)
          stage("params")

          S3 = 1 + S2
          jb = Job()
          jb.name, jb.nseq = "m", S3
          jb.Ts = [Tp] + [16] * S2
          jb.bases = [0] + [Tp + 16 * i for i in range(S2)]
          jb.Ttot = Tp + 16 * S2
          jb.modcol = list(range(S3))
          jb.x_in = [I["xp"]] + [I["xs"][s] for s in range(S2)]
          jb.y_out = [O["y_p"]] + [O["y_s"][s] for s in range(S2)]
          jb.fox_kout = [O["fox_k_p"]] + [O["fox_k_s"][s] for s in range(S2)]
          jb.fox_vout = [O["fox_v_p"]] + [O["fox_v_s"][s] for s in range(S2)]
          jb.fox_logf_out = [O["fox_logf_p"]] + [O["fox_logf_s"][s] for s in range(S2)]
          jb.rwkv_out = [O["rwkv_p"]] + [O["rwkv_s"][s] for s in range(S2)]
          jb.rwkv_shift_out = [O["rwkv_shift_p"]] + [O["rwkv_shift_s"][s] for s in range(S2)]
          jb.chunk_kout = [O["chunk_k_p"]] + [O["chunk_k_s"][s] for s in range(S2)]
          jb.chunk_vout = [O["chunk_v_p"]] + [O["chunk_v_s"][s] for s in range(S2)]
          jb.hgrn_out = [O["hgrn_p"]] + [O["hgrn_s"][s] for s in range(S2)]
          jb.ffn_out = [[O["ffn_conv_p"][l]] + [O["ffn_conv_s"][l, s] for s in range(S2)] for l in range(2)]
          jb.P_fox = [0] + [P_FOX] * S2
          jb.P_chunk = [0] + [P_CHUNK] * S2
          jb.fox_ck = [None] + [I["fox_ck"][s] for s in range(S2)]
          jb.fox_cv = [None] + [I["fox_cv"][s] for s in range(S2)]
          jb.fox_clogf = [None] + [I["fox_clogf"][s] for s in range(S2)]
          jb.chunk_ck = [None] + [I["ch_ck"][s] for s in range(S2)]
          jb.chunk_cv = [None] + [I["ch_cv"][s] for s in range(S2)]
          jb.rwkv_s0 = [None] + [I["rw_s0"][s] for s in range(S2)]
          jb.rwkv_shift0 = [None] + [I["rw_sh0"][s] for s in range(S2)]
          jb.hgrn_s0 = [None] + [I["hg_s0"][s] for s in range(S2)]
          jb.ffn_buf = [[None] + [I["ffn_buf"][l, s] for s in range(S2)] for l in range(2)]
          for job in (jb,):
              Ttot = job.Ttot
              job.xT = C.dr("xT_" + job.name, [D, Ttot], F32)
              job.zT = C.dr("zT_" + job.name, [28 * 128, Ttot], F32)
              job.mixT = C.dr("mixT_" + job.name, [D, Ttot], BF16)
              job.gT = C.dr("gT_" + job.name, [DFF, Ttot], BF16)
              x_to_fm(C, job)
              stage("x_to_fm " + job.name)
              for l in range(2):
                  run_layer(C, job, l, I, W, prm)
              final_norm(C, job, I)
              stage("final " + job.name)
        except StopBuild:
            P.ops = []
        P.flush(final=True)
        print("ops:", P.n_total)
        if os.environ.get("K_FLUSHLOG"):
            import json as _json
            _json.dump(P.flush_log, open(os.environ["K_FLUSHLOG"], "w"))
    return nc


_CACHE = {}


def kernel(**inp):
    f = lambda a: np.ascontiguousarray(np.asarray(a, dtype=np.float32))
    xpr = f(inp["x_prompt"])
    B, Tp, _ = xpr.shape
    if Tp not in _CACHE:
        _CACHE[Tp] = build_program(Tp)
    nc = _CACHE[Tp]
    wnames = ["ada_w", "ada_b", "norm_mix_g", "norm_ffn_g", "ab_w_in", "rwkv_mu", "rwkv_w0", "rwkv_w2", "rwkv_a0", "rwkv_a2",
              "rwkv_g2", "rwkv_k_k", "rwkv_k_a", "rwkv_r_k", "rwkv_lnx_g", "rwkv_lnx_b", "fox_b_f", "ab_w_out", "cd_w_in",
              "chunk_rel_bias", "hgrn_lb_table", "hgrn_norm_g", "cd_w_out", "ffn_w_up", "ffn_conv_w", "ffn_conv_b", "ffn_w_down",
              "final_norm_g"]
    wts = {n: f(inp[n]) for n in wnames}
    xs = f(inp["x_sample"]); cp = f(inp["c_prompt"]); csv = f(inp["c_sample"])
    fk = f(inp["cache_fox_k"])[0].reshape(16, P_FOX, 512); fv = f(inp["cache_fox_v"])[0].reshape(16, P_FOX, 512)
    fl = f(inp["cache_fox_logf"])[0]
    rs0 = f(inp["state_rwkv"])[0]; rsh = f(inp["state_rwkv_shift"])[0]
    ckk = f(inp["cache_chunk_k"])[0].reshape(16, P_CHUNK, 512); ckv = f(inp["cache_chunk_v"])[0].reshape(16, P_CHUNK, 512)
    hs0 = f(inp["state_hgrn"])[0]; fb = f(inp["state_ffn_conv"])
    in_maps = []
    for c in range(N_CORES):
        b = c % B
        sl = slice(2 * c, 2 * c + 2)
        m = dict(xp=xpr[b], xs=xs[sl], cp=cp[b], csv=csv[sl], fox_ck=fk[sl], fox_cv=fv[sl], fox_clogf=fl[sl], rw_s0=rs0[sl], rw_sh0=rsh[sl],
                 ch_ck=ckk[sl], ch_cv=ckv[sl], hg_s0=hs0[sl], ffn_buf=np.ascontiguousarray(fb[:, sl]))
        m.update(wts)
        in_maps.append({k_: np.ascontiguousarray(v_) for k_, v_ in m.items()})
    res = run_bass_kernel_spmd(nc, in_maps, core_ids=list(range(N_CORES)))
    R = res.results
    pc = lambda name: np.stack([R[b][name] for b in range(B)])
    sc = lambda name, ax=0: np.concatenate([R[c][name] for c in range(N_CORES)], axis=ax)
    cK = min(512, Tp)
    outs = (
        pc("y_p"), sc("y_s"),
        pc("fox_k_p").reshape(1, B, Tp, 8, 64), pc("fox_v_p").reshape(1, B, Tp, 8, 64), pc("fox_logf_p").reshape(1, B, Tp, 8),
        pc("rwkv_p")[None], pc("rwkv_shift_p")[None],
        pc("chunk_k_p").reshape(1, B, cK, 8, 64), pc("chunk_v_p").reshape(1, B, cK, 8, 64), pc("hgrn_p")[None],
        np.stack([R[b]["ffn_conv_p"] for b in range(B)], axis=1),
        sc("fox_k_s").reshape(1, 16, 16, 8, 64), sc("fox_v_s").reshape(1, 16, 16, 8, 64), sc("fox_logf_s").reshape(1, 16, 16, 8),
        sc("rwkv_s")[None], sc("rwkv_shift_s")[None],
        sc("chunk_k_s").reshape(1, 16, 16, 8, 64), sc("chunk_v_s").reshape(1, 16, 16, 8, 64), sc("hgrn_s")[None],
        sc("ffn_conv_s", ax=1),
    )
    return tuple(np.ascontiguousarray(o.astype(np.float32)) for o in outs)


def chunk_scan2(C, ts, SEG, L, delta, ops8, tok4, eposL, S32, Sb, yT8):
    P, k = C.P, C.k
    NCH = SEG // L
    nlev = int(np.log2(L))
    psT = Rot(C, ts, "c2_psT", [64, 4, 128], BF16, 2, psum=True)
    psH = Rot(C, ts, "c2_psH", [64, 8, 64], F32, 6, psum=True)
    r8, r8k = ops8["r"]
    k8, k8k = ops8["k"]
    if delta:
        a8, a8k = ops8["a"]
        b8, b8k = ops8["b"]

    def bt(nm, dt=BF16):
        return [(C.sb(ts, "c2_%s%d" % (nm, c), [64, 8, 64], dt), C.name("c2_" + nm)) for c in range(NCH)]
    Ktok, Vtok = bt("Ktok"), bt("Vtok")
    Mrk = bt("Mrk")
    if delta:
        Btok, Mak, Mrb = bt("Btok"), bt("Mak"), bt("Mrb")
        PT = [bt("PTa"), bt("PTb")]
        Pm = [bt("Pma"), bt("Pmb")]
        TTb = [bt("TTba"), bt("TTbb")]
        TTf = bt("TTf", F32)
    cs_of = lambda c: slice(c * L, (c + 1) * L)

    def mm8(c, lhs_of, rhs_of, reads):
        ps, pk = psH.next()
        for h in range(8):
            P.mm(ps[0:L, h, 0:L], lhs_of(h), rhs_of(h), reads=reads, writes=[pk])
        return ps, pk

    for c in range(NCH):
        cs = cs_of(c)
        lst = [("k", Ktok), ("v", Vtok)] + ([("b", Btok)] if delta else [])
        for nm, dstl in lst:
            src, sk = tok4[nm]
            ps, pk = psT.next()
            for blk in range(4):
                P.tr(ps[0:L, blk, :], src[:, blk, cs], k["identB"][:, :], reads=[sk, "identB"], writes=[pk])
            dst, dk = dstl[c]
            P.copy("act" if nm == "v" else "dve", dst[0:L, :, :], ps[0:L, :, :].rearrange("p a (e x) -> p (a e) x", e=2), reads=[pk], writes=[dk])
    if delta:
        for c in range(NCH):
            cs = cs_of(c)
            psA, pkA = mm8(c, lambda h: b8[:, h, cs], lambda h: a8[:, h, cs], [a8k, b8k])
            psB, pkB = mm8(c, lambda h: a8[:, h, cs], lambda h: b8[:, h, cs], [a8k, b8k])
            tf, tfk = TTf[c]
            pt, ptk = PT[0][c]
            pm, pmk = Pm[0][c]
            tb, tbk = TTb[0][c]
            P.tt("dve", tf[0:L, :, 0:L], psA[0:L, :, 0:L], k["m_su"][0:L, :, 0:L], ALU.mult, reads=[pkA, "m_su"], writes=[tfk])
            P.copy("act", pt[0:L, :, 0:L], tf[0:L, :, 0:L], reads=[tfk], writes=[ptk])
            P.tt("dve", pm[0:L, :, 0:L], psB[0:L, :, 0:L], k["m_sl"][0:L, :, 0:L], ALU.mult, reads=[pkB, "m_sl"], writes=[pmk])
            P.tt("pool", tf[0:L, :, 0:L], tf[0:L, :, 0:L], k["i8"][0:L, :, 0:L], ALU.add, reads=[tfk, "i8"], writes=[tfk])
            P.copy("act", tb[0:L, :, 0:L], tf[0:L, :, 0:L], reads=[tfk], writes=[tbk])
        cur = 0
        for j in range(1, nlev):
            last = (j == nlev - 1)
            nxt = 1 - cur
            pend_ = []
            for c in range(NCH):
                pt, ptk = PT[cur][c]
                pm, pmk = Pm[cur][c]
                psA, pkA = mm8(c, lambda h: pt[0:L, h, 0:L], lambda h: pm[0:L, h, 0:L], [ptk, pmk])
                pm2, pm2k = Pm[nxt][c]
                P.copy("act", pm2[0:L, :, 0:L], psA[0:L, :, 0:L], reads=[pkA], writes=[pm2k])
                if not last:
                    psB, pkB = mm8(c, lambda h: pm[0:L, h, 0:L], lambda h: pt[0:L, h, 0:L], [ptk, pmk])
                    pt2, pt2k = PT[nxt][c]
                    P.copy("dve", pt2[0:L, :, 0:L], psB[0:L, :, 0:L], reads=[pkB], writes=[pt2k])
                if c >= 1:
                    pend_.append(c - 1)
                    cc_ = pend_.pop(0)
                    pm2_, pm2k_ = Pm[nxt][cc_]
                    tb, tbk = TTb[cur][cc_]
                    psC, pkC = mm8(cc_, lambda h: pm2_[0:L, h, 0:L], lambda h: tb[0:L, h, 0:L], [pm2k_, tbk])
                    tf, tfk = TTf[cc_]
                    P.tt("dve", tf[0:L, :, 0:L], tf[0:L, :, 0:L], psC[0:L, :, 0:L], ALU.add, reads=[pkC, tfk], writes=[tfk])
                    tb2, tb2k = TTb[nxt][cc_]
                    P.copy("act", tb2[0:L, :, 0:L], tf[0:L, :, 0:L], reads=[tfk], writes=[tb2k])
            cc_ = NCH - 1
            pm2_, pm2k_ = Pm[nxt][cc_]
            tb, tbk = TTb[cur][cc_]
            psC, pkC = mm8(cc_, lambda h: pm2_[0:L, h, 0:L], lambda h: tb[0:L, h, 0:L], [pm2k_, tbk])
            tf, tfk = TTf[cc_]
            P.tt("dve", tf[0:L, :, 0:L], tf[0:L, :, 0:L], psC[0:L, :, 0:L], ALU.add, reads=[pkC, tfk], writes=[tfk])
            tb2, tb2k = TTb[nxt][cc_]
            P.copy("act", tb2[0:L, :, 0:L], tf[0:L, :, 0:L], reads=[tfk], writes=[tb2k])
            cur = nxt
        TTfin = TTb[cur]
    for c in range(NCH):
        cs = cs_of(c)
        if delta:
            psA, pkA = mm8(c, lambda h: k8[:, h, cs], lambda h: a8[:, h, cs], [k8k, a8k])
            m_, mk_ = Mak[c]
            P.tt("dve", m_[0:L, :, 0:L], psA[0:L, :, 0:L], k["m_su"][0:L, :, 0:L], ALU.mult, reads=[pkA, "m_su"], writes=[mk_])
            psA, pkA = mm8(c, lambda h: b8[:, h, cs], lambda h: r8[:, h, cs], [b8k, r8k])
            m_, mk_ = Mrb[c]
            P.tt("dve", m_[0:L, :, 0:L], psA[0:L, :, 0:L], k["m_ui"][0:L, :, 0:L], ALU.mult, reads=[pkA, "m_ui"], writes=[mk_])
        psA, pkA = mm8(c, lambda h: k8[:, h, cs], lambda h: r8[:, h, cs], [k8k, r8k])
        m_, mk_ = Mrk[c]
        P.tt("dve", m_[0:L, :, 0:L], psA[0:L, :, 0:L], k["m_ui"][0:L, :, 0:L], ALU.mult, reads=[pkA, "m_ui"], writes=[mk_])
    W1r = Rot(C, ts, "c2_W1", [64, 8, 64], BF16, 2)
    Ur = Rot(C, ts, "c2_U", [64, 8, 64], BF16, 2)
    yt, ytk = yT8
    el, elk = eposL
    for c in range(NCH):
        cs = cs_of(c)
        Kt, Ktk = Ktok[c]
        Vt_, Vtk = Vtok[c]
        mrk, mrkk = Mrk[c]
        if delta:
            Bt, Btk = Btok[c]
            mak, makk = Mak[c]
            mrb, mrbk = Mrb[c]
            tb, tbk = TTfin[c]
            ps, pk = psH.next()
            for h in range(8):
                P.mm(ps[0:L, h, :], a8[:, h, cs], Sb[:, h, :], start=True, stop=False, reads=[a8k, "Sb"], writes=[pk])
                P.mm(ps[0:L, h, :], mak[0:L, h, 0:L], Vt_[0:L, h, :], start=False, stop=True, reads=[makk, Vtk], writes=[pk])
            W1, W1k = W1r.next()
            P.copy("act", W1[0:L, :, :], ps[0:L, :, :], reads=[pk], writes=[W1k])
            ps, pk = psH.next()
            for h in range(8):
                P.mm(ps[0:L, h, :], tb[0:L, h, 0:L], W1[0:L, h, :], reads=[tbk, W1k], writes=[pk])
            U, Uk = Ur.next()
            P.copy("dve", U[0:L, :, :], ps[0:L, :, :], reads=[pk], writes=[Uk])
        ps, pk = psH.next()
        for h in range(8):
            P.mm(ps[:, h, 0:L], Sb[:, h, :], r8[:, h, cs], start=True, stop=False, reads=["Sb", r8k], writes=[pk])
            if delta:
                P.mm(ps[:, h, 0:L], U[0:L, h, :], mrb[0:L, h, 0:L], start=False, stop=False, reads=[Uk, mrbk], writes=[pk])
            P.mm(ps[:, h, 0:L], Vt_[0:L, h, :], mrk[0:L, h, 0:L], start=False, stop=True, reads=[Vtk, mrkk], writes=[pk])
        P.copy("act", yt[:, :, cs], ps[:, :, 0:L], reads=[pk], writes=[ytk])
        ps, pk = psH.next()
        for h in range(8):
            if delta:
                P.mm(ps[:, h, :], Bt[0:L, h, :], U[0:L, h, :], start=True, stop=False, reads=[Btk, Uk], writes=[pk])
            P.mm(ps[:, h, :], Kt[0:L, h, :], Vt_[0:L, h, :], start=(not delta), stop=True, reads=[Ktk, Vtk], writes=[pk])
        skeys = [("S32", h) for h in range(8)]
        P.tt("dve", S32[:, :, :], S32[:, :, :], ps[:, :, :], ALU.add, reads=[pk] + skeys, writes=skeys)
        for h in range(8):
            P.ts("dve", S32[:, h, :], S32[:, h, :], el[:, h, c:c + 1], None, ALU.mult, reads=[("S32", h), elk], writes=[("S32", h)])
        P.copy("act", Sb[:, :, :], S32[:, :, :], reads=skeys, writes=["Sb"])


def to8(P, dst8, dkey, src4, skey, q="sp"):
    d4 = dst8[:, :, :].rearrange("p (a e) t -> p a e t", e=2)
    for e in range(2):
        P.dma(d4[:, :, e, :], src4[e * 64:(e + 1) * 64, :, :], reads=[skey], writes=[dkey], q=q)


def rwkv_mixer2(C, job, s, I, prm):
    P, k = C.P, C.k
    T = job.Ts[s]
    L = min(64, T)
    SEG = min(256, T)
    NCH = SEG // L
    base = job.bases[s]
    S8K = [("S32", h) for h in range(8)]
    with contextlib.ExitStack() as ts:
        S32 = C.sb(ts, "S32", [64, 8, 64], F32)
        Sb = C.sb(ts, "Sb", [64, 8, 64], BF16)
        k4, v4, b4 = [C.sb(ts, n, [128, 4, SEG], BF16) for n in ("k4", "v4", "b4")]
        r8, a8, b8, k8, v8, t8 = [C.sb(ts, n, [64, 8, SEG], BF16) for n in ("r8", "a8", "b8", "k8", "v8", "t8")]
        el = C.sb(ts, "el", [64, 8, NCH], F32)
        yT8 = C.sb(ts, "yT8", [64, 8, SEG], F32)
        sg = C.sb(ts, "sg", [128, SEG], BF16)
        zlast = C.sb(ts, "zlast", [128, 14], F32)
        mu, w0, a0, k_k, k_a, r_k, omka, wa2, g2, lng8, lnb8 = [prm[n] for n in
            ("mu", "w0", "a0", "k_k", "k_a", "r_k", "omka", "wa2", "g2", "lng8", "lnb8")]
        pk_ = ["rwprm", "rwprm2"]
        with contextlib.ExitStack() as t0s:
            if job.rwkv_s0[s] is None:
                P.memset("pool", S32[:], 0.0, writes=S8K)
            else:
                s0t = C.sb(t0s, "s0t", [64, 8, 64], F32)
                psI = C.ps(t0s, "rw_psI", [64, 8, 64], F32)
                P.dma(s0t[:], job.rwkv_s0[s].rearrange("h v k -> v h k"), writes=["s0t"])
                for h in range(8):
                    P.tr(psI[:, h, :], s0t[:, h, :], k["identF"][0:64, 0:64], reads=["s0t", "identF"], writes=["psI"])
                P.copy("dve", S32[:], psI[:], reads=["psI"], writes=S8K)
            P.copy("act", Sb[:], S32[:], reads=S8K, writes=["Sb"])
            P.flush()
        for t0 in range(0, T, SEG):
            col0 = base + t0
            with contextlib.ExitStack() as tp:
                psP = Rot(C, tp, "rw_psP", [128, 512], F32, 4, psum=True)
                zt = C.sb(tp, "zt", [128, 14, SEG + 1], F32)
                dd = C.sb(tp, "dd", [128, 14, SEG], F32)
                zs = C.sb(tp, "zs", [128, 14, SEG], F32)
                f4 = lambda nm: C.sb(tp, nm, [128, 4, SEG], F32)
                b4_ = lambda nm: C.sb(tp, nm, [128, 4, SEG], BF16)
                lw, asig, kkn, kmod, cc, epos, eneg, eprev, tmpa, tmpb_ = [f4(n) for n in
                    ("lw", "asig", "kkn", "kmod", "cc", "epos", "eneg", "eprev", "tmpa", "tmpb")]
                rT4, aT4, t4, sqb = [b4_(n) for n in ("rT4", "aT4", "t4", "sqb")]
                tw = C.sb(tp, "tw", [128, SEG], BF16)
                zrows = job.zT[0:A_COLS, :].rearrange("(c p) t -> p c t", p=128)
                zk = [("zT", job.name, nb, i) for nb in range(14) for i in range(col0 // 512, (col0 + SEG - 1) // 512 + 1)]
                if t0 == 0:
                    P.dma(zt[:, :, 1:SEG + 1], zrows[:, :, col0:col0 + SEG], reads=zk, writes=["zt"])
                    if job.rwkv_shift0[s] is None:
                        P.memset("pool", zt[:, :, 0:1], 0.0, writes=["zt"])
                    else:
                        P.dma(zt[:, :, 0], job.rwkv_shift0[s].rearrange("(c p) -> p c", p=128), writes=["zt"], allow_slow_non_contiguous=True)
                else:
                    P.dma(zt[:, :, 1:SEG + 1], zrows[:, :, col0:col0 + SEG], reads=zk, writes=["zt"])
                    P.copy("pool", zt[:, :, 0], zlast[:, :], reads=["zlast"], writes=["zt"])
                P.copy("pool", zlast[:, :], zt[:, :, SEG], reads=["zt"], writes=["zlast"])
                if t0 + SEG == T:
                    P.dma(job.rwkv_shift_out[s].rearrange("(c p) -> p c", p=128), zlast[:, :], reads=["zlast"], writes=[("rwsh", job.name, s)],
                          q="pool", allow_slow_non_contiguous=True)
                P.tt("pool", dd[:], zt[:, :, 0:SEG], zt[:, :, 1:SEG + 1], ALU.subtract, reads=["zt"], writes=["dd"])
                for blk in range(14):
                    P.stt(zs[:, blk, :], dd[:, blk, :], mu[:, blk:blk + 1], zt[:, blk, 1:SEG + 1], ALU.mult, ALU.add, reads=["dd", "zt"] + pk_, writes=["zs"])
                P.act(tw[0:64, :], zs[0:64, 12, :], AF.Tanh, reads=["zs"], writes=["tw"])
                P.copy("dve", tw[64:128, :], zs[64:128, 12, :], reads=["zs"], writes=["tw"])
                P.act(sg[:], zs[:, 13, :], AF.Sigmoid, reads=["zs"], writes=["sg"])
                for blk in range(4):
                    bs = slice(blk * 128, (blk + 1) * 128)
                    ps, pk = psP.next()
                    P.mm(ps[:, 0:SEG], wa2[0:64, bs], tw[0:64, :], reads=["tw"] + pk_, writes=[pk])
                    P.act(lw[:, blk, :], ps[:, 0:SEG], AF.Sigmoid, bias=w0[:, blk:blk + 1], reads=[pk] + pk_, writes=["lw"])
                    ps, pk = psP.next()
                    P.mm(ps[:, 0:SEG], wa2[64:128, bs], tw[64:128, :], reads=["tw"] + pk_, writes=[pk])
                    P.act(asig[:, blk, :], ps[:, 0:SEG], AF.Sigmoid, bias=a0[:, blk:blk + 1], reads=[pk] + pk_, writes=["asig"])
                P.ts("dve", lw[:], lw[:], -DECAY_C, None, ALU.mult, reads=["lw"], writes=["lw"])
                for blk in range(4):
                    P.ts("dve", tmpa[:, blk, :], zs[:, 4 + blk, :], k_k[:, blk:blk + 1], None, ALU.mult, reads=["zs"] + pk_, writes=["tmpa"])
                P.act(sqb[:], tmpa[:], AF.Square, reads=["tmpa"], writes=["sqb"])
                for blk in range(4):
                    ps, pk = psP.next()
                    P.mm(ps[:, 0:SEG], k["bo"][:], sqb[:, blk, :], reads=["sqb", "bo"], writes=[pk])
                    P.act(tmpb_[:, blk, :], ps[:, 0:SEG], AF.Sqrt, reads=[pk], writes=["tmpb"])
                P.ts("dve", tmpb_[:], tmpb_[:], 1e-12, None, ALU.max, reads=["tmpb"], writes=["tmpb"])
                P.op("dve", lambda e: e.reciprocal(tmpb_[:], tmpb_[:]), reads=["tmpb"], writes=["tmpb"])
                P.tt("dve", kkn[:], tmpa[:], tmpb_[:], ALU.mult, reads=["tmpa", "tmpb"], writes=["kkn"])
                for blk in range(4):
                    P.ts("dve", tmpa[:, blk, :], asig[:, blk, :], k_a[:, blk:blk + 1], omka[:, blk:blk + 1], ALU.mult, ALU.add,
                         reads=["asig"] + pk_, writes=["tmpa"])
                P.tt("dve", kmod[:], tmpa[:], zs[:, 4:8, :], ALU.mult, reads=["tmpa", "zs"], writes=["kmod"])
                rm = k["rmask%d" % L]
                for blk in range(4):
                    P.op("dve", lambda e, blk=blk: e.tensor_tensor_scan(cc[:, blk, :], rm[:, 0:SEG], lw[:, blk, :], 0.0, ALU.mult, ALU.add),
                         reads=["lw", "rmask%d" % L], writes=["cc"])
                P.act(epos[:], cc[:], AF.Exp, reads=["cc"], writes=["epos"])
                P.act(eneg[:], cc[:], AF.Exp, scale=-1.0, reads=["cc"], writes=["eneg"])
                P.tt("pool", tmpa[:], cc[:], lw[:], ALU.subtract, reads=["cc", "lw", "kmod"], writes=["tmpa"])
                P.act(eprev[:], tmpa[:], AF.Exp, reads=["tmpa"], writes=["eprev"])
                P.tt("dve", rT4[:], zs[:, 0:4, :], epos[:], ALU.mult, reads=["zs", "epos"], writes=["rT4"])
                P.stt(aT4[:], kkn[:], -1.0, eprev[:], ALU.mult, ALU.mult, reads=["kkn", "eprev"], writes=["aT4"])
                P.tt("pool", tmpb_[:], kkn[:], asig[:], ALU.mult, reads=["kkn", "asig"], writes=["tmpb"])
                P.tt("dve", b4[:], tmpb_[:], eneg[:], ALU.mult, reads=["tmpb", "eneg"], writes=["b4"])
                P.tt("dve", k4[:], kmod[:], eneg[:], ALU.mult, reads=["kmod", "eneg"], writes=["k4"])
                P.copy("act", v4[:], zs[:, 8:12, :], reads=["zs"], writes=["v4"])
                for blk in range(4):
                    P.stt(t4[:, blk, :], zs[:, blk, :], r_k[:, blk:blk + 1], kmod[:, blk, :], ALU.mult, ALU.mult, reads=["zs", "kmod"] + pk_, writes=["t4"])
                for dst, dkey, src, skey in ((r8, "r8", rT4, "rT4"), (a8, "a8", aT4, "aT4"), (b8, "b8", b4, "b4"), (k8, "k8", k4, "k4"),
                                             (v8, "v8", v4, "v4"), (t8, "t8", t4, "t4")):
                    to8(P, dst, dkey, src, skey)
                ec = epos[:, :, :].rearrange("p a (c l) -> p a c l", l=L)
                e4 = el[:, :, :].rearrange("p (a e) c -> p a e c", e=2)
                ecomp = C.sb(tp, "ecomp", [128, 4, NCH], F32)
                P.copy("pool", ecomp[:], ec[:, :, :, L - 1], reads=["epos"], writes=["ecomp"])
                for e in range(2):
                    P.dma(e4[:, :, e, :], ecomp[e * 64:(e + 1) * 64, :, :], reads=["ecomp"], writes=["el"], allow_slow_non_contiguous=True)
                P.flush()
            with contextlib.ExitStack() as tsc:
                ops8 = dict(r=(r8, "r8"), k=(k8, "k8"), a=(a8, "a8"), b=(b8, "b8"))
                tok4 = dict(k=(k4, "k4"), v=(v4, "v4"), b=(b4, "b4"))
                chunk_scan2(C, tsc, SEG, L, True, ops8, tok4, (el, "el"), S32, Sb, (yT8, "yT8"))
                P.flush()
            with contextlib.ExitStack() as tq:
                psQ = Rot(C, tq, "rw_psQ", [64, 4, SEG], F32, 3 if SEG > 128 else 6, psum=True)
                yb8 = C.sb(tq, "yb8", [64, 8, SEG], BF16)
                d8 = C.sb(tq, "d8", [64, 8, SEG], F32)
                rs8 = C.sb(tq, "rs8", [64, 8, SEG], F32)
                tm8 = C.sb(tq, "tm8", [64, 8, SEG], F32)
                o8 = C.sb(tq, "o8", [64, 8, SEG], BF16)
                one64 = k["onesB"][0:64, 0:64]
                P.copy("act", yb8[:], yT8[:], reads=["yT8"], writes=["yb8"])
                for half in range(2):
                    hs = slice(half * 4, half * 4 + 4)
                    ps, pk = psQ.next()
                    for i in range(4):
                        P.mm(ps[:, i, :], one64, yb8[:, half * 4 + i, :], reads=["yb8", "onesB"], writes=[pk])
                    P.stt(d8[:, hs, :], ps[:, :, :], -1.0 / 64, yT8[:, hs, :], ALU.mult, ALU.add, reads=[pk, "yT8"], writes=[("d8", half)])
                P.act(yb8[:], d8[:], AF.Square, reads=[("d8", 0), ("d8", 1), "yb8"], writes=["yb8"])
                for half in range(2):
                    hs = slice(half * 4, half * 4 + 4)
                    ps, pk = psQ.next()
                    for i in range(4):
                        P.mm(ps[:, i, :], one64, yb8[:, half * 4 + i, :], reads=["yb8", "onesB"], writes=[pk])
                    P.act(rs8[:, hs, :], ps[:, :, :], AF.Ln, bias=k["eps_gn"][0:64, 0:1], scale=1.0 / 64, reads=[pk, "epsv"], writes=[("rs8", half)])
                P.act(rs8[:], rs8[:], AF.Exp, scale=-0.5, reads=[("rs8", 0), ("rs8", 1)], writes=["rs8r"])
                P.tt("dve", d8[:], d8[:], rs8[:], ALU.mult, reads=[("d8", 0), ("d8", 1), "rs8r"], writes=["d8n"])
                for h in range(8):
                    P.ts("dve", d8[:, h, :], d8[:, h, :], lng8[:, h:h + 1], lnb8[:, h:h + 1], ALU.mult, ALU.add, reads=["d8n"] + pk_, writes=[("d8a", h)])
                for half in range(2):
                    hs = slice(half * 4, half * 4 + 4)
                    ps, pk = psQ.next()
                    for i in range(4):
                        P.mm(ps[:, i, :], one64, t8[:, half * 4 + i, :], reads=["t8", "onesB"], writes=[pk])
                    P.tt("dve", tm8[:, hs, :], ps[:, :, :], v8[:, hs, :], ALU.mult, reads=[pk, "v8"], writes=[("tm8", half)])
                P.tt("dve", d8[:], d8[:], tm8[:], ALU.add, reads=[("d8a", h) for h in range(8)] + [("tm8", 0), ("tm8", 1)], writes=["d8f"])
                for half in range(2):
                    hs = slice(half * 4, half * 4 + 4)
                    ps, pk = psQ.next()
                    for i in range(4):
                        h = half * 4 + i
                        P.mm(ps[:, i, :], g2[:, h * 64:(h + 1) * 64], sg[:, :], reads=["sg"] + pk_, writes=[pk])
                    P.tt("dve", o8[:, hs, :], ps[:, :, :], d8[:, hs, :], ALU.mult, reads=[pk, "d8f"], writes=[("o8", half)])
                P.dma(job.mixT[0:512, col0:col0 + SEG].rearrange("(h k) t -> k h t", k=64), o8[:], reads=[("o8", 0), ("o8", 1)],
                      writes=[("mixT", job.name, c, col0 // 512, hh) for c in range(4) for hh in range(2)], q="pool")
                P.flush()
        with contextlib.ExitStack() as tf_:
            psO = C.ps(tf_, "rw_psO", [64, 8, 64], F32)
            so = C.sb(tf_, "so", [64, 8, 64], F32)
            for h in range(8):
                P.tr(psO[:, h, :], S32[:, h, :], k["identF"][0:64, 0:64], reads=S8K + ["identF"], writes=["psO"])
            P.copy("dve", so[:], psO[:], reads=["psO"], writes=["so"])
            P.dma(job.rwkv_out[s].rearrange("h v k -> v h k"), so[:], reads=["so"], writes=[("rwst", job.name, s)], q="pool")
            P.flush()


def hgrn_mixer2(C, job, s, I, prm):
    P, k = C.P, C.k
    T = job.Ts[s]
    L = min(32, T)
    SEG = min(256, T)
    NCH = SEG // L
    base = job.bases[s]
    S8K = [("S32", h) for h in range(8)]
    with contextlib.ExitStack() as ts:
        S32 = C.sb(ts, "S32", [64, 8, 64], F32)
        Sb = C.sb(ts, "Sb", [64, 8, 64], BF16)
        k4, v4 = [C.sb(ts, n, [128, 4, SEG], BF16) for n in ("k4", "v4")]
        r8, k8 = [C.sb(ts, n, [64, 8, SEG], BF16) for n in ("r8", "k8")]
        el = C.sb(ts, "el", [64, 8, NCH], F32)
        yT8 = C.sb(ts, "yT8", [64, 8, SEG], F32)
        lb, oml, noml, ng8 = prm["lb"], prm["oml"], prm["noml"], prm["ng8"]
        pk_ = ["hgprm"]
        if job.hgrn_s0[s] is None:
            P.memset("pool", S32[:], 0.0, writes=S8K)
        else:
            P.dma(S32[:], job.hgrn_s0[s].rearrange("h k v -> k h v"), writes=S8K)
        P.copy("act", Sb[:], S32[:], reads=S8K, writes=["Sb"])
        P.flush()
        rm = k["rmask%d" % L]
        for t0 in range(0, T, SEG):
            col0 = base + t0
            with contextlib.ExitStack() as tp:
                z4 = C.sb(tp, "z4", [128, 12, SEG], F32)
                f4 = lambda nm: C.sb(tp, nm, [128, 4, SEG], F32)
                qf, sgf, lw, kin, cc, epos, eneg = [f4(n) for n in ("qf", "sgf", "lw", "kin", "cc", "epos", "eneg")]
                rT4 = C.sb(tp, "rT4", [128, 4, SEG], BF16)
                zk = [("zT", job.name, nb, i) for nb in range(12, 24) for i in range(col0 // 512, (col0 + SEG - 1) // 512 + 1)]
                P.dma(z4[:], job.zT[1536:3072, col0:col0 + SEG].rearrange("(c p) t -> p c t", p=128), reads=zk, writes=["z4"])
                P.act(qf[:], z4[:, 0:4, :], AF.Silu, reads=["z4"], writes=["qf"])
                P.act(sgf[:], z4[:, 4:8, :], AF.Sigmoid, reads=["z4"], writes=["sgf"])
                for blk in range(4):
                    P.ts("dve", lw[:, blk, :], sgf[:, blk, :], oml[:, blk:blk + 1], lb[:, blk:blk + 1], ALU.mult, ALU.add, reads=["sgf"] + pk_, writes=["lw"])
                    P.ts("dve", kin[:, blk, :], sgf[:, blk, :], noml[:, blk:blk + 1], oml[:, blk:blk + 1], ALU.mult, ALU.add, reads=["sgf"] + pk_, writes=["kin"])
                P.act(lw[:], lw[:], AF.Ln, reads=["lw"], writes=["lw"])
                for blk in range(4):
                    P.op("dve", lambda e, blk=blk: e.tensor_tensor_scan(cc[:, blk, :], rm[:, 0:SEG], lw[:, blk, :], 0.0, ALU.mult, ALU.add),
                         reads=["lw", "rmask%d" % L], writes=["cc"])
                P.act(epos[:], cc[:], AF.Exp, reads=["cc"], writes=["epos"])
                P.act(eneg[:], cc[:], AF.Exp, scale=-1.0, reads=["cc"], writes=["eneg"])
                P.tt("dve", rT4[:], qf[:], epos[:], ALU.mult, reads=["qf", "epos"], writes=["rT4"])
                P.tt("dve", k4[:], kin[:], eneg[:], ALU.mult, reads=["kin", "eneg"], writes=["k4"])
                P.copy("pool", v4[:], z4[:, 8:12, :], reads=["z4"], writes=["v4"])
                to8(P, r8, "r8", rT4, "rT4")
                to8(P, k8, "k8", k4, "k4")
                ec = epos[:, :, :].rearrange("p a (c l) -> p a c l", l=L)
                e4 = el[:, :, :].rearrange("p (a e) c -> p a e c", e=2)
                ecomp = C.sb(tp, "ecomp", [128, 4, NCH], F32)
                P.copy("pool", ecomp[:], ec[:, :, :, L - 1], reads=["epos"], writes=["ecomp"])
                for e in range(2):
                    P.dma(e4[:, :, e, :], ecomp[e * 64:(e + 1) * 64, :, :], reads=["ecomp"], writes=["el"], allow_slow_non_contiguous=True)
                P.flush()
            with contextlib.ExitStack() as tsc:
                chunk_scan2(C, tsc, SEG, L, False, dict(r=(r8, "r8"), k=(k8, "k8")), dict(k=(k4, "k4"), v=(v4, "v4")),
                            (el, "el"), S32, Sb, (yT8, "yT8"))
                P.flush()
            with contextlib.ExitStack() as tq:
                psQ = Rot(C, tq, "hg_psQ", [64, 4, SEG], F32, 3 if SEG > 128 else 6, psum=True)
                yb8 = C.sb(tq, "yb8", [64, 8, SEG], BF16)
                rs8 = C.sb(tq, "rs8", [64, 8, SEG], F32)
                zg8 = C.sb(tq, "zg8", [64, 8, SEG], F32)
                o8 = C.sb(tq, "o8", [64, 8, SEG], BF16)
                one64 = k["onesB"][0:64, 0:64]
                zkg = [("zT", job.name, nb, i) for nb in range(24, 28) for i in range(col0 // 512, (col0 + SEG - 1) // 512 + 1)]
                P.dma(zg8[:], job.zT[3072:3584, col0:col0 + SEG].rearrange("(h k) t -> k h t", k=64), reads=zkg, writes=["zg8"])
                P.act(zg8[:], zg8[:], AF.Silu, reads=["zg8"], writes=["zg8"])
                P.act(yb8[:], yT8[:], AF.Square, reads=["yT8"], writes=["yb8"])
                for half in range(2):
                    hs = slice(half * 4, half * 4 + 4)
                    ps, pk = psQ.next()
                    for i in range(4):
                        P.mm(ps[:, i, :], one64, yb8[:, half * 4 + i, :], reads=["yb8", "onesB"], writes=[pk])
                    P.act(rs8[:, hs, :], ps[:, :, :], AF.Ln, bias=k["eps_rms"][0:64, 0:1], scale=1.0 / 64, reads=[pk, "epsv"], writes=[("rs8", half)])
                P.act(rs8[:], rs8[:], AF.Exp, scale=-0.5, reads=[("rs8", 0), ("rs8", 1)], writes=["rs8r"])
                P.tt("dve", rs8[:], rs8[:], yT8[:], ALU.mult, reads=["rs8r", "yT8"], writes=["rs8y"])
                for h in range(8):
                    P.stt(o8[:, h, :], rs8[:, h, :], ng8[:, h:h + 1], zg8[:, h, :], ALU.mult, ALU.mult, reads=["rs8y", "zg8"] + pk_, writes=[("o8", h)])
                P.dma(job.mixT[512:1024, col0:col0 + SEG].rearrange("(h k) t -> k h t", k=64), o8[:], reads=[("o8", h) for h in range(8)],
                      writes=[("mixT", job.name, 4 + c, col0 // 512, hh) for c in range(4) for hh in range(2)], q="pool")
                P.flush()
        P.dma(job.hgrn_out[s].rearrange("h k v -> k h v"), S32[:], reads=S8K, writes=[("hgst", job.name, s)], q="pool")
        P.flush()
```

```python
import os
import numpy as np
import concourse.bass as bass
import concourse.mybir as mybir
from concourse.bass_utils import run_bass_kernel_spmd

F32 = mybir.dt.float32
BF16 = mybir.dt.bfloat16
AF = mybir.ActivationFunctionType
ALU = mybir.AluOpType
AX = mybir.AxisListType

N_CORES = 8


class Prog:
    ENGS = ("pe", "act", "dve", "pool", "sp")
    DMA_SLOTS = {"sp": 20, "pool": 12, "act": 8}

    def __init__(self, nc, stack):
        self.nc = nc
        self.ops = []
        self.same_engine_sync = True
        self.esem = {e: stack.enter_context(nc.semaphore("s_" + e)) for e in ("pe", "act", "dve", "pool")}
        self.dsem = {q: [stack.enter_context(nc.semaphore("d_%s%d" % (q, i))) for i in range(k)]
                     for q, k in self.DMA_SLOTS.items()}
        self.ecount = {e: 0 for e in self.esem}
        self.dcount = {q: 0 for q in self.dsem}
        self.slot_last = {}
        self.last_w = {}
        self.readers = {}
        self.known = {e: {} for e in self.ENGS}
        self.final_dma = {}
        self.n_total = 0

    def op(self, eng, fn, reads=(), writes=(), dma=False):
        self.n_rec = getattr(self, "n_rec", 0) + 1
        if self.n_rec == int(os.environ.get("K_SHOW", "-1")):
            import traceback
            traceback.print_stack(limit=4)
            print("SHOW op", eng, reads, writes)
        if self.n_rec > int(os.environ.get("K_MAXOPS", "100000000")):
            return
        self.ops.append(dict(eng=eng, fn=fn, reads=tuple(reads), writes=tuple(writes), dma=dma, serial=getattr(self, "pe_serial", False)))

    def mm(self, out, lhsT, rhs, start=True, stop=True, reads=(), writes=()):
        self.op("pe", lambda e: e.matmul(out, lhsT, rhs, start=start, stop=stop), reads, writes)

    def tr(self, out, in_, ident, reads=(), writes=()):
        self.op("pe", lambda e: e.transpose(out, in_, ident), reads, writes)

    def act(self, out, in_, func, bias=0.0, scale=1.0, reads=(), writes=(), accum_out=None):
        if accum_out is None:
            self.op("act", lambda e: e.activation(out, in_, func, bias=bias, scale=scale), reads, writes)
        else:
            self.op("act", lambda e: e.activation(out, in_, func, bias=bias, scale=scale, accum_out=accum_out), reads, writes)

    def tt(self, eng, out, in0, in1, op, reads=(), writes=()):
        self.op(eng, lambda e: e.tensor_tensor(out, in0, in1, op), reads, writes)

    def ts(self, eng, out, in0, s1, s2, op0, op1=None, reads=(), writes=()):
        if op1 is None:
            self.op(eng, lambda e: e.tensor_scalar(out, in0, s1, None, op0), reads, writes)
        else:
            self.op(eng, lambda e: e.tensor_scalar(out, in0, s1, s2, op0, op1), reads, writes)

    def stt(self, out, in0, scalar, in1, op0, op1, reads=(), writes=()):
        self.op("dve", lambda e: e.scalar_tensor_tensor(out, in0, scalar, in1, op0, op1), reads, writes)

    def copy(self, eng, out, in_, reads=(), writes=()):
        if eng == "act":
            self.op("act", lambda e: e.copy(out, in_), reads, writes)
        else:
            self.op(eng, lambda e: e.tensor_copy(out, in_), reads, writes)

    def memset(self, eng, ap, val, writes=()):
        self.op(eng, lambda e: e.memset(ap, val), (), writes)

    def dma(self, out, in_, reads=(), writes=(), q="sp", **kw):
        self.op(q, lambda e: e.dma_start(out=out, in_=in_, **kw), reads, writes, dma=True)

    def semof(self, key):
        return self.esem[key[1]] if key[0] == "e" else self.dsem[key[1]][key[2]]

    def flush(self, final=False):
        import inspect
        fr = inspect.stack()[1]
        if not hasattr(self, "flush_log"):
            self.flush_log = []
        _cnt = {}
        for _o in self.ops:
            _cnt[_o["eng"]] = _cnt.get(_o["eng"], 0) + 1
        self.flush_log.append(("%s:%d" % (fr.function, fr.lineno), len(self.ops), _cnt))
        nc = self.nc
        ops = self.ops
        self.ops = []
        n = len(ops)
        self.n_total += n
        plan = {e: [] for e in self.ENGS}
        for o in ops:
            e = o["eng"]
            deps = []
            if o["dma"]:
                j = self.dcount[e]
                self.dcount[e] += 1
                k = len(self.dsem[e])
                key = ("d", e, j % k)
                val = 16 * (j // k + 1)
                if key in self.slot_last:
                    deps.append(self.slot_last[key])
                mysig = (key, val, e, True)
                self.slot_last[key] = mysig
                self.final_dma[key] = val
            else:
                self.ecount[e] += 1
                mysig = (("e", e), self.ecount[e], e, False)
            for r in o["reads"]:
                if r in self.last_w:
                    deps.append(self.last_w[r])
            for w in o["writes"]:
                if w in self.last_w:
                    deps.append(self.last_w[w])
                deps.extend(self.readers.get(w, ()))
            for r in o["reads"]:
                self.readers.setdefault(r, []).append(mysig)
            for w in o["writes"]:
                self.last_w[w] = mysig
                self.readers[w] = []
            need = {}
            for (dkey, dval, deng, ddma) in deps:
                if (not ddma) and deng == e:
                    if (not self.same_engine_sync) or e == "pe":
                        continue
                if need.get(dkey, 0) < dval:
                    need[dkey] = dval
            if e == "pe" and o.get("serial") and self.ecount["pe"] > 1:
                need[("e", "pe")] = max(need.get(("e", "pe"), 0), self.ecount["pe"] - 1)
            wl = []
            for dkey, dval in need.items():
                if self.known[e].get(dkey, 0) >= dval:
                    continue
                self.known[e][dkey] = dval
                wl.append((dkey, dval))
            plan[e].append((wl, o["fn"], mysig[0], o["dma"]))

        def run_engine(ename, eng):
            for wl, fn, key, is_dma in plan[ename]:
                for dkey, dval in wl[:-1]:
                    eng.wait_ge(self.semof(dkey), dval)
                ins = fn(eng)
                if wl:
                    ins._wait_ge(self.semof(wl[-1][0]), wl[-1][1])
                ins.then_inc(self.semof(key), 16 if is_dma else 1)
            if final and ename == "sp":
                for key, val in self.final_dma.items():
                    eng.wait_ge(self.semof(key), val)
                for e2 in self.esem:
                    if self.ecount[e2] > 0:
                        eng.wait_ge(self.esem[e2], self.ecount[e2])

        with nc.Block() as block:
            if plan["pe"]:
                @block.tensor
                def _(eng):
                    run_engine("pe", eng)
            if plan["act"]:
                @block.scalar
                def _(eng):
                    run_engine("act", eng)
            if plan["dve"]:
                @block.vector
                def _(eng):
                    run_engine("dve", eng)
            if plan["pool"]:
                @block.gpsimd
                def _(eng):
                    run_engine("pool", eng)
            if plan["sp"] or final:
                @block.sync
                def _(eng):
                    run_engine("sp", eng)
        return n


D = 1024
KC = 8
T_PROMPT = 4096
T_SAMPLE = 16
NSEQ_S = 2
P_FOX = 2048
P_CHUNK = 512
A_COLS = 1792
AB_COLS = 3336
CD_COLS = 3584
DFF = 2816
RMS_EPS = 1e-6
GN_EPS = 64e-5
DECAY_C = float(np.exp(-0.5))


class Ctx:
    def __init__(self, nc, P, st):
        self.nc = nc
        self.P = P
        self.st = st
        self.uid = 0
        self.dram = {}

    def name(self, base):
        self.uid += 1
        return "%s_%d" % (base, self.uid)

    def sb(self, st, base, shape, dt=F32):
        return st.enter_context(self.nc.sbuf_tensor(self.name(base), list(shape), dt))

    def ps(self, st, base, shape, dt=F32):
        return st.enter_context(self.nc.psum_tensor(self.name(base), list(shape), dt))

    def dr(self, base, shape, dt=F32, kind=None):
        if kind is None:
            t = self.nc.dram_tensor(base, list(shape), dt)
        else:
            t = self.nc.dram_tensor(base, list(shape), dt, kind=kind)
        ap = t.ap()
        self.dram[base] = ap
        return ap


class Rot:
    def __init__(self, C, st, base, shape, dt, n, psum=False):
        self.tiles = [(C.ps if psum else C.sb)(st, base, shape, dt) for _ in range(n)]
        self.keys = [(base, C.uid, i) for i in range(n)]
        self.i = -1

    def next(self):
        self.i = (self.i + 1) % len(self.tiles)
        return self.tiles[self.i], self.keys[self.i]


def build_consts(C):
    P, st = C.P, C.st
    k = {}
    identF = C.sb(st, "identF", [128, 128], F32)
    P.memset("pool", identF[:], 1.0, writes=["identF"])
    P.op("pool", lambda e: e.affine_select(identF[:], identF[:], [[-1, 128]], ALU.is_equal, 0.0, base=0, channel_multiplier=1),
         reads=["identF"], writes=["identF"])
    identB = C.sb(st, "identB", [128, 128], BF16)
    P.copy("pool", identB[:], identF[:], reads=["identF"], writes=["identB"])
    flipF = C.sb(st, "flipF", [128, 128], F32)
    P.memset("pool", flipF[:], 1.0, writes=["flipF"])
    P.op("pool", lambda e: e.affine_select(flipF[:], flipF[:], [[1, 128]], ALU.is_equal, 0.0, base=-127, channel_multiplier=1),
         reads=["flipF"], writes=["flipF"])
    onesB = C.sb(st, "onesB", [128, 128], BF16)
    P.memset("pool", onesB[:], 1.0, writes=["onesB"])
    onesF = C.sb(st, "onesF", [128, 128], F32)
    P.memset("pool", onesF[:], 1.0, writes=["onesF"])
    bo = C.sb(st, "blockones", [128, 128], BF16)
    P.memset("pool", bo[:], 0.0, writes=["bo"])
    P.memset("pool", bo[0:64, 0:64], 1.0, writes=["bo"])
    P.memset("pool", bo[64:128, 64:128], 1.0, writes=["bo"])
    triF = C.sb(st, "triF", [128, 128], F32)
    P.memset("pool", triF[:], 1.0, writes=["triF"])
    P.op("pool", lambda e: e.affine_select(triF[:], triF[:], [[1, 128]], ALU.is_ge, 0.0, base=0, channel_multiplier=-1),
         reads=["triF"], writes=["triF"])
    def cmask(nm, pattern, base, cm, op):
        m = C.sb(st, nm, [64, 8, 64], F32)
        P.memset("pool", m[:], 1.0, writes=[nm])
        for h in range(8):
            P.op("pool", lambda e, h=h: e.affine_select(m[:, h, :], m[:, h, :], pattern, op, 0.0, base=base, channel_multiplier=cm),
                 reads=[nm], writes=[nm])
        return m
    k["m_su"] = cmask("m_su", [[1, 64]], 0, -1, ALU.is_gt)
    k["m_ui"] = cmask("m_ui", [[1, 64]], 0, -1, ALU.is_ge)
    k["m_sl"] = cmask("m_sl", [[-1, 64]], 0, 1, ALU.is_gt)
    k["i8"] = cmask("i8", [[-1, 64]], 0, 1, ALU.is_equal)
    cm_f = C.sb(st, "cm_f", [128, 4, 512], F32)
    P.memset("pool", cm_f[:], 1.0, writes=["cm_f"])
    for d in range(4):
        P.op("pool", lambda e, d=d: e.affine_select(cm_f[:, d, :], cm_f[:, d, :], [[1, 512]], ALU.is_ge, 0.0, base=-128 * d, channel_multiplier=-1),
             reads=["cm_f"], writes=["cm_f"])
    cmB = C.sb(st, "cmB", [128, 4, 512], BF16)
    P.copy("pool", cmB[:], cm_f[:], reads=["cm_f"], writes=["cmB"])
    k.update(identF=identF, identB=identB, flipF=flipF, onesB=onesB, onesF=onesF, bo=bo, triF=triF, cmB=cmB)
    C.k = k
    P.flush()


import contextlib


def cast_weight(C, W, K, N, name):
    P = C.P
    NB = (N + 127) // 128
    kc_n = K // 128
    Wb = C.dr(name, [NB, 128, kc_n, 128], BF16)
    nfull = N // 128
    rem = N - nfull * 128
    with contextlib.ExitStack() as st:
        ld = Rot(C, st, "cw_ld", [128, NB * 128], F32, 2)
        cb = Rot(C, st, "cw_cb", [128, NB * 128], BF16, 2)
        for kc in range(kc_n):
            t, tk = ld.next()
            b, bk = cb.next()
            P.dma(t[:, 0:N], W[kc * 128:(kc + 1) * 128, :], writes=[tk])
            eng = ("dve", "act")[kc % 2]
            P.copy(eng, b[:, 0:N], t[:, 0:N], reads=[tk], writes=[bk])
            if nfull:
                dst = Wb[0:nfull, :, kc, :].rearrange("nb p n -> p nb n")
                src = b[:, 0:nfull * 128].rearrange("p (nb n) -> p nb n", n=128)
                P.dma(dst, src, reads=[bk], writes=[(name, kc)], q="pool")
            if rem:
                P.dma(Wb[nfull, :, kc, 0:rem], b[:, nfull * 128:N], reads=[bk], writes=[(name, kc, "r")], q="pool")
        P.flush()
    return Wb, [(name, kc) for kc in range(kc_n)] + ([(name, kc, "r") for kc in range(kc_n)] if rem else [])


def adaln_phase(C, I, ncol, c_cols):
    P, st, k = C.P, C.st, C.k
    mods = {}
    for l in range(2):
        for j in range(2):
            mods[(l, j)] = C.sb(st, "mod", [128, 24, ncol], F32)
    with contextlib.ExitStack() as ts:
        cT = C.sb(ts, "cT", [128, KC, ncol], F32)
        for j, cap in enumerate(c_cols):
            P.dma(cT[:, :, j], cap.rearrange("(c p) -> p c", p=128), writes=["cT"], allow_slow_non_contiguous=True)
        csT = C.sb(ts, "csT", [128, KC, ncol], F32)
        P.act(csT[:], cT[:], AF.Silu, reads=["cT"], writes=["csT"])
        wl = Rot(C, ts, "ada_w", [128, 3072], F32, 3)
        pacc = [C.ps(ts, "ada_acc", [128, 512], F32) for _ in range(6)]
        ptr = Rot(C, ts, "ada_ptr", [128, 24, 4], F32, 1, psum=True)
        mrow = Rot(C, ts, "ada_mrow", [4, 3072], F32, 2)
        brow = Rot(C, ts, "ada_brow", [4, 3072], F32, 2)
        for l in range(2):
            for j in range(2):
                m = mods[(l, j)]
                bt, btk = brow.next()
                for cc in range(ncol):
                    P.dma(bt[cc:cc + 1, :], I["ada_b"][l, j:j + 1, :], writes=[btk])
                for kc in range(KC):
                    w, wk = wl.next()
                    P.dma(w[:], I["ada_w"][l, j, kc * 128:(kc + 1) * 128, :], writes=[wk])
                    for c6 in range(6):
                        P.mm(pacc[c6][0:ncol, :], csT[:, kc, 0:ncol], w[:, c6 * 512:(c6 + 1) * 512], start=(kc == 0), stop=(kc == KC - 1),
                             reads=[wk, "csT"], writes=[("ada_acc", c6)])
                mr, mrk = mrow.next()
                for c6 in range(6):
                    P.tt("dve", mr[0:ncol, c6 * 512:(c6 + 1) * 512], pacc[c6][0:ncol, :], bt[0:ncol, c6 * 512:(c6 + 1) * 512], ALU.add,
                         reads=[("ada_acc", c6), btk], writes=[mrk])
                pt, ptk = ptr.next()
                for nb in range(24):
                    P.tr(pt[:, nb, 0:ncol], mr[0:ncol, nb * 128:(nb + 1) * 128], k["identF"][0:ncol, 0:ncol], reads=[mrk, "identF"], writes=[ptk])
                P.copy("act", m[:, :, :], pt[:, :, 0:ncol], reads=[ptk], writes=[("mod", l, j)])
        P.flush()
    C.mods = mods


def load_vec_fm(C, st, ap, n, name, q="sp"):
    t = C.sb(st, name, [128, n], F32)
    key = C.name(name)
    C.P.dma(t[:], ap.rearrange("(c p) -> p c", p=128), writes=[key], allow_slow_non_contiguous=True, q=q)
    return t, key


class Job:
    pass


def seq_groups(job, G):
    out = []
    for s in range(job.nseq):
        T = job.Ts[s]
        for t0 in range(0, T, G):
            gs = min(G, T - t0)
            out.append((s, t0, gs, job.bases[s] + t0))
    return out


def x_to_fm(C, job):
    P, k = C.P, C.k
    with contextlib.ExitStack() as st:
        xin = Rot(C, st, "xin", [128, 1024], F32, 2)
        pst = Rot(C, st, "x_ps", [128, 4, 128], F32, 4, psum=True)
        xo = Rot(C, st, "xo", [128, KC, 128], F32, 2)
        for (s, t0, gs, col0) in seq_groups(job, 128):
            t, tk = xin.next()
            P.dma(t[0:gs, :], job.x_in[s][t0:t0 + gs, :], writes=[tk])
            o, ok = xo.next()
            for half in range(2):
                ps, pk = pst.next()
                for c4 in range(4):
                    c = half * 4 + c4
                    P.tr(ps[:, c4, 0:gs], t[0:gs, c * 128:(c + 1) * 128], k["identF"][0:gs, 0:gs], reads=[tk, "identF"], writes=[pk])
                P.copy("act" if half else "dve", o[:, half * 4:half * 4 + 4, 0:gs], ps[:, :, 0:gs], reads=[pk], writes=[ok])
            P.dma(job.xT[:, col0:col0 + gs].rearrange("(c p) t -> p c t", p=128), o[:, :, 0:gs], reads=[ok], writes=[("xT", job.name, c, col0 // 512) for c in range(8)], q="pool")
        P.flush()


def xkeys(job, col0, gs):
    return [("xT", job.name, i) for i in range(col0 // 128, (col0 + gs - 1) // 128 + 1)]


def norm_to_hT(C, st, job, gT, l, j, hT, hkey):
    P, k = C.P, C.k
    m = C.mods[(l, j)]
    with contextlib.ExitStack() as ts:
        gsc = C.sb(ts, "gsc", [128, KC, 3], F32)
        for col in range(3):
            P.stt(gsc[:, :, col], m[:, 8:16, col], 1.0, gT[:], ALU.add, ALU.mult, reads=[("mod", l, j), "smallprm"], writes=["gsc"])
        xg = Rot(C, ts, "xg", [128, KC, 512], F32, 2)
        sq = Rot(C, ts, "sq", [128, KC, 512], BF16, 2)
        pss = Rot(C, ts, "ss_ps", [128, 512], F32, 2, psum=True)
        rs = Rot(C, ts, "rstd", [128, 512], F32, 2)
        tmp = Rot(C, ts, "htmp", [128, 512], F32, 3)
        for (s, t0, gs, col0) in seq_groups(job, 512):
            x, xk = xg.next()
            P.dma(x[:, :, 0:gs], job.xT[:, col0:col0 + gs].rearrange("(c p) t -> p c t", p=128),
                  reads=[key for c in range(8) for key in xk1(job, c, col0, gs)], writes=[xk])
            q, qk = sq.next()
            P.act(q[:, :, 0:gs], x[:, :, 0:gs], AF.Square, reads=[xk], writes=[qk])
            ps, pk = pss.next()
            for c in range(KC):
                P.mm(ps[:, 0:gs], k["onesB"][:], q[:, c, 0:gs], start=(c == 0), stop=(c == KC - 1), reads=[qk, "onesB"], writes=[pk])
            r, rk = rs.next()
            P.act(r[:, 0:gs], ps[:, 0:gs], AF.Ln, bias=k["eps_rms"][:, 0:1], scale=1.0 / D, reads=[pk, "epsv"], writes=[rk])
            P.act(r[:, 0:gs], r[:, 0:gs], AF.Exp, scale=-0.5, reads=[rk], writes=[rk])
            col = job.modcol[s]
            for c in range(KC):
                t_, tk_ = tmp.next()
                P.stt(t_[:, 0:gs], x[:, c, 0:gs], gsc[:, c, col:col + 1], r[:, 0:gs], ALU.mult, ALU.mult, reads=[xk, rk, "gsc"], writes=[tk_])
                P.act(hT[:, c, col0:col0 + gs], t_[:, 0:gs], AF.Identity, bias=m[:, c, col:col + 1], reads=[tk_, ("mod", l, j)], writes=[hkey])
        P.flush()


def proj(C, job, srcT, skey, Wb, wkeys, nbs, kcn, post, G=512):
    P = C.P
    with contextlib.ExitStack() as ts:
        wr = Rot(C, ts, "pw", [128, kcn, 128], BF16, 3)
        pr = Rot(C, ts, "pp", [128, 512], F32, 6, psum=True)
        groups = seq_groups(job, G)
        for nb in nbs:
            w, wk = wr.next()
            P.dma(w[:], Wb[nb], reads=wkeys, writes=[wk])
            for (s, t0, gs, col0) in groups:
                ps, pk = pr.next()
                for kc in range(kcn):
                    P.mm(ps[:, 0:gs], w[:, kc, :], srcT[:, kc, col0:col0 + gs], start=(kc == 0), stop=(kc == kcn - 1),
                         reads=[wk, skey], writes=[pk])
                post(nb, s, t0, gs, col0, ps, pk)


def chunk_scan(C, ts, nseg_cols, L, delta, rT, kT, vT, aT, bT, epos, S32, Spad, yT, keys, pools):
    P, k = C.P, C.k
    P.pe_serial = True
    psT, psH = pools["psT"], pools["psH"]
    nlev = int(np.log2(L))
    kin = [keys[n] for n in ("rT", "kT", "vT")] + ([keys["aT"], keys["bT"]] if delta else [])
    for c in range(nseg_cols // L):
        cs = slice(c * L, (c + 1) * L)
        pads = {}
        for nm, src in (("K", kT), ("V", vT)) + ((("B", bT),) if delta else ()):
            ps, pk = psT.next()
            for blk in range(4):
                P.tr(ps[0:L, blk, :], src[:, blk, cs], k["identB"][:, :], reads=kin + ["identB"], writes=[pk])
            pad, padk = pools["pad" + nm].next()
            for e in range(2):
                P.copy("act" if e else "dve", pad[0:L, :, e, e * 64:(e + 1) * 64], ps[0:L, :, e * 64:(e + 1) * 64], reads=[pk], writes=[padk])
            pads[nm] = (pad, padk)
        Kp, Kk = pads["K"]
        Vp, Vk = pads["V"]

        def hsl(h):
            return h // 2, h % 2, (h % 2) * 64

        def mm8(lhs_of, rhs_of, reads):
            ps, pk = psH.next()
            for h in range(8):
                P.mm(ps[0:L, h, 0:L], lhs_of(h), rhs_of(h), reads=reads, writes=[pk])
            return ps, pk

        def fm(t, h):
            pr, e, pb = hsl(h)
            return t[pb:pb + 64, pr, cs]

        if delta:
            psA, pkA = mm8(lambda h: fm(bT, h), lambda h: fm(aT, h), kin)
            psB, pkB = mm8(lambda h: fm(aT, h), lambda h: fm(bT, h), kin)
            PT, PTk = pools["mb"].next()
            Pm, Pmk = pools["mb"].next()
            TTf, TTfk = pools["ttf"].next()
            TTb, TTbk = pools["mb"].next()
            P.tt("dve", TTf[0:L, :, 0:L], psA[0:L, :, 0:L], k["m_su"][0:L, :, 0:L], ALU.mult, reads=[pkA, "m_su"], writes=[TTfk])
            P.copy("act", PT[0:L, :, 0:L], TTf[0:L, :, 0:L], reads=[TTfk], writes=[PTk])
            P.tt("dve", Pm[0:L, :, 0:L], psB[0:L, :, 0:L], k["m_sl"][0:L, :, 0:L], ALU.mult, reads=[pkB, "m_sl"], writes=[Pmk])
            P.tt("pool", TTf[0:L, :, 0:L], TTf[0:L, :, 0:L], k["i8"][0:L, :, 0:L], ALU.add, reads=[TTfk, "i8"], writes=[TTfk])
            P.copy("act", TTb[0:L, :, 0:L], TTf[0:L, :, 0:L], reads=[TTfk], writes=[TTbk])
            for j in range(1, nlev):
                psA, pkA = mm8(lambda h: PT[0:L, h, 0:L], lambda h: Pm[0:L, h, 0:L], [PTk, Pmk])
                last = (j == nlev - 1)
                if not last:
                    psB, pkB = mm8(lambda h: Pm[0:L, h, 0:L], lambda h: PT[0:L, h, 0:L], [PTk, Pmk])
                Pm2, Pm2k = pools["mb"].next()
                P.copy("act", Pm2[0:L, :, 0:L], psA[0:L, :, 0:L], reads=[pkA], writes=[Pm2k])
                if not last:
                    PT2, PT2k = pools["mb"].next()
                    P.copy("dve", PT2[0:L, :, 0:L], psB[0:L, :, 0:L], reads=[pkB], writes=[PT2k])
                    PT, PTk = PT2, PT2k
                Pm, Pmk = Pm2, Pm2k
                psC, pkC = mm8(lambda h: Pm[0:L, h, 0:L], lambda h: TTb[0:L, h, 0:L], [Pmk, TTbk])
                P.tt("dve", TTf[0:L, :, 0:L], TTf[0:L, :, 0:L], psC[0:L, :, 0:L], ALU.add, reads=[pkC, TTfk], writes=[TTfk])
                TTb, TTbk = pools["mb"].next()
                P.copy("act", TTb[0:L, :, 0:L], TTf[0:L, :, 0:L], reads=[TTfk], writes=[TTbk])
            psA, pkA = mm8(lambda h: fm(kT, h), lambda h: fm(aT, h), kin)
            Mak, Makk = pools["mb"].next()
            P.tt("dve", Mak[0:L, :, 0:L], psA[0:L, :, 0:L], k["m_su"][0:L, :, 0:L], ALU.mult, reads=[pkA, "m_su"], writes=[Makk])
            psA, pkA = mm8(lambda h: fm(bT, h), lambda h: fm(rT, h), kin)
            Mrb, Mrbk = pools["mb"].next()
            P.tt("dve", Mrb[0:L, :, 0:L], psA[0:L, :, 0:L], k["m_ui"][0:L, :, 0:L], ALU.mult, reads=[pkA, "m_ui"], writes=[Mrbk])
        psA, pkA = mm8(lambda h: fm(kT, h), lambda h: fm(rT, h), kin)
        Mrk, Mrkk = pools["mb"].next()
        P.tt("dve", Mrk[0:L, :, 0:L], psA[0:L, :, 0:L], k["m_ui"][0:L, :, 0:L], ALU.mult, reads=[pkA, "m_ui"], writes=[Mrkk])
        if delta:
            Bp, Bk = pads["B"]
            ps, pk = psH.next()
            for h in range(8):
                pr, e, pb = hsl(h)
                P.mm(ps[0:L, h, :], fm(aT, h), Spad[pb:pb + 64, pr, pb:pb + 64], start=True, stop=False, reads=kin + ["Spad"], writes=[pk])
                P.mm(ps[0:L, h, :], Mak[0:L, h, 0:L], Vp[0:L, pr, e, pb:pb + 64], start=False, stop=True, reads=[Makk, Vk], writes=[pk])
            W1, W1k = pools["mb"].next()
            P.copy("act", W1[0:L, :, :], ps[0:L, :, :], reads=[pk], writes=[W1k])
            ps, pk = psH.next()
            for h in range(8):
                P.mm(ps[0:L, h, :], TTb[0:L, h, 0:L], W1[0:L, h, :], reads=[TTbk, W1k], writes=[pk])
            Up, Uk = pools["padU"].next()
            for e in range(2):
                P.copy("act" if e else "dve", Up[0:L, :, e, e * 64:(e + 1) * 64],
                       ps[0:L, :, :].rearrange("p (a b) v -> p a b v", b=2)[:, :, e, :], reads=[pk], writes=[Uk])
        ps, pk = psH.next()
        for pr in range(4):
            n_mm = (3 if delta else 2) * 2
            i_mm = 0
            for e in range(2):
                pb = e * 64
                h = pr * 2 + e
                lst = [(Spad[pb:pb + 64, pr, :], rT[pb:pb + 64, pr, cs], kin + ["Spad"])]
                if delta:
                    lst.append((Up[0:L, pr, e, :], Mrb[0:L, h, 0:L], [Uk, Mrbk]))
                lst.append((Vp[0:L, pr, e, :], Mrk[0:L, h, 0:L], [Vk, Mrkk]))
                for (lt, rh, rd) in lst:
                    P.mm(ps[:, pr, 0:L], lt, rh, start=(i_mm == 0), stop=(i_mm == n_mm - 1), reads=rd, writes=[pk])
                    i_mm += 1
        P.copy("act", yT[:, :, cs], ps[:, 0:4, 0:L], reads=[pk], writes=[keys["yT"]])
        ps, pk = psH.next()
        for pr in range(4):
            n_mm = (2 if delta else 1) * 2
            i_mm = 0
            for e in range(2):
                pb = e * 64
                lst = []
                if delta:
                    lst.append((Bp[0:L, pr, e, :], Up[0:L, pr, e, pb:pb + 64], [Bk, Uk]))
                lst.append((Kp[0:L, pr, e, :], Vp[0:L, pr, e, pb:pb + 64], [Kk, Vk]))
                for (lt, rh, rd) in lst:
                    P.mm(ps[:, pr, :], lt, rh, start=(i_mm == 0), stop=(i_mm == n_mm - 1), reads=rd, writes=[pk])
                    i_mm += 1
        P.tt("dve", S32[:, :, :], S32[:, :, :], ps[:, 0:4, :], ALU.add, reads=[pk, "S32"], writes=["S32"])
        for pr in range(4):
            P.ts("dve", S32[:, pr, :], S32[:, pr, :], epos[:, pr, (c + 1) * L - 1:(c + 1) * L], None, ALU.mult, reads=["S32", keys["epos"]], writes=["S32"])
        for e in range(2):
            pb = e * 64
            P.copy("act" if e else "pool", Spad[pb:pb + 64, :, pb:pb + 64], S32[pb:pb + 64, :, :], reads=["S32"], writes=["Spad"])
    P.pe_serial = False


def scan_pools(C, ts, delta):
    pools = {}
    pools["psT"] = Rot(C, ts, "cs_psT", [64, 4, 128], BF16, 2, psum=True)
    pools["psH"] = Rot(C, ts, "cs_psH", [128, 8, 64], F32, 4, psum=True)
    names = ["K", "V"] + (["B", "U"] if delta else [])
    for nm in names:
        r = Rot(C, ts, "pad" + nm, [64, 4, 2, 128], BF16, 2)
        for t_, tk_ in zip(r.tiles, r.keys):
            C.P.memset("pool", t_[:], 0.0, writes=[tk_])
        pools["pad" + nm] = r
    pools["mb"] = Rot(C, ts, "cs_mb", [64, 8, 64], BF16, 10)
    pools["ttf"] = Rot(C, ts, "cs_ttf", [64, 8, 64], F32, 2)
    return pools


def more_consts(C):
    P, st, k = C.P, C.st, C.k
    for L in (64, 32, 16):
        m = C.sb(st, "rmask%d" % L, [128, 512], F32)
        P.memset("pool", m[:], 1.0, writes=["rmask%d" % L])
        P.memset("pool", m[:, :].rearrange("p (c l) -> p c l", l=L)[:, :, 0:1], 0.0, writes=["rmask%d" % L])
        k["rmask%d" % L] = m
    for nm, val in (("eps_rms", RMS_EPS), ("eps_gn", GN_EPS), ("tiny", 1e-12), ("one", 1.0), ("lnc", -40.0 * float(np.log(2.0)))):
        t = C.sb(st, nm, [128, 1], F32)
        P.memset("pool", t[:], val, writes=["epsv"])
        k[nm] = t
    P.flush()


def block_norm_stats(C, P, ps_pool, tmpb, src, srckey, SEG, scale):
    pass


def rwkv_mixer(C, job, s, I, prm):
    P, k = C.P, C.k
    T = job.T
    L = min(64, T)
    SEG = min(256, T)
    base = s * T
    with contextlib.ExitStack() as ts:
        pools = scan_pools(C, ts, True)
        psP = Rot(C, ts, "rw_psP", [128, 512], F32, 2, psum=True)
        S32 = C.sb(ts, "S32", [128, 4, 64], F32)
        Spad = C.sb(ts, "Spad", [128, 4, 128], BF16)
        P.memset("pool", Spad[:], 0.0, writes=["Spad"])
        if job.rwkv_s0 is None:
            P.memset("pool", S32[:], 0.0, writes=["S32"])
        else:
            s0t = C.sb(ts, "s0t", [64, 8, 64], F32)
            P.dma(s0t[:], job.rwkv_s0[s].rearrange("h v k -> v h k"), writes=["s0t"])
            for pr in range(4):
                ps, pk = psP.next()
                P.tr(ps[:, 0:64], s0t[0:64, 2 * pr:2 * pr + 2, :].rearrange("v e k -> v (e k)"), k["identF"][0:64, 0:64], reads=["s0t", "identF"], writes=[pk])
                P.copy("dve", S32[:, pr, :], ps[:, 0:64], reads=[pk], writes=["S32"])
            for e in range(2):
                pb = e * 64
                P.copy("act", Spad[pb:pb + 64, :, pb:pb + 64], S32[pb:pb + 64, :, :], reads=["S32"], writes=["Spad"])
        zt = C.sb(ts, "zt", [128, 14, SEG + 1], F32)
        dd = C.sb(ts, "dd", [128, 14, SEG], F32)
        zs = C.sb(ts, "zs", [128, 14, SEG], F32)
        f4 = lambda nm: C.sb(ts, nm, [128, 4, SEG], F32)
        b4 = lambda nm: C.sb(ts, nm, [128, 4, SEG], BF16)
        lw, asig, gT, kkn, kmod, cc, epos, eneg, eprev, tmpa, tmpb_, rkb, yT = [f4(n) for n in
            ("lw", "asig", "gT", "kkn", "kmod", "cc", "epos", "eneg", "eprev", "tmpa", "tmpb", "rkb", "yT")]
        rT, kT, vT, aT, bT, sqb, yb = [b4(n) for n in ("rT", "kT", "vT", "aT", "bT", "sqb", "yb")]
        tw = C.sb(ts, "tw", [128, SEG], BF16)
        sg = C.sb(ts, "sg", [128, SEG], BF16)
        outb = C.sb(ts, "outb", [128, 4, SEG], BF16)
        keys = dict(rT="rT", kT="kT", vT="vT", aT="aT", bT="bT", epos="epos", yT="yT")
        mu, w0, a0, k_k, k_a, lng, lnb, r_k, omka, wa2, g2 = [prm[n] for n in
            ("mu", "w0", "a0", "k_k", "k_a", "lng", "lnb", "r_k", "omka", "wa2", "g2")]
        pk_ = ["rwprm", "rwprm2"]
        for t0 in range(0, T, SEG):
            col0 = base + t0
            zrows = job.zT[0:A_COLS, :].rearrange("(c p) t -> p c t", p=128)
            zk = [("zT", job.name, nb, i) for nb in range(14) for i in range(col0 // 512, (col0 + SEG - 1) // 512 + 1)]
            if t0 == 0:
                P.dma(zt[:, :, 1:SEG + 1], zrows[:, :, col0:col0 + SEG], reads=zk, writes=["zt"])
                if job.rwkv_shift0 is None:
                    P.memset("pool", zt[:, :, 0:1], 0.0, writes=["zt"])
                else:
                    P.dma(zt[:, :, 0], job.rwkv_shift0[s].rearrange("(c p) -> p c", p=128), writes=["zt"], allow_slow_non_contiguous=True)
            else:
                zk2 = zk + [("zT", job.name, nb, (col0 - 1) // 512) for nb in range(14)]
                P.dma(zt[:, :, 0:SEG + 1], zrows[:, :, col0 - 1:col0 + SEG], reads=zk2, writes=["zt"])
            if t0 + SEG == T:
                P.dma(job.rwkv_shift_out[s].rearrange("(c p) -> p c", p=128), zt[:, :, SEG], reads=["zt"], writes=[("rwsh", job.name, s)],
                      q="pool", allow_slow_non_contiguous=True)
            P.tt("pool", dd[:], zt[:, :, 0:SEG], zt[:, :, 1:SEG + 1], ALU.subtract, reads=["zt"], writes=["dd"])
            for blk in range(14):
                P.stt(zs[:, blk, :], dd[:, blk, :], mu[:, blk:blk + 1], zt[:, blk, 1:SEG + 1], ALU.mult, ALU.add, reads=["dd", "zt"] + pk_, writes=["zs"])
            P.act(tw[0:64, :], zs[0:64, 12, :], AF.Tanh, reads=["zs"], writes=["tw"])
            P.copy("dve", tw[64:128, :], zs[64:128, 12, :], reads=["zs"], writes=["tw"])
            P.act(sg[:], zs[:, 13, :], AF.Sigmoid, reads=["zs"], writes=["sg"])
            for blk in range(4):
                bs = slice(blk * 128, (blk + 1) * 128)
                ps, pk = psP.next()
                P.mm(ps[:, 0:SEG], wa2[0:64, bs], tw[0:64, :], reads=["tw"] + pk_, writes=[pk])
                P.act(lw[:, blk, :], ps[:, 0:SEG], AF.Sigmoid, bias=w0[:, blk:blk + 1], reads=[pk] + pk_, writes=["lw"])
                ps, pk = psP.next()
                P.mm(ps[:, 0:SEG], wa2[64:128, bs], tw[64:128, :], reads=["tw"] + pk_, writes=[pk])
                P.act(asig[:, blk, :], ps[:, 0:SEG], AF.Sigmoid, bias=a0[:, blk:blk + 1], reads=[pk] + pk_, writes=["asig"])
                ps, pk = psP.next()
                P.mm(ps[:, 0:SEG], g2[:, bs], sg[:], reads=["sg"] + pk_, writes=[pk])
                P.copy("dve", gT[:, blk, :], ps[:, 0:SEG], reads=[pk], writes=["gT"])
            P.ts("dve", lw[:], lw[:], -DECAY_C, None, ALU.mult, reads=["lw"], writes=["lw"])
            for blk in range(4):
                P.ts("dve", tmpa[:, blk, :], zs[:, 4 + blk, :], k_k[:, blk:blk + 1], None, ALU.mult, reads=["zs"] + pk_, writes=["tmpa"])
            P.act(sqb[:], tmpa[:], AF.Square, reads=["tmpa"], writes=["sqb"])
            for blk in range(4):
                ps, pk = psP.next()
                P.mm(ps[:, 0:SEG], k["bo"][:], sqb[:, blk, :], reads=["sqb", "bo"], writes=[pk])
                P.act(tmpb_[:, blk, :], ps[:, 0:SEG], AF.Sqrt, reads=[pk], writes=["tmpb"])
            P.ts("dve", tmpb_[:], tmpb_[:], 1e-12, None, ALU.max, reads=["tmpb"], writes=["tmpb"])
            P.op("dve", lambda e: e.reciprocal(tmpb_[:], tmpb_[:]), reads=["tmpb"], writes=["tmpb"])
            P.tt("dve", kkn[:], tmpa[:], tmpb_[:], ALU.mult, reads=["tmpa", "tmpb"], writes=["kkn"])
            for blk in range(4):
                P.ts("dve", tmpa[:, blk, :], asig[:, blk, :], k_a[:, blk:blk + 1], omka[:, blk:blk + 1], ALU.mult, ALU.add,
                     reads=["asig"] + pk_, writes=["tmpa"])
            P.tt("dve", kmod[:], tmpa[:], zs[:, 4:8, :], ALU.mult, reads=["tmpa", "zs"], writes=["kmod"])
            rm = k["rmask%d" % L]
            for blk in range(4):
                P.op("dve", lambda e, blk=blk: e.tensor_tensor_scan(cc[:, blk, :], rm[:, 0:SEG], lw[:, blk, :], 0.0, ALU.mult, ALU.add),
                     reads=["lw", "rmask%d" % L], writes=["cc"])
            P.act(epos[:], cc[:], AF.Exp, reads=["cc"], writes=["epos"])
            P.act(eneg[:], cc[:], AF.Exp, scale=-1.0, reads=["cc"], writes=["eneg"])
            P.tt("pool", tmpa[:], cc[:], lw[:], ALU.subtract, reads=["cc", "lw", "kmod"], writes=["tmpa"])
            P.act(eprev[:], tmpa[:], AF.Exp, reads=["tmpa"], writes=["eprev"])
            P.tt("dve", rT[:], zs[:, 0:4, :], epos[:], ALU.mult, reads=["zs", "epos"], writes=["rT"])
            P.stt(aT[:], kkn[:], -1.0, eprev[:], ALU.mult, ALU.mult, reads=["kkn", "eprev"], writes=["aT"])
            P.tt("pool", tmpb_[:], kkn[:], asig[:], ALU.mult, reads=["kkn", "asig"], writes=["tmpb"])
            P.tt("dve", bT[:], tmpb_[:], eneg[:], ALU.mult, reads=["tmpb", "eneg"], writes=["bT"])
            P.tt("dve", kT[:], kmod[:], eneg[:], ALU.mult, reads=["kmod", "eneg"], writes=["kT"])
            P.copy("act", vT[:], zs[:, 8:12, :], reads=["zs"], writes=["vT"])
            for blk in range(4):
                P.stt(sqb[:, blk, :], zs[:, blk, :], r_k[:, blk:blk + 1], kmod[:, blk, :], ALU.mult, ALU.mult, reads=["zs", "kmod"] + pk_, writes=["sqb"])
            for blk in range(4):
                ps, pk = psP.next()
                P.mm(ps[:, 0:SEG], k["bo"][:], sqb[:, blk, :], reads=["sqb", "bo"], writes=[pk])
                P.copy("act", rkb[:, blk, :], ps[:, 0:SEG], reads=[pk], writes=["rkb"])
            chunk_scan(C, ts, SEG, L, True, rT, kT, vT, aT, bT, epos, S32, Spad, yT, keys, pools)
            P.copy("act", yb[:], yT[:], reads=["yT"], writes=["yb"])
            for blk in range(4):
                ps, pk = psP.next()
                P.mm(ps[:, 0:SEG], k["bo"][:], yb[:, blk, :], reads=["yb", "bo"], writes=[pk])
                P.stt(tmpa[:, blk, :], ps[:, 0:SEG], -1.0 / 64, yT[:, blk, :], ALU.mult, ALU.add, reads=[pk, "yT"], writes=["tmpa"])
            P.act(sqb[:], tmpa[:], AF.Square, reads=["tmpa"], writes=["sqb"])
            for blk in range(4):
                ps, pk = psP.next()
                P.mm(ps[:, 0:SEG], k["bo"][:], sqb[:, blk, :], reads=["sqb", "bo"], writes=[pk])
                P.act(tmpb_[:, blk, :], ps[:, 0:SEG], AF.Sqrt, bias=k["eps_gn"][:, 0:1], scale=1.0 / 64, reads=[pk, "epsv"], writes=["tmpb"])
            P.op("dve", lambda e: e.reciprocal(tmpb_[:], tmpb_[:]), reads=["tmpb"], writes=["tmpb"])
            P.tt("dve", tmpa[:], tmpa[:], tmpb_[:], ALU.mult, reads=["tmpa", "tmpb"], writes=["tmpa"])
            for blk in range(4):
                P.ts("dve", tmpa[:, blk, :], tmpa[:, blk, :], lng[:, blk:blk + 1], lnb[:, blk:blk + 1], ALU.mult, ALU.add, reads=["tmpa"] + pk_, writes=["tmpa"])
            P.tt("pool", tmpb_[:], rkb[:], zs[:, 8:12, :], ALU.mult, reads=["rkb", "zs", "tmpb"], writes=["tmpb"])
            P.tt("dve", tmpa[:], tmpa[:], tmpb_[:], ALU.add, reads=["tmpa", "tmpb"], writes=["tmpa"])
            P.tt("dve", outb[:], tmpa[:], gT[:], ALU.mult, reads=["tmpa", "gT"], writes=["outb"])
            P.dma(job.mixT[0:512, col0:col0 + SEG].rearrange("(c p) t -> p c t", p=128), outb[:], reads=["outb"],
                  writes=[("mixT", job.name, c, col0 // 512, hh) for c in range(4) for hh in range(2)], q="pool")
        so = C.sb(ts, "so", [64, 4, 128], F32)
        for pr in range(4):
            ps, pk = psP.next()
            P.tr(ps[0:64, 0:128], S32[:, pr, :], k["identF"][:, :], reads=["S32", "identF"], writes=[pk])
            P.copy("dve", so[:, pr, :], ps[0:64, 0:128], reads=[pk], writes=["so"])
        P.dma(job.rwkv_out[s].rearrange("(a e) v k -> v a e k", e=2), so[:, :, :].rearrange("v a (e k) -> v a e k", e=2), reads=["so"],
              writes=[("rwst", job.name, s)], q="pool")
        P.flush()


def load_rwkv_params(C, I):
    P, st = C.P, C.st
    prm = {}
    def vec(nm, ap, n):
        t = C.sb(st, "p_" + nm, [128, n], F32)
        P.dma(t[:], ap.rearrange("(c p) -> p c", p=128), writes=["rwprm"], allow_slow_non_contiguous=True)
        prm[nm] = t
    vec("mu", I["rwkv_mu"][0], 14)
    vec("w0", I["rwkv_w0"][0], 4)
    vec("a0", I["rwkv_a0"][0], 4)
    vec("k_k", I["rwkv_k_k"][0], 4)
    vec("k_a", I["rwkv_k_a"][0], 4)
    vec("lng", I["rwkv_lnx_g"][0], 4)
    vec("lnb", I["rwkv_lnx_b"][0], 4)
    vec("r_k", I["rwkv_r_k"][0].rearrange("h k -> (h k)"), 4)
    for nm, src in (("lng8", "rwkv_lnx_g"), ("lnb8", "rwkv_lnx_b")):
        t = C.sb(st, "p_" + nm, [64, 8], F32)
        P.dma(t[:], I[src][0].rearrange("(h k) -> k h", k=64), writes=["rwprm"], allow_slow_non_contiguous=True)
        prm[nm] = t
    omka = C.sb(st, "p_omka", [128, 4], F32)
    P.ts("dve", omka[:], prm["k_a"][:], -1.0, 1.0, ALU.mult, ALU.add, reads=["rwprm"], writes=["rwprm2"])
    prm["omka"] = omka
    wa2 = C.sb(st, "p_wa2", [128, 512], BF16)
    g2 = C.sb(st, "p_g2", [128, 512], BF16)
    with contextlib.ExitStack() as ts:
        wa2f = C.sb(ts, "wa2f", [128, 512], F32)
        g2f = C.sb(ts, "g2f", [128, 512], F32)
        P.dma(wa2f[0:64, :], I["rwkv_w2"][0], writes=["wa2f"])
        P.dma(wa2f[64:128, :], I["rwkv_a2"][0], writes=["wa2f"])
        P.dma(g2f[:], I["rwkv_g2"][0], writes=["g2f"])
        P.copy("dve", wa2[:], wa2f[:], reads=["wa2f"], writes=["rwprm2"])
        P.copy("dve", g2[:], g2f[:], reads=["g2f"], writes=["rwprm2"])
        prm["wa2"], prm["g2"] = wa2, g2
        P.flush()
    return prm


def attn_mixer(C, job, s, I, kind):
    P, k = C.P, C.k
    T = job.Ts[s]
    base = job.bases[s]
    fox = (kind == "fox")
    zoff = A_COLS if fox else 0
    Pc = (job.P_fox[s] if fox else job.P_chunk[s])
    NPt = Pc // 128
    NTt = (T + 127) // 128
    NKT = NPt + NTt
    QG = min(512, T)
    NG = T // QG
    mix_off = 512 if fox else 0
    ck = (job.fox_ck if fox else job.chunk_ck)
    cv = (job.fox_cv if fox else job.chunk_cv)
    kout = (job.fox_kout if fox else job.chunk_kout)
    vout = (job.fox_vout if fox else job.chunk_vout)
    with contextlib.ExitStack() as ts:
        QT = C.sb(ts, "QT", [128, 4, T], BF16)
        KT = C.sb(ts, "KT", [128, 4, NKT * 128], BF16)
        Vt = C.sb(ts, "Vt", [128, NKT, 8, 65], BF16)
        P.memset("pool", Vt[:, :, :, 64:65], 1.0, writes=["Vt"])
        psS = Rot(C, ts, "at_psS", [128, 512], F32, 4, psum=True)
        psM = Rot(C, ts, "at_psM", [128, 512], F32, 1, psum=True)
        psN = Rot(C, ts, "at_psN", [128, 512], F32, 2, psum=True)
        psD = Rot(C, ts, "at_psD", [64, 512], F32, 1, psum=True)
        zrows = lambda off: job.zT[zoff + off:zoff + off + 512, :].rearrange("(c p) t -> p c t", p=128)
        zrows64 = lambda off, a: job.zT[zoff + off + a * 256:zoff + off + a * 256 + 256, :].rearrange("(h d) t -> d h t", d=64)
        zkey = lambda off, col: [("zT", job.name, (zoff + off) // 128 + c, col // 512) for c in range(4)]
        with contextlib.ExitStack() as t2:
            ldq = Rot(C, t2, "at_ldq", [128, 4, 128], F32, 2)
            ldk2 = Rot(C, t2, "at_ldk2", [128, 4, 128], F32, 2)
            ldk = Rot(C, t2, "at_ldk", [128, 4, 128], F32, 2)
            ldv = Rot(C, t2, "at_ldv", [128, 4, 128], F32, 2)
            stg = Rot(C, t2, "at_stg", [128, 512], F32, 4)
            ldc = Rot(C, t2, "at_ldc", [128, 512], F32, 3)
            for ct in range(NPt):
                kc_, kck = ldc.next()
                for a_ in range(2):
                    P.dma(kc_[:, :].rearrange("p (b a d) -> p b a d", b=4, a=2)[:, :, a_, :],
                          ck[s][ct * 128:(ct + 1) * 128, a_ * 256:(a_ + 1) * 256].rearrange("t (b d) -> t b d", b=4), writes=[kck])
                ps, pk = psS.next()
                for h4 in range(4):
                    P.tr(ps[:, h4 * 128:(h4 + 1) * 128], kc_[:, h4 * 128:(h4 + 1) * 128], k["identF"][:, :], reads=[kck, "identF"], writes=[pk])
                P.copy("act", KT[:, :, ct * 128:(ct + 1) * 128], ps[:, :].rearrange("p (c t) -> p c t", t=128), reads=[pk], writes=["KT"])
                vc_, vck = ldc.next()
                P.dma(vc_[:], cv[s][ct * 128:(ct + 1) * 128, :], writes=[vck])
                P.copy("dve", Vt[:, ct, :, 0:64], vc_[:, :].rearrange("p (h d) -> p h d", d=64), reads=[vck], writes=["Vt"])
            for it in range(NTt):
                rows = min(128, T - it * 128)
                col0 = base + it * 128
                q_, qk_ = ldq.next()
                k_, kk_ = ldk.next()
                v_, vk_ = ldv.next()
                k2_, k2k_ = ldk2.next()
                for a in range(2):
                    P.dma(q_[a * 64:(a + 1) * 64, :, 0:rows], zrows64(0, a)[:, :, col0:col0 + rows], reads=zkey(0, col0), writes=[qk_])
                    P.dma(k2_[a * 64:(a + 1) * 64, :, 0:rows], zrows64(512, a)[:, :, col0:col0 + rows], reads=zkey(512, col0), writes=[k2k_])
                P.dma(k_[:, :, 0:rows], zrows(512)[:, :, col0:col0 + rows], reads=zkey(512, col0), writes=[kk_])
                P.dma(v_[:, :, 0:rows], zrows(1024)[:, :, col0:col0 + rows], reads=zkey(1024, col0), writes=[vk_])
                P.copy("act", QT[:, :, it * 128:it * 128 + rows], q_[:, :, 0:rows], reads=[qk_], writes=["QT"])
                P.copy("pool", KT[:, :, (NPt + it) * 128:(NPt + it) * 128 + rows], k2_[:, :, 0:rows], reads=[k2k_], writes=["KT"])
                for src, sk, outap, isv in ((k_, kk_, kout, False), (v_, vk_, vout, True)):
                    ps, pk = psS.next()
                    for blk in range(4):
                        P.tr(ps[0:rows, blk * 128:(blk + 1) * 128], src[:, blk, 0:rows], k["identF"][:, :], reads=[sk, "identF"], writes=[pk])
                    sg_, sgk = stg.next()
                    P.copy("dve" if isv else "act", sg_[0:rows, :], ps[0:rows, :], reads=[pk], writes=[sgk])
                    if isv:
                        P.copy("pool", Vt[0:rows, NPt + it, :, 0:64], sg_[0:rows, :].rearrange("p (h d) -> p h d", d=64), reads=[sgk], writes=["Vt"])
                    if fox or T <= 512:
                        P.dma(outap[s][it * 128:it * 128 + rows, :], sg_[0:rows, :], reads=[sgk], writes=[("kvout", kind, job.name, s, it, isv)], q="pool")
                    elif it * 128 >= T - 512:
                        r0 = it * 128 - (T - 512)
                        P.dma(outap[s][r0:r0 + rows, :], sg_[0:rows, :], reads=[sgk], writes=[("kvout", kind, job.name, s, it, isv)], q="pool")
            P.flush()
        biasT = None
        if fox:
            biasT = C.sb(ts, "biasT", [128, 8, NG, NKT], F32)
            with contextlib.ExitStack() as t2:
                lfT = C.sb(t2, "lfT", [8, T], F32)
                nbf = C.sb(t2, "nbf", [8, 1], F32)
                P.dma(nbf[:], I["fox_b_f"][0:1, :].rearrange("o h -> h o"), writes=["nbf"], allow_slow_non_contiguous=True)
                P.ts("dve", nbf[:], nbf[:], -1.0, None, ALU.mult, reads=["nbf"], writes=["nbf"])
                gk = [("zT", job.name, (A_COLS + 1536) // 128, i) for i in range(base // 512, (base + T - 1) // 512 + 1)]
                P.dma(lfT[:], job.zT[A_COLS + 1536:A_COLS + 1544, base:base + T], reads=gk, writes=["lfT"])
                P.act(lfT[:], lfT[:], AF.Exp, bias=nbf[:, 0:1], scale=-1.0, reads=["lfT", "nbf"], writes=["lfT"])
                P.act(lfT[:], lfT[:], AF.Ln, bias=k["one"][0:8, 0:1], reads=["lfT", "epsv"], writes=["lfT"])
                P.ts("dve", lfT[:], lfT[:], -1.0, None, ALU.mult, reads=["lfT"], writes=["lfT"])
                lft = C.sb(t2, "lft", [128, NKT, 8], F32)
                P.memset("pool", lft[:], 0.0, writes=["lft"])
                if NPt:
                    P.dma(lft[:, 0:NPt, :], job.fox_clogf[s].rearrange("(n p) h -> p n h", p=128), writes=["lft"])
                ps, pk = psS.next()
                for it in range(NTt):
                    rows = min(128, T - it * 128)
                    P.tr(ps[0:rows, it * 8:(it + 1) * 8], lfT[0:8, it * 128:it * 128 + rows], k["identF"][0:8, 0:8], reads=["lfT", "identF"], writes=[pk])
                rows_l = min(128, T)
                P.copy("dve", lft[0:rows_l, NPt:NKT, :], ps[0:rows_l, 0:NTt * 8].rearrange("p (n h) -> p n h", h=8), reads=[pk], writes=["lft"])
                if T >= 128:
                    P.dma(job.fox_logf_out[s].rearrange("(n p) h -> p n h", p=128), lft[:, NPt:NKT, :], reads=["lft"], writes=[("lfout", job.name, s)], q="pool")
                else:
                    P.dma(job.fox_logf_out[s], lft[0:T, NPt, :], reads=["lft"], writes=[("lfout", job.name, s)], q="pool")
                Wt = C.sb(t2, "Wt", [128, 8, NKT], F32)
                TOTt = C.sb(t2, "TOTt", [128, 8, NKT], F32)
                offs = C.sb(t2, "offs", [128, 8, NKT], F32)
                rmk = C.sb(t2, "rmk", [128, 8, NKT], F32)
                P.memset("pool", rmk[:], 1.0, writes=["rmk"])
                P.memset("pool", rmk[:, :, 0:1], 0.0, writes=["rmk"])
                lft2 = lft[:, :, :].rearrange("p n h -> p (n h)")
                ps, pk = psS.next()
                P.mm(ps[:, 0:NKT * 8], k["triF"][:], lft2, reads=["lft", "triF"], writes=[pk])
                P.copy("dve", Wt[:], ps[:, 0:NKT * 8].rearrange("p (n h) -> p h n", h=8), reads=[pk], writes=["Wt"])
                ps, pk = psS.next()
                P.mm(ps[:, 0:NKT * 8], k["onesF"][:], lft2, reads=["lft", "onesF"], writes=[pk])
                P.copy("dve", TOTt[:], ps[:, 0:NKT * 8].rearrange("p (n h) -> p h n", h=8), reads=[pk], writes=["TOTt"])
                P.op("dve", lambda e: e.tensor_tensor_scan(offs[:, :, :].rearrange("p h n -> p (h n)"), rmk[:, :, :].rearrange("p h n -> p (h n)"),
                                                           TOTt[:, :, :].rearrange("p h n -> p (h n)"), 0.0, ALU.mult, ALU.add),
                     reads=["TOTt", "rmk"], writes=["offs"])
                P.tt("dve", offs[:], offs[:], TOTt[:], ALU.subtract, reads=["offs", "TOTt"], writes=["offs"])
                P.tt("dve", Wt[:], Wt[:], offs[:], ALU.add, reads=["offs", "Wt"], writes=["Wt"])
                for h in range(8):
                    for g in range(NG):
                        nq0 = NPt + g * (QG // 128)
                        P.ts("dve", biasT[:, h, g, :], Wt[:, h, :], offs[:, h, nq0:nq0 + 1], -1.0, ALU.subtract, ALU.mult,
                             reads=["Wt", "offs"], writes=["biasT"])
                P.flush()
        pb_ = Rot(C, ts, "at_pb", [128, 512], BF16, 6)
        rd_ = Rot(C, ts, "at_rd", [128, 512], F32, 3)
        nb_ = Rot(C, ts, "at_nb", [64, 512], F32, 3)
        ob_ = Rot(C, ts, "at_ob", [64, 512], BF16, 2)
        MEs = None
        if not fox:
            MEs = [C.sb(ts, "ME", [128, 8, 512], BF16) for _ in range(2)]
            hk = Rot(C, ts, "at_hk", [128, 512], F32, 2)
            me32 = Rot(C, ts, "at_me32", [128, 512], F32, 2)

        def build_me(h):
            ME = MEs[h % 2]
            for rt in range(8):
                if T <= 16 and rt >= NKT:
                    continue
                hh, hhk = hk.next()
                src = bass.AP(C.dram["relext"].tensor, h * 1536 + 896 - 128 * rt, [[1, 128], [1, QG]])
                P.dma(hh[:, 0:QG], src, reads=["relext"], writes=[hhk])
                ps, pk = psM.next()
                P.mm(ps[:, 0:QG], k["flipF"][:], hh[:, 0:QG], reads=[hhk, "flipF"], writes=[pk])
                m32, m32k = me32.next()
                P.act(m32[:, 0:QG], ps[:, 0:QG], AF.Exp, reads=[pk], writes=[m32k])
                P.tt("dve", ME[:, rt, 0:QG], m32[:, 0:QG], k["bandB"][:, rt, 0:QG], ALU.mult, reads=[m32k, "bandB"], writes=[("ME", h % 2)])

        its = []
        for h in range(8):
            for g in range(NG):
                q0 = g * QG
                tiles = []
                if fox:
                    last_kt = NPt + (q0 + QG - 1) // 128
                    for kt in range(0, last_kt + 1):
                        rows = 128 if kt < NPt else min(128, T - (kt - NPt) * 128)
                        d = kt - NPt - q0 // 128
                        mfn = (lambda c0, c1, d=d, rows=rows: k["cmB"][0:rows, d, c0:c1]) if d >= 0 else None
                        c0 = min(128 * d, QG - 1) if d > 0 else 0
                        tiles.append([kt, rows, biasT[0:rows, h, g, kt:kt + 1], mfn, "cmB", c0, QG])
                else:
                    order = [3, 2, 5, 1, 6, 0, 7, 4] if T > 16 else list(range(8))
                    for rt in order:
                        kt = (q0 // 128 - 4 + rt) if T > 16 else rt
                        if kt < 0 or kt >= NKT:
                            continue
                        rows = 128 if kt < NPt else min(128, T - (kt - NPt) * 128)
                        mfn = (lambda c0, c1, rt=rt, rows=rows, hh_=h: MEs[hh_ % 2][0:rows, rt, c0:c1])
                        if T > 16:
                            lo, hi = max(0, 2 * rt - 8), min(7, 2 * rt + 1)
                            c0, c1 = lo * 64, (hi + 1) * 64
                        else:
                            c0, c1 = 0, QG
                        tiles.append([kt, rows, 0.0, mfn, ("ME", h % 2), c0, c1])
                tiles[0][5], tiles[0][6] = 0, QG
                tiles[-1][5], tiles[-1][6] = 0, QG
                G = dict(h=h, g=g, q0=q0, n=len(tiles))
                for i, tl in enumerate(tiles):
                    its.append((G, i, tl))
        D = 3
        qk = {}
        pend = []

        def finalize1(G):
            pn, pnk = G["pn"]
            rd, rdk = rd_.next()
            P.act(rd[64:65, 0:QG], pn[64:65, 0:QG], AF.Ln, scale=float(2.0 ** -40), reads=[pnk], writes=[rdk])
            P.act(rd[64:65, 0:QG], rd[64:65, 0:QG], AF.Exp, bias=k["lnc"][64:65, 0:1], scale=-1.0, reads=[rdk, "epsv"], writes=[rdk])
            nb, nbk = nb_.next()
            P.copy("act", nb[:, 0:QG], pn[0:64, 0:QG], reads=[pnk], writes=[nbk])
            G["rd"], G["nb"] = (rd, rdk), (nb, nbk)

        def finalize(G):
            h, q0 = G["h"], G["q0"]
            rd, rdk = G["rd"]
            nb, nbk = G["nb"]
            pd, pdk = psD.next()
            P.mm(pd[:, 0:QG], k["onesF"][64:65, 0:64], rd[64:65, 0:QG], reads=["onesF", rdk], writes=[pdk])
            ob, obk = ob_.next()
            P.tt("dve", ob[:, 0:QG], nb[:, 0:QG], pd[:, 0:QG], ALU.mult, reads=[nbk, pdk], writes=[obk])
            r0 = mix_off + h * 64
            P.dma(job.mixT[r0:r0 + 64, base + q0:base + q0 + QG], ob[:, 0:QG], reads=[obk],
                  writes=[("mixT", job.name, r0 // 128, (base + q0) // 512, h % 2)], q="pool")

        for n in range(len(its) + D):
            if n < len(its):
                G, i, (kt, rows, bias, mfn, mkey, c0, c1) = its[n]
                h = G["h"]
                hb = (h // 4) * 64
                if (not fox) and G["g"] == 0 and i == 0:
                    build_me(h)
                ps, pk = psS.next()
                P.mm(ps[0:rows, c0:c1], KT[hb:hb + 64, h % 4, kt * 128:kt * 128 + rows], QT[hb:hb + 64, h % 4, G["q0"] + c0:G["q0"] + c1],
                     reads=["KT", "QT"], writes=[pk])
                qk[n] = (ps, pk)
            m = n - D
            if m >= 0:
                G, i, (kt, rows, bias, mfn, mkey, c0, c1) = its[m]
                h = G["h"]
                ps, pk = qk.pop(m)
                if i == 0:
                    while len(pend) > 1:
                        finalize(pend.pop(0))
                    G["pn"] = psN.next()
                pn, pnk = G["pn"]
                pt, ptk = pb_.next()
                P.act(pt[0:rows, c0:c1], ps[0:rows, c0:c1], AF.Exp, bias=bias, scale=0.125, reads=[pk, "biasT"], writes=[ptk])
                if mfn is not None:
                    P.tt("dve", pt[0:rows, c0:c1], pt[0:rows, c0:c1], mfn(c0, c1), ALU.mult, reads=[ptk, mkey], writes=[ptk])
                P.mm(pn[0:65, c0:c1], Vt[0:rows, kt, h, :], pt[0:rows, c0:c1], start=(i == 0), stop=(i == G["n"] - 1), reads=["Vt", ptk], writes=[pnk])
                if i == G["n"] - 1:
                    finalize1(G)
                    G["due"] = m + 4
                    pend.append(G)
                while pend and pend[0].get("due", 1 << 30) <= m and pend[0] is not G:
                    finalize(pend.pop(0))
        while pend:
            finalize(pend.pop(0))
        P.pe_serial = False
        P.flush()


def build_rel_tables(C, I):
    P, st, k = C.P, C.st, C.k
    ext = C.dr("relext", [8, 1536], F32)
    rb = I["chunk_rel_bias"]
    P.dma(ext[:, 384:639], rb[0, :, 1:256], writes=["relext"])
    P.dma(bass.AP(ext.tensor, 0, [[1536, 8], [1, 384], [1, 1]]), bass.AP(rb.tensor, 0, [[257, 8], [0, 384], [1, 1]]), writes=["relext"])
    P.dma(bass.AP(ext.tensor, 639, [[1536, 8], [1, 897], [1, 1]]), bass.AP(rb.tensor, 256, [[257, 8], [0, 897], [1, 1]]), writes=["relext"])
    band = C.sb(st, "bandB", [128, 8, 512], BF16)
    P.memset("pool", band[:], 0.0, writes=["bandB"])
    for rt in range(8):
        for e in range(2):
            kcr = 2 * rt + e
            lo, hi = max(0, kcr - 8), min(7, kcr)
            if lo <= hi:
                P.memset("pool", band[e * 64:(e + 1) * 64, rt, lo * 64:(hi + 1) * 64], 1.0, writes=["bandB"])
    k["bandB"] = band
    P.flush()


def hgrn_mixer(C, job, s, I, prm):
    P, k = C.P, C.k
    T = job.T
    L = min(16, T)
    SEG = min(256, T)
    base = s * T
    with contextlib.ExitStack() as ts:
        pools = scan_pools(C, ts, False)
        psP = Rot(C, ts, "hg_psP", [128, 512], F32, 2, psum=True)
        S32 = C.sb(ts, "S32", [128, 4, 64], F32)
        Spad = C.sb(ts, "Spad", [128, 4, 128], BF16)
        P.memset("pool", Spad[:], 0.0, writes=["Spad"])
        if job.hgrn_s0 is None:
            P.memset("pool", S32[:], 0.0, writes=["S32"])
        else:
            P.dma(S32[:], job.hgrn_s0[s].rearrange("(a e) k v -> (e k) a v", e=2), writes=["S32"])
            for e in range(2):
                pb = e * 64
                P.copy("act", Spad[pb:pb + 64, :, pb:pb + 64], S32[pb:pb + 64, :, :], reads=["S32"], writes=["Spad"])
        z4 = C.sb(ts, "z4", [128, 16, SEG], F32)
        f4 = lambda nm: C.sb(ts, nm, [128, 4, SEG], F32)
        b4 = lambda nm: C.sb(ts, nm, [128, 4, SEG], BF16)
        qf, sgf, lw, kin, cc, epos, eneg, yT, tmpa = [f4(n) for n in ("qf", "sgf", "lw", "kin", "cc", "epos", "eneg", "yT", "tmpa")]
        rT, kT, vT, sqb, outb = [b4(n) for n in ("rT", "kT", "vT", "sqb", "outb")]
        keys = dict(rT="rT", kT="kT", vT="vT", epos="epos", yT="yT")
        lb, oml, noml, ng = prm["lb"], prm["oml"], prm["noml"], prm["ng"]
        pk_ = ["hgprm"]
        rm = k["rmask%d" % L]
        for t0 in range(0, T, SEG):
            col0 = base + t0
            zk = [("zT", job.name, nb, i) for nb in range(12, 28) for i in range(col0 // 512, (col0 + SEG - 1) // 512 + 1)]
            P.dma(z4[:], job.zT[1536:3584, col0:col0 + SEG].rearrange("(c p) t -> p c t", p=128), reads=zk, writes=["z4"])
            P.act(qf[:], z4[:, 0:4, :], AF.Silu, reads=["z4"], writes=["qf"])
            P.act(sgf[:], z4[:, 4:8, :], AF.Sigmoid, reads=["z4"], writes=["sgf"])
            for blk in range(4):
                P.ts("dve", lw[:, blk, :], sgf[:, blk, :], oml[:, blk:blk + 1], lb[:, blk:blk + 1], ALU.mult, ALU.add, reads=["sgf"] + pk_, writes=["lw"])
                P.ts("dve", kin[:, blk, :], sgf[:, blk, :], noml[:, blk:blk + 1], oml[:, blk:blk + 1], ALU.mult, ALU.add, reads=["sgf"] + pk_, writes=["kin"])
            P.act(lw[:], lw[:], AF.Ln, reads=["lw"], writes=["lw"])
            for blk in range(4):
                P.op("dve", lambda e, blk=blk: e.tensor_tensor_scan(cc[:, blk, :], rm[:, 0:SEG], lw[:, blk, :], 0.0, ALU.mult, ALU.add),
                     reads=["lw", "rmask%d" % L], writes=["cc"])
            P.act(epos[:], cc[:], AF.Exp, reads=["cc"], writes=["epos"])
            P.act(eneg[:], cc[:], AF.Exp, scale=-1.0, reads=["cc"], writes=["eneg"])
            P.tt("dve", rT[:], qf[:], epos[:], ALU.mult, reads=["qf", "epos"], writes=["rT"])
            P.tt("dve", kT[:], kin[:], eneg[:], ALU.mult, reads=["kin", "eneg"], writes=["kT"])
            P.copy("pool", vT[:], z4[:, 8:12, :], reads=["z4"], writes=["vT"])
            chunk_scan(C, ts, SEG, L, False, rT, kT, vT, None, None, epos, S32, Spad, yT, keys, pools)
            P.act(sqb[:], yT[:], AF.Square, reads=["yT"], writes=["sqb"])
            for blk in range(4):
                ps, pk = psP.next()
                P.mm(ps[:, 0:SEG], k["bo"][:], sqb[:, blk, :], reads=["sqb", "bo"], writes=[pk])
                P.act(tmpa[:, blk, :], ps[:, 0:SEG], AF.Sqrt, bias=k["eps_rms"][:, 0:1], scale=1.0 / 64, reads=[pk, "epsv"], writes=["tmpa"])
            P.op("dve", lambda e: e.reciprocal(tmpa[:], tmpa[:]), reads=["tmpa"], writes=["tmpa"])
            P.tt("dve", tmpa[:], tmpa[:], yT[:], ALU.mult, reads=["tmpa", "yT"], writes=["tmpa"])
            P.act(qf[:], z4[:, 12:16, :], AF.Silu, reads=["z4", "rT"], writes=["qf"])
            for blk in range(4):
                P.stt(outb[:, blk, :], tmpa[:, blk, :], ng[:, blk:blk + 1], qf[:, blk, :], ALU.mult, ALU.mult, reads=["tmpa", "qf"] + pk_, writes=["outb"])
            P.dma(job.mixT[512:1024, col0:col0 + SEG].rearrange("(c p) t -> p c t", p=128), outb[:], reads=["outb"],
                  writes=[("mixT", job.name, 4 + c, col0 // 512, hh) for c in range(4) for hh in range(2)], q="pool")
        P.dma(job.hgrn_out[s].rearrange("(a e) k v -> (e k) a v", e=2), S32[:], reads=["S32"], writes=[("hgst", job.name, s)], q="pool")
        P.flush()


def load_hgrn_params(C, I):
    P, st = C.P, C.st
    prm = {}
    t0 = C.sb(st, "hg_t0", [128, 4], F32)
    t1 = C.sb(st, "hg_t1", [128, 4], F32)
    P.dma(t0[:], I["hgrn_lb_table"][0].rearrange("(c p) -> p c", p=128), writes=["hg_t0"], allow_slow_non_contiguous=True)
    P.dma(t1[:], I["hgrn_lb_table"][1].rearrange("(c p) -> p c", p=128), writes=["hg_t1"], allow_slow_non_contiguous=True)
    P.act(t0[:], t0[:], AF.Exp, reads=["hg_t0"], writes=["hg_t0"])
    P.act(t1[:], t1[:], AF.Exp, reads=["hg_t1"], writes=["hg_t1"])
    P.tt("dve", t0[:], t0[:], t1[:], ALU.add, reads=["hg_t0", "hg_t1"], writes=["hg_t0"])
    P.op("dve", lambda e: e.reciprocal(t0[:], t0[:]), reads=["hg_t0"], writes=["hg_t0"])
    lb = C.sb(st, "hg_lb", [128, 4], F32)
    oml = C.sb(st, "hg_oml", [128, 4], F32)
    noml = C.sb(st, "hg_noml", [128, 4], F32)
    ng = C.sb(st, "hg_ng", [128, 4], F32)
    P.tt("dve", lb[:], t1[:], t0[:], ALU.mult, reads=["hg_t0", "hg_t1"], writes=["hgprm"])
    P.ts("dve", oml[:], lb[:], -1.0, 1.0, ALU.mult, ALU.add, reads=["hgprm"], writes=["hgprm"])
    P.ts("dve", noml[:], oml[:], -1.0, None, ALU.mult, reads=["hgprm"], writes=["hgprm"])
    P.dma(ng[:], I["hgrn_norm_g"][0].rearrange("(c p) -> p c", p=128), writes=["hgprm"], allow_slow_non_contiguous=True)
    ng8 = C.sb(st, "hg_ng8", [64, 8], F32)
    P.dma(ng8[:], I["hgrn_norm_g"][0].rearrange("(h k) -> k h", k=64), writes=["hgprm"], allow_slow_non_contiguous=True)
    prm.update(lb=lb, oml=oml, noml=noml, ng=ng, ng8=ng8)
    P.flush()
    return prm


def xk1(job, c, col0, gs):
    return [("xT", job.name, c, i) for i in range(col0 // 512, (col0 + gs - 1) // 512 + 1)]


def out_proj(C, job, Wb, wkeys, l, j, srcT_dram, kcn, src_keys_fn):
    P = C.P
    m = C.mods[(l, j)]
    with contextlib.ExitStack() as ts:
        wr = Rot(C, ts, "op_w", [128, kcn, 128], BF16, 8)
        ws = []
        for nb in range(8):
            w, wk = wr.next()
            P.dma(w[:], Wb[nb], reads=wkeys, writes=[wk])
            ws.append((w, wk))
        sr = Rot(C, ts, "op_src", [128, kcn, 512], BF16, 2)
        pr = Rot(C, ts, "op_ps", [128, 512], F32, 8, psum=True)
        xr = Rot(C, ts, "op_x", [128, 512], F32, 6)
        for (s, t0, gs, col0) in seq_groups(job, 512):
            sT, sk = sr.next()
            P.dma(sT[:, :, 0:gs], srcT_dram[:, col0:col0 + gs].rearrange("(c p) t -> p c t", p=128), reads=src_keys_fn(col0), writes=[sk])
            col = job.modcol[s]
            for nb in range(8):
                w, wk = ws[nb]
                ps, pk = pr.next()
                for kc in range(kcn):
                    P.mm(ps[:, 0:gs], w[:, kc, :], sT[:, kc, 0:gs], start=(kc == 0), stop=(kc == kcn - 1), reads=[wk, sk], writes=[pk])
                x, xk = xr.next()
                P.dma(x[:, 0:gs], job.xT[nb * 128:(nb + 1) * 128, col0:col0 + gs], reads=xk1(job, nb, col0, gs), writes=[xk])
                P.stt(x[:, 0:gs], ps[:, 0:gs], m[:, 16 + nb, col:col + 1], x[:, 0:gs], ALU.mult, ALU.add, reads=[pk, xk, ("mod", l, j)], writes=[xk])
                P.dma(job.xT[nb * 128:(nb + 1) * 128, col0:col0 + gs], x[:, 0:gs], reads=[xk], writes=xk1(job, nb, col0, gs), q="pool")
        P.flush()


def ffn_up(C, job, l, I, Wup, wkeys, hT, hkey):
    P, k = C.P, C.k
    with contextlib.ExitStack() as ts:
        cw = C.sp[("cw", l)]
        cbv = C.sp[("cb", l)]
        wr = Rot(C, ts, "fu_w", [128, KC, 128], BF16, 4)
        pr = Rot(C, ts, "fu_ps", [128, 512], F32, 8, psum=True)
        Er = [Rot(C, ts, "fu_E%d" % i, [128, 514], F32, 4) for i in range(2)]
        t1r = Rot(C, ts, "fu_t1", [128, 512], F32, 6)
        gr = Rot(C, ts, "fu_g", [128, 512], BF16, 3)
        groups = seq_groups(job, 512)
        for jb in range(22):
            wts = []
            for half in range(2):
                w, wk = wr.next()
                P.dma(w[:], Wup[jb + 22 * half], reads=wkeys, writes=[wk])
                wts.append((w, wk))
            prevE = [None, None]
            for (s, t0, gs, col0) in groups:
                res = []
                for half in range(2):
                    blk = jb + 22 * half
                    w, wk = wts[half]
                    ps, pk = pr.next()
                    for kc in range(KC):
                        P.mm(ps[:, 0:gs], w[:, kc, :], hT[:, kc, col0:col0 + gs], start=(kc == 0), stop=(kc == KC - 1), reads=[wk, hkey], writes=[pk])
                    E, Ek = Er[half].next()
                    if t0 == 0:
                        if job.ffn_buf[l][s] is None:
                            P.memset("pool", E[:, 0:2], 0.0, writes=[Ek])
                        else:
                            P.dma(E[:, 0:2], job.ffn_buf[l][s][:, blk * 128:(blk + 1) * 128].rearrange("r f -> f r"), writes=[Ek], allow_slow_non_contiguous=True)
                    else:
                        pE, pEk, pgs = prevE[half]
                        P.copy("pool", E[:, 0:2], pE[:, pgs:pgs + 2], reads=[pEk], writes=[Ek])
                    P.copy("act", E[:, 2:2 + gs], ps[:, 0:gs], reads=[pk], writes=[Ek])
                    prevE[half] = (E, Ek, gs)
                    if t0 + gs == job.Ts[s]:
                        P.dma(job.ffn_out[l][s][:, blk * 128:(blk + 1) * 128].rearrange("r f -> f r"), E[:, gs:gs + 2], reads=[Ek],
                              writes=[("ffo", job.name, l, s, blk)], q="pool", allow_slow_non_contiguous=True)
                    t1, t1k = t1r.next()
                    P.act(t1[:, 0:gs], E[:, 0:gs], AF.Identity, bias=cbv[:, blk:blk + 1], scale=cw[:, 0, blk:blk + 1], reads=[Ek, "smallprm"], writes=[t1k])
                    P.stt(t1[:, 0:gs], E[:, 1:gs + 1], cw[:, 1, blk:blk + 1], t1[:, 0:gs], ALU.mult, ALU.add, reads=[Ek, "smallprm", t1k], writes=[t1k])
                    P.stt(t1[:, 0:gs], E[:, 2:gs + 2], cw[:, 2, blk:blk + 1], t1[:, 0:gs], ALU.mult, ALU.add, reads=[Ek, "smallprm", t1k], writes=[t1k])
                    res.append((t1, t1k))
                (ta, tak), (tb, tbk) = res
                P.act(ta[:, 0:gs], ta[:, 0:gs], AF.Silu, reads=[tak], writes=[tak])
                g, gk = gr.next()
                P.tt("pool", g[:, 0:gs], ta[:, 0:gs], tb[:, 0:gs], ALU.mult, reads=[tak, tbk], writes=[gk])
                P.dma(job.gT[jb * 128:(jb + 1) * 128, col0:col0 + gs], g[:, 0:gs], reads=[gk], writes=[("gT", job.name, jb, col0 // 512)], q="pool")
        P.flush()


def final_norm(C, job, I):
    P, k = C.P, C.k
    with contextlib.ExitStack() as ts:
        gf, gfk = C.sp["gfin"], "smallprm"
        xg = Rot(C, ts, "fn_x", [128, KC, 128], F32, 3)
        sq = Rot(C, ts, "fn_sq", [128, KC, 128], BF16, 2)
        pss = Rot(C, ts, "fn_ss", [128, 128], F32, 2, psum=True)
        rs = Rot(C, ts, "fn_r", [128, 128], F32, 2)
        pst = Rot(C, ts, "fn_pt", [128, 512], F32, 4, psum=True)
        yo = Rot(C, ts, "fn_yo", [128, 1024], F32, 2)

        def stage_a(grp):
            (s, t0, gs, col0) = grp
            x, xk = xg.next()
            rk_ = [key for c in range(8) for key in xk1(job, c, col0, gs)]
            P.dma(x[:, :, 0:gs], job.xT[:, col0:col0 + gs].rearrange("(c p) t -> p c t", p=128), reads=rk_, writes=[xk])
            q, qk = sq.next()
            P.act(q[:, :, 0:gs], x[:, :, 0:gs], AF.Square, reads=[xk], writes=[qk])
            ps, pk = pss.next()
            for c in range(KC):
                P.mm(ps[:, 0:gs], k["onesB"][:], q[:, c, 0:gs], start=(c == 0), stop=(c == KC - 1), reads=[qk, "onesB"], writes=[pk])
            r, rk = rs.next()
            P.act(r[:, 0:gs], ps[:, 0:gs], AF.Ln, bias=k["eps_rms"][:, 0:1], scale=1.0 / D, reads=[pk, "epsv"], writes=[rk])
            P.act(r[:, 0:gs], r[:, 0:gs], AF.Exp, scale=-0.5, reads=[rk], writes=[rk])
            for c in range(KC):
                P.stt(x[:, c, 0:gs], x[:, c, 0:gs], gf[:, c:c + 1], r[:, 0:gs], ALU.mult, ALU.mult, reads=[xk, rk, gfk], writes=[xk])
            return (x, xk, grp)

        def stage_b(st_):
            x, xk, (s, t0, gs, col0) = st_
            y, yk = yo.next()
            for half in range(2):
                pt, ptk = pst.next()
                for c4 in range(4):
                    c = half * 4 + c4
                    P.tr(pt[0:gs, c4 * 128:(c4 + 1) * 128], x[:, c, 0:gs], k["identF"][:, :], reads=[xk, "identF"], writes=[ptk])
                P.copy("act" if half else "dve", y[0:gs, half * 512:(half + 1) * 512], pt[0:gs, :], reads=[ptk], writes=[yk])
            P.dma(job.y_out[s][t0:t0 + gs, :], y[0:gs, :], reads=[yk], writes=[("yout", job.name, s, t0)], q="pool")

        groups = seq_groups(job, 128)
        prev = None
        for grp in groups:
            cur = stage_a(grp)
            if prev is not None:
                stage_b(prev)
            prev = cur
        stage_b(prev)
        P.flush()


def load_small_params(C, I):
    P, st, k = C.P, C.st, C.k
    sp = {}
    specs = []
    for l in range(2):
        specs.append((("gmix", l), I["norm_mix_g"][l].rearrange("(c p) -> c p", p=128), 8))
        specs.append((("gffn", l), I["norm_ffn_g"][l].rearrange("(c p) -> c p", p=128), 8))
        specs.append((("cw", l), I["ffn_conv_w"][l].rearrange("j (c p) -> (j c) p", p=128), 132))
        specs.append((("cb", l), I["ffn_conv_b"][l].rearrange("(c p) -> c p", p=128), 44))
    specs.append(("gfin", I["final_norm_g"].rearrange("(c p) -> c p", p=128), 8))
    tiles = {}
    for key, ap, R in specs:
        tiles[key] = C.sb(st, "sp_%s" % str(key).replace(" ", ""), [128, R], F32)
    with contextlib.ExitStack() as ts:
        ld = Rot(C, ts, "sp_ld", [128, 128], F32, 3)
        pp = Rot(C, ts, "sp_ps", [128, 128], F32, 2, psum=True)
        for key, ap, R in specs:
            dst = tiles[key]
            for r0 in range(0, R, 128):
                rows = min(128, R - r0)
                t, tk = ld.next()
                P.dma(t[0:rows, :], ap[r0:r0 + rows, :], writes=[tk])
                ps, pk = pp.next()
                P.tr(ps[:, 0:rows], t[0:rows, :], k["identF"][0:rows, 0:rows], reads=[tk, "identF"], writes=[pk])
                P.copy("dve", dst[:, r0:r0 + rows], ps[:, 0:rows], reads=[pk], writes=["smallprm"])
        P.flush()
    for l in range(2):
        sp[("gmix", l)] = tiles[("gmix", l)]
        sp[("gffn", l)] = tiles[("gffn", l)]
        sp[("cw", l)] = tiles[("cw", l)][:, :].rearrange("p (j c) -> p j c", j=3)
        sp[("cb", l)] = tiles[("cb", l)]
    sp["gfin"] = tiles["gfin"]
    C.sp = sp


def run_layer(C, job, l, I, W, prm):
    P = C.P
    wname_in = "ab_in" if l == 0 else "cd_in"
    wname_out = "ab_out" if l == 0 else "cd_out"
    nb_in = 27 if l == 0 else 28
    Ttot = job.Ttot
    with contextlib.ExitStack() as ts:
        hT = C.sb(ts, "hT", [128, KC, Ttot], BF16)
        hkey = C.name("hT")
        with contextlib.ExitStack() as t2:
            norm_to_hT(C, t2, job, C.sp[("gmix", l)], l, 0, hT, hkey)
        with contextlib.ExitStack() as t2:
            stg = Rot(C, t2, "pj_stg", [128, 512], F32, 8)
            cnt = [0]

            def post(nb, s, t0, gs, col0, ps, pk):
                st_, sk_ = stg.next()
                cnt[0] += 1
                P.copy("act" if cnt[0] % 2 else "dve", st_[:, 0:gs], ps[:, 0:gs], reads=[pk], writes=[sk_])
                P.dma(job.zT[nb * 128:(nb + 1) * 128, col0:col0 + gs], st_[:, 0:gs], reads=[sk_], writes=[("zT", job.name, nb, col0 // 512)], q="pool")
            Wb, wk = W[wname_in]
            proj(C, job, hT, hkey, Wb, wk, range(nb_in), KC, post)
            P.flush()
    stage("inproj %s %d" % (job.name, l))
    for s in range(job.nseq):
        if l == 0:
            rwkv_mixer2(C, job, s, I, prm["rwkv"])
            stage("rwkv")
            attn_mixer(C, job, s, I, "fox")
            stage("fox")
        else:
            attn_mixer(C, job, s, I, "chunk")
            stage("chunk")
            hgrn_mixer2(C, job, s, I, prm["hgrn"])
            stage("hgrn")
    Wb, wk = W[wname_out]
    out_proj(C, job, Wb, wk, l, 0, job.mixT, 8,
             lambda col0: [("mixT", job.name, c, col0 // 512, hh) for c in range(8) for hh in range(2)])
    stage("outproj")
    with contextlib.ExitStack() as ts:
        hT = C.sb(ts, "hT2", [128, KC, Ttot], BF16)
        hkey = C.name("hT2")
        with contextlib.ExitStack() as t2:
            norm_to_hT(C, t2, job, C.sp[("gffn", l)], l, 1, hT, hkey)
        Wb, wk = W["up%d" % l]
        ffn_up(C, job, l, I, Wb, wk, hT, hkey)
    stage("ffn_up")
    Wb, wk = W["down%d" % l]
    out_proj(C, job, Wb, wk, l, 1, job.gT, 22, lambda col0: [("gT", job.name, jb, col0 // 512) for jb in range(22)])


class StopBuild(Exception):
    pass


import os
_STOP = int(os.environ.get("K_STOP", "999"))
_stage = [0]


def stage(msg=""):
    _stage[0] += 1
    if os.environ.get("K_VERBOSE"):
        print("stage", _stage[0], msg)
    if _stage[0] >= _STOP:
        raise StopBuild()


def build_program(Tp):
    nc = bass.Bass("TRN2", target_bir_lowering=False)
    _stage[0] = 0
    I = {}
    O = {}

    def inp(name, shape):
        I[name] = nc.dram_tensor(name, list(shape), F32, kind="ExternalInput").ap()

    def outp(name, shape):
        O[name] = nc.dram_tensor(name, list(shape), F32, kind="ExternalOutput").ap()
    S2 = NSEQ_S
    inp("xp", [Tp, D]); inp("xs", [S2, 16, D]); inp("cp", [D]); inp("csv", [S2, D])
    inp("fox_ck", [S2, P_FOX, 512]); inp("fox_cv", [S2, P_FOX, 512]); inp("fox_clogf", [S2, P_FOX, 8])
    inp("rw_s0", [S2, 8, 64, 64]); inp("rw_sh0", [S2, A_COLS])
    inp("ch_ck", [S2, P_CHUNK, 512]); inp("ch_cv", [S2, P_CHUNK, 512]); inp("hg_s0", [S2, 8, 64, 64])
    inp("ffn_buf", [2, S2, 2, 2 * DFF])
    for nm, shp in (("ada_w", [2, 2, D, 3 * D]), ("ada_b", [2, 2, 3 * D]), ("norm_mix_g", [2, D]), ("norm_ffn_g", [2, D]),
                    ("ab_w_in", [1, D, AB_COLS]), ("rwkv_mu", [1, A_COLS]), ("rwkv_w0", [1, 512]), ("rwkv_w2", [1, 64, 512]),
                    ("rwkv_a0", [1, 512]), ("rwkv_a2", [1, 64, 512]), ("rwkv_g2", [1, 128, 512]), ("rwkv_k_k", [1, 512]),
                    ("rwkv_k_a", [1, 512]), ("rwkv_r_k", [1, 8, 64]), ("rwkv_lnx_g", [1, 512]), ("rwkv_lnx_b", [1, 512]),
                    ("fox_b_f", [1, 8]), ("ab_w_out", [1, D, D]), ("cd_w_in", [1, D, CD_COLS]), ("chunk_rel_bias", [1, 8, 257]),
                    ("hgrn_lb_table", [2, 512]), ("hgrn_norm_g", [1, 512]), ("cd_w_out", [1, D, D]), ("ffn_w_up", [2, D, 2 * DFF]),
                    ("ffn_conv_w", [2, 3, 2 * DFF]), ("ffn_conv_b", [2, 2 * DFF]), ("ffn_w_down", [2, DFF, D]), ("final_norm_g", [D])):
        inp(nm, shp)
    cK = min(512, Tp)
    outp("y_p", [Tp, D]); outp("y_s", [S2, 16, D])
    outp("fox_k_p", [Tp, 512]); outp("fox_v_p", [Tp, 512]); outp("fox_logf_p", [Tp, 8])
    outp("rwkv_p", [8, 64, 64]); outp("rwkv_shift_p", [A_COLS])
    outp("chunk_k_p", [cK, 512]); outp("chunk_v_p", [cK, 512]); outp("hgrn_p", [8, 64, 64]); outp("ffn_conv_p", [2, 2, 2 * DFF])
    outp("fox_k_s", [S2, 16, 512]); outp("fox_v_s", [S2, 16, 512]); outp("fox_logf_s", [S2, 16, 8])
    outp("rwkv_s", [S2, 8, 64, 64]); outp("rwkv_shift_s", [S2, A_COLS])
    outp("chunk_k_s", [S2, 16, 512]); outp("chunk_v_s", [S2, 16, 512]); outp("hgrn_s", [S2, 8, 64, 64]); outp("ffn_conv_s", [2, S2, 2, 2 * DFF])

    with contextlib.ExitStack() as st:
        P = Prog(nc, st)
        C = Ctx(nc, P, st)
        try:
          build_consts(C)
          more_consts(C)
          load_small_params(C, I)
          stage("consts")
          W = {}
          W["ab_in"] = cast_weight(C, I["ab_w_in"][0], D, AB_COLS, "Wab_in")
          W["ab_out"] = cast_weight(C, I["ab_w_out"][0], D, D, "Wab_out")
          W["cd_in"] = cast_weight(C, I["cd_w_in"][0], D, CD_COLS, "Wcd_in")
          W["cd_out"] = cast_weight(C, I["cd_w_out"][0], D, D, "Wcd_out")
          for l in range(2):
              W["up%d" % l] = cast_weight(C, I["ffn_w_up"][l], D, 2 * DFF, "Wup%d" % l)
              W["down%d" % l] = cast_weight(C, I["ffn_w_down"][l], DFF, D, "Wdown%d" % l)
          stage("cast")
          adaln_phase(C, I, 1 + S2, [I["cp"]] + [I["csv"][s] for s in range(S2)])
          stage("adaln")
          prm = dict(rwkv=load_rwkv_params(C, I), hgrn=load_hgrn_params(C, I))
          build_rel_tables(C, I)
          stage("params")

          S3 = 1 + S2
          jb = Job()
          jb.name, jb.nseq = "m", S3
          jb.Ts = [Tp] + [16] * S2
          jb.bases = [0] + [Tp + 16 * i for i in range(S2)]
          jb.Ttot = Tp + 16 * S2
          jb.modcol = list(range(S3))
          jb.x_in = [I["xp"]] + [I["xs"][s] for s in range(S2)]
          jb.y_out = [O["y_p"]] + [O["y_s"][s] for s in range(S2)]
          jb.fox_kout = [O["fox_k_p"]] + [O["fox_k_s"][s] for s in range(S2)]
          jb.fox_vout = [O["fox_v_p"]] + [O["fox_v_s"][s] for s in range(S2)]
          jb.fox_logf_out = [O["fox_logf_p"]] + [O["fox_logf_s"][s] for s in range(S2)]
          jb.rwkv_out = [O["rwkv_p"]] + [O["rwkv_s"][s] for s in range(S2)]
          jb.rwkv_shift_out = [O["rwkv_shift_p"]] + [O["rwkv_shift_s"][s] for s in range(S2)]
          jb.chunk_kout = [O["chunk_k_p"]] + [O["chunk_k_s"][s] for s in range(S2)]
          jb.chunk_vout = [O["chunk_v_p"]] + [O["chunk_v_s"][s] for s in range(S2)]
          jb.hgrn_out = [O["hgrn_p"]] + [O["hgrn_s"][s] for s in range(S2)]
          jb.ffn_out = [[O["ffn_conv_p"][l]] + [O["ffn_conv_s"][l, s] for s in range(S2)] for l in range(2)]
          jb.P_fox = [0] + [P_FOX] * S2
          jb.P_chunk = [0] + [P_CHUNK] * S2
          jb.fox_ck = [None] + [I["fox_ck"][s] for s in range(S2)]
          jb.fox_cv = [None] + [I["fox_cv"][s] for s in range(S2)]
          jb.fox_clogf = [None] + [I["fox_clogf"][s] for s in range(S2)]
          jb.chunk_ck = [None] + [I["ch_ck"][s] for s in range(S2)]
          jb.chunk_cv = [None] + [I["ch_cv"][s] for s in range(S2)]
          jb.rwkv_s0 = [None] + [I["rw_s0"][s] for s in range(S2)]
          jb.rwkv_shift0 = [None] + [I["rw_sh0"][s] for s in range(S2)]
          jb.hgrn_s0 = [None] + [I["hg_s0"][s] for s in range(S2)]
          jb.ffn_buf = [[None] + [I["ffn_buf"][l, s] for s in range(S2)] for l in range(2)]
          for job in (jb,):
              Ttot = job.Ttot
              job.xT = C.dr("xT_" + job.name, [D, Ttot], F32)
              job.zT = C.dr("zT_" + job.name, [28 * 128, Ttot], F32)
              job.mixT = C.dr("mixT_" + job.name, [D, Ttot], BF16)
              job.gT = C.dr("gT_" + job.name, [DFF, Ttot], BF16)
              x_to_fm(C, job)
              stage("x_to_fm " + job.name)
              for l in range(2):
                  run_layer(C, job, l, I, W, prm)
              final_norm(C, job, I)
              stage("final " + job.name)
        except StopBuild:
            P.ops = []
        P.flush(final=True)
        print("ops:", P.n_total)
        if os.environ.get("K_FLUSHLOG"):
            import json as _json
            _json.dump(P.flush_log, open(os.environ["K_FLUSHLOG"], "w"))
    return nc


_CACHE = {}


def kernel(**inp):
    f = lambda a: np.ascontiguousarray(np.asarray(a, dtype=np.float32))
    xpr = f(inp["x_prompt"])
    B, Tp, _ = xpr.shape
    if Tp not in _CACHE:
        _CACHE[Tp] = build_program(Tp)
    nc = _CACHE[Tp]
    wnames = ["ada_w", "ada_b", "norm_mix_g", "norm_ffn_g", "ab_w_in", "rwkv_mu", "rwkv_w0", "rwkv_w2", "rwkv_a0", "rwkv_a2",
              "rwkv_g2", "rwkv_k_k", "rwkv_k_a", "rwkv_r_k", "rwkv_lnx_g", "rwkv_lnx_b", "fox_b_f", "ab_w_out", "cd_w_in",
              "chunk_rel_bias", "hgrn_lb_table", "hgrn_norm_g", "cd_w_out", "ffn_w_up", "ffn_conv_w", "ffn_conv_b", "ffn_w_down",
              "final_norm_g"]
    wts = {n: f(inp[n]) for n in wnames}
    xs = f(inp["x_sample"]); cp = f(inp["c_prompt"]); csv = f(inp["c_sample"])
    fk = f(inp["cache_fox_k"])[0].reshape(16, P_FOX, 512); fv = f(inp["cache_fox_v"])[0].reshape(16, P_FOX, 512)
    fl = f(inp["cache_fox_logf"])[0]
    rs0 = f(inp["state_rwkv"])[0]; rsh = f(inp["state_rwkv_shift"])[0]
    ckk = f(inp["cache_chunk_k"])[0].reshape(16, P_CHUNK, 512); ckv = f(inp["cache_chunk_v"])[0].reshape(16, P_CHUNK, 512)
    hs0 = f(inp["state_hgrn"])[0]; fb = f(inp["state_ffn_conv"])
    in_maps = []
    for c in range(N_CORES):
        b = c % B
        sl = slice(2 * c, 2 * c + 2)
        m = dict(xp=xpr[b], xs=xs[sl], cp=cp[b], csv=csv[sl], fox_ck=fk[sl], fox_cv=fv[sl], fox_clogf=fl[sl], rw_s0=rs0[sl], rw_sh0=rsh[sl],
                 ch_ck=ckk[sl], ch_cv=ckv[sl], hg_s0=hs0[sl], ffn_buf=np.ascontiguousarray(fb[:, sl]))
        m.update(wts)
        in_maps.append({k_: np.ascontiguousarray(v_) for k_, v_ in m.items()})
    res = run_bass_kernel_spmd(nc, in_maps, core_ids=list(range(N_CORES)))
    R = res.results
    pc = lambda name: np.stack([R[b][name] for b in range(B)])
    sc = lambda name, ax=0: np.concatenate([R[c][name] for c in range(N_CORES)], axis=ax)
    cK = min(512, Tp)
    outs = (
        pc("y_p"), sc("y_s"),
        pc("fox_k_p").reshape(1, B, Tp, 8, 64), pc("fox_v_p").reshape(1, B, Tp, 8, 64), pc("fox_logf_p").reshape(1, B, Tp, 8),
        pc("rwkv_p")[None], pc("rwkv_shift_p")[None],
        pc("chunk_k_p").reshape(1, B, cK, 8, 64), pc("chunk_v_p").reshape(1, B, cK, 8, 64), pc("hgrn_p")[None],
        np.stack([R[b]["ffn_conv_p"] for b in range(B)], axis=1),
        sc("fox_k_s").reshape(1, 16, 16, 8, 64), sc("fox_v_s").reshape(1, 16, 16, 8, 64), sc("fox_logf_s").reshape(1, 16, 16, 8),
        sc("rwkv_s")[None], sc("rwkv_shift_s")[None],
        sc("chunk_k_s").reshape(1, 16, 16, 8, 64), sc("chunk_v_s").reshape(1, 16, 16, 8, 64), sc("hgrn_s")[None],
        sc("ffn_conv_s", ax=1),
    )
    return tuple(np.ascontiguousarray(o.astype(np.float32)) for o in outs)


def chunk_scan2(C, ts, SEG, L, delta, ops8, tok4, eposL, S32, Sb, yT8):
    P, k = C.P, C.k
    NCH = SEG // L
    nlev = int(np.log2(L))
    psT = Rot(C, ts, "c2_psT", [64, 4, 128], BF16, 2, psum=True)
    psH = Rot(C, ts, "c2_psH", [64, 8, 64], F32, 6, psum=True)
    r8, r8k = ops8["r"]
    k8, k8k = ops8["k"]
    if delta:
        a8, a8k = ops8["a"]
        b8, b8k = ops8["b"]

    def bt(nm, dt=BF16):
        return [(C.sb(ts, "c2_%s%d" % (nm, c), [64, 8, 64], dt), C.name("c2_" + nm)) for c in range(NCH)]
    Ktok, Vtok = bt("Ktok"), bt("Vtok")
    Mrk = bt("Mrk")
    if delta:
        Btok, Mak, Mrb = bt("Btok"), bt("Mak"), bt("Mrb")
        PT = [bt("PTa"), bt("PTb")]
        Pm = [bt("Pma"), bt("Pmb")]
        TTb = [bt("TTba"), bt("TTbb")]
        TTf = bt("TTf", F32)
    cs_of = lambda c: slice(c * L, (c + 1) * L)

    def mm8(c, lhs_of, rhs_of, reads):
        ps, pk = psH.next()
        for h in range(8):
            P.mm(ps[0:L, h, 0:L], lhs_of(h), rhs_of(h), reads=reads, writes=[pk])
        return ps, pk

    for c in range(NCH):
        cs = cs_of(c)
        lst = [("k", Ktok), ("v", Vtok)] + ([("b", Btok)] if delta else [])
        for nm, dstl in lst:
            src, sk = tok4[nm]
            ps, pk = psT.next()
            for blk in range(4):
                P.tr(ps[0:L, blk, :], src[:, blk, cs], k["identB"][:, :], reads=[sk, "identB"], writes=[pk])
            dst, dk = dstl[c]
            P.copy("act" if nm == "v" else "dve", dst[0:L, :, :], ps[0:L, :, :].rearrange("p a (e x) -> p (a e) x", e=2), reads=[pk], writes=[dk])
    if delta:
        for c in range(NCH):
            cs = cs_of(c)
            psA, pkA = mm8(c, lambda h: b8[:, h, cs], lambda h: a8[:, h, cs], [a8k, b8k])
            psB, pkB = mm8(c, lambda h: a8[:, h, cs], lambda h: b8[:, h, cs], [a8k, b8k])
            tf, tfk = TTf[c]
            pt, ptk = PT[0][c]
            pm, pmk = Pm[0][c]
            tb, tbk = TTb[0][c]
            P.tt("dve", tf[0:L, :, 0:L], psA[0:L, :, 0:L], k["m_su"][0:L, :, 0:L], ALU.mult, reads=[pkA, "m_su"], writes=[tfk])
            P.copy("act", pt[0:L, :, 0:L], tf[0:L, :, 0:L], reads=[tfk], writes=[ptk])
            P.tt("dve", pm[0:L, :, 0:L], psB[0:L, :, 0:L], k["m_sl"][0:L, :, 0:L], ALU.mult, reads=[pkB, "m_sl"], writes=[pmk])
            P.tt("pool", tf[0:L, :, 0:L], tf[0:L, :, 0:L], k["i8"][0:L, :, 0:L], ALU.add, reads=[tfk, "i8"], writes=[tfk])
            P.copy("act", tb[0:L, :, 0:L], tf[0:L, :, 0:L], reads=[tfk], writes=[tbk])
        cur = 0
        for j in range(1, nlev):
            last = (j == nlev - 1)
            nxt = 1 - cur
            pend_ = []
            for c in range(NCH):
                pt, ptk = PT[cur][c]
                pm, pmk = Pm[cur][c]
                psA, pkA = mm8(c, lambda h: pt[0:L, h, 0:L], lambda h: pm[0:L, h, 0:L], [ptk, pmk])
                pm2, pm2k = Pm[nxt][c]
                P.copy("act", pm2[0:L, :, 0:L], psA[0:L, :, 0:L], reads=[pkA], writes=[pm2k])
                if not last:
                    psB, pkB = mm8(c, lambda h: pm[0:L, h, 0:L], lambda h: pt[0:L, h, 0:L], [ptk, pmk])
                    pt2, pt2k = PT[nxt][c]
                    P.copy("dve", pt2[0:L, :, 0:L], psB[0:L, :, 0:L], reads=[pkB], writes=[pt2k])
                if c >= 1:
                    pend_.append(c - 1)
                    cc_ = pend_.pop(0)
                    pm2_, pm2k_ = Pm[nxt][cc_]
                    tb, tbk = TTb[cur][cc_]
                    psC, pkC = mm8(cc_, lambda h: pm2_[0:L, h, 0:L], lambda h: tb[0:L, h, 0:L], [pm2k_, tbk])
                    tf, tfk = TTf[cc_]
                    P.tt("dve", tf[0:L, :, 0:L], tf[0:L, :, 0:L], psC[0:L, :, 0:L], ALU.add, reads=[pkC, tfk], writes=[tfk])
                    tb2, tb2k = TTb[nxt][cc_]
                    P.copy("act", tb2[0:L, :, 0:L], tf[0:L, :, 0:L], reads=[tfk], writes=[tb2k])
            cc_ = NCH - 1
            pm2_, pm2k_ = Pm[nxt][cc_]
            tb, tbk = TTb[cur][cc_]
            psC, pkC = mm8(cc_, lambda h: pm2_[0:L, h, 0:L], lambda h: tb[0:L, h, 0:L], [pm2k_, tbk])
            tf, tfk = TTf[cc_]
            P.tt("dve", tf[0:L, :, 0:L], tf[0:L, :, 0:L], psC[0:L, :, 0:L], ALU.add, reads=[pkC, tfk], writes=[tfk])
            tb2, tb2k = TTb[nxt][cc_]
            P.copy("act", tb2[0:L, :, 0:L], tf[0:L, :, 0:L], reads=[tfk], writes=[tb2k])
            cur = nxt
        TTfin = TTb[cur]
    for c in range(NCH):
        cs = cs_of(c)
        if delta:
            psA, pkA = mm8(c, lambda h: k8[:, h, cs], lambda h: a8[:, h, cs], [k8k, a8k])
            m_, mk_ = Mak[c]
            P.tt("dve", m_[0:L, :, 0:L], psA[0:L, :, 0:L], k["m_su"][0:L, :, 0:L], ALU.mult, reads=[pkA, "m_su"], writes=[mk_])
            psA, pkA = mm8(c, lambda h: b8[:, h, cs], lambda h: r8[:, h, cs], [b8k, r8k])
            m_, mk_ = Mrb[c]
            P.tt("dve", m_[0:L, :, 0:L], psA[0:L, :, 0:L], k["m_ui"][0:L, :, 0:L], ALU.mult, reads=[pkA, "m_ui"], writes=[mk_])
        psA, pkA = mm8(c, lambda h: k8[:, h, cs], lambda h: r8[:, h, cs], [k8k, r8k])
        m_, mk_ = Mrk[c]
        P.tt("dve", m_[0:L, :, 0:L], psA[0:L, :, 0:L], k["m_ui"][0:L, :, 0:L], ALU.mult, reads=[pkA, "m_ui"], writes=[mk_])
    W1r = Rot(C, ts, "c2_W1", [64, 8, 64], BF16, 2)
    Ur = Rot(C, ts, "c2_U", [64, 8, 64], BF16, 2)
    yt, ytk = yT8
    el, elk = eposL
    for c in range(NCH):
        cs = cs_of(c)
        Kt, Ktk = Ktok[c]
        Vt_, Vtk = Vtok[c]
        mrk, mrkk = Mrk[c]
        if delta:
            Bt, Btk = Btok[c]
            mak, makk = Mak[c]
            mrb, mrbk = Mrb[c]
            tb, tbk = TTfin[c]
            ps, pk = psH.next()
            for h in range(8):
                P.mm(ps[0:L, h, :], a8[:, h, cs], Sb[:, h, :], start=True, stop=False, reads=[a8k, "Sb"], writes=[pk])
                P.mm(ps[0:L, h, :], mak[0:L, h, 0:L], Vt_[0:L, h, :], start=False, stop=True, reads=[makk, Vtk], writes=[pk])
            W1, W1k = W1r.next()
            P.copy("act", W1[0:L, :, :], ps[0:L, :, :], reads=[pk], writes=[W1k])
            ps, pk = psH.next()
            for h in range(8):
                P.mm(ps[0:L, h, :], tb[0:L, h, 0:L], W1[0:L, h, :], reads=[tbk, W1k], writes=[pk])
            U, Uk = Ur.next()
            P.copy("dve", U[0:L, :, :], ps[0:L, :, :], reads=[pk], writes=[Uk])
        ps, pk = psH.next()
        for h in range(8):
            P.mm(ps[:, h, 0:L], Sb[:, h, :], r8[:, h, cs], start=True, stop=False, reads=["Sb", r8k], writes=[pk])
            if delta:
                P.mm(ps[:, h, 0:L], U[0:L, h, :], mrb[0:L, h, 0:L], start=False, stop=False, reads=[Uk, mrbk], writes=[pk])
            P.mm(ps[:, h, 0:L], Vt_[0:L, h, :], mrk[0:L, h, 0:L], start=False, stop=True, reads=[Vtk, mrkk], writes=[pk])
        P.copy("act", yt[:, :, cs], ps[:, :, 0:L], reads=[pk], writes=[ytk])
        ps, pk = psH.next()
        for h in range(8):
            if delta:
                P.mm(ps[:, h, :], Bt[0:L, h, :], U[0:L, h, :], start=True, stop=False, reads=[Btk, Uk], writes=[pk])
            P.mm(ps[:, h, :], Kt[0:L, h, :], Vt_[0:L, h, :], start=(not delta), stop=True, reads=[Ktk, Vtk], writes=[pk])
        skeys = [("S32", h) for h in range(8)]
        P.tt("dve", S32[:, :, :], S32[:, :, :], ps[:, :, :], ALU.add, reads=[pk] + skeys, writes=skeys)
        for h in range(8):
            P.ts("dve", S32[:, h, :], S32[:, h, :], el[:, h, c:c + 1], None, ALU.mult, reads=[("S32", h), elk], writes=[("S32", h)])
        P.copy("act", Sb[:, :, :], S32[:, :, :], reads=skeys, writes=["Sb"])


def to8(P, dst8, dkey, src4, skey, q="sp"):
    d4 = dst8[:, :, :].rearrange("p (a e) t -> p a e t", e=2)
    for e in range(2):
        P.dma(d4[:, :, e, :], src4[e * 64:(e + 1) * 64, :, :], reads=[skey], writes=[dkey], q=q)


def rwkv_mixer2(C, job, s, I, prm):
    P, k = C.P, C.k
    T = job.Ts[s]
    L = min(64, T)
    SEG = min(256, T)
    NCH = SEG // L
    base = job.bases[s]
    S8K = [("S32", h) for h in range(8)]
    with contextlib.ExitStack() as ts:
        S32 = C.sb(ts, "S32", [64, 8, 64], F32)
        Sb = C.sb(ts, "Sb", [64, 8, 64], BF16)
        k4, v4, b4 = [C.sb(ts, n, [128, 4, SEG], BF16) for n in ("k4", "v4", "b4")]
        r8, a8, b8, k8, v8, t8 = [C.sb(ts, n, [64, 8, SEG], BF16) for n in ("r8", "a8", "b8", "k8", "v8", "t8")]
        el = C.sb(ts, "el", [64, 8, NCH], F32)
        yT8 = C.sb(ts, "yT8", [64, 8, SEG], F32)
        sg = C.sb(ts, "sg", [128, SEG], BF16)
        zlast = C.sb(ts, "zlast", [128, 14], F32)
        mu, w0, a0, k_k, k_a, r_k, omka, wa2, g2, lng8, lnb8 = [prm[n] for n in
            ("mu", "w0", "a0", "k_k", "k_a", "r_k", "omka", "wa2", "g2", "lng8", "lnb8")]
        pk_ = ["rwprm", "rwprm2"]
        with contextlib.ExitStack() as t0s:
            if job.rwkv_s0[s] is None:
                P.memset("pool", S32[:], 0.0, writes=S8K)
            else:
                s0t = C.sb(t0s, "s0t", [64, 8, 64], F32)
                psI = C.ps(t0s, "rw_psI", [64, 8, 64], F32)
                P.dma(s0t[:], job.rwkv_s0[s].rearrange("h v k -> v h k"), writes=["s0t"])
                for h in range(8):
                    P.tr(psI[:, h, :], s0t[:, h, :], k["identF"][0:64, 0:64], reads=["s0t", "identF"], writes=["psI"])
                P.copy("dve", S32[:], psI[:], reads=["psI"], writes=S8K)
            P.copy("act", Sb[:], S32[:], reads=S8K, writes=["Sb"])
            P.flush()
        for t0 in range(0, T, SEG):
            col0 = base + t0
            with contextlib.ExitStack() as tp:
                psP = Rot(C, tp, "rw_psP", [128, 512], F32, 4, psum=True)
                zt = C.sb(tp, "zt", [128, 14, SEG + 1], F32)
                dd = C.sb(tp, "dd", [128, 14, SEG], F32)
                zs = C.sb(tp, "zs", [128, 14, SEG], F32)
                f4 = lambda nm: C.sb(tp, nm, [128, 4, SEG], F32)
                b4_ = lambda nm: C.sb(tp, nm, [128, 4, SEG], BF16)
                lw, asig, kkn, kmod, cc, epos, eneg, eprev, tmpa, tmpb_ = [f4(n) for n in
                    ("lw", "asig", "kkn", "kmod", "cc", "epos", "eneg", "eprev", "tmpa", "tmpb")]
                rT4, aT4, t4, sqb = [b4_(n) for n in ("rT4", "aT4", "t4", "sqb")]
                tw = C.sb(tp, "tw", [128, SEG], BF16)
                zrows = job.zT[0:A_COLS, :].rearrange("(c p) t -> p c t", p=128)
                zk = [("zT", job.name, nb, i) for nb in range(14) for i in range(col0 // 512, (col0 + SEG - 1) // 512 + 1)]
                if t0 == 0:
                    P.dma(zt[:, :, 1:SEG + 1], zrows[:, :, col0:col0 + SEG], reads=zk, writes=["zt"])
                    if job.rwkv_shift0[s] is None:
                        P.memset("pool", zt[:, :, 0:1], 0.0, writes=["zt"])
                    else:
                        P.dma(zt[:, :, 0], job.rwkv_shift0[s].rearrange("(c p) -> p c", p=128), writes=["zt"], allow_slow_non_contiguous=True)
                else:
                    P.dma(zt[:, :, 1:SEG + 1], zrows[:, :, col0:col0 + SEG], reads=zk, writes=["zt"])
                    P.copy("pool", zt[:, :, 0], zlast[:, :], reads=["zlast"], writes=["zt"])
                P.copy("pool", zlast[:, :], zt[:, :, SEG], reads=["zt"], writes=["zlast"])
                if t0 + SEG == T:
                    P.dma(job.rwkv_shift_out[s].rearrange("(c p) -> p c", p=128), zlast[:, :], reads=["zlast"], writes=[("rwsh", job.name, s)],
                          q="pool", allow_slow_non_contiguous=True)
                P.tt("pool", dd[:], zt[:, :, 0:SEG], zt[:, :, 1:SEG + 1], ALU.subtract, reads=["zt"], writes=["dd"])
                for blk in range(14):
                    P.stt(zs[:, blk, :], dd[:, blk, :], mu[:, blk:blk + 1], zt[:, blk, 1:SEG + 1], ALU.mult, ALU.add, reads=["dd", "zt"] + pk_, writes=["zs"])
                P.act(tw[0:64, :], zs[0:64, 12, :], AF.Tanh, reads=["zs"], writes=["tw"])
                P.copy("dve", tw[64:128, :], zs[64:128, 12, :], reads=["zs"], writes=["tw"])
                P.act(sg[:], zs[:, 13, :], AF.Sigmoid, reads=["zs"], writes=["sg"])
                for blk in range(4):
                    bs = slice(blk * 128, (blk + 1) * 128)
                    ps, pk = psP.next()
                    P.mm(ps[:, 0:SEG], wa2[0:64, bs], tw[0:64, :], reads=["tw"] + pk_, writes=[pk])
                    P.act(lw[:, blk, :], ps[:, 0:SEG], AF.Sigmoid, bias=w0[:, blk:blk + 1], reads=[pk] + pk_, writes=["lw"])
                    ps, pk = psP.next()
                    P.mm(ps[:, 0:SEG], wa2[64:128, bs], tw[64:128, :], reads=["tw"] + pk_, writes=[pk])
                    P.act(asig[:, blk, :], ps[:, 0:SEG], AF.Sigmoid, bias=a0[:, blk:blk + 1], reads=[pk] + pk_, writes=["asig"])
                P.ts("dve", lw[:], lw[:], -DECAY_C, None, ALU.mult, reads=["lw"], writes=["lw"])
                for blk in range(4):
                    P.ts("dve", tmpa[:, blk, :], zs[:, 4 + blk, :], k_k[:, blk:blk + 1], None, ALU.mult, reads=["zs"] + pk_, writes=["tmpa"])
                P.act(sqb[:], tmpa[:], AF.Square, reads=["tmpa"], writes=["sqb"])
                for blk in range(4):
                    ps, pk = psP.next()
                    P.mm(ps[:, 0:SEG], k["bo"][:], sqb[:, blk, :], reads=["sqb", "bo"], writes=[pk])
                    P.act(tmpb_[:, blk, :], ps[:, 0:SEG], AF.Sqrt, reads=[pk], writes=["tmpb"])
                P.ts("dve", tmpb_[:], tmpb_[:], 1e-12, None, ALU.max, reads=["tmpb"], writes=["tmpb"])
                P.op("dve", lambda e: e.reciprocal(tmpb_[:], tmpb_[:]), reads=["tmpb"], writes=["tmpb"])
                P.tt("dve", kkn[:], tmpa[:], tmpb_[:], ALU.mult, reads=["tmpa", "tmpb"], writes=["kkn"])
                for blk in range(4):
                    P.ts("dve", tmpa[:, blk, :], asig[:, blk, :], k_a[:, blk:blk + 1], omka[:, blk:blk + 1], ALU.mult, ALU.add,
                         reads=["asig"] + pk_, writes=["tmpa"])
                P.tt("dve", kmod[:], tmpa[:], zs[:, 4:8, :], ALU.mult, reads=["tmpa", "zs"], writes=["kmod"])
                rm = k["rmask%d" % L]
                for blk in range(4):
                    P.op("dve", lambda e, blk=blk: e.tensor_tensor_scan(cc[:, blk, :], rm[:, 0:SEG], lw[:, blk, :], 0.0, ALU.mult, ALU.add),
                         reads=["lw", "rmask%d" % L], writes=["cc"])
                P.act(epos[:], cc[:], AF.Exp, reads=["cc"], writes=["epos"])
                P.act(eneg[:], cc[:], AF.Exp, scale=-1.0, reads=["cc"], writes=["eneg"])
                P.tt("pool", tmpa[:], cc[:], lw[:], ALU.subtract, reads=["cc", "lw", "kmod"], writes=["tmpa"])
                P.act(eprev[:], tmpa[:], AF.Exp, reads=["tmpa"], writes=["eprev"])
                P.tt("dve", rT4[:], zs[:, 0:4, :], epos[:], ALU.mult, reads=["zs", "epos"], writes=["rT4"])
                P.stt(aT4[:], kkn[:], -1.0, eprev[:], ALU.mult, ALU.mult, reads=["kkn", "eprev"], writes=["aT4"])
                P.tt("pool", tmpb_[:], kkn[:], asig[:], ALU.mult, reads=["kkn", "asig"], writes=["tmpb"])
                P.tt("dve", b4[:], tmpb_[:], eneg[:], ALU.mult, reads=["tmpb", "eneg"], writes=["b4"])
                P.tt("dve", k4[:], kmod[:], eneg[:], ALU.mult, reads=["kmod", "eneg"], writes=["k4"])
                P.copy("act", v4[:], zs[:, 8:12, :], reads=["zs"], writes=["v4"])
                for blk in range(4):
                    P.stt(t4[:, blk, :], zs[:, blk, :], r_k[:, blk:blk + 1], kmod[:, blk, :], ALU.mult, ALU.mult, reads=["zs", "kmod"] + pk_, writes=["t4"])
                for dst, dkey, src, skey in ((r8, "r8", rT4, "rT4"), (a8, "a8", aT4, "aT4"), (b8, "b8", b4, "b4"), (k8, "k8", k4, "k4"),
                                             (v8, "v8", v4, "v4"), (t8, "t8", t4, "t4")):
                    to8(P, dst, dkey, src, skey)
                ec = epos[:, :, :].rearrange("p a (c l) -> p a c l", l=L)
                e4 = el[:, :, :].rearrange("p (a e) c -> p a e c", e=2)
                ecomp = C.sb(tp, "ecomp", [128, 4, NCH], F32)
                P.copy("pool", ecomp[:], ec[:, :, :, L - 1], reads=["epos"], writes=["ecomp"])
                for e in range(2):
                    P.dma(e4[:, :, e, :], ecomp[e * 64:(e + 1) * 64, :, :], reads=["ecomp"], writes=["el"], allow_slow_non_contiguous=True)
                P.flush()
            with contextlib.ExitStack() as tsc:
                ops8 = dict(r=(r8, "r8"), k=(k8, "k8"), a=(a8, "a8"), b=(b8, "b8"))
                tok4 = dict(k=(k4, "k4"), v=(v4, "v4"), b=(b4, "b4"))
                chunk_scan2(C, tsc, SEG, L, True, ops8, tok4, (el, "el"), S32, Sb, (yT8, "yT8"))
                P.flush()
            with contextlib.ExitStack() as tq:
                psQ = Rot(C, tq, "rw_psQ", [64, 4, SEG], F32, 3 if SEG > 128 else 6, psum=True)
                yb8 = C.sb(tq, "yb8", [64, 8, SEG], BF16)
                d8 = C.sb(tq, "d8", [64, 8, SEG], F32)
                rs8 = C.sb(tq, "rs8", [64, 8, SEG], F32)
                tm8 = C.sb(tq, "tm8", [64, 8, SEG], F32)
                o8 = C.sb(tq, "o8", [64, 8, SEG], BF16)
                one64 = k["onesB"][0:64, 0:64]
                P.copy("act", yb8[:], yT8[:], reads=["yT8"], writes=["yb8"])
                for half in range(2):
                    hs = slice(half * 4, half * 4 + 4)
                    ps, pk = psQ.next()
                    for i in range(4):
                        P.mm(ps[:, i, :], one64, yb8[:, half * 4 + i, :], reads=["yb8", "onesB"], writes=[pk])
                    P.stt(d8[:, hs, :], ps[:, :, :], -1.0 / 64, yT8[:, hs, :], ALU.mult, ALU.add, reads=[pk, "yT8"], writes=[("d8", half)])
                P.act(yb8[:], d8[:], AF.Square, reads=[("d8", 0), ("d8", 1), "yb8"], writes=["yb8"])
                for half in range(2):
                    hs = slice(half * 4, half * 4 + 4)
                    ps, pk = psQ.next()
                    for i in range(4):
                        P.mm(ps[:, i, :], one64, yb8[:, half * 4 + i, :], reads=["yb8", "onesB"], writes=[pk])
                    P.act(rs8[:, hs, :], ps[:, :, :], AF.Ln, bias=k["eps_gn"][0:64, 0:1], scale=1.0 / 64, reads=[pk, "epsv"], writes=[("rs8", half)])
                P.act(rs8[:], rs8[:], AF.Exp, scale=-0.5, reads=[("rs8", 0), ("rs8", 1)], writes=["rs8r"])
                P.tt("dve", d8[:], d8[:], rs8[:], ALU.mult, reads=[("d8", 0), ("d8", 1), "rs8r"], writes=["d8n"])
                for h in range(8):
                    P.ts("dve", d8[:, h, :], d8[:, h, :], lng8[:, h:h + 1], lnb8[:, h:h + 1], ALU.mult, ALU.add, reads=["d8n"] + pk_, writes=[("d8a", h)])
                for half in range(2):
                    hs = slice(half * 4, half * 4 + 4)
                    ps, pk = psQ.next()
                    for i in range(4):
                        P.mm(ps[:, i, :], one64, t8[:, half * 4 + i, :], reads=["t8", "onesB"], writes=[pk])
                    P.tt("dve", tm8[:, hs, :], ps[:, :, :], v8[:, hs, :], ALU.mult, reads=[pk, "v8"], writes=[("tm8", half)])
                P.tt("dve", d8[:], d8[:], tm8[:], ALU.add, reads=[("d8a", h) for h in range(8)] + [("tm8", 0), ("tm8", 1)], writes=["d8f"])
                for half in range(2):
                    hs = slice(half * 4, half * 4 + 4)
                    ps, pk = psQ.next()
                    for i in range(4):
                        h = half * 4 + i
                        P.mm(ps[:, i, :], g2[:, h * 64:(h + 1) * 64], sg[:, :], reads=["sg"] + pk_, writes=[pk])
                    P.tt("dve", o8[:, hs, :], ps[:, :, :], d8[:, hs, :], ALU.mult, reads=[pk, "d8f"], writes=[("o8", half)])
                P.dma(job.mixT[0:512, col0:col0 + SEG].rearrange("(h k) t -> k h t", k=64), o8[:], reads=[("o8", 0), ("o8", 1)],
                      writes=[("mixT", job.name, c, col0 // 512, hh) for c in range(4) for hh in range(2)], q="pool")
                P.flush()
        with contextlib.ExitStack() as tf_:
            psO = C.ps(tf_, "rw_psO", [64, 8, 64], F32)
            so = C.sb(tf_, "so", [64, 8, 64], F32)
            for h in range(8):
                P.tr(psO[:, h, :], S32[:, h, :], k["identF"][0:64, 0:64], reads=S8K + ["identF"], writes=["psO"])
            P.copy("dve", so[:], psO[:], reads=["psO"], writes=["so"])
            P.dma(job.rwkv_out[s].rearrange("h v k -> v h k"), so[:], reads=["so"], writes=[("rwst", job.name, s)], q="pool")
            P.flush()


def hgrn_mixer2(C, job, s, I, prm):
    P, k = C.P, C.k
    T = job.Ts[s]
    L = min(32, T)
    SEG = min(256, T)
    NCH = SEG // L
    base = job.bases[s]
    S8K = [("S32", h) for h in range(8)]
    with contextlib.ExitStack() as ts:
        S32 = C.sb(ts, "S32", [64, 8, 64], F32)
        Sb = C.sb(ts, "Sb", [64, 8, 64], BF16)
        k4, v4 = [C.sb(ts, n, [128, 4, SEG], BF16) for n in ("k4", "v4")]
        r8, k8 = [C.sb(ts, n, [64, 8, SEG], BF16) for n in ("r8", "k8")]
        el = C.sb(ts, "el", [64, 8, NCH], F32)
        yT8 = C.sb(ts, "yT8", [64, 8, SEG], F32)
        lb, oml, noml, ng8 = prm["lb"], prm["oml"], prm["noml"], prm["ng8"]
        pk_ = ["hgprm"]
        if job.hgrn_s0[s] is None:
            P.memset("pool", S32[:], 0.0, writes=S8K)
        else:
            P.dma(S32[:], job.hgrn_s0[s].rearrange("h k v -> k h v"), writes=S8K)
        P.copy("act", Sb[:], S32[:], reads=S8K, writes=["Sb"])
        P.flush()
        rm = k["rmask%d" % L]
        for t0 in range(0, T, SEG):
            col0 = base + t0
            with contextlib.ExitStack() as tp:
                z4 = C.sb(tp, "z4", [128, 12, SEG], F32)
                f4 = lambda nm: C.sb(tp, nm, [128, 4, SEG], F32)
                qf, sgf, lw, kin, cc, epos, eneg = [f4(n) for n in ("qf", "sgf", "lw", "kin", "cc", "epos", "eneg")]
                rT4 = C.sb(tp, "rT4", [128, 4, SEG], BF16)
                zk = [("zT", job.name, nb, i) for nb in range(12, 24) for i in range(col0 // 512, (col0 + SEG - 1) // 512 + 1)]
                P.dma(z4[:], job.zT[1536:3072, col0:col0 + SEG].rearrange("(c p) t -> p c t", p=128), reads=zk, writes=["z4"])
                P.act(qf[:], z4[:, 0:4, :], AF.Silu, reads=["z4"], writes=["qf"])
                P.act(sgf[:], z4[:, 4:8, :], AF.Sigmoid, reads=["z4"], writes=["sgf"])
                for blk in range(4):
                    P.ts("dve", lw[:, blk, :], sgf[:, blk, :], oml[:, blk:blk + 1], lb[:, blk:blk + 1], ALU.mult, ALU.add, reads=["sgf"] + pk_, writes=["lw"])
                    P.ts("dve", kin[:, blk, :], sgf[:, blk, :], noml[:, blk:blk + 1], oml[:, blk:blk + 1], ALU.mult, ALU.add, reads=["sgf"] + pk_, writes=["kin"])
                P.act(lw[:], lw[:], AF.Ln, reads=["lw"], writes=["lw"])
                for blk in range(4):
                    P.op("dve", lambda e, blk=blk: e.tensor_tensor_scan(cc[:, blk, :], rm[:, 0:SEG], lw[:, blk, :], 0.0, ALU.mult, ALU.add),
                         reads=["lw", "rmask%d" % L], writes=["cc"])
                P.act(epos[:], cc[:], AF.Exp, reads=["cc"], writes=["epos"])
                P.act(eneg[:], cc[:], AF.Exp, scale=-1.0, reads=["cc"], writes=["eneg"])
                P.tt("dve", rT4[:], qf[:], epos[:], ALU.mult, reads=["qf", "epos"], writes=["rT4"])
                P.tt("dve", k4[:], kin[:], eneg[:], ALU.mult, reads=["kin", "eneg"], writes=["k4"])
                P.copy("pool", v4[:], z4[:, 8:12, :], reads=["z4"], writes=["v4"])
                to8(P, r8, "r8", rT4, "rT4")
                to8(P, k8, "k8", k4, "k4")
                ec = epos[:, :, :].rearrange("p a (c l) -> p a c l", l=L)
                e4 = el[:, :, :].rearrange("p (a e) c -> p a e c", e=2)
                ecomp = C.sb(tp, "ecomp", [128, 4, NCH], F32)
                P.copy("pool", ecomp[:], ec[:, :, :, L - 1], reads=["epos"], writes=["ecomp"])
                for e in range(2):
                    P.dma(e4[:, :, e, :], ecomp[e * 64:(e + 1) * 64, :, :], reads=["ecomp"], writes=["el"], allow_slow_non_contiguous=True)
                P.flush()
            with contextlib.ExitStack() as tsc:
                chunk_scan2(C, tsc, SEG, L, False, dict(r=(r8, "r8"), k=(k8, "k8")), dict(k=(k4, "k4"), v=(v4, "v4")),
                            (el, "el"), S32, Sb, (yT8, "yT8"))
                P.flush()
            with contextlib.ExitStack() as tq:
                psQ = Rot(C, tq, "hg_psQ", [64, 4, SEG], F32, 3 if SEG > 128 else 6, psum=True)
                yb8 = C.sb(tq, "yb8", [64, 8, SEG], BF16)
                rs8 = C.sb(tq, "rs8", [64, 8, SEG], F32)
                zg8 = C.sb(tq, "zg8", [64, 8, SEG], F32)
                o8 = C.sb(tq, "o8", [64, 8, SEG], BF16)
                one64 = k["onesB"][0:64, 0:64]
                zkg = [("zT", job.name, nb, i) for nb in range(24, 28) for i in range(col0 // 512, (col0 + SEG - 1) // 512 + 1)]
                P.dma(zg8[:], job.zT[3072:3584, col0:col0 + SEG].rearrange("(h k) t -> k h t", k=64), reads=zkg, writes=["zg8"])
                P.act(zg8[:], zg8[:], AF.Silu, reads=["zg8"], writes=["zg8"])
                P.act(yb8[:], yT8[:], AF.Square, reads=["yT8"], writes=["yb8"])
                for half in range(2):
                    hs = slice(half * 4, half * 4 + 4)
                    ps, pk = psQ.next()
                    for i in range(4):
                        P.mm(ps[:, i, :], one64, yb8[:, half * 4 + i, :], reads=["yb8", "onesB"], writes=[pk])
                    P.act(rs8[:, hs, :], ps[:, :, :], AF.Ln, bias=k["eps_rms"][0:64, 0:1], scale=1.0 / 64, reads=[pk, "epsv"], writes=[("rs8", half)])
                P.act(rs8[:], rs8[:], AF.Exp, scale=-0.5, reads=[("rs8", 0), ("rs8", 1)], writes=["rs8r"])
                P.tt("dve", rs8[:], rs8[:], yT8[:], ALU.mult, reads=["rs8r", "yT8"], writes=["rs8y"])
                for h in range(8):
                    P.stt(o8[:, h, :], rs8[:, h, :], ng8[:, h:h + 1], zg8[:, h, :], ALU.mult, ALU.mult, reads=["rs8y", "zg8"] + pk_, writes=[("o8", h)])
                P.dma(job.mixT[512:1024, col0:col0 + SEG].rearrange("(h k) t -> k h t", k=64), o8[:], reads=[("o8", h) for h in range(8)],
                      writes=[("mixT", job.name, 4 + c, col0 // 512, hh) for c in range(4) for hh in range(2)], q="pool")
                P.flush()
        P.dma(job.hgrn_out[s].rearrange("h k v -> k h v"), S32[:], reads=S8K, writes=[("hgst", job.name, s)], q="pool")
        P.flush()
```
